# Optimizing a Trainium2 kernel written in Bass

```python
import jax, jax.numpy as jnp
from jax import lax
import numpy as np

D_MODEL = 1024
BATCH = 8
SEQ = 2048
DEPTH = 1

CHUNK = 64
PLE_DIM = 256
D_FF = 2816
EPS = 1e-6
HG_HEADS = 8
HG_DK = 128
HG_DV = D_MODEL // HG_HEADS
HG_KDIM = HG_HEADS * HG_DK
HG_VDIM = HG_HEADS * HG_DV
HG_BLOCK = 16
M_DINNER = 2 * D_MODEL
M_HEADDIM = 64
M_HEADS = M_DINNER // M_HEADDIM
M_STATE = 128
M_GROUPS = 4
M_CONV = 4
M_CONV_DIM = M_DINNER + 2 * M_GROUPS * M_STATE
N_BRANCH = 2

kernel_name = 'hgrn2_mamba2_gated_macaron_block'


def rms_norm(x, g):
    xf = x.astype(jnp.float32)
    y = xf * lax.rsqrt(jnp.mean(xf * xf, axis=-1, keepdims=True) + EPS)
    return (y * g.astype(jnp.float32)).astype(x.dtype)


def swiglu(x, w13, w2):
    gate, up = jnp.split(x @ w13, 2, axis=-1)
    return (jax.nn.silu(gate) * up) @ w2


def causal_depthwise_conv(x, w, b):
    c = x.shape[-1]
    k = w.shape[0]
    y = lax.conv_general_dilated(x, w[:, None, :].astype(x.dtype), window_strides=(1,),
                                 padding=[(k - 1, 0)], dimension_numbers=('NWC', 'WIO', 'NWC'),
                                 feature_group_count=c)
    return y + b.astype(x.dtype)


def hgrn2_recurrence(q, f_raw, v, lb):
    f32 = jnp.float32
    b_, s_ = q.shape[:2]
    nb = s_ // HG_BLOCK
    f = lb + (1.0 - lb) * jax.nn.sigmoid(f_raw.astype(f32))
    k = 1.0 - f

    def blocks(t, d):
        return t.astype(f32).reshape(b_, nb, HG_BLOCK, HG_HEADS, d)

    q = blocks(q, HG_DK) * (HG_DK ** -0.5)
    k = blocks(k, HG_DK)
    v = blocks(v, HG_DV)
    g_cum = jnp.cumsum(blocks(jnp.log(f), HG_DK), axis=2)
    g_last = g_cum[:, :, -1]
    q_dec = q * jnp.exp(g_cum)
    k_inv = k * jnp.exp(-g_cum)
    k_end = k * jnp.exp(g_last[:, :, None] - g_cum)
    causal = jnp.tril(jnp.ones((HG_BLOCK, HG_BLOCK), dtype=bool))
    att = jnp.einsum('bnthd,bnshd->bnhts', q_dec, k_inv)
    att = jnp.where(causal, att, 0.0)
    o_intra = jnp.einsum('bnhts,bnshv->bnthv', att, v)

    def step(state, blk):
        q_b, k_b, v_b, decay_b = blk
        o_b = jnp.einsum('bthd,bhdv->bthv', q_b, state)
        state = state * decay_b[..., None] + jnp.einsum('bshd,bshv->bhdv', k_b, v_b)
        return state, o_b

    state0 = jnp.zeros((b_, HG_HEADS, HG_DK, HG_DV), f32)
    xs = (jnp.moveaxis(q_dec, 1, 0), jnp.moveaxis(k_end, 1, 0),
          jnp.moveaxis(v, 1, 0), jnp.moveaxis(jnp.exp(g_last), 1, 0))
    _, o_inter = lax.scan(step, state0, xs)
    o = o_intra + jnp.moveaxis(o_inter, 0, 1)
    return o.reshape(b_, s_, HG_HEADS, HG_DV)


def ssd_scan(xh, dt, a_neg, b_in, c_in):
    f32 = jnp.float32
    b_, s_ = xh.shape[:2]
    nc = s_ // CHUNK
    r = M_HEADS // M_GROUPS
    x = (xh.astype(f32) * dt[..., None]).reshape(b_, nc, CHUNK, M_GROUPS, r, M_HEADDIM)
    a = (dt * a_neg).reshape(b_, nc, CHUNK, M_GROUPS, r)
    bc = b_in.astype(f32).reshape(b_, nc, CHUNK, M_GROUPS, M_STATE)
    cc = c_in.astype(f32).reshape(b_, nc, CHUNK, M_GROUPS, M_STATE)
    acs = jnp.cumsum(a, axis=2)
    causal = jnp.tril(jnp.ones((CHUNK, CHUNK), dtype=bool))
    seg = acs[:, :, :, None] - acs[:, :, None, :]
    decay = jnp.exp(jnp.where(causal[:, :, None, None], seg, -jnp.inf))
    cb = jnp.einsum('bctgn,bcsgn->bctsg', cc, bc)
    y_intra = jnp.einsum('bctsgr,bcsgrp->bctgrp', decay * cb[..., None], x)
    acs_last = acs[:, :, -1]
    w_end = jnp.exp(acs_last[:, :, None] - acs)
    w_start = jnp.exp(acs)

    def step(state, blk):
        c_b, b_b, x_b, ws_b, we_b, dl_b = blk
        y_b = jnp.einsum('btgn,bgrpn->btgrp', c_b, state) * ws_b[..., None]
        state = state * dl_b[..., None, None] + jnp.einsum('bsgn,bsgr,bsgrp->bgrpn', b_b, we_b, x_b)
        return state, y_b

    state0 = jnp.zeros((b_, M_GROUPS, r, M_HEADDIM, M_STATE), f32)
    xs = tuple(jnp.moveaxis(t, 1, 0) for t in (cc, bc, x, w_start, w_end, jnp.exp(acs_last)))
    _, y_inter = lax.scan(step, state0, xs)
    y = y_intra + jnp.moveaxis(y_inter, 0, 1)
    return y.reshape(b_, s_, M_HEADS, M_HEADDIM)


def setup_inputs(seed: int = 0) -> dict:
    key = jax.random.key(seed)
    ks = jax.random.split(key, 32)
    f32 = jnp.float32
    d_in = 2 * HG_KDIM + 2 * HG_VDIM + M_DINNER + M_CONV_DIM + M_HEADS + N_BRANCH * D_MODEL

    def nrm(k, shape, fan_in):
        return jax.random.normal(k, shape, f32) * (fan_in ** -0.5)

    def gain(k, shape):
        return 1.0 + 0.02 * jax.random.normal(k, shape, f32)

    dt0 = jnp.exp(jax.random.uniform(ks[9], (DEPTH, M_HEADS), f32, np.log(1e-3), np.log(1e-1)))
    return {
        'x': jax.random.normal(ks[0], (BATCH, SEQ, D_MODEL), f32),
        'p': jax.random.normal(ks[1], (DEPTH, BATCH, SEQ, PLE_DIM), f32),
        'ffn1_norm': gain(ks[2], (DEPTH, D_MODEL)),
        'ffn1_w13': nrm(ks[3], (DEPTH, D_MODEL, 2 * D_FF), D_MODEL),
        'ffn1_w2': nrm(ks[4], (DEPTH, D_FF, D_MODEL), D_FF),
        'mix_norm': gain(ks[5], (DEPTH, D_MODEL)),
        'w_in': nrm(ks[6], (DEPTH, D_MODEL, d_in), D_MODEL),
        'conv_w': 0.5 * jax.random.normal(ks[7], (DEPTH, M_CONV, M_CONV_DIM), f32),
        'conv_b': 0.02 * jax.random.normal(ks[8], (DEPTH, M_CONV_DIM), f32),
        'dt_bias': dt0 + jnp.log(-jnp.expm1(-dt0)),
        'a_log': jnp.log(jax.random.uniform(ks[10], (DEPTH, M_HEADS), f32, 1.0, 16.0)),
        'd_skip': gain(ks[11], (DEPTH, M_HEADS)),
        'ssm_norm': gain(ks[12], (DEPTH, M_DINNER)),
        'hg_lb': 0.1 * jax.random.normal(ks[13], (DEPTH + 1, HG_KDIM), f32),
        'hg_norm': gain(ks[14], (DEPTH, HG_VDIM)),
        'w_hg_out': nrm(ks[15], (DEPTH, HG_VDIM, D_MODEL), HG_VDIM),
        'w_ssm_out': nrm(ks[16], (DEPTH, M_DINNER, D_MODEL), M_DINNER),
        'w_out': nrm(ks[17], (DEPTH, D_MODEL, D_MODEL), D_MODEL),
        'ffn2_norm': gain(ks[18], (DEPTH, D_MODEL)),
        'ffn2_w13': nrm(ks[19], (DEPTH, D_MODEL, 2 * D_FF), D_MODEL),
        'ffn2_w2': nrm(ks[20], (DEPTH, D_FF, D_MODEL), D_FF),
        'ple_norm': gain(ks[21], (DEPTH, D_MODEL)),
        'w_ple_gate': nrm(ks[22], (DEPTH, D_MODEL, D_MODEL), D_MODEL),
        'w_ple_proj': nrm(ks[23], (DEPTH, PLE_DIM, D_MODEL), PLE_DIM),
        'final_norm': gain(ks[24], (D_MODEL,)),
    }


def reference(x, p, ffn1_norm, ffn1_w13, ffn1_w2, mix_norm, w_in, conv_w, conv_b, dt_bias,
              a_log, d_skip, ssm_norm, hg_lb, hg_norm, w_hg_out, w_ssm_out, w_out,
              ffn2_norm, ffn2_w13, ffn2_w2, ple_norm, w_ple_gate, w_ple_proj, final_norm):
    f32 = jnp.float32
    b_, s_, _ = x.shape
    sizes = (HG_KDIM, HG_KDIM, HG_VDIM, HG_VDIM, M_DINNER, M_CONV_DIM, M_HEADS, N_BRANCH * D_MODEL)
    split_at = np.cumsum(sizes)[:-1].tolist()
    lower_bounds = jnp.cumsum(jax.nn.softmax(hg_lb.astype(f32), axis=0), axis=0)
    h = x
    for l in range(DEPTH):
        h = h + 0.5 * swiglu(rms_norm(h, ffn1_norm[l]), ffn1_w13[l], ffn1_w2[l])
        n = rms_norm(h, mix_norm[l])
        hq, hf, hi, hgate, mz, mxbc, mdt, br_gates = jnp.split(n @ w_in[l], split_at, axis=-1)
        o_hg = hgrn2_recurrence(hq, hf, hi, lower_bounds[l])
        o_hg = rms_norm(o_hg, hg_norm[l].reshape(HG_HEADS, HG_DV)).reshape(b_, s_, HG_VDIM)
        o_hg = (o_hg * jax.nn.silu(hgate.astype(f32))).astype(h.dtype)
        xbc = jax.nn.silu(causal_depthwise_conv(mxbc, conv_w[l], conv_b[l]))
        xs, bm, cm = jnp.split(xbc, [M_DINNER, M_DINNER + M_GROUPS * M_STATE], axis=-1)
        xh = xs.reshape(b_, s_, M_HEADS, M_HEADDIM)
        dt = jax.nn.softplus(mdt.astype(f32) + dt_bias[l].astype(f32))
        a_neg = -jnp.exp(a_log[l].astype(f32))
        y = ssd_scan(xh, dt, a_neg, bm.reshape(b_, s_, M_GROUPS, M_STATE),
                     cm.reshape(b_, s_, M_GROUPS, M_STATE))
        y = y + d_skip[l].astype(f32)[:, None] * xh.astype(f32)
        y = y.reshape(b_, s_, M_DINNER) * jax.nn.silu(mz.astype(f32))
        y = rms_norm(y.reshape(b_, s_, M_GROUPS, M_DINNER // M_GROUPS),
                     ssm_norm[l].reshape(M_GROUPS, M_DINNER // M_GROUPS))
        y = y.reshape(b_, s_, M_DINNER).astype(h.dtype)
        gate_a, gate_b = jnp.split(jax.nn.sigmoid(br_gates), 2, axis=-1)
        mixed = gate_a * (o_hg @ w_hg_out[l]) + gate_b * (y @ w_ssm_out[l])
        h = h + mixed @ w_out[l]
        h = h + 0.5 * swiglu(rms_norm(h, ffn2_norm[l]), ffn2_w13[l], ffn2_w2[l])
        ple_gate = jax.nn.sigmoid(rms_norm(h, ple_norm[l]) @ w_ple_gate[l])
        h = h + ple_gate * (p[l].astype(h.dtype) @ w_ple_proj[l])
    return rms_norm(h, final_norm)
```

```python
import contextlib
import numpy as np
import concourse.bass as bass
import concourse.mybir as mybir
from concourse.bass_utils import run_bass_kernel_spmd

F32 = mybir.dt.float32
BF16 = mybir.dt.bfloat16
AF = mybir.ActivationFunctionType
ALU = mybir.AluOpType

PE, ACT, DVE, POOL, SP = 'tensor', 'scalar', 'vector', 'gpsimd', 'sync'
ENGS = (PE, ACT, DVE, POOL, SP)
CENGS = (PE, ACT, DVE)

D = 1024
T = 2048
TB = 1024
NBLK = T // TB
DFF = 2816
NFC = DFF // 128
DIN = 11296
EPS = 1e-6
OFF_Q, OFF_F, OFF_V, OFF_G = 0, 1024, 2048, 3072
OFF_MZ = 4096
OFF_XBC = 6144
OFF_DT = 9216
OFF_BRG = 9248


class Dep:
    __slots__ = ('name', 'w', 'r')

    def __init__(self, name=''):
        self.name = name
        self.w = {}
        self.r = {}


def mkdeps(n):
    return [Dep() for _ in range(n)]


class Op:
    __slots__ = ('eng', 'fn', 'deps', 'idx', 'semkey', 'count', 'sig', 'is_dma')


class Prog:
    def __init__(self, nc, block):
        self.nc = nc
        self.block = block
        self.ops = []
        self.flushed = 0
        self.cnt = {e: 0 for e in ENGS}
        self.pending = {e: [] for e in ENGS}
        self.dma_counts = {}
        self.dsem = {}
        self.esem = {e: nc.alloc_semaphore('c_' + e) for e in (PE, ACT, DVE, POOL)}
        self.seen = {e: {} for e in ENGS}
        self.last = {e: None for e in ENGS}
        self.scope_dmas = []
        self.nwaits = 0

    def add(self, eng, fn, reads=(), writes=(), sig=True, is_dma=False, semkey=None, accum=False,
            extra_deps=()):
        op = Op()
        op.eng = eng
        op.fn = fn
        op.is_dma = is_dma
        op.idx = len(self.ops)
        op.semkey = semkey
        op.sig = sig
        op.count = None
        ds = set(extra_deps)
        for d in reads:
            ds.update(d.w.values())
        for d in writes:
            ds.update(d.r.values())
            if not accum:
                ds.update(d.w.values())
        op.deps = ds
        if is_dma:
            if semkey not in self.dsem:
                self.dsem[semkey] = self.nc.alloc_semaphore('d_' + str(semkey))
            c = self.dma_counts.get(semkey, 0) + 16
            self.dma_counts[semkey] = c
            op.count = c
            k = ('d', op.idx)
        else:
            k = eng
            if fn is not None:
                if sig:
                    self.cnt[eng] += 1
                    op.count = self.cnt[eng]
                    for p in self.pending[eng]:
                        p.count = op.count
                    self.pending[eng] = []
                else:
                    self.pending[eng].append(op)
                self.last[eng] = op
        for d in reads:
            d.r[k] = op
        for d in writes:
            if not accum:
                d.w = {}
            d.w[k] = op
            d.r = {}
        self.ops.append(op)
        return op

    def dma(self, eng, fn, reads=(), writes=(), semkey=None, accum=False):
        op = self.add(eng, fn, reads, writes, is_dma=True, semkey=semkey, accum=accum)
        if eng == SP:
            self.scope_dmas.append(op)
        return op

    def barrier(self):
        lasts = [self.last[e] for e in CENGS + (POOL,) if self.last[e] is not None]
        dm = list(self.scope_dmas)
        self.scope_dmas = []
        for e in CENGS + (SP,):
            self.add(e, None, extra_deps=[o for o in lasts if o.eng != e] + dm)

    def flush(self):
        ops = self.ops[self.flushed:]
        self.flushed = len(self.ops)
        for e in ENGS:
            assert not self.pending[e], "pending non-signalling ops on %s" % e
            ops_e = [op for op in ops if op.eng == e]
            if not ops_e:
                continue
            getattr(self.block, e)(lambda engh, e=e, ops_e=ops_e: self._run(e, engh, ops_e))

    def _run(self, e, engh, ops_e):
        seen = self.seen[e]
        for op in ops_e:
            need = {}
            for dj in op.deps:
                if dj.is_dma:
                    s = self.dsem[dj.semkey]
                else:
                    if dj.fn is None:
                        continue
                    if dj.eng == PE and e == PE and not op.is_dma and op.fn is not None:
                        continue
                    s = self.esem[dj.eng]
                v = dj.count
                assert v is not None
                key = id(s)
                if v > need.get(key, (None, 0))[1]:
                    need[key] = (s, v)
            for key, (s, v) in need.items():
                if v > seen.get(key, 0):
                    engh.wait_ge(s, v)
                    seen[key] = v
                    self.nwaits += 1
            if op.fn is None:
                continue
            inst = op.fn(engh)
            if op.is_dma:
                inst.then_inc(self.dsem[op.semkey], 16)
            elif op.sig:
                inst.then_inc(self.esem[e], 1)


class KB:
    def __init__(self, nc, block, es, dram, stop_after=None):
        self.nc = nc
        self.P = Prog(nc, block)
        self.es = es
        self.dr = dram
        self.stop_after = stop_after
        self.uid = 0
        self.banks = []
        self.bank_deps = []
        for i in range(8):
            self.banks.append(es.enter_context(nc.psum_tensor("bank%d" % i, [128, 512], F32)))
            self.bank_deps.append(Dep())
        self.bank_next = 0
        self.long_next = 0
        self.copy_rr = 0

    def sb(self, shape, dt, name=None, stack=None):
        self.uid += 1
        name = "%s_%d" % (name or "t", self.uid)
        return (stack or self.es).enter_context(self.nc.sbuf_tensor(name, shape, dt))

    def ps(self, long=False):
        if long:
            i = 6 + self.long_next
            self.long_next = (self.long_next + 1) % 2
        else:
            i = self.bank_next
            self.bank_next = (i + 1) % 6
        return self.banks[i], self.bank_deps[i]

    def mm(self, out, lhsT, rhs, start, stop, reads, writes):
        self.P.add(PE, lambda e: e.matmul(out, lhsT=lhsT, rhs=rhs, start=start, stop=stop),
                   reads, writes, sig=True)

    def tr(self, out, in_, ident, reads, writes, sig=True):
        self.P.add(PE, lambda e: e.transpose(out=out, in_=in_, identity=ident), reads, writes, sig=True)

    def act(self, out, in_, func, reads, writes, bias=None, scale=None, accum=None):
        kw = {}
        if bias is not None:
            kw['bias'] = bias
        if scale is not None:
            kw['scale'] = scale
        if accum is not None:
            kw['accum_out'] = accum
        self.P.add(ACT, lambda e: e.activation(out=out, in_=in_, func=func, **kw), reads, writes)

    def tt(self, out, in0, in1, op, reads, writes, eng=DVE):
        self.P.add(eng, lambda e: e.tensor_tensor(out=out, in0=in0, in1=in1, op=op), reads, writes)

    def stt(self, out, in0, scalar, in1, op0, op1, reads, writes):
        self.P.add(DVE, lambda e: e.scalar_tensor_tensor(out=out, in0=in0, scalar=scalar, in1=in1,
                                                         op0=op0, op1=op1), reads, writes)

    def ts(self, out, in0, s1, op0, reads, writes, s2=None, op1=None, eng=DVE):
        if op1 is None:
            self.P.add(eng, lambda e: e.tensor_scalar(out=out, in0=in0, scalar1=s1, scalar2=None, op0=op0),
                       reads, writes)
        else:
            self.P.add(eng, lambda e: e.tensor_scalar(out=out, in0=in0, scalar1=s1, scalar2=s2, op0=op0,
                                                      op1=op1), reads, writes)

    def cp(self, out, in_, reads, writes, eng=None):
        if eng is None:
            eng = (ACT, DVE)[self.copy_rr % 2]
            self.copy_rr += 1
        if eng == ACT:
            self.act(out, in_, AF.Copy, reads, writes)
        else:
            self.P.add(eng, lambda e: e.tensor_copy(out=out, in_=in_), reads, writes)

    def recip(self, out, in_, reads, writes):
        self.P.add(DVE, lambda e: e.reciprocal(out=out, in_=in_), reads, writes)

    def memset(self, ap, val, writes, eng=DVE):
        self.P.add(eng, lambda e: e.memset(ap, val), (), writes)

    def end_scope(self):
        self.P.barrier()
        self.P.flush()

    def init_wpool(self, nslots=3, nel=4096):
        self.wslots = [self.sb([128, nel], BF16, "wslot") for _ in range(nslots)]
        self.wdeps = mkdeps(nslots)
        self.wnext = 0
        self.wnel = nel

    def wload(self, parts):
        i = self.wnext
        self.wnext = (i + 1) % len(self.wslots)
        slot = self.wslots[i]
        dep = self.wdeps[i]
        views = []
        off = 0
        first = True
        for (src, nk, ncols) in parts:
            n = nk * ncols
            assert off + n <= self.wnel
            v = slot[:, off:off + n].rearrange("p (k c) -> p k c", k=nk)
            s = src.rearrange("(k p) c -> p k c", p=128)
            self.P.dma(POOL, lambda e, v=v, s=s: e.dma_start(out=v, in_=s), (), [dep],
                       semkey='w%d' % i, accum=not first)
            first = False
            views.append(v)
            off += n
        return views, dep

    def init(self):
        dr = self.dr
        P = self.P
        self.cf = self.sb([128, 7, 128], F32, "cf")
        self.cb16 = self.sb([128, 4, 128], BF16, "cb16")
        self.d_c = Dep()
        cdep = self.d_c
        P.dma(SP, lambda e: e.dma_start(out=self.cf[:], in_=dr['consts']), (), [cdep], semkey='c', accum=True)
        self.ident_f = self.cf[:, 0, :]
        self.triL64 = self.cf[:, 1, :]
        self.triU64 = self.cf[:, 2, :]
        self.triL128 = self.cf[:, 3, :]
        self.ones_f = self.cf[:, 4, :]
        self.gn = self.sb([128, 40], F32, "gn")
        P.dma(SP, lambda e: e.dma_start(out=self.gn[:], in_=dr['gains']), (), [cdep], semkey='c', accum=True)
        self.hgn = self.sb([128, 8], F32, "hgn")
        P.dma(SP, lambda e: e.dma_start(out=self.hgn[:], in_=dr['hgn']), (), [cdep], semkey='c', accum=True)
        self.cw = self.sb([128, 24, 4], F32, "cw")
        P.dma(SP, lambda e: e.dma_start(out=self.cw[:], in_=dr['cw'].rearrange("p (a b) -> p a b", a=24)),
              (), [cdep], semkey='c', accum=True)
        self.cbp = self.sb([128, 24], F32, "cbp")
        P.dma(SP, lambda e: e.dma_start(out=self.cbp[:], in_=dr['cbp']), (), [cdep], semkey='c', accum=True)
        self.dtb = self.sb([128, 32], F32, "dtb")
        self.aneg = self.sb([128, 32], F32, "aneg")
        self.dsk = self.sb([128, 32], F32, "dsk")
        P.dma(SP, lambda e: e.dma_start(out=self.dtb[:], in_=dr['dt_bias'].partition_broadcast(128)),
              (), [cdep], semkey='c', accum=True)
        P.dma(SP, lambda e: e.dma_start(out=self.aneg[:], in_=dr['a_log'].partition_broadcast(128)),
              (), [cdep], semkey='c', accum=True)
        P.dma(SP, lambda e: e.dma_start(out=self.dsk[:], in_=dr['d_skip'].partition_broadcast(128)),
              (), [cdep], semkey='c', accum=True)
        self.oml = self.sb([128, 1024], F32, "oml")
        self.S_hg = self.sb([128, 8, 128], F32, "S_hg")
        self.d_S_hg = mkdeps(8)
        self.S_ssd = self.sb([128, 2048], F32, "S_ssd")
        self.S_ssd_bf = self.sb([128, 2048], BF16, "S_ssd_bf")
        self.d_S_ssd = mkdeps(4)
        self.d_S_ssd_bf = mkdeps(4)
        self.xtail = self.sb([128, 24, 3], BF16, "xtail")
        self.d_xtail = mkdeps(24)
        self.hT = self.sb([128, 8, TB], F32, "hT")
        self.d_h = [[Dep() for _ in range(2)] for _ in range(8)]
        self.nT = self.sb([128, 8, TB], BF16, "nT")
        self.d_n = [[Dep() for _ in range(2)] for _ in range(8)]
        self.big = self.sb([128, 24, TB], BF16, "big")
        self.d_big = [[Dep() for _ in range(2)] for _ in range(24)]
        self.init_wpool()
        self.d_const = Dep()
        with contextlib.ExitStack() as st:
            lbt = self.sb([128, 2, 1024], F32, "lbt", st)
            P.dma(SP, lambda e: e.dma_start(out=lbt[:, 0, :], in_=dr['hg_lb'][0].partition_broadcast(128)),
                  (), [cdep], semkey='c', accum=True)
            P.dma(SP, lambda e: e.dma_start(out=lbt[:, 1, :], in_=dr['hg_lb'][1].partition_broadcast(128)),
                  (), [cdep], semkey='c', accum=True)
            dc = self.d_const
            self.tt(lbt[:, 0, :], lbt[:, 1, :], lbt[:, 0, :], ALU.subtract, [cdep], [dc])
            self.act(self.oml[:], lbt[:, 0, :], AF.Sigmoid, [dc], [dc])
            self.cp(self.cb16[:, 0, :], self.ident_f, [cdep], [dc], eng=DVE)
            self.cp(self.cb16[:, 1, :], self.ones_f, [cdep], [dc], eng=DVE)
            self.cp(self.cb16[:, 2, :], self.cf[:, 5, :], [cdep], [dc], eng=DVE)
            self.negm_b = self.cb16[:, 2, :]
            self.cp(self.cb16[:, 3, :], self.cf[:, 6, :], [cdep], [dc], eng=DVE)
            self.e0_b = self.cb16[:, 3, :]
            self.act(self.aneg[:], self.aneg[:], AF.Exp, [cdep], [dc])
            self.ts(self.aneg[:], self.aneg[:], -1.0, ALU.mult, [dc], [dc])
            self.memset(self.S_hg[:], 0.0, self.d_S_hg)
            self.memset(self.S_ssd[:], 0.0, self.d_S_ssd)
            self.memset(self.S_ssd_bf[:], 0.0, self.d_S_ssd_bf)
            self.memset(self.xtail[:], 0.0, self.d_xtail)
            if 'dbg' in dr:
                self.memset(self.big[:], 0.0, [d for row in self.d_big for d in row])
            self.ident_b = self.cb16[:, 0, :]
            self.ones_b = self.cb16[:, 1, :]
            self.end_scope()

    def load_x(self, blk):
        dr = self.dr
        with contextlib.ExitStack() as st:
            xs = [self.sb([128, 8, 128], F32, "xstage", st) for _ in range(2)]
            dxs = mkdeps(2)
            for kc in range(8):
                s = xs[kc % 2]
                ds = dxs[kc % 2]
                src = dr['x'][blk * TB:(blk + 1) * TB, kc * 128:(kc + 1) * 128].rearrange("(t p) c -> p t c", p=128)
                self.P.dma(SP, lambda e, s=s, src=src: e.dma_start(out=s[:], in_=src), (), [ds],
                           semkey='xs%d' % (kc % 2))
                for half in range(2):
                    bank, bd = self.ps()
                    for q in range(4):
                        tt_ = half * 4 + q
                        self.tr(bank[:, q * 128:(q + 1) * 128], s[:, tt_, :], self.ident_f, [ds, self.d_c], [bd],
                                sig=(q == 3))
                    self.cp(self.hT[:, kc, half * 512:(half + 1) * 512], bank[:], [bd], [self.d_h[kc][half]])
            self.end_scope()

    def norm(self, gi, final=False):
        with contextlib.ExitStack() as st:
            sq = [self.sb([128, TB], BF16, "sq", st) for _ in range(2)]
            dsq = mkdeps(2)
            rstd = self.sb([128, TB], F32, "rstd", st)
            drs = mkdeps(2)
            b0, bd0 = self.ps()
            b1, bd1 = self.ps()
            bks = ((b0, bd0), (b1, bd1))
            for kc in range(8):
                s = sq[kc % 2]
                ds = dsq[kc % 2]
                self.act(s[:], self.hT[:, kc, :], AF.Square, self.d_h[kc], [ds])
                for tb in range(2):
                    self.mm(bks[tb][0][:], self.ones_b, s[:, tb * 512:(tb + 1) * 512], kc == 0, kc == 7,
                            [ds, self.d_const], [bks[tb][1]])
            for tb in range(2):
                sl = slice(tb * 512, (tb + 1) * 512)
                self.act(rstd[:, sl], bks[tb][0][:], AF.Sqrt, [bks[tb][1]], [drs[tb]], bias=EPS, scale=1.0 / D)
                self.recip(rstd[:, sl], rstd[:, sl], [drs[tb]], [drs[tb]])
            for kc in range(8):
                for tb in range(2):
                    sl = slice(tb * 512, (tb + 1) * 512)
                    if final:
                        self.stt(self.hT[:, kc, sl], self.hT[:, kc, sl], self.gn[:, gi * 8 + kc:gi * 8 + kc + 1],
                                 rstd[:, sl], ALU.mult, ALU.mult, [self.d_h[kc][tb], drs[tb], self.d_c],
                                 [self.d_h[kc][tb]])
                    else:
                        self.stt(self.nT[:, kc, sl], self.hT[:, kc, sl], self.gn[:, gi * 8 + kc:gi * 8 + kc + 1],
                                 rstd[:, sl], ALU.mult, ALU.mult, [self.d_h[kc][tb], drs[tb], self.d_c],
                                 [self.d_n[kc][tb]])
            self.end_scope()

    def ffn(self, w13, w2):
        with contextlib.ExitStack() as st:
            sg = [self.sb([128, 512], F32, "sg", st) for _ in range(2)]
            dsg = mkdeps(2)
            it = 0
            for g2 in range(NFC // 2):
                (wg, wu), wd = self.wload([(w13[:, g2 * 256:(g2 + 1) * 256], 8, 256),
                                           (w13[:, DFF + g2 * 256:DFF + (g2 + 1) * 256], 8, 256)])
                for j in range(2):
                    fc = g2 * 2 + j
                    for tb in range(2):
                        sl = slice(tb * 512, (tb + 1) * 512)
                        pg, dg = self.ps()
                        pu, du = self.ps()
                        for kc in range(8):
                            self.mm(pg[:], wg[:, kc, j * 128:(j + 1) * 128], self.nT[:, kc, sl], kc == 0, kc == 7,
                                    [wd, self.d_n[kc][tb]], [dg])
                        for kc in range(8):
                            self.mm(pu[:], wu[:, kc, j * 128:(j + 1) * 128], self.nT[:, kc, sl], kc == 0, kc == 7,
                                    [wd, self.d_n[kc][tb]], [du])
                        s = sg[it % 2]
                        ds = dsg[it % 2]
                        it += 1
                        self.act(s[:], pg[:], AF.Silu, [dg], [ds])
                        self.tt(self.big[:, fc, sl], s[:], pu[:], ALU.mult, [ds, du], [self.d_big[fc][tb]])
            for dc in range(8):
                (wa,), wda = self.wload([(w2[:, dc * 128:(dc + 1) * 128], NFC, 128)])
                for tb in range(2):
                    sl = slice(tb * 512, (tb + 1) * 512)
                    po, do = self.ps()
                    for fc in range(NFC):
                        self.mm(po[:], wa[:, fc, :], self.big[:, fc, sl], fc == 0, fc == NFC - 1,
                                [wda, self.d_big[fc][tb]], [do])
                    self.stt(self.hT[:, dc, sl], po[:], 0.5, self.hT[:, dc, sl], ALU.mult, ALU.add,
                             [do, self.d_h[dc][tb]], [self.d_h[dc][tb]])
            self.end_scope()

    def hgrn(self, blk):
        w_in = self.dr['w_in']
        nT, dn = self.nT, self.d_n
        with contextlib.ExitStack() as st:
            sbs = lambda shape, dt, name: self.sb(shape, dt, name, st)
            dd2 = lambda: [mkdeps(2) for _ in range(2)]
            k_tm = sbs([128, 8, 128], F32, "k_tm"); d_k = mkdeps(2)
            logf = sbs([128, 8, 128], F32, "logf"); d_lf = mkdeps(2)
            qs = sbs([128, TB], BF16, "qs"); d_qs = mkdeps(2)
            eGi = sbs([128, TB], BF16, "eGi"); d_eGi = mkdeps(2)
            eGe = sbs([128, 8, 128], BF16, "eGe"); d_eGe = mkdeps(2)
            k_inv = sbs([128, TB], BF16, "k_inv"); d_ki = mkdeps(2)
            v_bf = [sbs([128, 8, 128], BF16, "v_bf") for _ in range(2)]; d_v = dd2()
            sgate = [sbs([128, TB], BF16, "sgate") for _ in range(2)]; d_sg = dd2()
            eG = [sbs([128, TB], F32, "eG") for _ in range(2)]; d_eG = dd2()
            q_dec = [sbs([128, TB], BF16, "q_dec") for _ in range(2)]; d_qd = dd2()
            k_end = [sbs([128, 8, 128], BF16, "k_end") for _ in range(2)]; d_ke = dd2()
            att = [sbs([128, 8, 128], BF16, "att") for _ in range(2)]; d_att = dd2()
            osq = [sbs([128, 512], BF16, "osq") for _ in range(2)]; d_osq = mkdeps(2)
            sd = [sbs([128, 512], F32, "sd") for _ in range(2)]; d_sd = mkdeps(2)
            tmp = [sbs([128, 512], F32, "tmp") for _ in range(2)]; d_tmp = mkdeps(2)
            Sbf = [sbs([128, 128], BF16, "Sbf") for _ in range(4)]; d_Sbf = mkdeps(4)
            self._sbi = 0
            v4 = lambda bank: bank[:].rearrange("p (a b) -> p a b", a=4)

            hw = {}

            def hload(j):
                hw[j] = self.wload([(w_in[:, OFF_Q + j * 128:OFF_Q + (j + 1) * 128], 8, 128),
                                    (w_in[:, OFF_F + j * 128:OFF_F + (j + 1) * 128], 8, 128),
                                    (w_in[:, OFF_V + j * 128:OFF_V + (j + 1) * 128], 8, 128),
                                    (w_in[:, OFF_G + j * 128:OFF_G + (j + 1) * 128], 8, 128)])

            hload(0)

            def front(j):
                p = j % 2
                if j + 1 < 8:
                    hload(j + 1)
                (wq, wf, wv, wg), wd = hw[j]
                for tb in range(2):
                    sl = slice(tb * 512, (tb + 1) * 512)
                    pq, dq = self.ps()
                    for kc in range(8):
                        self.mm(pq[:], wq[:, kc, :], nT[:, kc, sl], kc == 0, kc == 7, [wd, dn[kc][tb]], [dq])
                    self.act(qs[:, sl], pq[:], AF.Copy, [dq], [d_qs[tb]], scale=float(128 ** -0.5))
                    yield
                    pg, dg = self.ps()
                    for kc in range(8):
                        self.mm(pg[:], wg[:, kc, :], nT[:, kc, sl], kc == 0, kc == 7, [wd, dn[kc][tb]], [dg])
                    self.act(sgate[p][:, sl], pg[:], AF.Silu, [dg], [d_sg[p][tb]])
                    yield
                for hf in range(2):
                    sl = slice(hf * 512, (hf + 1) * 512)
                    h4 = slice(hf * 4, hf * 4 + 4)
                    pf, df = self.ps()
                    for q in range(4):
                        t_ = hf * 4 + q
                        for kc in range(8):
                            self.mm(pf[:, q * 128:(q + 1) * 128], nT[:, kc, t_ * 128:(t_ + 1) * 128], wf[:, kc, :],
                                    kc == 0, kc == 7, [wd, dn[kc][hf]], [df])
                    self.act(k_tm[:, h4, :], v4(pf), AF.Exp, [df], [d_k[hf]])
                    self.act(k_tm[:, h4, :], k_tm[:, h4, :], AF.Ln, [d_k[hf]], [d_k[hf]], bias=1.0)
                    self.act(k_tm[:, h4, :], k_tm[:, h4, :], AF.Exp, [d_k[hf]], [d_k[hf]], scale=-1.0)
                    yield
                    pv, dv = self.ps()
                    for q in range(4):
                        t_ = hf * 4 + q
                        for kc in range(8):
                            self.mm(pv[:, q * 128:(q + 1) * 128], nT[:, kc, t_ * 128:(t_ + 1) * 128], wv[:, kc, :],
                                    kc == 0, kc == 7, [wd, dn[kc][hf]], [dv])
                    self.cp(v_bf[p][:, h4, :], v4(pv), [dv], [d_v[p][hf]])
                    yield
                    self.tt(k_tm[:, h4, :], k_tm[:, h4, :],
                            self.oml[:, j * 128:(j + 1) * 128].unsqueeze(1).to_broadcast([128, 4, 128]), ALU.mult,
                            [d_k[hf], self.d_const], [d_k[hf]])
                    self.act(logf[:, h4, :], k_tm[:, h4, :], AF.Ln, [d_k[hf]], [d_lf[hf]], scale=-1.0, bias=1.0)
                    pG, dG = self.ps()
                    pE, dE = self.ps()
                    pK, dK = self.ps()
                    for q in range(4):
                        t_ = hf * 4 + q
                        cs = slice(q * 128, (q + 1) * 128)
                        self.mm(pG[:, cs], logf[:, t_, :], self.triL64, True, True, [d_lf[hf], self.d_c], [dG])
                        self.mm(pE[:, cs], self.triU64, logf[:, t_, :], True, True, [d_lf[hf], self.d_c], [dE])
                        self.tr(pK[:, cs], k_tm[:, t_, :], self.ident_f, [d_k[hf], self.d_c], [dK])
                    self.act(eG[p][:, sl], pG[:], AF.Exp, [dG], [d_eG[p][hf]])
                    self.act(eGi[:, sl], pG[:], AF.Exp, [dG], [d_eGi[hf]], scale=-1.0)
                    self.act(eGe[:, h4, :], v4(pE), AF.Exp, [dE], [d_eGe[hf]])
                    self.tt(q_dec[p][:, sl], qs[:, sl], eG[p][:, sl], ALU.mult, [d_qs[hf], d_eG[p][hf]], [d_qd[p][hf]])
                    self.tt(k_inv[:, sl], pK[:], eGi[:, sl], ALU.mult, [dK, d_eGi[hf]], [d_ki[hf]])
                    self.tt(k_end[p][:, h4, :], k_tm[:, h4, :], eGe[:, h4, :], ALU.mult, [d_k[hf], d_eGe[hf]],
                            [d_ke[p][hf]])
                    yield
                    pA, dA = self.ps()
                    for q in range(4):
                        t_ = hf * 4 + q
                        cs = slice(q * 128, (q + 1) * 128)
                        ts_ = slice(t_ * 128, (t_ + 1) * 128)
                        self.mm(pA[:, cs], k_inv[:, ts_], q_dec[p][:, ts_], True, True, [d_ki[hf], d_qd[p][hf]], [dA])
                    self.tt(att[p][:, h4, :], v4(pA), self.triL64.unsqueeze(1).to_broadcast([128, 4, 128]), ALU.mult,
                            [dA, self.d_c], [d_att[p][hf]])
                    yield

            def back(j):
                p = j % 2
                dS = self.d_S_hg[j]
                S = self.S_hg[:, j, :]
                cur = self._sbi % 4
                self._sbi += 1
                self.cp(Sbf[cur][:], S, [dS], [d_Sbf[cur]], eng=ACT)
                for hf in range(2):
                    sl = slice(hf * 512, (hf + 1) * 512)
                    pO, dO = self.ps(long=True)
                    for q in range(4):
                        t_ = hf * 4 + q
                        cs = slice(q * 128, (q + 1) * 128)
                        pS, dSp = self.ps()
                        pS1, dSp1 = self.ps()
                        self.mm(pS[:, 0:128], k_end[p][0:64, t_, :], v_bf[p][0:64, t_, :], True, True,
                                [d_ke[p][hf], d_v[p][hf]], [dSp])
                        self.mm(pS1[:, 0:128], k_end[p][64:128, t_, :], v_bf[p][64:128, t_, :], True, True,
                                [d_ke[p][hf], d_v[p][hf]], [dSp1])
                        self.mm(pO[:, cs], v_bf[p][:, t_, :], att[p][:, t_, :], True, False, [d_v[p][hf], d_att[p][hf]], [dO])
                        self.mm(pO[:, q * 128:q * 128 + 64], Sbf[cur][:], q_dec[p][:, t_ * 128:t_ * 128 + 64], False, False,
                                [d_Sbf[cur], d_qd[p][hf]], [dO])
                        c0 = t_ * 128 + 63
                        n1 = self._sbi % 4
                        self._sbi += 1
                        self.stt(Sbf[n1][:], S, eG[p][:, c0:c0 + 1], pS[:, 0:128], ALU.mult, ALU.add,
                                 [dS, d_eG[p][hf], dSp], [d_Sbf[n1]])
                        self.stt(S, S, eG[p][:, c0:c0 + 1], pS[:, 0:128], ALU.mult, ALU.add, [dS, d_eG[p][hf], dSp], [dS])
                        self.mm(pO[:, q * 128 + 64:q * 128 + 128], Sbf[n1][:], q_dec[p][:, t_ * 128 + 64:t_ * 128 + 128],
                                False, True, [d_Sbf[n1], d_qd[p][hf]], [dO])
                        c1 = t_ * 128 + 127
                        cur = self._sbi % 4
                        self._sbi += 1
                        self.stt(Sbf[cur][:], S, eG[p][:, c1:c1 + 1], pS1[:, 0:128], ALU.mult, ALU.add,
                                 [dS, d_eG[p][hf], dSp1], [d_Sbf[cur]])
                        self.stt(S, S, eG[p][:, c1:c1 + 1], pS1[:, 0:128], ALU.mult, ALU.add, [dS, d_eG[p][hf], dSp1], [dS])
                        yield
                    r = hf
                    self.act(osq[r][:], pO[:], AF.Square, [dO], [d_osq[r]])
                    pSS, dSS = self.ps()
                    self.mm(pSS[:], self.ones_b, osq[r][:], True, True, [d_osq[r], self.d_const], [dSS])
                    self.act(sd[r][:], pSS[:], AF.Ln, [dSS], [d_sd[r]], scale=1.0 / 128, bias=EPS)
                    self.act(sd[r][:], sd[r][:], AF.Exp, [d_sd[r]], [d_sd[r]], scale=-0.5)
                    self.tt(tmp[r][:], pO[:], sd[r][:], ALU.mult, [dO, d_sd[r]], [d_tmp[r]])
                    self.stt(self.big[:, j, sl], tmp[r][:], self.hgn[:, j:j + 1], sgate[p][:, sl], ALU.mult, ALU.mult,
                             [d_tmp[r], self.d_c, d_sg[p][hf]], [self.d_big[j][hf]])
                    yield

            for _ in front(0):
                pass
            for j in range(8):
                b = back(j)
                f = front(j + 1) if j < 7 else None
                alive_b, alive_f = True, f is not None
                while alive_b or alive_f:
                    if alive_b:
                        try:
                            next(b)
                        except StopIteration:
                            alive_b = False
                    if alive_f:
                        try:
                            next(f)
                        except StopIteration:
                            alive_f = False
            self.end_scope()

    def ssd(self, blk):
        dr = self.dr
        w_in = dr['w_in']
        nT, dn = self.nT, self.d_n
        with contextlib.ExitStack() as st:
            sbs = lambda shape, dt, name: self.sb(shape, dt, name, st)
            sm = lambda name, dt=F32: sbs([128, 8, 32], dt, name)
            dtv, lndt, a_, acs, wst, wd_, dl, b2 = [sm(n) for n in ("dtv", "lndt", "a_", "acs", "wst", "wd_", "dl", "b2")]
            acs_hi = sm("acs_hi", BF16)
            acs_lo = sm("acs_lo", BF16)
            d_dt = Dep()
            xpT = sbs([128, 6, 3 + TB], BF16, "xpT"); d_xp = [[Dep() for _ in range(3)] for _ in range(6)]
            BT = sbs([128, TB], BF16, "BT"); d_BT = mkdeps(2)
            CT = sbs([128, TB], BF16, "CT"); d_CT = mkdeps(2)
            diag = sbs([128, 6, 4, 128], BF16, "diag"); d_dg = mkdeps(6)
            cbr = sbs([128, 640], BF16, "cbr"); d_cbr = Dep()
            self.memset(cbr[:], 0.0, [d_cbr])
            gss = sbs([128, 512], F32, "gss"); d_gss = Dep()
            R2 = 2
            xs_bf = [sbs([128, 512], BF16, "xs_bf") for _ in range(R2)]; d_xs = mkdeps(R2)
            Dd = sbs([128, 8, 128], BF16, "Dd"); d_Dd = Dep()
            xsw = [sbs([128, 512], BF16, "xsw") for _ in range(R2)]; d_xsw = mkdeps(R2)
            B_tm = [sbs([128, 128], BF16, "B_tm") for _ in range(R2)]; d_Bt = mkdeps(R2)
            smz = [sbs([128, 512], BF16, "smz") for _ in range(R2)]; d_smz = mkdeps(R2)
            E = [sbs([128, 4, 128], F32, "E") for _ in range(2)]; d_E = mkdeps(2)
            Mp = [sbs([128, 8, 128], BF16, "Mp") for _ in range(2)]; d_Mp = [mkdeps(2) for _ in range(2)]
            t1_ = sbs([128, 512], F32, "t1"); t1 = [t1_] * R2; d_t1_ = Dep(); d_t1 = [d_t1_] * R2
            yv = [sbs([128, 512], F32, "yv") for _ in range(R2)]; d_yv = mkdeps(R2)
            ssq = [sbs([128, 1], F32, "ssq") for _ in range(R2)]; d_ssq = mkdeps(R2)
            yn = [sbs([128, 512], BF16, "yn") for _ in range(R2)]; d_yn = mkdeps(R2)

            (wdt,), wdd = self.wload([(w_in[:, OFF_DT:OFF_DT + 32], 8, 32)])
            pD, dD = self.ps()
            for t_ in range(8):
                for kc in range(8):
                    self.mm(pD[:, t_ * 32:(t_ + 1) * 32], nT[:, kc, t_ * 128:(t_ + 1) * 128], wdt[:, kc, :], kc == 0, kc == 7,
                            [wdd, dn[kc][t_ // 4]], [dD])
            v8 = lambda bank: bank[:, 0:256].rearrange("p (a b) -> p a b", a=8)
            bc8 = lambda ap: ap.unsqueeze(1).to_broadcast([128, 8, 32])
            dd = [d_dt]
            self.tt(dtv[:], v8(pD), bc8(self.dtb[:]), ALU.add, [dD, self.d_c], dd)
            self.act(dtv[:], dtv[:], AF.Exp, dd, dd)
            self.act(dtv[:], dtv[:], AF.Ln, dd, dd, bias=1.0)
            self.act(lndt[:], dtv[:], AF.Ln, dd, dd)
            self.tt(a_[:], dtv[:], bc8(self.aneg[:]), ALU.mult, dd + [self.d_const], dd)
            pAc, dAc = self.ps()
            pTo, dTo = self.ps()
            for t_ in range(8):
                self.mm(pAc[:, t_ * 32:(t_ + 1) * 32], self.triL128, a_[:, t_, :], True, True, dd + [self.d_c], [dAc])
                self.mm(pTo[:, t_ * 32:(t_ + 1) * 32], self.ones_f, a_[:, t_, :], True, True, dd + [self.d_c], [dTo])
            self.cp(acs[:], v8(pAc), [dAc], dd, eng=ACT)
            self.act(wst[:], v8(pAc), AF.Exp, [dAc], dd)
            self.act(dl[:], v8(pTo), AF.Exp, [dTo], dd)
            self.tt(wd_[:], v8(pTo), acs[:], ALU.subtract, [dTo] + dd, dd)
            self.act(wd_[:], wd_[:], AF.Exp, dd, dd)
            self.tt(wd_[:], wd_[:], dtv[:], ALU.mult, dd, dd)
            self.tt(b2[:], lndt[:], acs[:], ALU.subtract, dd, dd)
            self.cp(acs_hi[:], acs[:], dd, dd, eng=DVE)
            self.tt(acs_lo[:], acs[:], acs_hi[:], ALU.subtract, dd, dd)

            self._ei = 0
            gw = {}

            def gload_xb(g):
                gw[g] = (self.wload([(w_in[:, OFF_XBC + g * 512:OFF_XBC + (g + 1) * 512], 8, 512)]),
                         self.wload([(w_in[:, OFF_XBC + 2048 + g * 128:OFF_XBC + 2048 + (g + 1) * 128], 8, 128),
                                     (w_in[:, OFF_XBC + 2560 + g * 128:OFF_XBC + 2560 + (g + 1) * 128], 8, 128)]))

            gload_xb(0)
            for g in range(4):
                ((wx,), wxd), ((wB, wC), wbd) = gw[g]
                (wz,), wzd = self.wload([(w_in[:, OFF_MZ + g * 512:OFF_MZ + (g + 1) * 512], 8, 512)])
                chs = [4 * g, 4 * g + 1, 4 * g + 2, 4 * g + 3, 16 + g, 20 + g]
                self.P.dma(POOL, lambda e, g=g: e.dma_start(out=cbr[0:1, 0:512], in_=dr['cbrow'][0:1, g * 512:(g + 1) * 512]),
                           (), [d_cbr], semkey='cbr')
                self.P.dma(POOL, lambda e, g=g: e.dma_start(out=cbr[0:1, 512:640],
                                                           in_=dr['cbrow'][0:1, 2048 + g * 128:2048 + (g + 1) * 128]),
                           (), [d_cbr], semkey='cbr', accum=True)
                self.P.dma(SP, lambda e, g=g: e.dma_start(out=gss[:], in_=dr['ssm_norm'][g * 512:(g + 1) * 512].partition_broadcast(128)),
                           (), [d_gss], semkey='gss')
                for c6 in range(6):
                    ch = chs[c6]
                    self.cp(xpT[:, c6, 0:3], self.xtail[:, ch, :], [self.d_xtail[ch]], [d_xp[c6][0]], eng=DVE)
                    wsrc = wx[:, :, c6 * 128:(c6 + 1) * 128] if c6 < 4 else (wB if c6 == 4 else wC)
                    wdp = wxd if c6 < 4 else wbd
                    for tb in range(2):
                        sl = slice(tb * 512, (tb + 1) * 512)
                        px, dpx = self.ps()
                        for kc in range(8):
                            self.mm(px[:], wsrc[:, kc, :], nT[:, kc, sl], kc == 0, kc == 7, [wdp, dn[kc][tb]], [dpx])
                        self.cp(xpT[:, c6, 3 + tb * 512:3 + (tb + 1) * 512], px[:], [dpx], [d_xp[c6][1 + tb]])
                    self.cp(self.xtail[:, ch, :], xpT[:, c6, TB:TB + 3], [d_xp[c6][2]], [self.d_xtail[ch]], eng=DVE)
                    for tap in range(4):
                        self.ts(diag[:, c6, tap, :], self.ident_f, self.cw[:, ch, tap:tap + 1], ALU.mult,
                                [self.d_c], [d_dg[c6]])
                for (c6, dst, dd_, col) in ((4, BT, d_BT, 16 + g), (5, CT, d_CT, 20 + g)):
                    for tb in range(2):
                        sl = slice(tb * 512, (tb + 1) * 512)
                        pb, dpb = self.ps()
                        for tap in range(4):
                            self.mm(pb[:], diag[:, c6, tap, :], xpT[:, c6, tb * 512 + tap:tb * 512 + tap + 512], tap == 0,
                                    tap == 3, [d_dg[c6]] + d_xp[c6], [dpb])
                        self.act(dst[:, sl], pb[:], AF.Silu, [dpb, self.d_c], [dd_[tb]], bias=self.cbp[:, col:col + 1])
                v864 = lambda ap: ap.rearrange("p (a b) -> p a b", a=8)
                bch = lambda ap: ap.unsqueeze(2).to_broadcast([128, 8, 64])
                hs = slice(g * 8, (g + 1) * 8)
                for hh in range(8):
                    self.ts(Dd[:, hh, :], self.ident_f, self.dsk[:, g * 8 + hh:g * 8 + hh + 1], ALU.mult, [self.d_c], [d_Dd])
                Sg = self.S_ssd[:, g * 512:(g + 1) * 512]

                def front(t_, g=g, wz=wz, wzd=wzd, hs=hs):
                    hf = t_ // 4
                    tsl = slice(t_ * 128, (t_ + 1) * 128)
                    r = t_ % 2
                    pxs, dxs_ = self.ps()
                    for cc in range(4):
                        cs = slice(cc * 128, (cc + 1) * 128)
                        for tap in range(4):
                            self.mm(pxs[:, cs], xpT[:, cc, t_ * 128 + tap:t_ * 128 + tap + 128], diag[:, cc, tap, :], tap == 0,
                                    False, d_xp[cc] + [d_dg[cc]], [dxs_])
                        self.mm(pxs[:, cs], self.e0_b, cbr[:, cs], False, True, [self.d_const, d_cbr], [dxs_])
                    self.act(xs_bf[r][:], pxs[:], AF.Silu, [dxs_], [d_xs[r]])
                    pbt, dbt = self.ps()
                    for tap in range(4):
                        self.mm(pbt[:, 0:128], xpT[:, 4, t_ * 128 + tap:t_ * 128 + tap + 128], diag[:, 4, tap, :], tap == 0,
                                False, d_xp[4] + [d_dg[4]], [dbt])
                    self.mm(pbt[:, 0:128], self.e0_b, cbr[:, 512:640], False, True, [self.d_const, d_cbr], [dbt])
                    self.act(B_tm[r][:], pbt[:, 0:128], AF.Silu, [dbt], [d_Bt[r]])
                    pz, dz = self.ps()
                    for kc in range(8):
                        self.mm(pz[:], nT[:, kc, tsl], wz[:, kc, :], kc == 0, kc == 7, [wzd, dn[kc][hf]], [dz])
                    self.act(smz[r][:], pz[:], AF.Silu, [dz], [d_smz[r]])
                    self.tt(v864(xsw[r][:]), v864(xs_bf[r][:]), bch(wd_[:, t_, hs]), ALU.mult, [d_xs[r], d_dt], [d_xsw[r]])
                    pcb, dcb = self.ps()
                    self.mm(pcb[:, 0:128], BT[:, tsl], CT[:, tsl], True, True, [d_BT[hf], d_CT[hf]], [dcb])
                    for hq in range(2):
                        pab, dab = self.ps()
                        for q in range(4):
                            h = g * 8 + hq * 4 + q
                            cs = slice(q * 128, (q + 1) * 128)
                            self.mm(pab[:, cs], acs_hi[:, t_, h:h + 1].to_broadcast([128, 128]), self.ident_b, True, False,
                                    [d_dt, self.d_const], [dab])
                            self.mm(pab[:, cs], acs_lo[:, t_, h:h + 1].to_broadcast([128, 128]), self.ident_b, False, False,
                                    [d_dt, self.d_const], [dab])
                            self.mm(pab[:, cs], self.ident_b, self.negm_b, False, True, [self.d_const], [dab])
                        e_ = self._ei % 2
                        self._ei += 1
                        for q in range(4):
                            h = g * 8 + hq * 4 + q
                            cs = slice(q * 128, (q + 1) * 128)
                            self.act(E[e_][:, q, :], pab[:, cs], AF.Exp, [dab, d_dt], [d_E[e_]], bias=b2[:, t_, h:h + 1])
                        self.tt(Mp[r][:, hq * 4:hq * 4 + 4, :], E[e_][:],
                                pcb[:, 0:128].unsqueeze(1).to_broadcast([128, 4, 128]), ALU.mult, [d_E[e_], dcb],
                                [d_Mp[r][hq]])

                def mid(t_, g=g, hs=hs, Sg=Sg):
                    hf = t_ // 4
                    tsl = slice(t_ * 128, (t_ + 1) * 128)
                    r = t_ % 2
                    py, dy = self.ps(long=True)
                    for hh in range(8):
                        m_ = r * 8 + hh
                        self.mm(py[:, hh * 64:(hh + 1) * 64], Dd[:, hh, :], xs_bf[r][:, hh * 64:(hh + 1) * 64], True,
                                False, [d_Dd, d_xs[r]], [dy])
                        self.mm(py[:, hh * 64:(hh + 1) * 64], Mp[r][:, hh, :], xs_bf[r][:, hh * 64:(hh + 1) * 64], False,
                                True, [d_Mp[r][hh // 4], d_xs[r]], [dy])
                    pyb, dyb = self.ps()
                    self.mm(pyb[:], CT[:, tsl], self.S_ssd_bf[:, g * 512:(g + 1) * 512], True, True,
                            [d_CT[hf], self.d_S_ssd_bf[g]], [dyb])
                    pds, dds = self.ps()
                    self.mm(pds[:], B_tm[r][:], xsw[r][:], True, True, [d_Bt[r], d_xsw[r]], [dds])
                    self.tt(v864(t1[r][:]), v864(pyb[:]), bch(wst[:, t_, hs]), ALU.mult, [dyb, d_dt], [d_t1[r]])
                    self.tt(v864(Sg), v864(Sg), bch(dl[:, t_, hs]), ALU.mult, [self.d_S_ssd[g], d_dt], [self.d_S_ssd[g]])
                    self.tt(self.S_ssd_bf[:, g * 512:(g + 1) * 512], Sg, pds[:], ALU.add, [self.d_S_ssd[g], dds],
                            [self.d_S_ssd_bf[g]])
                    self.tt(Sg, Sg, pds[:], ALU.add, [self.d_S_ssd[g], dds], [self.d_S_ssd[g]])
                    self.tt(yv[r][:], py[:], t1[r][:], ALU.add, [dy, d_t1[r]], [d_yv[r]])
                    self.tt(yv[r][:], yv[r][:], smz[r][:], ALU.mult, [d_yv[r], d_smz[r]], [d_yv[r]])
                    self.act(t1[r][:], yv[r][:], AF.Square, [d_yv[r]], [d_t1[r], d_ssq[r]], accum=ssq[r][:])
                    self.act(ssq[r][:], ssq[r][:], AF.Ln, [d_ssq[r]], [d_ssq[r]], scale=1.0 / 512, bias=EPS)
                    self.act(ssq[r][:], ssq[r][:], AF.Exp, [d_ssq[r]], [d_ssq[r]], scale=-0.5)

                def tail_a(t_, g=g):
                    r = t_ % 2
                    self.stt(yn[r][:], yv[r][:], ssq[r][:, 0:1], gss[:], ALU.mult, ALU.mult, [d_yv[r], d_ssq[r], d_gss],
                             [d_yn[r]])

                def tail(t_, g=g):
                    hf = t_ // 4
                    tsl = slice(t_ * 128, (t_ + 1) * 128)
                    r = t_ % 2
                    pyt, dyt = self.ps()
                    pytb = pyt[:].bitcast(BF16)
                    for cc in range(4):
                        self.tr(pytb[:, cc * 128:(cc + 1) * 128], yn[r][:, cc * 128:(cc + 1) * 128], self.ident_b,
                                [d_yn[r], self.d_const], [dyt])
                    self.cp(self.big[:, 8 + 4 * g:12 + 4 * g, tsl], pytb[:, 0:512].rearrange("p (a b) -> p a b", a=4), [dyt],
                            [self.d_big[8 + 4 * g + cc][hf] for cc in range(4)], eng=ACT)

                if g + 1 < 4:
                    gload_xb(g + 1)
                front(0)
                for t_ in range(8):
                    if t_ > 0:
                        tail_a(t_ - 1)
                    mid(t_)
                    if t_ < 7:
                        front(t_ + 1)
                    if t_ > 0:
                        tail(t_ - 1)
                tail_a(7)
                tail(7)
            self.end_scope()

    def outproj(self, blk):
        dr = self.dr
        w_in = dr['w_in']
        nT, dn = self.nT, self.d_n
        with contextlib.ExitStack() as st:
            sbs = lambda shape, dt, name: self.sb(shape, dt, name, st)
            mix = sbs([128, 8, TB], BF16, "mix"); d_mix = [[Dep() for _ in range(2)] for _ in range(8)]
            sga = [sbs([128, 512], F32, "sga") for _ in range(2)]; d_sga = mkdeps(2)
            sgb = [sbs([128, 512], F32, "sgb") for _ in range(2)]; d_sgb = mkdeps(2)
            m1 = [sbs([128, 512], F32, "m1") for _ in range(2)]; d_m1 = mkdeps(2)
            it = 0
            for fc in range(8):
                cs = slice(fc * 128, (fc + 1) * 128)
                (wA, wga, wgb), wad = self.wload([(dr['w_hg_out'][:, cs], 8, 128),
                                                  (w_in[:, OFF_BRG + fc * 128:OFF_BRG + (fc + 1) * 128], 8, 128),
                                                  (w_in[:, OFF_BRG + 1024 + fc * 128:OFF_BRG + 1024 + (fc + 1) * 128], 8, 128)])
                (wBm,), wbd = self.wload([(dr['w_ssm_out'][:, cs], 16, 128)])
                for tb in range(2):
                    sl = slice(tb * 512, (tb + 1) * 512)
                    r = it % 2
                    it += 1
                    pA, dA = self.ps()
                    for kc in range(8):
                        self.mm(pA[:], wA[:, kc, :], self.big[:, kc, sl], kc == 0, kc == 7, [wad, self.d_big[kc][tb]], [dA])
                    pB, dB = self.ps()
                    for kc in range(16):
                        self.mm(pB[:], wBm[:, kc, :], self.big[:, 8 + kc, sl], kc == 0, kc == 15,
                                [wbd, self.d_big[8 + kc][tb]], [dB])
                    pga, dga = self.ps()
                    for kc in range(8):
                        self.mm(pga[:], wga[:, kc, :], nT[:, kc, sl], kc == 0, kc == 7, [wad, dn[kc][tb]], [dga])
                    pgb, dgb = self.ps()
                    for kc in range(8):
                        self.mm(pgb[:], wgb[:, kc, :], nT[:, kc, sl], kc == 0, kc == 7, [wad, dn[kc][tb]], [dgb])
                    self.act(sga[r][:], pga[:], AF.Sigmoid, [dga], [d_sga[r]])
                    self.act(sgb[r][:], pgb[:], AF.Sigmoid, [dgb], [d_sgb[r]])
                    self.tt(m1[r][:], pA[:], sga[r][:], ALU.mult, [dA, d_sga[r]], [d_m1[r]])
                    self.tt(sgb[r][:], pB[:], sgb[r][:], ALU.mult, [dB, d_sgb[r]], [d_sgb[r]])
                    self.tt(mix[:, fc, sl], m1[r][:], sgb[r][:], ALU.add, [d_m1[r], d_sgb[r]], [d_mix[fc][tb]])
            for dcp in range(4):
                (wo,), wod = self.wload([(dr['w_out'][:, dcp * 256:(dcp + 1) * 256], 8, 256)])
                for j in range(2):
                    dc = dcp * 2 + j
                    for tb in range(2):
                        sl = slice(tb * 512, (tb + 1) * 512)
                        po, do = self.ps()
                        for kc in range(8):
                            self.mm(po[:], wo[:, kc, j * 128:(j + 1) * 128], mix[:, kc, sl], kc == 0, kc == 7,
                                    [wod, d_mix[kc][tb]], [do])
                        self.tt(self.hT[:, dc, sl], po[:], self.hT[:, dc, sl], ALU.add, [do, self.d_h[dc][tb]],
                                [self.d_h[dc][tb]])
            self.end_scope()

    def ple(self, blk):
        dr = self.dr
        nT, dn = self.nT, self.d_n
        with contextlib.ExitStack() as st:
            sbs = lambda shape, dt, name: self.sb(shape, dt, name, st)
            pst = [sbs([128, 256], F32, "pst") for _ in range(2)]; d_pst = mkdeps(2)
            pT = sbs([128, 2, TB], BF16, "pT"); d_pT = mkdeps(2)
            sg = [sbs([128, 512], F32, "sgp") for _ in range(2)]; d_sg = mkdeps(2)
            for t_ in range(8):
                r = t_ % 2
                src = dr['p'][blk * TB + t_ * 128:blk * TB + (t_ + 1) * 128, :]
                self.P.dma(SP, lambda e, r=r, src=src: e.dma_start(out=pst[r][:], in_=src), (), [d_pst[r]], semkey='pst%d' % r)
                bank, bd = self.ps()
                for kc in range(2):
                    self.tr(bank[:, kc * 128:(kc + 1) * 128], pst[r][:, kc * 128:(kc + 1) * 128], self.ident_f,
                            [d_pst[r], self.d_c], [bd])
                self.cp(pT[:, :, t_ * 128:(t_ + 1) * 128], bank[:, 0:256].rearrange("p (a b) -> p a b", a=2), [bd],
                        [d_pT[t_ // 4]])
            it = 0
            for fcp in range(4):
                (wg, wp), wd = self.wload([(dr['w_ple_gate'][:, fcp * 256:(fcp + 1) * 256], 8, 256),
                                           (dr['w_ple_proj'][:, fcp * 256:(fcp + 1) * 256], 2, 256)])
                for j in range(2):
                    fc = fcp * 2 + j
                    for tb in range(2):
                        sl = slice(tb * 512, (tb + 1) * 512)
                        r = it % 2
                        it += 1
                        pg, dg = self.ps()
                        for kc in range(8):
                            self.mm(pg[:], wg[:, kc, j * 128:(j + 1) * 128], nT[:, kc, sl], kc == 0, kc == 7,
                                    [wd, dn[kc][tb]], [dg])
                        pp, dp = self.ps()
                        for kc in range(2):
                            self.mm(pp[:], wp[:, kc, j * 128:(j + 1) * 128], pT[:, kc, sl], kc == 0, kc == 1,
                                    [wd, d_pT[tb]], [dp])
                        self.act(sg[r][:], pg[:], AF.Sigmoid, [dg], [d_sg[r]])
                        self.tt(sg[r][:], pp[:], sg[r][:], ALU.mult, [dp, d_sg[r]], [d_sg[r]])
                        self.tt(self.hT[:, fc, sl], self.hT[:, fc, sl], sg[r][:], ALU.add, [self.d_h[fc][tb], d_sg[r]],
                                [self.d_h[fc][tb]])
            self.end_scope()

    def store_out(self, blk, raw=False):
        dr = self.dr
        with contextlib.ExitStack() as st:
            os_ = [self.sb([128, 8, 128], F32, "ostage", st) for _ in range(2)]
            dos = mkdeps(2)
            for kc in range(8):
                s = os_[kc % 2]
                ds = dos[kc % 2]
                for half in range(2):
                    bank, bd = self.ps()
                    for q in range(4):
                        tt_ = half * 4 + q
                        self.tr(bank[:, q * 128:(q + 1) * 128], self.hT[:, kc, tt_ * 128:(tt_ + 1) * 128], self.ident_f,
                                [self.d_h[kc][half], self.d_c], [bd], sig=(q == 3))
                    self.cp(s[:, half * 4:(half + 1) * 4, :], bank[:].rearrange("p (a b) -> p a b", a=4), [bd], [ds])
                dst = dr['out'][blk * TB:(blk + 1) * TB, kc * 128:(kc + 1) * 128].rearrange("(t p) c -> p t c", p=128)
                self.P.dma(SP, lambda e, s=s, dst=dst: e.dma_start(out=dst, in_=s[:]), [ds], (),
                           semkey='os%d' % (kc % 2))
            self.end_scope()

    def run(self):
        dr = self.dr
        self.init()
        stages = ['load', 'norm', 'ffn1', 'hgrn', 'ssd', 'outproj', 'ffn2', 'ple', 'final']
        upto = len(stages) if self.stop_after is None else stages.index(self.stop_after) + 1
        act = stages[:upto]
        for blk in range(NBLK):
            self.load_x(blk)
            if 'norm' in act:
                self.norm(0)
            if 'ffn1' in act:
                self.ffn(dr['ffn1_w13'], dr['ffn1_w2'])
            if 'hgrn' in act:
                self.norm(1)
                self.hgrn(blk)
            if 'ssd' in act:
                self.ssd(blk)
            if 'outproj' in act:
                self.outproj(blk)
            if 'dbg' in dr:
                allb = [d for row in self.d_big for d in row]
                dst = dr['dbg'][blk * 128:(blk + 1) * 128, :]
                self.P.dma(SP, lambda e, dst=dst: e.dma_start(out=dst, in_=self.big[:].rearrange("p a b -> p (a b)")),
                           allb, (), semkey='dbg')
            if 'ffn2' in act:
                self.norm(2)
                self.ffn(dr['ffn2_w13'], dr['ffn2_w2'])
            if 'ple' in act:
                self.norm(3)
                self.ple(blk)
            if 'final' in act:
                self.norm(4, final=True)
            self.store_out(blk)


def _consts():
    c = np.zeros((128, 7, 128), np.float32)
    i = np.arange(128)
    same = (i[:, None] // 64) == (i[None, :] // 64)
    c[:, 0, :] = np.eye(128)
    c[:, 1, :] = ((i[:, None] <= i[None, :]) & same)
    c[:, 2, :] = ((i[:, None] > i[None, :]) & same)
    c[:, 3, :] = (i[:, None] <= i[None, :])
    c[:, 4, :] = 1.0
    c[0, 6, :] = 1.0
    c[:, 5, :] = np.where(i[:, None] > i[None, :], -1e30, 0.0)
    return c


_IN_SPECS = [
    ('x', [T, D]), ('p', [T, 256]), ('consts', [128, 7, 128]), ('gains', [128, 40]), ('hgn', [128, 8]),
    ('cw', [128, 96]), ('cbp', [128, 24]), ('cbrow', [1, 3072]), ('dt_bias', [32]), ('a_log', [32]),
    ('d_skip', [32]), ('ssm_norm', [2048]), ('hg_lb', [2, 1024]),
    ('ffn1_w13', [D, 2 * DFF]), ('ffn1_w2', [DFF, D]), ('w_in', [D, DIN]),
    ('w_hg_out', [D, D]), ('w_ssm_out', [2 * D, D]), ('w_out', [D, D]),
    ('ffn2_w13', [D, 2 * DFF]), ('ffn2_w2', [DFF, D]), ('w_ple_gate', [D, D]), ('w_ple_proj', [256, D]),
]


def build(stop_after=None):
    nc = bass.Bass("TRN2", target_bir_lowering=False)
    dram = {}
    for name, shape in _IN_SPECS:
        dram[name] = nc.dram_tensor(name, shape, F32, kind="ExternalInput").ap()
    dram['out'] = nc.dram_tensor("out", [T, D], F32, kind="ExternalOutput").ap()
    if stop_after in ('hgrn', 'ssd'):
        dram['dbg'] = nc.dram_tensor("dbg", [NBLK * 128, 24 * TB], BF16, kind="ExternalOutput").ap()
    with contextlib.ExitStack() as es:
        block = es.enter_context(nc.Block())
        kb = KB(nc, block, es, dram, stop_after=stop_after)
        kb.run()
        print("ops", len(kb.P.ops), "waits", kb.P.nwaits)
    return nc


def make_in_maps(inp):
    f = lambda a: np.ascontiguousarray(np.asarray(a, dtype=np.float32))
    gains = np.stack([f(inp['ffn1_norm'])[0], f(inp['mix_norm'])[0], f(inp['ffn2_norm'])[0],
                      f(inp['ple_norm'])[0], f(inp['final_norm'])], 0)
    gains = gains.reshape(5, 8, 128).transpose(2, 0, 1).reshape(128, 40)
    hgn = f(inp['hg_norm'])[0].reshape(8, 128).T
    cw = f(inp['conv_w'])[0].T.reshape(24, 128, 4).transpose(1, 0, 2).reshape(128, 96)
    cbp = f(inp['conv_b'])[0].reshape(24, 128).T
    shared = dict(
        consts=_consts(), gains=f(gains), hgn=f(hgn), cw=f(cw), cbp=f(cbp), cbrow=f(inp['conv_b']),
        dt_bias=f(inp['dt_bias'])[0], a_log=f(inp['a_log'])[0], d_skip=f(inp['d_skip'])[0],
        ssm_norm=f(inp['ssm_norm'])[0], hg_lb=f(inp['hg_lb']),
        ffn1_w13=f(inp['ffn1_w13'])[0], ffn1_w2=f(inp['ffn1_w2'])[0], w_in=f(inp['w_in'])[0],
        w_hg_out=f(inp['w_hg_out'])[0], w_ssm_out=f(inp['w_ssm_out'])[0], w_out=f(inp['w_out'])[0],
        ffn2_w13=f(inp['ffn2_w13'])[0], ffn2_w2=f(inp['ffn2_w2'])[0], w_ple_gate=f(inp['w_ple_gate'])[0],
        w_ple_proj=f(inp['w_ple_proj'])[0],
    )
    x = f(inp['x'])
    p = f(inp['p'])[0]
    maps = []
    for b in range(8):
        m = dict(shared)
        m['x'] = x[b]
        m['p'] = p[b]
        maps.append(m)
    return maps


_NC_CACHE = {}


def kernel(**inputs):
    if 'nc' not in _NC_CACHE:
        _NC_CACHE['nc'] = build()
    nc = _NC_CACHE['nc']
    maps = make_in_maps(inputs)
    res = run_bass_kernel_spmd(nc, maps, core_ids=list(range(8)))
    out = np.stack([np.asarray(r['out'], dtype=np.float32) for r in res.results], 0)
    return out
```

```python
import contextlib
import numpy as np
import concourse.bass as bass
import concourse.mybir as mybir
from concourse.bass_utils import run_bass_kernel_spmd

F32 = mybir.dt.float32
BF16 = mybir.dt.bfloat16
AF = mybir.ActivationFunctionType
ALU = mybir.AluOpType

PE, ACT, DVE, POOL, SP = 'tensor', 'scalar', 'vector', 'gpsimd', 'sync'
ENGS = (PE, ACT, DVE, POOL, SP)
CENGS = (PE, ACT, DVE)

D = 1024
T = 2048
TB = 1024
NBLK = T // TB
DFF = 2816
NFC = DFF // 128
DIN = 11296
EPS = 1e-6
OFF_Q, OFF_F, OFF_V, OFF_G = 0, 1024, 2048, 3072
OFF_MZ = 4096
OFF_XBC = 6144
OFF_DT = 9216
OFF_BRG = 9248


class Dep:
    __slots__ = ('name', 'w', 'r')

    def __init__(self, name=''):
        self.name = name
        self.w = {}
        self.r = {}


def mkdeps(n):
    return [Dep() for _ in range(n)]


class Op:
    __slots__ = ('eng', 'fn', 'deps', 'idx', 'semkey', 'count', 'sig', 'is_dma')


class Prog:
    def __init__(self, nc, block):
        self.nc = nc
        self.block = block
        self.ops = []
        self.flushed = 0
        self.cnt = {e: 0 for e in ENGS}
        self.pending = {e: [] for e in ENGS}
        self.dma_counts = {}
        self.dsem = {}
        self.esem = {e: nc.alloc_semaphore('c_' + e) for e in (PE, ACT, DVE, POOL)}
        self.seen = {e: {} for e in ENGS}
        self.last = {e: None for e in ENGS}
        self.scope_dmas = []
        self.nwaits = 0

    def add(self, eng, fn, reads=(), writes=(), sig=True, is_dma=False, semkey=None, accum=False,
            extra_deps=()):
        op = Op()
        op.eng = eng
        op.fn = fn
        op.is_dma = is_dma
        op.idx = len(self.ops)
        op.semkey = semkey
        op.sig = sig
        op.count = None
        ds = set(extra_deps)
        for d in reads:
            ds.update(d.w.values())
        for d in writes:
            ds.update(d.r.values())
            if not accum:
                ds.update(d.w.values())
        op.deps = ds
        if is_dma:
            if semkey not in self.dsem:
                self.dsem[semkey] = self.nc.alloc_semaphore('d_' + str(semkey))
            c = self.dma_counts.get(semkey, 0) + 16
            self.dma_counts[semkey] = c
            op.count = c
            k = ('d', op.idx)
        else:
            k = eng
            if fn is not None:
                if sig:
                    self.cnt[eng] += 1
                    op.count = self.cnt[eng]
                    for p in self.pending[eng]:
                        p.count = op.count
                    self.pending[eng] = []
                else:
                    self.pending[eng].append(op)
                self.last[eng] = op
        for d in reads:
            d.r[k] = op
        for d in writes:
            if not accum:
                d.w = {}
            d.w[k] = op
            d.r = {}
        self.ops.append(op)
        return op

    def dma(self, eng, fn, reads=(), writes=(), semkey=None, accum=False):
        op = self.add(eng, fn, reads, writes, is_dma=True, semkey=semkey, accum=accum)
        if eng == SP:
            self.scope_dmas.append(op)
        return op

    def barrier(self):
        lasts = [self.last[e] for e in CENGS + (POOL,) if self.last[e] is not None]
        dm = list(self.scope_dmas)
        self.scope_dmas = []
        for e in CENGS + (SP,):
            self.add(e, None, extra_deps=[o for o in lasts if o.eng != e] + dm)

    def flush(self):
        ops = self.ops[self.flushed:]
        self.flushed = len(self.ops)
        for e in ENGS:
            assert not self.pending[e], "pending non-signalling ops on %s" % e
            ops_e = [op for op in ops if op.eng == e]
            if not ops_e:
                continue
            getattr(self.block, e)(lambda engh, e=e, ops_e=ops_e: self._run(e, engh, ops_e))

    def _run(self, e, engh, ops_e):
        seen = self.seen[e]
        for op in ops_e:
            need = {}
            for dj in op.deps:
                if dj.is_dma:
                    s = self.dsem[dj.semkey]
                else:
                    if dj.fn is None:
                        continue
                    if dj.eng == PE and e == PE and not op.is_dma and op.fn is not None:
                        continue
                    s = self.esem[dj.eng]
                v = dj.count
                assert v is not None
                key = id(s)
                if v > need.get(key, (None, 0))[1]:
                    need[key] = (s, v)
            for key, (s, v) in need.items():
                if v > seen.get(key, 0):
                    engh.wait_ge(s, v)
                    seen[key] = v
                    self.nwaits += 1
            if op.fn is None:
                continue
            inst = op.fn(engh)
            if op.is_dma:
                inst.then_inc(self.dsem[op.semkey], 16)
            elif op.sig:
                inst.then_inc(self.esem[e], 1)


class KB:
    def __init__(self, nc, block, es, dram, stop_after=None):
        self.nc = nc
        self.P = Prog(nc, block)
        self.es = es
        self.dr = dram
        self.stop_after = stop_after
        self.uid = 0
        self.banks = []
        self.bank_deps = []
        for i in range(8):
            self.banks.append(es.enter_context(nc.psum_tensor("bank%d" % i, [128, 512], F32)))
            self.bank_deps.append(Dep())
        self.bank_next = 0
        self.long_next = 0
        self.copy_rr = 0

    def sb(self, shape, dt, name=None, stack=None):
        self.uid += 1
        name = "%s_%d" % (name or "t", self.uid)
        return (stack or self.es).enter_context(self.nc.sbuf_tensor(name, shape, dt))

    def ps(self, long=False):
        if long:
            i = 6 + self.long_next
            self.long_next = (self.long_next + 1) % 2
        else:
            i = self.bank_next
            self.bank_next = (i + 1) % 6
        return self.banks[i], self.bank_deps[i]

    def mm(self, out, lhsT, rhs, start, stop, reads, writes):
        self.P.add(PE, lambda e: e.matmul(out, lhsT=lhsT, rhs=rhs, start=start, stop=stop),
                   reads, writes, sig=True)

    def tr(self, out, in_, ident, reads, writes, sig=True):
        self.P.add(PE, lambda e: e.transpose(out=out, in_=in_, identity=ident), reads, writes, sig=True)

    def act(self, out, in_, func, reads, writes, bias=None, scale=None, accum=None):
        kw = {}
        if bias is not None:
            kw['bias'] = bias
        if scale is not None:
            kw['scale'] = scale
        if accum is not None:
            kw['accum_out'] = accum
        self.P.add(ACT, lambda e: e.activation(out=out, in_=in_, func=func, **kw), reads, writes)

    def tt(self, out, in0, in1, op, reads, writes, eng=DVE):
        self.P.add(eng, lambda e: e.tensor_tensor(out=out, in0=in0, in1=in1, op=op), reads, writes)

    def stt(self, out, in0, scalar, in1, op0, op1, reads, writes):
        self.P.add(DVE, lambda e: e.scalar_tensor_tensor(out=out, in0=in0, scalar=scalar, in1=in1,
                                                         op0=op0, op1=op1), reads, writes)

    def ts(self, out, in0, s1, op0, reads, writes, s2=None, op1=None, eng=DVE):
        if op1 is None:
            self.P.add(eng, lambda e: e.tensor_scalar(out=out, in0=in0, scalar1=s1, scalar2=None, op0=op0),
                       reads, writes)
        else:
            self.P.add(eng, lambda e: e.tensor_scalar(out=out, in0=in0, scalar1=s1, scalar2=s2, op0=op0,
                                                      op1=op1), reads, writes)

    def cp(self, out, in_, reads, writes, eng=None):
        if eng is None:
            eng = (ACT, DVE)[self.copy_rr % 2]
            self.copy_rr += 1
        if eng == ACT:
            self.act(out, in_, AF.Copy, reads, writes)
        else:
            self.P.add(eng, lambda e: e.tensor_copy(out=out, in_=in_), reads, writes)

    def recip(self, out, in_, reads, writes):
        self.P.add(DVE, lambda e: e.reciprocal(out=out, in_=in_), reads, writes)

    def memset(self, ap, val, writes, eng=DVE):
        self.P.add(eng, lambda e: e.memset(ap, val), (), writes)

    def end_scope(self):
        self.P.barrier()
        self.P.flush()

    def init_wpool(self, nslots=3, nel=4096):
        self.wslots = [self.sb([128, nel], BF16, "wslot") for _ in range(nslots)]
        self.wdeps = mkdeps(nslots)
        self.wnext = 0
        self.wnel = nel

    def wload(self, parts):
        i = self.wnext
        self.wnext = (i + 1) % len(self.wslots)
        slot = self.wslots[i]
        dep = self.wdeps[i]
        views = []
        off = 0
        first = True
        for (src, nk, ncols) in parts:
            n = nk * ncols
            assert off + n <= self.wnel
            v = slot[:, off:off + n].rearrange("p (k c) -> p k c", k=nk)
            s = src.rearrange("(k p) c -> p k c", p=128)
            self.P.dma(POOL, lambda e, v=v, s=s: e.dma_start(out=v, in_=s), (), [dep],
                       semkey='w%d' % i, accum=not first)
            first = False
            views.append(v)
            off += n
        return views, dep

    def init(self):
        dr = self.dr
        P = self.P
        self.cf = self.sb([128, 7, 128], F32, "cf")
        self.cb16 = self.sb([128, 4, 128], BF16, "cb16")
        self.d_c = Dep()
        cdep = self.d_c
        P.dma(SP, lambda e: e.dma_start(out=self.cf[:], in_=dr['consts']), (), [cdep], semkey='c', accum=True)
        self.ident_f = self.cf[:, 0, :]
        self.triL64 = self.cf[:, 1, :]
        self.triU64 = self.cf[:, 2, :]
        self.triL128 = self.cf[:, 3, :]
        self.ones_f = self.cf[:, 4, :]
        self.gn = self.sb([128, 40], F32, "gn")
        P.dma(SP, lambda e: e.dma_start(out=self.gn[:], in_=dr['gains']), (), [cdep], semkey='c', accum=True)
        self.hgn = self.sb([128, 8], F32, "hgn")
        P.dma(SP, lambda e: e.dma_start(out=self.hgn[:], in_=dr['hgn']), (), [cdep], semkey='c', accum=True)
        self.cw = self.sb([128, 24, 4], F32, "cw")
        P.dma(SP, lambda e: e.dma_start(out=self.cw[:], in_=dr['cw'].rearrange("p (a b) -> p a b", a=24)),
              (), [cdep], semkey='c', accum=True)
        self.cbp = self.sb([128, 24], F32, "cbp")
        P.dma(SP, lambda e: e.dma_start(out=self.cbp[:], in_=dr['cbp']), (), [cdep], semkey='c', accum=True)
        self.dtb = self.sb([128, 32], F32, "dtb")
        self.aneg = self.sb([128, 32], F32, "aneg")
        self.dsk = self.sb([128, 32], F32, "dsk")
        P.dma(SP, lambda e: e.dma_start(out=self.dtb[:], in_=dr['dt_bias'].partition_broadcast(128)),
              (), [cdep], semkey='c', accum=True)
        P.dma(SP, lambda e: e.dma_start(out=self.aneg[:], in_=dr['a_log'].partition_broadcast(128)),
              (), [cdep], semkey='c', accum=True)
        P.dma(SP, lambda e: e.dma_start(out=self.dsk[:], in_=dr['d_skip'].partition_broadcast(128)),
              (), [cdep], semkey='c', accum=True)
        self.oml = self.sb([128, 1024], F32, "oml")
        self.S_hg = self.sb([128, 8, 128], F32, "S_hg")
        self.d_S_hg = mkdeps(8)
        self.S_ssd = self.sb([128, 2048], F32, "S_ssd")
        self.S_ssd_bf = self.sb([128, 2048], BF16, "S_ssd_bf")
        self.d_S_ssd = mkdeps(4)
        self.d_S_ssd_bf = mkdeps(4)
        self.xtail = self.sb([128, 24, 3], BF16, "xtail")
        self.d_xtail = mkdeps(24)
        self.hT = self.sb([128, 8, TB], F32, "hT")
        self.d_h = [[Dep() for _ in range(2)] for _ in range(8)]
        self.nT = self.sb([128, 8, TB], BF16, "nT")
        self.d_n = [[Dep() for _ in range(2)] for _ in range(8)]
        self.big = self.sb([128, 24, TB], BF16, "big")
        self.d_big = [[Dep() for _ in range(2)] for _ in range(24)]
        self.init_wpool()
        self.d_const = Dep()
        with contextlib.ExitStack() as st:
            lbt = self.sb([128, 2, 1024], F32, "lbt", st)
            P.dma(SP, lambda e: e.dma_start(out=lbt[:, 0, :], in_=dr['hg_lb'][0].partition_broadcast(128)),
                  (), [cdep], semkey='c', accum=True)
            P.dma(SP, lambda e: e.dma_start(out=lbt[:, 1, :], in_=dr['hg_lb'][1].partition_broadcast(128)),
                  (), [cdep], semkey='c', accum=True)
            dc = self.d_const
            self.tt(lbt[:, 0, :], lbt[:, 1, :], lbt[:, 0, :], ALU.subtract, [cdep], [dc])
            self.act(self.oml[:], lbt[:, 0, :], AF.Sigmoid, [dc], [dc])
            self.cp(self.cb16[:, 0, :], self.ident_f, [cdep], [dc], eng=DVE)
            self.cp(self.cb16[:, 1, :], self.ones_f, [cdep], [dc], eng=DVE)
            self.cp(self.cb16[:, 2, :], self.cf[:, 5, :], [cdep], [dc], eng=DVE)
            self.negm_b = self.cb16[:, 2, :]
            self.cp(self.cb16[:, 3, :], self.cf[:, 6, :], [cdep], [dc], eng=DVE)
            self.e0_b = self.cb16[:, 3, :]
            self.act(self.aneg[:], self.aneg[:], AF.Exp, [cdep], [dc])
            self.ts(self.aneg[:], self.aneg[:], -1.0, ALU.mult, [dc], [dc])
            self.memset(self.S_hg[:], 0.0, self.d_S_hg)
            self.memset(self.S_ssd[:], 0.0, self.d_S_ssd)
            self.memset(self.S_ssd_bf[:], 0.0, self.d_S_ssd_bf)
            self.memset(self.xtail[:], 0.0, self.d_xtail)
            if 'dbg' in dr:
                self.memset(self.big[:], 0.0, [d for row in self.d_big for d in row])
            self.ident_b = self.cb16[:, 0, :]
            self.ones_b = self.cb16[:, 1, :]
            self.end_scope()

    def load_x(self, blk):
        dr = self.dr
        with contextlib.ExitStack() as st:
            xs = [self.sb([128, 8, 128], F32, "xstage", st) for _ in range(2)]
            dxs = mkdeps(2)
            for kc in range(8):
                s = xs[kc % 2]
                ds = dxs[kc % 2]
                src = dr['x'][blk * TB:(blk + 1) * TB, kc * 128:(kc + 1) * 128].rearrange("(t p) c -> p t c", p=128)
                self.P.dma(SP, lambda e, s=s, src=src: e.dma_start(out=s[:], in_=src), (), [ds],
                           semkey='xs%d' % (kc % 2))
                for half in range(2):
                    bank, bd = self.ps()
                    for q in range(4):
                        tt_ = half * 4 + q
                        self.tr(bank[:, q * 128:(q + 1) * 128], s[:, tt_, :], self.ident_f, [ds, self.d_c], [bd],
                                sig=(q == 3))
                    self.cp(self.hT[:, kc, half * 512:(half + 1) * 512], bank[:], [bd], [self.d_h[kc][half]])
            self.end_scope()

    def norm(self, gi, final=False):
        with contextlib.ExitStack() as st:
            sq = [self.sb([128, TB], BF16, "sq", st) for _ in range(2)]
            dsq = mkdeps(2)
            rstd = self.sb([128, TB], F32, "rstd", st)
            drs = mkdeps(2)
            b0, bd0 = self.ps()
            b1, bd1 = self.ps()
            bks = ((b0, bd0), (b1, bd1))
            for kc in range(8):
                s = sq[kc % 2]
                ds = dsq[kc % 2]
                self.act(s[:], self.hT[:, kc, :], AF.Square, self.d_h[kc], [ds])
                for tb in range(2):
                    self.mm(bks[tb][0][:], self.ones_b, s[:, tb * 512:(tb + 1) * 512], kc == 0, kc == 7,
                            [ds, self.d_const], [bks[tb][1]])
            for tb in range(2):
                sl = slice(tb * 512, (tb + 1) * 512)
                self.act(rstd[:, sl], bks[tb][0][:], AF.Sqrt, [bks[tb][1]], [drs[tb]], bias=EPS, scale=1.0 / D)
                self.recip(rstd[:, sl], rstd[:, sl], [drs[tb]], [drs[tb]])
            for kc in range(8):
                for tb in range(2):
                    sl = slice(tb * 512, (tb + 1) * 512)
                    if final:
                        self.stt(self.hT[:, kc, sl], self.hT[:, kc, sl], self.gn[:, gi * 8 + kc:gi * 8 + kc + 1],
                                 rstd[:, sl], ALU.mult, ALU.mult, [self.d_h[kc][tb], drs[tb], self.d_c],
                                 [self.d_h[kc][tb]])
                    else:
                        self.stt(self.nT[:, kc, sl], self.hT[:, kc, sl], self.gn[:, gi * 8 + kc:gi * 8 + kc + 1],
                                 rstd[:, sl], ALU.mult, ALU.mult, [self.d_h[kc][tb], drs[tb], self.d_c],
                                 [self.d_n[kc][tb]])
            self.end_scope()

    def ffn(self, w13, w2):
        with contextlib.ExitStack() as st:
            sg = [self.sb([128, 512], F32, "sg", st) for _ in range(2)]
            dsg = mkdeps(2)
            it = 0
            for g2 in range(NFC // 2):
                (wg, wu), wd = self.wload([(w13[:, g2 * 256:(g2 + 1) * 256], 8, 256),
                                           (w13[:, DFF + g2 * 256:DFF + (g2 + 1) * 256], 8, 256)])
                for j in range(2):
                    fc = g2 * 2 + j
                    for tb in range(2):
                        sl = slice(tb * 512, (tb + 1) * 512)
                        pg, dg = self.ps()
                        pu, du = self.ps()
                        for kc in range(8):
                            self.mm(pg[:], wg[:, kc, j * 128:(j + 1) * 128], self.nT[:, kc, sl], kc == 0, kc == 7,
                                    [wd, self.d_n[kc][tb]], [dg])
                        for kc in range(8):
                            self.mm(pu[:], wu[:, kc, j * 128:(j + 1) * 128], self.nT[:, kc, sl], kc == 0, kc == 7,
                                    [wd, self.d_n[kc][tb]], [du])
                        s = sg[it % 2]
                        ds = dsg[it % 2]
                        it += 1
                        self.act(s[:], pg[:], AF.Silu, [dg], [ds])
                        self.tt(self.big[:, fc, sl], s[:], pu[:], ALU.mult, [ds, du], [self.d_big[fc][tb]])
            for dc in range(8):
                (wa,), wda = self.wload([(w2[:, dc * 128:(dc + 1) * 128], NFC, 128)])
                for tb in range(2):
                    sl = slice(tb * 512, (tb + 1) * 512)
                    po, do = self.ps()
                    for fc in range(NFC):
                        self.mm(po[:], wa[:, fc, :], self.big[:, fc, sl], fc == 0, fc == NFC - 1,
                                [wda, self.d_big[fc][tb]], [do])
                    self.stt(self.hT[:, dc, sl], po[:], 0.5, self.hT[:, dc, sl], ALU.mult, ALU.add,
                             [do, self.d_h[dc][tb]], [self.d_h[dc][tb]])
            self.end_scope()

    def hgrn(self, blk):
        w_in = self.dr['w_in']
        nT, dn = self.nT, self.d_n
        with contextlib.ExitStack() as st:
            sbs = lambda shape, dt, name: self.sb(shape, dt, name, st)
            dd2 = lambda: [mkdeps(2) for _ in range(2)]
            k_tm = sbs([128, 8, 128], F32, "k_tm"); d_k = mkdeps(2)
            logf = sbs([128, 8, 128], F32, "logf"); d_lf = mkdeps(2)
            qs = sbs([128, TB], BF16, "qs"); d_qs = mkdeps(2)
            eGi = sbs([128, TB], BF16, "eGi"); d_eGi = mkdeps(2)
            eGe = sbs([128, 8, 128], BF16, "eGe"); d_eGe = mkdeps(2)
            k_inv = sbs([128, TB], BF16, "k_inv"); d_ki = mkdeps(2)
            v_bf = [sbs([128, 8, 128], BF16, "v_bf") for _ in range(2)]; d_v = dd2()
            sgate = [sbs([128, TB], BF16, "sgate") for _ in range(2)]; d_sg = dd2()
            eG = [sbs([128, TB], F32, "eG") for _ in range(2)]; d_eG = dd2()
            q_dec = [sbs([128, TB], BF16, "q_dec") for _ in range(2)]; d_qd = dd2()
            k_end = [sbs([128, 8, 128], BF16, "k_end") for _ in range(2)]; d_ke = dd2()
            att = [sbs([128, 8, 128], BF16, "att") for _ in range(2)]; d_att = dd2()
            osq = [sbs([128, 512], BF16, "osq") for _ in range(2)]; d_osq = mkdeps(2)
            sd = [sbs([128, 512], F32, "sd") for _ in range(2)]; d_sd = mkdeps(2)
            tmp = [sbs([128, 512], F32, "tmp") for _ in range(2)]; d_tmp = mkdeps(2)
            Sbf = [sbs([128, 128], BF16, "Sbf") for _ in range(4)]; d_Sbf = mkdeps(4)
            self._sbi = 0
            v4 = lambda bank: bank[:].rearrange("p (a b) -> p a b", a=4)

            hw = {}

            def hload(j):
                hw[j] = self.wload([(w_in[:, OFF_Q + j * 128:OFF_Q + (j + 1) * 128], 8, 128),
                                    (w_in[:, OFF_F + j * 128:OFF_F + (j + 1) * 128], 8, 128),
                                    (w_in[:, OFF_V + j * 128:OFF_V + (j + 1) * 128], 8, 128),
                                    (w_in[:, OFF_G + j * 128:OFF_G + (j + 1) * 128], 8, 128)])

            hload(0)

            def front(j):
                p = j % 2
                if j + 1 < 8:
                    hload(j + 1)
                (wq, wf, wv, wg), wd = hw[j]
                for tb in range(2):
                    sl = slice(tb * 512, (tb + 1) * 512)
                    pq, dq = self.ps()
                    for kc in range(8):
                        self.mm(pq[:], wq[:, kc, :], nT[:, kc, sl], kc == 0, kc == 7, [wd, dn[kc][tb]], [dq])
                    self.act(qs[:, sl], pq[:], AF.Copy, [dq], [d_qs[tb]], scale=float(128 ** -0.5))
                    yield
                    pg, dg = self.ps()
                    for kc in range(8):
                        self.mm(pg[:], wg[:, kc, :], nT[:, kc, sl], kc == 0, kc == 7, [wd, dn[kc][tb]], [dg])
                    self.act(sgate[p][:, sl], pg[:], AF.Silu, [dg], [d_sg[p][tb]])
                    yield
                for hf in range(2):
                    sl = slice(hf * 512, (hf + 1) * 512)
                    h4 = slice(hf * 4, hf * 4 + 4)
                    pf, df = self.ps()
                    for q in range(4):
                        t_ = hf * 4 + q
                        for kc in range(8):
                            self.mm(pf[:, q * 128:(q + 1) * 128], nT[:, kc, t_ * 128:(t_ + 1) * 128], wf[:, kc, :],
                                    kc == 0, kc == 7, [wd, dn[kc][hf]], [df])
                    self.act(k_tm[:, h4, :], v4(pf), AF.Exp, [df], [d_k[hf]])
                    self.act(k_tm[:, h4, :], k_tm[:, h4, :], AF.Ln, [d_k[hf]], [d_k[hf]], bias=1.0)
                    self.act(k_tm[:, h4, :], k_tm[:, h4, :], AF.Exp, [d_k[hf]], [d_k[hf]], scale=-1.0)
                    yield
                    pv, dv = self.ps()
                    for q in range(4):
                        t_ = hf * 4 + q
                        for kc in range(8):
                            self.mm(pv[:, q * 128:(q + 1) * 128], nT[:, kc, t_ * 128:(t_ + 1) * 128], wv[:, kc, :],
                                    kc == 0, kc == 7, [wd, dn[kc][hf]], [dv])
                    self.cp(v_bf[p][:, h4, :], v4(pv), [dv], [d_v[p][hf]])
                    yield
                    self.tt(k_tm[:, h4, :], k_tm[:, h4, :],
                            self.oml[:, j * 128:(j + 1) * 128].unsqueeze(1).to_broadcast([128, 4, 128]), ALU.mult,
                            [d_k[hf], self.d_const], [d_k[hf]])
                    self.act(logf[:, h4, :], k_tm[:, h4, :], AF.Ln, [d_k[hf]], [d_lf[hf]], scale=-1.0, bias=1.0)
                    pG, dG = self.ps()
                    pE, dE = self.ps()
                    pK, dK = self.ps()
                    for q in range(4):
                        t_ = hf * 4 + q
                        cs = slice(q * 128, (q + 1) * 128)
                        self.mm(pG[:, cs], logf[:, t_, :], self.triL64, True, True, [d_lf[hf], self.d_c], [dG])
                        self.mm(pE[:, cs], self.triU64, logf[:, t_, :], True, True, [d_lf[hf], self.d_c], [dE])
                        self.tr(pK[:, cs], k_tm[:, t_, :], self.ident_f, [d_k[hf], self.d_c], [dK])
                    self.act(eG[p][:, sl], pG[:], AF.Exp, [dG], [d_eG[p][hf]])
                    self.act(eGi[:, sl], pG[:], AF.Exp, [dG], [d_eGi[hf]], scale=-1.0)
                    self.act(eGe[:, h4, :], v4(pE), AF.Exp, [dE], [d_eGe[hf]])
                    self.tt(q_dec[p][:, sl], qs[:, sl], eG[p][:, sl], ALU.mult, [d_qs[hf], d_eG[p][hf]], [d_qd[p][hf]])
                    self.tt(k_inv[:, sl], pK[:], eGi[:, sl], ALU.mult, [dK, d_eGi[hf]], [d_ki[hf]])
                    self.tt(k_end[p][:, h4, :], k_tm[:, h4, :], eGe[:, h4, :], ALU.mult, [d_k[hf], d_eGe[hf]],
                            [d_ke[p][hf]])
                    yield
                    pA, dA = self.ps()
                    for q in range(4):
                        t_ = hf * 4 + q
                        cs = slice(q * 128, (q + 1) * 128)
                        ts_ = slice(t_ * 128, (t_ + 1) * 128)
                        self.mm(pA[:, cs], k_inv[:, ts_], q_dec[p][:, ts_], True, True, [d_ki[hf], d_qd[p][hf]], [dA])
                    self.tt(att[p][:, h4, :], v4(pA), self.triL64.unsqueeze(1).to_broadcast([128, 4, 128]), ALU.mult,
                            [dA, self.d_c], [d_att[p][hf]])
                    yield

            def back(j):
                p = j % 2
                dS = self.d_S_hg[j]
                S = self.S_hg[:, j, :]
                cur = self._sbi % 4
                self._sbi += 1
                self.cp(Sbf[cur][:], S, [dS], [d_Sbf[cur]], eng=ACT)
                for hf in range(2):
                    sl = slice(hf * 512, (hf + 1) * 512)
                    pO, dO = self.ps(long=True)
                    for q in range(4):
                        t_ = hf * 4 + q
                        cs = slice(q * 128, (q + 1) * 128)
                        pS, dSp = self.ps()
                        pS1, dSp1 = self.ps()
                        self.mm(pS[:, 0:128], k_end[p][0:64, t_, :], v_bf[p][0:64, t_, :], True, True,
                                [d_ke[p][hf], d_v[p][hf]], [dSp])
                        self.mm(pS1[:, 0:128], k_end[p][64:128, t_, :], v_bf[p][64:128, t_, :], True, True,
                                [d_ke[p][hf], d_v[p][hf]], [dSp1])
                        self.mm(pO[:, cs], v_bf[p][:, t_, :], att[p][:, t_, :], True, False, [d_v[p][hf], d_att[p][hf]], [dO])
                        self.mm(pO[:, q * 128:q * 128 + 64], Sbf[cur][:], q_dec[p][:, t_ * 128:t_ * 128 + 64], False, False,
                                [d_Sbf[cur], d_qd[p][hf]], [dO])
                        c0 = t_ * 128 + 63
                        n1 = self._sbi % 4
                        self._sbi += 1
                        self.stt(Sbf[n1][:], S, eG[p][:, c0:c0 + 1], pS[:, 0:128], ALU.mult, ALU.add,
                                 [dS, d_eG[p][hf], dSp], [d_Sbf[n1]])
                        self.stt(S, S, eG[p][:, c0:c0 + 1], pS[:, 0:128], ALU.mult, ALU.add, [dS, d_eG[p][hf], dSp], [dS])
                        self.mm(pO[:, q * 128 + 64:q * 128 + 128], Sbf[n1][:], q_dec[p][:, t_ * 128 + 64:t_ * 128 + 128],
                                False, True, [d_Sbf[n1], d_qd[p][hf]], [dO])
                        c1 = t_ * 128 + 127
                        cur = self._sbi % 4
                        self._sbi += 1
                        self.stt(Sbf[cur][:], S, eG[p][:, c1:c1 + 1], pS1[:, 0:128], ALU.mult, ALU.add,
                                 [dS, d_eG[p][hf], dSp1], [d_Sbf[cur]])
                        self.stt(S, S, eG[p][:, c1:c1 + 1], pS1[:, 0:128], ALU.mult, ALU.add, [dS, d_eG[p][hf], dSp1], [dS])
                        yield
                    r = hf
                    self.act(osq[r][:], pO[:], AF.Square, [dO], [d_osq[r]])
                    pSS, dSS = self.ps()
                    self.mm(pSS[:], self.ones_b, osq[r][:], True, True, [d_osq[r], self.d_const], [dSS])
                    self.act(sd[r][:], pSS[:], AF.Ln, [dSS], [d_sd[r]], scale=1.0 / 128, bias=EPS)
                    self.act(sd[r][:], sd[r][:], AF.Exp, [d_sd[r]], [d_sd[r]], scale=-0.5)
                    self.tt(tmp[r][:], pO[:], sd[r][:], ALU.mult, [dO, d_sd[r]], [d_tmp[r]])
                    self.stt(self.big[:, j, sl], tmp[r][:], self.hgn[:, j:j + 1], sgate[p][:, sl], ALU.mult, ALU.mult,
                             [d_tmp[r], self.d_c, d_sg[p][hf]], [self.d_big[j][hf]])
                    yield

            for _ in front(0):
                pass
            for j in range(8):
                b = back(j)
                f = front(j + 1) if j < 7 else None
                alive_b, alive_f = True, f is not None
                while alive_b or alive_f:
                    if alive_b:
                        try:
                            next(b)
                        except StopIteration:
                            alive_b = False
                    if alive_f:
                        try:
                            next(f)
                        except StopIteration:
                            alive_f = False
            self.end_scope()

    def ssd(self, blk):
        dr = self.dr
        w_in = dr['w_in']
        nT, dn = self.nT, self.d_n
        with contextlib.ExitStack() as st:
            sbs = lambda shape, dt, name: self.sb(shape, dt, name, st)
            sm = lambda name, dt=F32: sbs([128, 8, 32], dt, name)
            dtv, lndt, a_, acs, wst, wd_, dl = [sm(n) for n in ("dtv", "lndt", "a_", "acs", "wst", "wd_", "dl")]
            b2 = lndt
            acs_hi = sm("acs_hi", BF16)
            acs_lo = sm("acs_lo", BF16)
            d_dt = Dep()
            xpT = sbs([128, 6, 3 + TB], BF16, "xpT"); d_xp = [[Dep() for _ in range(3)] for _ in range(6)]
            BT = sbs([128, TB], BF16, "BT"); d_BT = mkdeps(2)
            CT = sbs([128, TB], BF16, "CT"); d_CT = mkdeps(2)
            diag = sbs([128, 6, 4, 128], BF16, "diag"); d_dg = mkdeps(6)
            cbr = sbs([128, 640], BF16, "cbr"); d_cbr = Dep()
            self.memset(cbr[:], 0.0, [d_cbr])
            gss = sbs([128, 512], F32, "gss"); d_gss = Dep()
            R2 = 2
            R3 = 3
            xs_bf = [sbs([128, 512], BF16, "xs_bf") for _ in range(R3)]; d_xs = mkdeps(R3)
            Dd = sbs([128, 8, 128], BF16, "Dd"); d_Dd = Dep()
            xsw = [sbs([128, 512], BF16, "xsw") for _ in range(R3)]; d_xsw = mkdeps(R3)
            B_tm = [sbs([128, 128], BF16, "B_tm") for _ in range(R3)]; d_Bt = mkdeps(R3)
            smz = [sbs([128, 512], BF16, "smz") for _ in range(R3)]; d_smz = mkdeps(R3)
            E_ = sbs([128, 4, 128], F32, "E"); E = [E_, E_]; d_E_ = Dep(); d_E = [d_E_, d_E_]
            Mp = [sbs([128, 8, 128], BF16, "Mp") for _ in range(3)]; d_Mp = [mkdeps(2) for _ in range(3)]
            t1_ = sbs([128, 512], F32, "t1"); t1 = [t1_] * R2; d_t1_ = Dep(); d_t1 = [d_t1_] * R2
            yv = [sbs([128, 512], F32, "yv") for _ in range(R2)]; d_yv = mkdeps(R2)
            ssq = [sbs([128, 1], F32, "ssq") for _ in range(R2)]; d_ssq = mkdeps(R2)
            yn = [sbs([128, 512], BF16, "yn") for _ in range(R2)]; d_yn = mkdeps(R2)

            (wdt,), wdd = self.wload([(w_in[:, OFF_DT:OFF_DT + 32], 8, 32)])
            pD, dD = self.ps()
            for t_ in range(8):
                for kc in range(8):
                    self.mm(pD[:, t_ * 32:(t_ + 1) * 32], nT[:, kc, t_ * 128:(t_ + 1) * 128], wdt[:, kc, :], kc == 0, kc == 7,
                            [wdd, dn[kc][t_ // 4]], [dD])
            v8 = lambda bank: bank[:, 0:256].rearrange("p (a b) -> p a b", a=8)
            bc8 = lambda ap: ap.unsqueeze(1).to_broadcast([128, 8, 32])
            dd = [d_dt]
            self.tt(dtv[:], v8(pD), bc8(self.dtb[:]), ALU.add, [dD, self.d_c], dd)
            self.act(dtv[:], dtv[:], AF.Exp, dd, dd)
            self.act(dtv[:], dtv[:], AF.Ln, dd, dd, bias=1.0)
            self.act(lndt[:], dtv[:], AF.Ln, dd, dd)
            self.tt(a_[:], dtv[:], bc8(self.aneg[:]), ALU.mult, dd + [self.d_const], dd)
            pAc, dAc = self.ps()
            pTo, dTo = self.ps()
            for t_ in range(8):
                self.mm(pAc[:, t_ * 32:(t_ + 1) * 32], self.triL128, a_[:, t_, :], True, True, dd + [self.d_c], [dAc])
                self.mm(pTo[:, t_ * 32:(t_ + 1) * 32], self.ones_f, a_[:, t_, :], True, True, dd + [self.d_c], [dTo])
            self.cp(acs[:], v8(pAc), [dAc], dd, eng=ACT)
            self.act(wst[:], v8(pAc), AF.Exp, [dAc], dd)
            self.act(dl[:], v8(pTo), AF.Exp, [dTo], dd)
            self.tt(wd_[:], v8(pTo), acs[:], ALU.subtract, [dTo] + dd, dd)
            self.act(wd_[:], wd_[:], AF.Exp, dd, dd)
            self.tt(wd_[:], wd_[:], dtv[:], ALU.mult, dd, dd)
            self.tt(b2[:], lndt[:], acs[:], ALU.subtract, dd, dd)
            self.cp(acs_hi[:], acs[:], dd, dd, eng=DVE)
            self.tt(acs_lo[:], acs[:], acs_hi[:], ALU.subtract, dd, dd)

            self._ei = 0
            gw = {}

            def gload_xb(g):
                gw[g] = (self.wload([(w_in[:, OFF_XBC + g * 512:OFF_XBC + (g + 1) * 512], 8, 512)]),
                         self.wload([(w_in[:, OFF_XBC + 2048 + g * 128:OFF_XBC + 2048 + (g + 1) * 128], 8, 128),
                                     (w_in[:, OFF_XBC + 2560 + g * 128:OFF_XBC + 2560 + (g + 1) * 128], 8, 128)]))

            gload_xb(0)
            for g in range(4):
                ((wx,), wxd), ((wB, wC), wbd) = gw[g]
                (wz,), wzd = self.wload([(w_in[:, OFF_MZ + g * 512:OFF_MZ + (g + 1) * 512], 8, 512)])
                chs = [4 * g, 4 * g + 1, 4 * g + 2, 4 * g + 3, 16 + g, 20 + g]
                self.P.dma(POOL, lambda e, g=g: e.dma_start(out=cbr[0:1, 0:512], in_=dr['cbrow'][0:1, g * 512:(g + 1) * 512]),
                           (), [d_cbr], semkey='cbr')
                self.P.dma(POOL, lambda e, g=g: e.dma_start(out=cbr[0:1, 512:640],
                                                           in_=dr['cbrow'][0:1, 2048 + g * 128:2048 + (g + 1) * 128]),
                           (), [d_cbr], semkey='cbr', accum=True)
                self.P.dma(SP, lambda e, g=g: e.dma_start(out=gss[:], in_=dr['ssm_norm'][g * 512:(g + 1) * 512].partition_broadcast(128)),
                           (), [d_gss], semkey='gss')
                for c6 in range(6):
                    ch = chs[c6]
                    self.cp(xpT[:, c6, 0:3], self.xtail[:, ch, :], [self.d_xtail[ch]], [d_xp[c6][0]], eng=DVE)
                    wsrc = wx[:, :, c6 * 128:(c6 + 1) * 128] if c6 < 4 else (wB if c6 == 4 else wC)
                    wdp = wxd if c6 < 4 else wbd
                    for tb in range(2):
                        sl = slice(tb * 512, (tb + 1) * 512)
                        px, dpx = self.ps()
                        for kc in range(8):
                            self.mm(px[:], wsrc[:, kc, :], nT[:, kc, sl], kc == 0, kc == 7, [wdp, dn[kc][tb]], [dpx])
                        self.cp(xpT[:, c6, 3 + tb * 512:3 + (tb + 1) * 512], px[:], [dpx], [d_xp[c6][1 + tb]])
                    self.cp(self.xtail[:, ch, :], xpT[:, c6, TB:TB + 3], [d_xp[c6][2]], [self.d_xtail[ch]], eng=DVE)
                    for tap in range(4):
                        self.ts(diag[:, c6, tap, :], self.ident_f, self.cw[:, ch, tap:tap + 1], ALU.mult,
                                [self.d_c], [d_dg[c6]])
                for (c6, dst, dd_, col) in ((4, BT, d_BT, 16 + g), (5, CT, d_CT, 20 + g)):
                    for tb in range(2):
                        sl = slice(tb * 512, (tb + 1) * 512)
                        pb, dpb = self.ps()
                        for tap in range(4):
                            self.mm(pb[:], diag[:, c6, tap, :], xpT[:, c6, tb * 512 + tap:tb * 512 + tap + 512], tap == 0,
                                    tap == 3, [d_dg[c6]] + d_xp[c6], [dpb])
                        self.act(dst[:, sl], pb[:], AF.Silu, [dpb, self.d_c], [dd_[tb]], bias=self.cbp[:, col:col + 1])
                v864 = lambda ap: ap.rearrange("p (a b) -> p a b", a=8)
                bch = lambda ap: ap.unsqueeze(2).to_broadcast([128, 8, 64])
                hs = slice(g * 8, (g + 1) * 8)
                for hh in range(8):
                    self.ts(Dd[:, hh, :], self.ident_f, self.dsk[:, g * 8 + hh:g * 8 + hh + 1], ALU.mult, [self.d_c], [d_Dd])
                Sg = self.S_ssd[:, g * 512:(g + 1) * 512]

                def front(t_, g=g, wz=wz, wzd=wzd, hs=hs):
                    hf = t_ // 4
                    tsl = slice(t_ * 128, (t_ + 1) * 128)
                    r = t_ % 3
                    pxs, dxs_ = self.ps()
                    for cc in range(4):
                        cs = slice(cc * 128, (cc + 1) * 128)
                        for tap in range(4):
                            self.mm(pxs[:, cs], xpT[:, cc, t_ * 128 + tap:t_ * 128 + tap + 128], diag[:, cc, tap, :], tap == 0,
                                    False, d_xp[cc] + [d_dg[cc]], [dxs_])
                        self.mm(pxs[:, cs], self.e0_b, cbr[:, cs], False, True, [self.d_const, d_cbr], [dxs_])
                    self.act(xs_bf[r][:], pxs[:], AF.Silu, [dxs_], [d_xs[r]])
                    pbt, dbt = self.ps()
                    for tap in range(4):
                        self.mm(pbt[:, 0:128], xpT[:, 4, t_ * 128 + tap:t_ * 128 + tap + 128], diag[:, 4, tap, :], tap == 0,
                                False, d_xp[4] + [d_dg[4]], [dbt])
                    self.mm(pbt[:, 0:128], self.e0_b, cbr[:, 512:640], False, True, [self.d_const, d_cbr], [dbt])
                    self.act(B_tm[r][:], pbt[:, 0:128], AF.Silu, [dbt], [d_Bt[r]])
                    pz, dz = self.ps()
                    for kc in range(8):
                        self.mm(pz[:], nT[:, kc, tsl], wz[:, kc, :], kc == 0, kc == 7, [wzd, dn[kc][hf]], [dz])
                    self.act(smz[r][:], pz[:], AF.Silu, [dz], [d_smz[r]])
                    self.tt(v864(xsw[r][:]), v864(xs_bf[r][:]), bch(wd_[:, t_, hs]), ALU.mult, [d_xs[r], d_dt], [d_xsw[r]])
                    pcb, dcb = self.ps()
                    self.mm(pcb[:, 0:128], BT[:, tsl], CT[:, tsl], True, True, [d_BT[hf], d_CT[hf]], [dcb])
                    for hq in range(2):
                        pab, dab = self.ps()
                        for q in range(4):
                            h = g * 8 + hq * 4 + q
                            cs = slice(q * 128, (q + 1) * 128)
                            self.mm(pab[:, cs], acs_hi[:, t_, h:h + 1].to_broadcast([128, 128]), self.ident_b, True, False,
                                    [d_dt, self.d_const], [dab])
                            self.mm(pab[:, cs], acs_lo[:, t_, h:h + 1].to_broadcast([128, 128]), self.ident_b, False, False,
                                    [d_dt, self.d_const], [dab])
                            self.mm(pab[:, cs], self.ident_b, self.negm_b, False, True, [self.d_const], [dab])
                        e_ = self._ei % 2
                        self._ei += 1
                        for q in range(4):
                            h = g * 8 + hq * 4 + q
                            cs = slice(q * 128, (q + 1) * 128)
                            self.act(E[e_][:, q, :], pab[:, cs], AF.Exp, [dab, d_dt], [d_E[e_]], bias=b2[:, t_, h:h + 1])
                        self.tt(Mp[r][:, hq * 4:hq * 4 + 4, :], E[e_][:],
                                pcb[:, 0:128].unsqueeze(1).to_broadcast([128, 4, 128]), ALU.mult, [d_E[e_], dcb],
                                [d_Mp[r][hq]])

                def mid(t_, g=g, hs=hs, Sg=Sg):
                    hf = t_ // 4
                    tsl = slice(t_ * 128, (t_ + 1) * 128)
                    r = t_ % 3
                    r2 = t_ % 2
                    py, dy = self.ps(long=True)
                    for hh in range(8):
                        m_ = r * 8 + hh
                        self.mm(py[:, hh * 64:(hh + 1) * 64], Dd[:, hh, :], xs_bf[r][:, hh * 64:(hh + 1) * 64], True,
                                False, [d_Dd, d_xs[r]], [dy])
                        self.mm(py[:, hh * 64:(hh + 1) * 64], Mp[r][:, hh, :], xs_bf[r][:, hh * 64:(hh + 1) * 64], False,
                                True, [d_Mp[r][hh // 4], d_xs[r]], [dy])
                    pyb, dyb = self.ps()
                    self.mm(pyb[:], CT[:, tsl], self.S_ssd_bf[:, g * 512:(g + 1) * 512], True, True,
                            [d_CT[hf], self.d_S_ssd_bf[g]], [dyb])
                    pds, dds = self.ps()
                    self.mm(pds[:], B_tm[r][:], xsw[r][:], True, True, [d_Bt[r], d_xsw[r]], [dds])
                    self.tt(v864(t1[r2][:]), v864(pyb[:]), bch(wst[:, t_, hs]), ALU.mult, [dyb, d_dt], [d_t1[r2]])
                    self.tt(v864(Sg), v864(Sg), bch(dl[:, t_, hs]), ALU.mult, [self.d_S_ssd[g], d_dt], [self.d_S_ssd[g]])
                    self.tt(self.S_ssd_bf[:, g * 512:(g + 1) * 512], Sg, pds[:], ALU.add, [self.d_S_ssd[g], dds],
                            [self.d_S_ssd_bf[g]])
                    self.tt(Sg, Sg, pds[:], ALU.add, [self.d_S_ssd[g], dds], [self.d_S_ssd[g]])
                    self.tt(yv[r2][:], py[:], t1[r2][:], ALU.add, [dy, d_t1[r2]], [d_yv[r2]])
                    self.tt(yv[r2][:], yv[r2][:], smz[r][:], ALU.mult, [d_yv[r2], d_smz[r]], [d_yv[r2]])
                    self.act(t1[r2][:], yv[r2][:], AF.Square, [d_yv[r2]], [d_t1[r2], d_ssq[r2]], accum=ssq[r2][:])
                    self.act(ssq[r2][:], ssq[r2][:], AF.Ln, [d_ssq[r2]], [d_ssq[r2]], scale=1.0 / 512, bias=EPS)
                    self.act(ssq[r2][:], ssq[r2][:], AF.Exp, [d_ssq[r2]], [d_ssq[r2]], scale=-0.5)

                def tail_a(t_, g=g):
                    r = t_ % 2
                    self.stt(yn[r][:], yv[r][:], ssq[r][:, 0:1], gss[:], ALU.mult, ALU.mult, [d_yv[r], d_ssq[r], d_gss],
                             [d_yn[r]])

                def tail(t_, g=g):
                    hf = t_ // 4
                    tsl = slice(t_ * 128, (t_ + 1) * 128)
                    r = t_ % 2
                    pyt, dyt = self.ps()
                    pytb = pyt[:].bitcast(BF16)
                    for cc in range(4):
                        self.tr(pytb[:, cc * 128:(cc + 1) * 128], yn[r][:, cc * 128:(cc + 1) * 128], self.ident_b,
                                [d_yn[r], self.d_const], [dyt])
                    self.cp(self.big[:, 8 + 4 * g:12 + 4 * g, tsl], pytb[:, 0:512].rearrange("p (a b) -> p a b", a=4), [dyt],
                            [self.d_big[8 + 4 * g + cc][hf] for cc in range(4)], eng=ACT)

                if g + 1 < 4:
                    gload_xb(g + 1)
                front(0)
                front(1)
                for t_ in range(8):
                    if t_ > 0:
                        tail_a(t_ - 1)
                    mid(t_)
                    if t_ + 2 < 8:
                        front(t_ + 2)
                    if t_ > 0:
                        tail(t_ - 1)
                tail_a(7)
                tail(7)
            self.end_scope()

    def outproj(self, blk):
        dr = self.dr
        w_in = dr['w_in']
        nT, dn = self.nT, self.d_n
        with contextlib.ExitStack() as st:
            sbs = lambda shape, dt, name: self.sb(shape, dt, name, st)
            mix = sbs([128, 8, TB], BF16, "mix"); d_mix = [[Dep() for _ in range(2)] for _ in range(8)]
            sga = [sbs([128, 512], F32, "sga") for _ in range(2)]; d_sga = mkdeps(2)
            sgb = [sbs([128, 512], F32, "sgb") for _ in range(2)]; d_sgb = mkdeps(2)
            m1 = [sbs([128, 512], F32, "m1") for _ in range(2)]; d_m1 = mkdeps(2)
            it = 0
            for fc in range(8):
                cs = slice(fc * 128, (fc + 1) * 128)
                (wA, wga, wgb), wad = self.wload([(dr['w_hg_out'][:, cs], 8, 128),
                                                  (w_in[:, OFF_BRG + fc * 128:OFF_BRG + (fc + 1) * 128], 8, 128),
                                                  (w_in[:, OFF_BRG + 1024 + fc * 128:OFF_BRG + 1024 + (fc + 1) * 128], 8, 128)])
                (wBm,), wbd = self.wload([(dr['w_ssm_out'][:, cs], 16, 128)])
                for tb in range(2):
                    sl = slice(tb * 512, (tb + 1) * 512)
                    r = it % 2
                    it += 1
                    pA, dA = self.ps()
                    for kc in range(8):
                        self.mm(pA[:], wA[:, kc, :], self.big[:, kc, sl], kc == 0, kc == 7, [wad, self.d_big[kc][tb]], [dA])
                    pB, dB = self.ps()
                    for kc in range(16):
                        self.mm(pB[:], wBm[:, kc, :], self.big[:, 8 + kc, sl], kc == 0, kc == 15,
                                [wbd, self.d_big[8 + kc][tb]], [dB])
                    pga, dga = self.ps()
                    for kc in range(8):
                        self.mm(pga[:], wga[:, kc, :], nT[:, kc, sl], kc == 0, kc == 7, [wad, dn[kc][tb]], [dga])
                    pgb, dgb = self.ps()
                    for kc in range(8):
                        self.mm(pgb[:], wgb[:, kc, :], nT[:, kc, sl], kc == 0, kc == 7, [wad, dn[kc][tb]], [dgb])
                    self.act(sga[r][:], pga[:], AF.Sigmoid, [dga], [d_sga[r]])
                    self.act(sgb[r][:], pgb[:], AF.Sigmoid, [dgb], [d_sgb[r]])
                    self.tt(m1[r][:], pA[:], sga[r][:], ALU.mult, [dA, d_sga[r]], [d_m1[r]])
                    self.tt(sgb[r][:], pB[:], sgb[r][:], ALU.mult, [dB, d_sgb[r]], [d_sgb[r]])
                    self.tt(mix[:, fc, sl], m1[r][:], sgb[r][:], ALU.add, [d_m1[r], d_sgb[r]], [d_mix[fc][tb]])
            for dcp in range(4):
                (wo,), wod = self.wload([(dr['w_out'][:, dcp * 256:(dcp + 1) * 256], 8, 256)])
                for j in range(2):
                    dc = dcp * 2 + j
                    for tb in range(2):
                        sl = slice(tb * 512, (tb + 1) * 512)
                        po, do = self.ps()
                        for kc in range(8):
                            self.mm(po[:], wo[:, kc, j * 128:(j + 1) * 128], mix[:, kc, sl], kc == 0, kc == 7,
                                    [wod, d_mix[kc][tb]], [do])
                        self.tt(self.hT[:, dc, sl], po[:], self.hT[:, dc, sl], ALU.add, [do, self.d_h[dc][tb]],
                                [self.d_h[dc][tb]])
            self.end_scope()

    def ple(self, blk):
        dr = self.dr
        nT, dn = self.nT, self.d_n
        with contextlib.ExitStack() as st:
            sbs = lambda shape, dt, name: self.sb(shape, dt, name, st)
            pst = [sbs([128, 256], F32, "pst") for _ in range(2)]; d_pst = mkdeps(2)
            pT = sbs([128, 2, TB], BF16, "pT"); d_pT = mkdeps(2)
            sg = [sbs([128, 512], F32, "sgp") for _ in range(2)]; d_sg = mkdeps(2)
            for t_ in range(8):
                r = t_ % 2
                src = dr['p'][blk * TB + t_ * 128:blk * TB + (t_ + 1) * 128, :]
                self.P.dma(SP, lambda e, r=r, src=src: e.dma_start(out=pst[r][:], in_=src), (), [d_pst[r]], semkey='pst%d' % r)
                bank, bd = self.ps()
                for kc in range(2):
                    self.tr(bank[:, kc * 128:(kc + 1) * 128], pst[r][:, kc * 128:(kc + 1) * 128], self.ident_f,
                            [d_pst[r], self.d_c], [bd])
                self.cp(pT[:, :, t_ * 128:(t_ + 1) * 128], bank[:, 0:256].rearrange("p (a b) -> p a b", a=2), [bd],
                        [d_pT[t_ // 4]])
            it = 0
            for fcp in range(4):
                (wg, wp), wd = self.wload([(dr['w_ple_gate'][:, fcp * 256:(fcp + 1) * 256], 8, 256),
                                           (dr['w_ple_proj'][:, fcp * 256:(fcp + 1) * 256], 2, 256)])
                for j in range(2):
                    fc = fcp * 2 + j
                    for tb in range(2):
                        sl = slice(tb * 512, (tb + 1) * 512)
                        r = it % 2
                        it += 1
                        pg, dg = self.ps()
                        for kc in range(8):
                            self.mm(pg[:], wg[:, kc, j * 128:(j + 1) * 128], nT[:, kc, sl], kc == 0, kc == 7,
                                    [wd, dn[kc][tb]], [dg])
                        pp, dp = self.ps()
                        for kc in range(2):
                            self.mm(pp[:], wp[:, kc, j * 128:(j + 1) * 128], pT[:, kc, sl], kc == 0, kc == 1,
                                    [wd, d_pT[tb]], [dp])
                        self.act(sg[r][:], pg[:], AF.Sigmoid, [dg], [d_sg[r]])
                        self.tt(sg[r][:], pp[:], sg[r][:], ALU.mult, [dp, d_sg[r]], [d_sg[r]])
                        self.tt(self.hT[:, fc, sl], self.hT[:, fc, sl], sg[r][:], ALU.add, [self.d_h[fc][tb], d_sg[r]],
                                [self.d_h[fc][tb]])
            self.end_scope()

    def store_out(self, blk, raw=False):
        dr = self.dr
        with contextlib.ExitStack() as st:
            os_ = [self.sb([128, 8, 128], F32, "ostage", st) for _ in range(2)]
            dos = mkdeps(2)
            for kc in range(8):
                s = os_[kc % 2]
                ds = dos[kc % 2]
                for half in range(2):
                    bank, bd = self.ps()
                    for q in range(4):
                        tt_ = half * 4 + q
                        self.tr(bank[:, q * 128:(q + 1) * 128], self.hT[:, kc, tt_ * 128:(tt_ + 1) * 128], self.ident_f,
                                [self.d_h[kc][half], self.d_c], [bd], sig=(q == 3))
                    self.cp(s[:, half * 4:(half + 1) * 4, :], bank[:].rearrange("p (a b) -> p a b", a=4), [bd], [ds])
                dst = dr['out'][blk * TB:(blk + 1) * TB, kc * 128:(kc + 1) * 128].rearrange("(t p) c -> p t c", p=128)
                self.P.dma(SP, lambda e, s=s, dst=dst: e.dma_start(out=dst, in_=s[:]), [ds], (),
                           semkey='os%d' % (kc % 2))
            self.end_scope()

    def run(self):
        dr = self.dr
        self.init()
        stages = ['load', 'norm', 'ffn1', 'hgrn', 'ssd', 'outproj', 'ffn2', 'ple', 'final']
        upto = len(stages) if self.stop_after is None else stages.index(self.stop_after) + 1
        act = stages[:upto]
        for blk in range(NBLK):
            self.load_x(blk)
            if 'norm' in act:
                self.norm(0)
            if 'ffn1' in act:
                self.ffn(dr['ffn1_w13'], dr['ffn1_w2'])
            if 'hgrn' in act:
                self.norm(1)
                self.hgrn(blk)
            if 'ssd' in act:
                self.ssd(blk)
            if 'outproj' in act:
                self.outproj(blk)
            if 'dbg' in dr:
                allb = [d for row in self.d_big for d in row]
                dst = dr['dbg'][blk * 128:(blk + 1) * 128, :]
                self.P.dma(SP, lambda e, dst=dst: e.dma_start(out=dst, in_=self.big[:].rearrange("p a b -> p (a b)")),
                           allb, (), semkey='dbg')
            if 'ffn2' in act:
                self.norm(2)
                self.ffn(dr['ffn2_w13'], dr['ffn2_w2'])
            if 'ple' in act:
                self.norm(3)
                self.ple(blk)
            if 'final' in act:
                self.norm(4, final=True)
            self.store_out(blk)


def _consts():
    c = np.zeros((128, 7, 128), np.float32)
    i = np.arange(128)
    same = (i[:, None] // 64) == (i[None, :] // 64)
    c[:, 0, :] = np.eye(128)
    c[:, 1, :] = ((i[:, None] <= i[None, :]) & same)
    c[:, 2, :] = ((i[:, None] > i[None, :]) & same)
    c[:, 3, :] = (i[:, None] <= i[None, :])
    c[:, 4, :] = 1.0
    c[0, 6, :] = 1.0
    c[:, 5, :] = np.where(i[:, None] > i[None, :], -1e30, 0.0)
    return c


_IN_SPECS = [
    ('x', [T, D]), ('p', [T, 256]), ('consts', [128, 7, 128]), ('gains', [128, 40]), ('hgn', [128, 8]),
    ('cw', [128, 96]), ('cbp', [128, 24]), ('cbrow', [1, 3072]), ('dt_bias', [32]), ('a_log', [32]),
    ('d_skip', [32]), ('ssm_norm', [2048]), ('hg_lb', [2, 1024]),
    ('ffn1_w13', [D, 2 * DFF]), ('ffn1_w2', [DFF, D]), ('w_in', [D, DIN]),
    ('w_hg_out', [D, D]), ('w_ssm_out', [2 * D, D]), ('w_out', [D, D]),
    ('ffn2_w13', [D, 2 * DFF]), ('ffn2_w2', [DFF, D]), ('w_ple_gate', [D, D]), ('w_ple_proj', [256, D]),
]


def build(stop_after=None):
    nc = bass.Bass("TRN2", target_bir_lowering=False)
    dram = {}
    for name, shape in _IN_SPECS:
        dram[name] = nc.dram_tensor(name, shape, F32, kind="ExternalInput").ap()
    dram['out'] = nc.dram_tensor("out", [T, D], F32, kind="ExternalOutput").ap()
    if stop_after in ('hgrn', 'ssd'):
        dram['dbg'] = nc.dram_tensor("dbg", [NBLK * 128, 24 * TB], BF16, kind="ExternalOutput").ap()
    with contextlib.ExitStack() as es:
        block = es.enter_context(nc.Block())
        kb = KB(nc, block, es, dram, stop_after=stop_after)
        kb.run()
        print("ops", len(kb.P.ops), "waits", kb.P.nwaits)
    return nc


def make_in_maps(inp):
    f = lambda a: np.ascontiguousarray(np.asarray(a, dtype=np.float32))
    gains = np.stack([f(inp['ffn1_norm'])[0], f(inp['mix_norm'])[0], f(inp['ffn2_norm'])[0],
                      f(inp['ple_norm'])[0], f(inp['final_norm'])], 0)
    gains = gains.reshape(5, 8, 128).transpose(2, 0, 1).reshape(128, 40)
    hgn = f(inp['hg_norm'])[0].reshape(8, 128).T
    cw = f(inp['conv_w'])[0].T.reshape(24, 128, 4).transpose(1, 0, 2).reshape(128, 96)
    cbp = f(inp['conv_b'])[0].reshape(24, 128).T
    shared = dict(
        consts=_consts(), gains=f(gains), hgn=f(hgn), cw=f(cw), cbp=f(cbp), cbrow=f(inp['conv_b']),
        dt_bias=f(inp['dt_bias'])[0], a_log=f(inp['a_log'])[0], d_skip=f(inp['d_skip'])[0],
        ssm_norm=f(inp['ssm_norm'])[0], hg_lb=f(inp['hg_lb']),
        ffn1_w13=f(inp['ffn1_w13'])[0], ffn1_w2=f(inp['ffn1_w2'])[0], w_in=f(inp['w_in'])[0],
        w_hg_out=f(inp['w_hg_out'])[0], w_ssm_out=f(inp['w_ssm_out'])[0], w_out=f(inp['w_out'])[0],
        ffn2_w13=f(inp['ffn2_w13'])[0], ffn2_w2=f(inp['ffn2_w2'])[0], w_ple_gate=f(inp['w_ple_gate'])[0],
        w_ple_proj=f(inp['w_ple_proj'])[0],
    )
    x = f(inp['x'])
    p = f(inp['p'])[0]
    maps = []
    for b in range(8):
        m = dict(shared)
        m['x'] = x[b]
        m['p'] = p[b]
        maps.append(m)
    return maps


_NC_CACHE = {}


def kernel(**inputs):
    if 'nc' not in _NC_CACHE:
        _NC_CACHE['nc'] = build()
    nc = _NC_CACHE['nc']
    maps = make_in_maps(inputs)
    res = run_bass_kernel_spmd(nc, maps, core_ids=list(range(8)))
    out = np.stack([np.asarray(r['out'], dtype=np.float32) for r in res.results], 0)
    return out
```

```python
import contextlib
import numpy as np
import concourse.bass as bass
import concourse.mybir as mybir
from concourse.bass_utils import run_bass_kernel_spmd

F32 = mybir.dt.float32
BF16 = mybir.dt.bfloat16
AF = mybir.ActivationFunctionType
ALU = mybir.AluOpType

PE, ACT, DVE, POOL, SP = 'tensor', 'scalar', 'vector', 'gpsimd', 'sync'
ENGS = (PE, ACT, DVE, POOL, SP)
CENGS = (PE, ACT, DVE)

D = 1024
T = 2048
TB = 1024
NBLK = T // TB
DFF = 2816
NFC = DFF // 128
DIN = 11296
EPS = 1e-6
OFF_Q, OFF_F, OFF_V, OFF_G = 0, 1024, 2048, 3072
OFF_MZ = 4096
OFF_XBC = 6144
OFF_DT = 9216
OFF_BRG = 9248


class Dep:
    __slots__ = ('name', 'w', 'r')

    def __init__(self, name=''):
        self.name = name
        self.w = {}
        self.r = {}


def mkdeps(n):
    return [Dep() for _ in range(n)]


class Op:
    __slots__ = ('eng', 'fn', 'deps', 'idx', 'semkey', 'count', 'sig', 'is_dma')


class Prog:
    def __init__(self, nc, block):
        self.nc = nc
        self.block = block
        self.ops = []
        self.flushed = 0
        self.cnt = {e: 0 for e in ENGS}
        self.pending = {e: [] for e in ENGS}
        self.dma_counts = {}
        self.dsem = {}
        self.esem = {e: nc.alloc_semaphore('c_' + e) for e in (PE, ACT, DVE, POOL)}
        self.seen = {e: {} for e in ENGS}
        self.last = {e: None for e in ENGS}
        self.scope_dmas = []
        self.nwaits = 0

    def add(self, eng, fn, reads=(), writes=(), sig=True, is_dma=False, semkey=None, accum=False,
            extra_deps=()):
        op = Op()
        op.eng = eng
        op.fn = fn
        op.is_dma = is_dma
        op.idx = len(self.ops)
        op.semkey = semkey
        op.sig = sig
        op.count = None
        ds = set(extra_deps)
        for d in reads:
            ds.update(d.w.values())
        for d in writes:
            ds.update(d.r.values())
            if not accum:
                ds.update(d.w.values())
        op.deps = ds
        if is_dma:
            if semkey not in self.dsem:
                self.dsem[semkey] = self.nc.alloc_semaphore('d_' + str(semkey))
            c = self.dma_counts.get(semkey, 0) + 16
            self.dma_counts[semkey] = c
            op.count = c
            k = ('d', op.idx)
        else:
            k = eng
            if fn is not None:
                if sig:
                    self.cnt[eng] += 1
                    op.count = self.cnt[eng]
                    for p in self.pending[eng]:
                        p.count = op.count
                    self.pending[eng] = []
                else:
                    self.pending[eng].append(op)
                self.last[eng] = op
        for d in reads:
            d.r[k] = op
        for d in writes:
            if not accum:
                d.w = {}
            d.w[k] = op
            d.r = {}
        self.ops.append(op)
        return op

    def dma(self, eng, fn, reads=(), writes=(), semkey=None, accum=False):
        op = self.add(eng, fn, reads, writes, is_dma=True, semkey=semkey, accum=accum)
        if eng == SP:
            self.scope_dmas.append(op)
        return op

    def barrier(self):
        lasts = [self.last[e] for e in CENGS + (POOL,) if self.last[e] is not None]
        dm = list(self.scope_dmas)
        self.scope_dmas = []
        for e in CENGS + (SP,):
            self.add(e, None, extra_deps=[o for o in lasts if o.eng != e] + dm)

    def flush(self):
        ops = self.ops[self.flushed:]
        self.flushed = len(self.ops)
        for e in ENGS:
            assert not self.pending[e], "pending non-signalling ops on %s" % e
            ops_e = [op for op in ops if op.eng == e]
            if not ops_e:
                continue
            getattr(self.block, e)(lambda engh, e=e, ops_e=ops_e: self._run(e, engh, ops_e))

    def _run(self, e, engh, ops_e):
        seen = self.seen[e]
        for op in ops_e:
            need = {}
            for dj in op.deps:
                if dj.is_dma:
                    s = self.dsem[dj.semkey]
                else:
                    if dj.fn is None:
                        continue
                    if dj.eng == PE and e == PE and not op.is_dma and op.fn is not None:
                        continue
                    s = self.esem[dj.eng]
                v = dj.count
                assert v is not None
                key = id(s)
                if v > need.get(key, (None, 0))[1]:
                    need[key] = (s, v)
            for key, (s, v) in need.items():
                if v > seen.get(key, 0):
                    engh.wait_ge(s, v)
                    seen[key] = v
                    self.nwaits += 1
            if op.fn is None:
                continue
            inst = op.fn(engh)
            if op.is_dma:
                inst.then_inc(self.dsem[op.semkey], 16)
            elif op.sig:
                inst.then_inc(self.esem[e], 1)


class KB:
    def __init__(self, nc, block, es, dram, stop_after=None):
        self.nc = nc
        self.P = Prog(nc, block)
        self.es = es
        self.dr = dram
        self.stop_after = stop_after
        self.uid = 0
        self.banks = []
        self.bank_deps = []
        for i in range(8):
            self.banks.append(es.enter_context(nc.psum_tensor("bank%d" % i, [128, 512], F32)))
            self.bank_deps.append(Dep())
        self.bank_next = 0
        self.long_next = 0
        self.copy_rr = 0

    def sb(self, shape, dt, name=None, stack=None):
        self.uid += 1
        name = "%s_%d" % (name or "t", self.uid)
        return (stack or self.es).enter_context(self.nc.sbuf_tensor(name, shape, dt))

    def ps(self, long=False):
        if long:
            i = 6 + self.long_next
            self.long_next = (self.long_next + 1) % 2
        else:
            i = self.bank_next
            self.bank_next = (i + 1) % 6
        return self.banks[i], self.bank_deps[i]

    def mm(self, out, lhsT, rhs, start, stop, reads, writes):
        self.P.add(PE, lambda e: e.matmul(out, lhsT=lhsT, rhs=rhs, start=start, stop=stop),
                   reads, writes, sig=True)

    def tr(self, out, in_, ident, reads, writes, sig=True):
        self.P.add(PE, lambda e: e.transpose(out=out, in_=in_, identity=ident), reads, writes, sig=True)

    def act(self, out, in_, func, reads, writes, bias=None, scale=None, accum=None):
        kw = {}
        if bias is not None:
            kw['bias'] = bias
        if scale is not None:
            kw['scale'] = scale
        if accum is not None:
            kw['accum_out'] = accum
        self.P.add(ACT, lambda e: e.activation(out=out, in_=in_, func=func, **kw), reads, writes)

    def tt(self, out, in0, in1, op, reads, writes, eng=DVE):
        self.P.add(eng, lambda e: e.tensor_tensor(out=out, in0=in0, in1=in1, op=op), reads, writes)

    def stt(self, out, in0, scalar, in1, op0, op1, reads, writes):
        self.P.add(DVE, lambda e: e.scalar_tensor_tensor(out=out, in0=in0, scalar=scalar, in1=in1,
                                                         op0=op0, op1=op1), reads, writes)

    def ts(self, out, in0, s1, op0, reads, writes, s2=None, op1=None, eng=DVE):
        if op1 is None:
            self.P.add(eng, lambda e: e.tensor_scalar(out=out, in0=in0, scalar1=s1, scalar2=None, op0=op0),
                       reads, writes)
        else:
            self.P.add(eng, lambda e: e.tensor_scalar(out=out, in0=in0, scalar1=s1, scalar2=s2, op0=op0,
                                                      op1=op1), reads, writes)

    def cp(self, out, in_, reads, writes, eng=None):
        if eng is None:
            eng = (ACT, DVE)[self.copy_rr % 2]
            self.copy_rr += 1
        if eng == ACT:
            self.act(out, in_, AF.Copy, reads, writes)
        else:
            self.P.add(eng, lambda e: e.tensor_copy(out=out, in_=in_), reads, writes)

    def recip(self, out, in_, reads, writes):
        self.P.add(DVE, lambda e: e.reciprocal(out=out, in_=in_), reads, writes)

    def memset(self, ap, val, writes, eng=DVE):
        self.P.add(eng, lambda e: e.memset(ap, val), (), writes)

    def end_scope(self):
        self.P.barrier()
        self.P.flush()

    def init_wpool(self, nslots=3, nel=4096):
        self.wslots = [self.sb([128, nel], BF16, "wslot") for _ in range(nslots)]
        self.wdeps = mkdeps(nslots)
        self.wnext = 0
        self.wnel = nel

    def wload(self, parts):
        i = self.wnext
        self.wnext = (i + 1) % len(self.wslots)
        slot = self.wslots[i]
        dep = self.wdeps[i]
        views = []
        off = 0
        first = True
        for part in parts:
            if len(part) == 4:
                (src, nk, ncols, _) = part
                n = nk * ncols
                assert off + n <= self.wnel
                v = slot[:, off:off + n].rearrange("p (k c) -> p k c", k=nk)
                nch = 1
                while n // nch > 2048 or n % nch:
                    nch += 1
                vo = slot[:, off:off + n].rearrange("p (a b) -> p a b", a=nch)
                s = src.rearrange("p (a b) -> p a b", a=nch)
                self.P.dma(POOL, lambda e, vo=vo, s=s: e.dma_start(out=vo, in_=s), (), [dep],
                           semkey='w%d' % i, accum=not first)
            else:
                (src, nk, ncols) = part
                n = nk * ncols
                assert off + n <= self.wnel
                v = slot[:, off:off + n].rearrange("p (k c) -> p k c", k=nk)
                s = src.rearrange("(k p) c -> p k c", p=128)
                self.P.dma(POOL, lambda e, v=v, s=s: e.dma_start(out=v, in_=s), (), [dep],
                           semkey='w%d' % i, accum=not first)
            first = False
            views.append(v)
            off += n
        return views, dep

    def init(self):
        dr = self.dr
        P = self.P
        self.cf = self.sb([128, 7, 128], F32, "cf")
        self.cb16 = self.sb([128, 4, 128], BF16, "cb16")
        self.d_c = Dep()
        cdep = self.d_c
        P.dma(SP, lambda e: e.dma_start(out=self.cf[:], in_=dr['consts']), (), [cdep], semkey='c', accum=True)
        self.ident_f = self.cf[:, 0, :]
        self.triL64 = self.cf[:, 1, :]
        self.triU64 = self.cf[:, 2, :]
        self.triL128 = self.cf[:, 3, :]
        self.ones_f = self.cf[:, 4, :]
        self.gn = self.sb([128, 40], F32, "gn")
        P.dma(SP, lambda e: e.dma_start(out=self.gn[:], in_=dr['gains']), (), [cdep], semkey='c', accum=True)
        self.hgn = self.sb([128, 8], F32, "hgn")
        P.dma(SP, lambda e: e.dma_start(out=self.hgn[:], in_=dr['hgn']), (), [cdep], semkey='c', accum=True)
        self.cw = self.sb([128, 24, 4], F32, "cw")
        P.dma(SP, lambda e: e.dma_start(out=self.cw[:], in_=dr['cw'].rearrange("p (a b) -> p a b", a=24)),
              (), [cdep], semkey='c', accum=True)
        self.cbp = self.sb([128, 24], F32, "cbp")
        P.dma(SP, lambda e: e.dma_start(out=self.cbp[:], in_=dr['cbp']), (), [cdep], semkey='c', accum=True)
        self.dtb = self.sb([128, 32], F32, "dtb")
        self.aneg = self.sb([128, 32], F32, "aneg")
        self.dsk = self.sb([128, 32], F32, "dsk")
        P.dma(SP, lambda e: e.dma_start(out=self.dtb[:], in_=dr['dt_bias'].partition_broadcast(128)),
              (), [cdep], semkey='c', accum=True)
        P.dma(SP, lambda e: e.dma_start(out=self.aneg[:], in_=dr['a_log'].partition_broadcast(128)),
              (), [cdep], semkey='c', accum=True)
        P.dma(SP, lambda e: e.dma_start(out=self.dsk[:], in_=dr['d_skip'].partition_broadcast(128)),
              (), [cdep], semkey='c', accum=True)
        self.oml = self.sb([128, 1024], F32, "oml")
        self.S_hg = self.sb([128, 8, 128], F32, "S_hg")
        self.d_S_hg = mkdeps(8)
        self.S_ssd = self.sb([128, 2048], F32, "S_ssd")
        self.S_ssd_bf = self.sb([128, 2048], BF16, "S_ssd_bf")
        self.d_S_ssd = mkdeps(4)
        self.d_S_ssd_bf = mkdeps(4)
        self.xtail = self.sb([128, 24, 3], BF16, "xtail")
        self.d_xtail = mkdeps(24)
        self.hT = self.sb([128, 8, TB], F32, "hT")
        self.d_h = [[Dep() for _ in range(2)] for _ in range(8)]
        self.nT = self.sb([128, 8, TB], BF16, "nT")
        self.d_n = [[Dep() for _ in range(2)] for _ in range(8)]
        self.big = self.sb([128, 24, TB], BF16, "big")
        self.d_big = [[Dep() for _ in range(2)] for _ in range(24)]
        self.init_wpool()
        self.d_const = Dep()
        with contextlib.ExitStack() as st:
            lbt = self.sb([128, 2, 1024], F32, "lbt", st)
            P.dma(SP, lambda e: e.dma_start(out=lbt[:, 0, :], in_=dr['hg_lb'][0].partition_broadcast(128)),
                  (), [cdep], semkey='c', accum=True)
            P.dma(SP, lambda e: e.dma_start(out=lbt[:, 1, :], in_=dr['hg_lb'][1].partition_broadcast(128)),
                  (), [cdep], semkey='c', accum=True)
            dc = self.d_const
            self.tt(lbt[:, 0, :], lbt[:, 1, :], lbt[:, 0, :], ALU.subtract, [cdep], [dc])
            self.act(self.oml[:], lbt[:, 0, :], AF.Sigmoid, [dc], [dc])
            self.cp(self.cb16[:, 0, :], self.ident_f, [cdep], [dc], eng=DVE)
            self.cp(self.cb16[:, 1, :], self.ones_f, [cdep], [dc], eng=DVE)
            self.cp(self.cb16[:, 2, :], self.cf[:, 5, :], [cdep], [dc], eng=DVE)
            self.negm_b = self.cb16[:, 2, :]
            self.cp(self.cb16[:, 3, :], self.cf[:, 6, :], [cdep], [dc], eng=DVE)
            self.e0_b = self.cb16[:, 3, :]
            self.act(self.aneg[:], self.aneg[:], AF.Exp, [cdep], [dc])
            self.ts(self.aneg[:], self.aneg[:], -1.0, ALU.mult, [dc], [dc])
            self.memset(self.S_hg[:], 0.0, self.d_S_hg)
            self.memset(self.S_ssd[:], 0.0, self.d_S_ssd)
            self.memset(self.S_ssd_bf[:], 0.0, self.d_S_ssd_bf)
            self.memset(self.xtail[:], 0.0, self.d_xtail)
            if 'dbg' in dr:
                self.memset(self.big[:], 0.0, [d for row in self.d_big for d in row])
            self.ident_b = self.cb16[:, 0, :]
            self.ones_b = self.cb16[:, 1, :]
            self.end_scope()

    def load_x(self, blk):
        dr = self.dr
        with contextlib.ExitStack() as st:
            xs = [self.sb([128, 8, 128], F32, "xstage", st) for _ in range(2)]
            dxs = mkdeps(2)
            for kc in range(8):
                s = xs[kc % 2]
                ds = dxs[kc % 2]
                src = dr['x'][blk * TB:(blk + 1) * TB, kc * 128:(kc + 1) * 128].rearrange("(t p) c -> p t c", p=128)
                self.P.dma(SP, lambda e, s=s, src=src: e.dma_start(out=s[:], in_=src), (), [ds],
                           semkey='xs%d' % (kc % 2))
                for half in range(2):
                    bank, bd = self.ps()
                    for q in range(4):
                        tt_ = half * 4 + q
                        self.tr(bank[:, q * 128:(q + 1) * 128], s[:, tt_, :], self.ident_f, [ds, self.d_c], [bd],
                                sig=(q == 3))
                    self.cp(self.hT[:, kc, half * 512:(half + 1) * 512], bank[:], [bd], [self.d_h[kc][half]])
            self.end_scope()

    def norm(self, gi, final=False):
        with contextlib.ExitStack() as st:
            sq = [self.sb([128, TB], BF16, "sq", st) for _ in range(2)]
            dsq = mkdeps(2)
            rstd = self.sb([128, TB], F32, "rstd", st)
            drs = mkdeps(2)
            b0, bd0 = self.ps()
            b1, bd1 = self.ps()
            bks = ((b0, bd0), (b1, bd1))
            for kc in range(8):
                s = sq[kc % 2]
                ds = dsq[kc % 2]
                if kc % 3 == 2:
                    self.tt(s[:], self.hT[:, kc, :], self.hT[:, kc, :], ALU.mult, self.d_h[kc], [ds])
                else:
                    self.act(s[:], self.hT[:, kc, :], AF.Square, self.d_h[kc], [ds])
                for tb in range(2):
                    self.mm(bks[tb][0][:], self.ones_b, s[:, tb * 512:(tb + 1) * 512], kc == 0, kc == 7,
                            [ds, self.d_const], [bks[tb][1]])
            for tb in range(2):
                sl = slice(tb * 512, (tb + 1) * 512)
                self.act(rstd[:, sl], bks[tb][0][:], AF.Ln, [bks[tb][1]], [drs[tb]], bias=EPS, scale=1.0 / D)
                self.act(rstd[:, sl], rstd[:, sl], AF.Exp, [drs[tb]], [drs[tb]], scale=-0.5)
            for kc in range(8):
                for tb in range(2):
                    sl = slice(tb * 512, (tb + 1) * 512)
                    if final:
                        self.stt(self.hT[:, kc, sl], self.hT[:, kc, sl], self.gn[:, gi * 8 + kc:gi * 8 + kc + 1],
                                 rstd[:, sl], ALU.mult, ALU.mult, [self.d_h[kc][tb], drs[tb], self.d_c],
                                 [self.d_h[kc][tb]])
                    else:
                        self.stt(self.nT[:, kc, sl], self.hT[:, kc, sl], self.gn[:, gi * 8 + kc:gi * 8 + kc + 1],
                                 rstd[:, sl], ALU.mult, ALU.mult, [self.d_h[kc][tb], drs[tb], self.d_c],
                                 [self.d_n[kc][tb]])
            self.end_scope()

    def ffn(self, w13, w2):
        with contextlib.ExitStack() as st:
            sg = [self.sb([128, 512], F32, "sg", st) for _ in range(2)]
            dsg = mkdeps(2)
            it = 0
            for g2 in range(NFC // 2):
                (wg, wu), wd = self.wload([(w13[:, g2 * 256:(g2 + 1) * 256], 8, 256),
                                           (w13[:, DFF + g2 * 256:DFF + (g2 + 1) * 256], 8, 256)])
                for j in range(2):
                    fc = g2 * 2 + j
                    for tb in range(2):
                        sl = slice(tb * 512, (tb + 1) * 512)
                        pg, dg = self.ps()
                        pu, du = self.ps()
                        for kc in range(8):
                            self.mm(pg[:], wg[:, kc, j * 128:(j + 1) * 128], self.nT[:, kc, sl], kc == 0, kc == 7,
                                    [wd, self.d_n[kc][tb]], [dg])
                        for kc in range(8):
                            self.mm(pu[:], wu[:, kc, j * 128:(j + 1) * 128], self.nT[:, kc, sl], kc == 0, kc == 7,
                                    [wd, self.d_n[kc][tb]], [du])
                        s = sg[it % 2]
                        ds = dsg[it % 2]
                        it += 1
                        self.act(s[:], pg[:], AF.Silu, [dg], [ds])
                        self.tt(self.big[:, fc, sl], s[:], pu[:], ALU.mult, [ds, du], [self.d_big[fc][tb]])
            for dc in range(8):
                (wa,), wda = self.wload([(w2[dc], NFC, 128, 'r')])
                for tb in range(2):
                    sl = slice(tb * 512, (tb + 1) * 512)
                    po, do = self.ps()
                    for fc in range(NFC):
                        self.mm(po[:], wa[:, fc, :], self.big[:, fc, sl], fc == 0, fc == NFC - 1,
                                [wda, self.d_big[fc][tb]], [do])
                    self.stt(self.hT[:, dc, sl], po[:], 0.5, self.hT[:, dc, sl], ALU.mult, ALU.add,
                             [do, self.d_h[dc][tb]], [self.d_h[dc][tb]])
            self.end_scope()

    def hgrn(self, blk):
        w_in = self.dr['w_in']
        nT, dn = self.nT, self.d_n
        with contextlib.ExitStack() as st:
            sbs = lambda shape, dt, name: self.sb(shape, dt, name, st)
            dd2 = lambda: [mkdeps(2) for _ in range(2)]
            k_tm = sbs([128, 8, 128], F32, "k_tm"); d_k = mkdeps(2)
            logf = sbs([128, 8, 128], F32, "logf"); d_lf = mkdeps(2)
            qs = sbs([128, TB], BF16, "qs"); d_qs = mkdeps(2)
            eGi = sbs([128, TB], BF16, "eGi"); d_eGi = mkdeps(2)
            eGe = sbs([128, 8, 128], BF16, "eGe"); d_eGe = mkdeps(2)
            k_inv = sbs([128, TB], BF16, "k_inv"); d_ki = mkdeps(2)
            v_bf = [sbs([128, 8, 128], BF16, "v_bf") for _ in range(2)]; d_v = dd2()
            sgate = [sbs([128, TB], BF16, "sgate") for _ in range(2)]; d_sg = dd2()
            eG = [sbs([128, TB], F32, "eG") for _ in range(2)]; d_eG = dd2()
            q_dec = [sbs([128, TB], BF16, "q_dec") for _ in range(2)]; d_qd = dd2()
            k_end = [sbs([128, 8, 128], BF16, "k_end") for _ in range(2)]; d_ke = dd2()
            att = [sbs([128, 8, 128], BF16, "att") for _ in range(2)]; d_att = dd2()
            osq = [sbs([128, 512], BF16, "osq") for _ in range(2)]; d_osq = mkdeps(2)
            sd = [sbs([128, 512], F32, "sd") for _ in range(2)]; d_sd = mkdeps(2)
            tmp = [sbs([128, 512], F32, "tmp") for _ in range(2)]; d_tmp = mkdeps(2)
            Sbf = [sbs([128, 128], BF16, "Sbf") for _ in range(4)]; d_Sbf = mkdeps(4)
            self._sbi = 0
            v4 = lambda bank: bank[:].rearrange("p (a b) -> p a b", a=4)

            hw = {}

            def hload(j):
                hw[j] = self.wload([(w_in[:, OFF_Q + j * 128:OFF_Q + (j + 1) * 128], 8, 128),
                                    (w_in[:, OFF_F + j * 128:OFF_F + (j + 1) * 128], 8, 128),
                                    (w_in[:, OFF_V + j * 128:OFF_V + (j + 1) * 128], 8, 128),
                                    (w_in[:, OFF_G + j * 128:OFF_G + (j + 1) * 128], 8, 128)])

            hload(0)

            def front(j):
                p = j % 2
                if j + 1 < 8:
                    hload(j + 1)
                (wq, wf, wv, wg), wd = hw[j]
                for tb in range(2):
                    sl = slice(tb * 512, (tb + 1) * 512)
                    pq, dq = self.ps()
                    for kc in range(8):
                        self.mm(pq[:], wq[:, kc, :], nT[:, kc, sl], kc == 0, kc == 7, [wd, dn[kc][tb]], [dq])
                    self.act(qs[:, sl], pq[:], AF.Copy, [dq], [d_qs[tb]], scale=float(128 ** -0.5))
                    yield
                    pg, dg = self.ps()
                    for kc in range(8):
                        self.mm(pg[:], wg[:, kc, :], nT[:, kc, sl], kc == 0, kc == 7, [wd, dn[kc][tb]], [dg])
                    self.act(sgate[p][:, sl], pg[:], AF.Silu, [dg], [d_sg[p][tb]])
                    yield
                for hf in range(2):
                    sl = slice(hf * 512, (hf + 1) * 512)
                    h4 = slice(hf * 4, hf * 4 + 4)
                    pf, df = self.ps()
                    for q in range(4):
                        t_ = hf * 4 + q
                        for kc in range(8):
                            self.mm(pf[:, q * 128:(q + 1) * 128], nT[:, kc, t_ * 128:(t_ + 1) * 128], wf[:, kc, :],
                                    kc == 0, kc == 7, [wd, dn[kc][hf]], [df])
                    self.act(k_tm[:, h4, :], v4(pf), AF.Exp, [df], [d_k[hf]])
                    self.act(k_tm[:, h4, :], k_tm[:, h4, :], AF.Ln, [d_k[hf]], [d_k[hf]], bias=1.0)
                    self.act(k_tm[:, h4, :], k_tm[:, h4, :], AF.Exp, [d_k[hf]], [d_k[hf]], scale=-1.0)
                    yield
                    pv, dv = self.ps()
                    for q in range(4):
                        t_ = hf * 4 + q
                        for kc in range(8):
                            self.mm(pv[:, q * 128:(q + 1) * 128], nT[:, kc, t_ * 128:(t_ + 1) * 128], wv[:, kc, :],
                                    kc == 0, kc == 7, [wd, dn[kc][hf]], [dv])
                    self.cp(v_bf[p][:, h4, :], v4(pv), [dv], [d_v[p][hf]])
                    yield
                    self.tt(k_tm[:, h4, :], k_tm[:, h4, :],
                            self.oml[:, j * 128:(j + 1) * 128].unsqueeze(1).to_broadcast([128, 4, 128]), ALU.mult,
                            [d_k[hf], self.d_const], [d_k[hf]])
                    self.act(logf[:, h4, :], k_tm[:, h4, :], AF.Ln, [d_k[hf]], [d_lf[hf]], scale=-1.0, bias=1.0)
                    pG, dG = self.ps()
                    pE, dE = self.ps()
                    pK, dK = self.ps()
                    for q in range(4):
                        t_ = hf * 4 + q
                        cs = slice(q * 128, (q + 1) * 128)
                        self.mm(pG[:, cs], logf[:, t_, :], self.triL64, True, True, [d_lf[hf], self.d_c], [dG])
                        self.mm(pE[:, cs], self.triU64, logf[:, t_, :], True, True, [d_lf[hf], self.d_c], [dE])
                        self.tr(pK[:, cs], k_tm[:, t_, :], self.ident_f, [d_k[hf], self.d_c], [dK])
                    self.act(eG[p][:, sl], pG[:], AF.Exp, [dG], [d_eG[p][hf]])
                    self.act(eGi[:, sl], pG[:], AF.Exp, [dG], [d_eGi[hf]], scale=-1.0)
                    self.act(eGe[:, h4, :], v4(pE), AF.Exp, [dE], [d_eGe[hf]])
                    self.tt(q_dec[p][:, sl], qs[:, sl], eG[p][:, sl], ALU.mult, [d_qs[hf], d_eG[p][hf]], [d_qd[p][hf]])
                    self.tt(k_inv[:, sl], pK[:], eGi[:, sl], ALU.mult, [dK, d_eGi[hf]], [d_ki[hf]])
                    self.tt(k_end[p][:, h4, :], k_tm[:, h4, :], eGe[:, h4, :], ALU.mult, [d_k[hf], d_eGe[hf]],
                            [d_ke[p][hf]])
                    yield
                    pA, dA = self.ps()
                    for q in range(4):
                        t_ = hf * 4 + q
                        cs = slice(q * 128, (q + 1) * 128)
                        ts_ = slice(t_ * 128, (t_ + 1) * 128)
                        self.mm(pA[:, cs], k_inv[:, ts_], q_dec[p][:, ts_], True, True, [d_ki[hf], d_qd[p][hf]], [dA])
                    self.tt(att[p][:, h4, :], v4(pA), self.triL64.unsqueeze(1).to_broadcast([128, 4, 128]), ALU.mult,
                            [dA, self.d_c], [d_att[p][hf]])
                    yield

            def back(j):
                p = j % 2
                dS = self.d_S_hg[j]
                S = self.S_hg[:, j, :]
                cur = self._sbi % 4
                self._sbi += 1
                self.cp(Sbf[cur][:], S, [dS], [d_Sbf[cur]], eng=ACT)
                for hf in range(2):
                    sl = slice(hf * 512, (hf + 1) * 512)
                    pO, dO = self.ps(long=True)
                    for q in range(4):
                        t_ = hf * 4 + q
                        cs = slice(q * 128, (q + 1) * 128)
                        pS, dSp = self.ps()
                        pS1, dSp1 = self.ps()
                        self.mm(pS[:, 0:128], k_end[p][0:64, t_, :], v_bf[p][0:64, t_, :], True, True,
                                [d_ke[p][hf], d_v[p][hf]], [dSp])
                        self.mm(pS1[:, 0:128], k_end[p][64:128, t_, :], v_bf[p][64:128, t_, :], True, True,
                                [d_ke[p][hf], d_v[p][hf]], [dSp1])
                        self.mm(pO[:, cs], v_bf[p][:, t_, :], att[p][:, t_, :], True, False, [d_v[p][hf], d_att[p][hf]], [dO])
                        self.mm(pO[:, q * 128:q * 128 + 64], Sbf[cur][:], q_dec[p][:, t_ * 128:t_ * 128 + 64], False, False,
                                [d_Sbf[cur], d_qd[p][hf]], [dO])
                        c0 = t_ * 128 + 63
                        n1 = self._sbi % 4
                        self._sbi += 1
                        self.stt(Sbf[n1][:], S, eG[p][:, c0:c0 + 1], pS[:, 0:128], ALU.mult, ALU.add,
                                 [dS, d_eG[p][hf], dSp], [d_Sbf[n1]])
                        self.stt(S, S, eG[p][:, c0:c0 + 1], pS[:, 0:128], ALU.mult, ALU.add, [dS, d_eG[p][hf], dSp], [dS])
                        self.mm(pO[:, q * 128 + 64:q * 128 + 128], Sbf[n1][:], q_dec[p][:, t_ * 128 + 64:t_ * 128 + 128],
                                False, True, [d_Sbf[n1], d_qd[p][hf]], [dO])
                        c1 = t_ * 128 + 127
                        cur = self._sbi % 4
                        self._sbi += 1
                        self.stt(Sbf[cur][:], S, eG[p][:, c1:c1 + 1], pS1[:, 0:128], ALU.mult, ALU.add,
                                 [dS, d_eG[p][hf], dSp1], [d_Sbf[cur]])
                        self.stt(S, S, eG[p][:, c1:c1 + 1], pS1[:, 0:128], ALU.mult, ALU.add, [dS, d_eG[p][hf], dSp1], [dS])
                        yield
                    r = hf
                    self.act(osq[r][:], pO[:], AF.Square, [dO], [d_osq[r]])
                    pSS, dSS = self.ps()
                    self.mm(pSS[:], self.ones_b, osq[r][:], True, True, [d_osq[r], self.d_const], [dSS])
                    self.act(sd[r][:], pSS[:], AF.Ln, [dSS], [d_sd[r]], scale=1.0 / 128, bias=EPS)
                    self.act(sd[r][:], sd[r][:], AF.Exp, [d_sd[r]], [d_sd[r]], scale=-0.5)
                    self.tt(tmp[r][:], pO[:], sd[r][:], ALU.mult, [dO, d_sd[r]], [d_tmp[r]])
                    self.stt(self.big[:, j, sl], tmp[r][:], self.hgn[:, j:j + 1], sgate[p][:, sl], ALU.mult, ALU.mult,
                             [d_tmp[r], self.d_c, d_sg[p][hf]], [self.d_big[j][hf]])
                    yield

            for _ in front(0):
                pass
            for j in range(8):
                b = back(j)
                f = front(j + 1) if j < 7 else None
                alive_b, alive_f = True, f is not None
                while alive_b or alive_f:
                    if alive_b:
                        try:
                            next(b)
                        except StopIteration:
                            alive_b = False
                    if alive_f:
                        try:
                            next(f)
                        except StopIteration:
                            alive_f = False
            self.end_scope()

    def ssd(self, blk):
        dr = self.dr
        w_in = dr['w_in']
        nT, dn = self.nT, self.d_n
        with contextlib.ExitStack() as st:
            sbs = lambda shape, dt, name: self.sb(shape, dt, name, st)
            sm = lambda name, dt=F32: sbs([128, 8, 32], dt, name)
            dtv, lndt, a_, acs, wst, wd_, dl, b2 = [sm(n) for n in ("dtv", "lndt", "a_", "acs", "wst", "wd_", "dl", "b2")]
            acs_hi = sm("acs_hi", BF16)
            acs_lo = sm("acs_lo", BF16)
            d_dt = Dep()
            xpT = sbs([128, 6, 3 + TB], BF16, "xpT"); d_xp = [[Dep() for _ in range(3)] for _ in range(6)]
            BT = sbs([128, TB], BF16, "BT"); d_BT = mkdeps(2)
            CT = sbs([128, TB], BF16, "CT"); d_CT = mkdeps(2)
            diag = sbs([128, 6, 4, 128], BF16, "diag"); d_dg = mkdeps(6)
            cbr = sbs([128, 640], BF16, "cbr"); d_cbr = Dep()
            self.memset(cbr[:], 0.0, [d_cbr])
            gss = sbs([128, 512], F32, "gss"); d_gss = Dep()
            R2 = 2
            xs_bf = [sbs([128, 512], BF16, "xs_bf") for _ in range(R2)]; d_xs = mkdeps(R2)
            xsD = [sbs([128, 512], BF16, "xsD") for _ in range(R2)]; d_xsD = mkdeps(R2)
            xsw = [sbs([128, 512], BF16, "xsw") for _ in range(R2)]; d_xsw = mkdeps(R2)
            B_tm = [sbs([128, 128], BF16, "B_tm") for _ in range(R2)]; d_Bt = mkdeps(R2)
            smz = [sbs([128, 512], BF16, "smz") for _ in range(R2)]; d_smz = mkdeps(R2)
            CBm = [sbs([128, 128], F32, "CBm") for _ in range(R2)]; d_CBm = mkdeps(R2)
            E = [sbs([128, 128], F32, "E") for _ in range(3)]; d_E = mkdeps(3)
            Mp = [sbs([128, 128], BF16, "Mp") for _ in range(16)]; d_Mp = mkdeps(16)
            t1_ = sbs([128, 512], F32, "t1"); t1 = [t1_] * R2; d_t1_ = Dep(); d_t1 = [d_t1_] * R2
            yv = [sbs([128, 512], F32, "yv") for _ in range(R2)]; d_yv = mkdeps(R2)
            ssq = [sbs([128, 1], F32, "ssq") for _ in range(R2)]; d_ssq = mkdeps(R2)
            yn = [sbs([128, 512], BF16, "yn") for _ in range(R2)]; d_yn = mkdeps(R2)

            (wdt,), wdd = self.wload([(w_in[:, OFF_DT:OFF_DT + 32], 8, 32)])
            pD, dD = self.ps()
            for t_ in range(8):
                for kc in range(8):
                    self.mm(pD[:, t_ * 32:(t_ + 1) * 32], nT[:, kc, t_ * 128:(t_ + 1) * 128], wdt[:, kc, :], kc == 0, kc == 7,
                            [wdd, dn[kc][t_ // 4]], [dD])
            v8 = lambda bank: bank[:, 0:256].rearrange("p (a b) -> p a b", a=8)
            bc8 = lambda ap: ap.unsqueeze(1).to_broadcast([128, 8, 32])
            dd = [d_dt]
            self.tt(dtv[:], v8(pD), bc8(self.dtb[:]), ALU.add, [dD, self.d_c], dd)
            self.act(dtv[:], dtv[:], AF.Exp, dd, dd)
            self.act(dtv[:], dtv[:], AF.Ln, dd, dd, bias=1.0)
            self.act(lndt[:], dtv[:], AF.Ln, dd, dd)
            self.tt(a_[:], dtv[:], bc8(self.aneg[:]), ALU.mult, dd + [self.d_const], dd)
            pAc, dAc = self.ps()
            pTo, dTo = self.ps()
            for t_ in range(8):
                self.mm(pAc[:, t_ * 32:(t_ + 1) * 32], self.triL128, a_[:, t_, :], True, True, dd + [self.d_c], [dAc])
                self.mm(pTo[:, t_ * 32:(t_ + 1) * 32], self.ones_f, a_[:, t_, :], True, True, dd + [self.d_c], [dTo])
            self.cp(acs[:], v8(pAc), [dAc], dd, eng=ACT)
            self.act(wst[:], v8(pAc), AF.Exp, [dAc], dd)
            self.act(dl[:], v8(pTo), AF.Exp, [dTo], dd)
            self.tt(wd_[:], v8(pTo), acs[:], ALU.subtract, [dTo] + dd, dd)
            self.act(wd_[:], wd_[:], AF.Exp, dd, dd)
            self.tt(wd_[:], wd_[:], dtv[:], ALU.mult, dd, dd)
            self.tt(b2[:], lndt[:], acs[:], ALU.subtract, dd, dd)
            self.cp(acs_hi[:], acs[:], dd, dd, eng=DVE)
            self.tt(acs_lo[:], acs[:], acs_hi[:], ALU.subtract, dd, dd)

            self._ei = 0
            gw = {}

            def gload_xb(g):
                gw[g] = (self.wload([(w_in[:, OFF_XBC + g * 512:OFF_XBC + (g + 1) * 512], 8, 512)]),
                         self.wload([(w_in[:, OFF_XBC + 2048 + g * 128:OFF_XBC + 2048 + (g + 1) * 128], 8, 128),
                                     (w_in[:, OFF_XBC + 2560 + g * 128:OFF_XBC + 2560 + (g + 1) * 128], 8, 128)]))

            gload_xb(0)
            for g in range(4):
                ((wx,), wxd), ((wB, wC), wbd) = gw[g]
                (wz,), wzd = self.wload([(w_in[:, OFF_MZ + g * 512:OFF_MZ + (g + 1) * 512], 8, 512)])
                chs = [4 * g, 4 * g + 1, 4 * g + 2, 4 * g + 3, 16 + g, 20 + g]
                self.P.dma(POOL, lambda e, g=g: e.dma_start(out=cbr[0:1, 0:512], in_=dr['cbrow'][0:1, g * 512:(g + 1) * 512]),
                           (), [d_cbr], semkey='cbr')
                self.P.dma(POOL, lambda e, g=g: e.dma_start(out=cbr[0:1, 512:640],
                                                           in_=dr['cbrow'][0:1, 2048 + g * 128:2048 + (g + 1) * 128]),
                           (), [d_cbr], semkey='cbr', accum=True)
                self.P.dma(SP, lambda e, g=g: e.dma_start(out=gss[:], in_=dr['ssm_norm'][g * 512:(g + 1) * 512].partition_broadcast(128)),
                           (), [d_gss], semkey='gss')
                for c6 in range(6):
                    ch = chs[c6]
                    self.cp(xpT[:, c6, 0:3], self.xtail[:, ch, :], [self.d_xtail[ch]], [d_xp[c6][0]], eng=DVE)
                    wsrc = wx[:, :, c6 * 128:(c6 + 1) * 128] if c6 < 4 else (wB if c6 == 4 else wC)
                    wdp = wxd if c6 < 4 else wbd
                    for tb in range(2):
                        sl = slice(tb * 512, (tb + 1) * 512)
                        px, dpx = self.ps()
                        for kc in range(8):
                            self.mm(px[:], wsrc[:, kc, :], nT[:, kc, sl], kc == 0, kc == 7, [wdp, dn[kc][tb]], [dpx])
                        self.cp(xpT[:, c6, 3 + tb * 512:3 + (tb + 1) * 512], px[:], [dpx], [d_xp[c6][1 + tb]])
                    self.cp(self.xtail[:, ch, :], xpT[:, c6, TB:TB + 3], [d_xp[c6][2]], [self.d_xtail[ch]], eng=DVE)
                    for tap in range(4):
                        self.ts(diag[:, c6, tap, :], self.ident_f, self.cw[:, ch, tap:tap + 1], ALU.mult,
                                [self.d_c], [d_dg[c6]])
                for (c6, dst, dd_, col) in ((4, BT, d_BT, 16 + g), (5, CT, d_CT, 20 + g)):
                    for tb in range(2):
                        sl = slice(tb * 512, (tb + 1) * 512)
                        pb, dpb = self.ps()
                        for tap in range(4):
                            self.mm(pb[:], diag[:, c6, tap, :], xpT[:, c6, tb * 512 + tap:tb * 512 + tap + 512], tap == 0,
                                    tap == 3, [d_dg[c6]] + d_xp[c6], [dpb])
                        self.act(dst[:, sl], pb[:], AF.Silu, [dpb, self.d_c], [dd_[tb]], bias=self.cbp[:, col:col + 1])
                v864 = lambda ap: ap.rearrange("p (a b) -> p a b", a=8)
                bch = lambda ap: ap.unsqueeze(2).to_broadcast([128, 8, 64])
                hs = slice(g * 8, (g + 1) * 8)
                Sg = self.S_ssd[:, g * 512:(g + 1) * 512]

                def front(t_, g=g, wz=wz, wzd=wzd, hs=hs):
                    hf = t_ // 4
                    tsl = slice(t_ * 128, (t_ + 1) * 128)
                    r = t_ % 2
                    pxs, dxs_ = self.ps()
                    for cc in range(4):
                        cs = slice(cc * 128, (cc + 1) * 128)
                        for tap in range(4):
                            self.mm(pxs[:, cs], xpT[:, cc, t_ * 128 + tap:t_ * 128 + tap + 128], diag[:, cc, tap, :], tap == 0,
                                    False, d_xp[cc] + [d_dg[cc]], [dxs_])
                        self.mm(pxs[:, cs], self.e0_b, cbr[:, cs], False, True, [self.d_const, d_cbr], [dxs_])
                    self.act(xs_bf[r][:], pxs[:], AF.Silu, [dxs_], [d_xs[r]])
                    pbt, dbt = self.ps()
                    for tap in range(4):
                        self.mm(pbt[:, 0:128], xpT[:, 4, t_ * 128 + tap:t_ * 128 + tap + 128], diag[:, 4, tap, :], tap == 0,
                                False, d_xp[4] + [d_dg[4]], [dbt])
                    self.mm(pbt[:, 0:128], self.e0_b, cbr[:, 512:640], False, True, [self.d_const, d_cbr], [dbt])
                    self.act(B_tm[r][:], pbt[:, 0:128], AF.Silu, [dbt], [d_Bt[r]])
                    pz, dz = self.ps()
                    for kc in range(8):
                        self.mm(pz[:], nT[:, kc, tsl], wz[:, kc, :], kc == 0, kc == 7, [wzd, dn[kc][hf]], [dz])
                    self.act(smz[r][:], pz[:], AF.Silu, [dz], [d_smz[r]])
                    self.tt(v864(xsD[r][:]), v864(xs_bf[r][:]), bch(self.dsk[:, hs]), ALU.mult, [d_xs[r], self.d_c],
                            [d_xsD[r]])
                    self.tt(v864(xsw[r][:]), v864(xs_bf[r][:]), bch(wd_[:, t_, hs]), ALU.mult, [d_xs[r], d_dt], [d_xsw[r]])
                    pcb, dcb = self.ps()
                    self.mm(pcb[:, 0:128], BT[:, tsl], CT[:, tsl], True, True, [d_BT[hf], d_CT[hf]], [dcb])
                    self.cp(CBm[r][:], pcb[:, 0:128], [dcb], [d_CBm[r]], eng=ACT)
                    for hq in range(2):
                        pab, dab = self.ps()
                        for q in range(4):
                            h = g * 8 + hq * 4 + q
                            cs = slice(q * 128, (q + 1) * 128)
                            self.mm(pab[:, cs], acs_hi[:, t_, h:h + 1].to_broadcast([128, 128]), self.ident_b, True, False,
                                    [d_dt, self.d_const], [dab])
                            self.mm(pab[:, cs], acs_lo[:, t_, h:h + 1].to_broadcast([128, 128]), self.ident_b, False, False,
                                    [d_dt, self.d_const], [dab])
                            self.mm(pab[:, cs], self.ident_b, self.negm_b, False, True, [self.d_const], [dab])
                        for q in range(4):
                            hh = hq * 4 + q
                            h = g * 8 + hh
                            cs = slice(q * 128, (q + 1) * 128)
                            e_ = self._ei % 3
                            self._ei += 1
                            m_ = r * 8 + hh
                            self.act(E[e_][:], pab[:, cs], AF.Exp, [dab, d_dt], [d_E[e_]], bias=b2[:, t_, h:h + 1])
                            self.tt(Mp[m_][:], E[e_][:], CBm[r][:], ALU.mult, [d_E[e_], d_CBm[r]], [d_Mp[m_]])

                def mid(t_, g=g, hs=hs, Sg=Sg):
                    hf = t_ // 4
                    tsl = slice(t_ * 128, (t_ + 1) * 128)
                    r = t_ % 2
                    py, dy = self.ps(long=True)
                    self.mm(py[:], self.ident_b, xsD[r][:], True, False, [self.d_const, d_xsD[r]], [dy])
                    for hh in range(8):
                        m_ = r * 8 + hh
                        self.mm(py[:, hh * 64:(hh + 1) * 64], Mp[m_][:], xs_bf[r][:, hh * 64:(hh + 1) * 64], False,
                                hh == 7, [d_Mp[m_], d_xs[r]], [dy])
                    pyb, dyb = self.ps()
                    self.mm(pyb[:], CT[:, tsl], self.S_ssd_bf[:, g * 512:(g + 1) * 512], True, True,
                            [d_CT[hf], self.d_S_ssd_bf[g]], [dyb])
                    pds, dds = self.ps()
                    self.mm(pds[:], B_tm[r][:], xsw[r][:], True, True, [d_Bt[r], d_xsw[r]], [dds])
                    self.tt(v864(t1[r][:]), v864(pyb[:]), bch(wst[:, t_, hs]), ALU.mult, [dyb, d_dt], [d_t1[r]])
                    self.tt(v864(Sg), v864(Sg), bch(dl[:, t_, hs]), ALU.mult, [self.d_S_ssd[g], d_dt], [self.d_S_ssd[g]])
                    self.tt(self.S_ssd_bf[:, g * 512:(g + 1) * 512], Sg, pds[:], ALU.add, [self.d_S_ssd[g], dds],
                            [self.d_S_ssd_bf[g]])
                    self.tt(Sg, Sg, pds[:], ALU.add, [self.d_S_ssd[g], dds], [self.d_S_ssd[g]])
                    self.tt(yv[r][:], py[:], t1[r][:], ALU.add, [dy, d_t1[r]], [d_yv[r]])
                    self.tt(yv[r][:], yv[r][:], smz[r][:], ALU.mult, [d_yv[r], d_smz[r]], [d_yv[r]])
                    self.act(t1[r][:], yv[r][:], AF.Square, [d_yv[r]], [d_t1[r], d_ssq[r]], accum=ssq[r][:])
                    self.act(ssq[r][:], ssq[r][:], AF.Ln, [d_ssq[r]], [d_ssq[r]], scale=1.0 / 512, bias=EPS)
                    self.act(ssq[r][:], ssq[r][:], AF.Exp, [d_ssq[r]], [d_ssq[r]], scale=-0.5)

                def tail_a(t_, g=g):
                    r = t_ % 2
                    self.stt(yn[r][:], yv[r][:], ssq[r][:, 0:1], gss[:], ALU.mult, ALU.mult, [d_yv[r], d_ssq[r], d_gss],
                             [d_yn[r]])

                def tail(t_, g=g):
                    hf = t_ // 4
                    tsl = slice(t_ * 128, (t_ + 1) * 128)
                    r = t_ % 2
                    pyt, dyt = self.ps()
                    pytb = pyt[:].bitcast(BF16)
                    for cc in range(4):
                        self.tr(pytb[:, cc * 128:(cc + 1) * 128], yn[r][:, cc * 128:(cc + 1) * 128], self.ident_b,
                                [d_yn[r], self.d_const], [dyt])
                    self.cp(self.big[:, 8 + 4 * g:12 + 4 * g, tsl], pytb[:, 0:512].rearrange("p (a b) -> p a b", a=4), [dyt],
                            [self.d_big[8 + 4 * g + cc][hf] for cc in range(4)], eng=ACT)

                if g + 1 < 4:
                    gload_xb(g + 1)
                front(0)
                for t_ in range(8):
                    if t_ > 0:
                        tail_a(t_ - 1)
                    if t_ < 7:
                        front(t_ + 1)
                    mid(t_)
                    if t_ > 0:
                        tail(t_ - 1)
                tail_a(7)
                tail(7)
            self.end_scope()

    def outproj(self, blk):
        dr = self.dr
        w_in = dr['w_in']
        nT, dn = self.nT, self.d_n
        with contextlib.ExitStack() as st:
            sbs = lambda shape, dt, name: self.sb(shape, dt, name, st)
            mix = sbs([128, 8, TB], BF16, "mix"); d_mix = [[Dep() for _ in range(2)] for _ in range(8)]
            sga = [sbs([128, 512], F32, "sga") for _ in range(2)]; d_sga = mkdeps(2)
            sgb = [sbs([128, 512], F32, "sgb") for _ in range(2)]; d_sgb = mkdeps(2)
            m1 = [sbs([128, 512], F32, "m1") for _ in range(2)]; d_m1 = mkdeps(2)
            it = 0
            for fc in range(8):
                cs = slice(fc * 128, (fc + 1) * 128)
                (wA3,), wad = self.wload([(dr['opA'][fc], 24, 128, 'r')])
                wA, wga, wgb = wA3[:, 0:8, :], wA3[:, 8:16, :], wA3[:, 16:24, :]
                (wBm,), wbd = self.wload([(dr['opB'][fc], 16, 128, 'r')])
                for tb in range(2):
                    sl = slice(tb * 512, (tb + 1) * 512)
                    r = it % 2
                    it += 1
                    pA, dA = self.ps()
                    for kc in range(8):
                        self.mm(pA[:], wA[:, kc, :], self.big[:, kc, sl], kc == 0, kc == 7, [wad, self.d_big[kc][tb]], [dA])
                    pB, dB = self.ps()
                    for kc in range(16):
                        self.mm(pB[:], wBm[:, kc, :], self.big[:, 8 + kc, sl], kc == 0, kc == 15,
                                [wbd, self.d_big[8 + kc][tb]], [dB])
                    pga, dga = self.ps()
                    for kc in range(8):
                        self.mm(pga[:], wga[:, kc, :], nT[:, kc, sl], kc == 0, kc == 7, [wad, dn[kc][tb]], [dga])
                    pgb, dgb = self.ps()
                    for kc in range(8):
                        self.mm(pgb[:], wgb[:, kc, :], nT[:, kc, sl], kc == 0, kc == 7, [wad, dn[kc][tb]], [dgb])
                    self.act(sga[r][:], pga[:], AF.Sigmoid, [dga], [d_sga[r]])
                    self.act(sgb[r][:], pgb[:], AF.Sigmoid, [dgb], [d_sgb[r]])
                    self.tt(m1[r][:], pA[:], sga[r][:], ALU.mult, [dA, d_sga[r]], [d_m1[r]])
                    self.tt(sgb[r][:], pB[:], sgb[r][:], ALU.mult, [dB, d_sgb[r]], [d_sgb[r]])
                    self.tt(mix[:, fc, sl], m1[r][:], sgb[r][:], ALU.add, [d_m1[r], d_sgb[r]], [d_mix[fc][tb]])
            for dcp in range(4):
                (wo,), wod = self.wload([(dr['w_outr'][dcp], 8, 256, 'r')])
                for j in range(2):
                    dc = dcp * 2 + j
                    for tb in range(2):
                        sl = slice(tb * 512, (tb + 1) * 512)
                        po, do = self.ps()
                        for kc in range(8):
                            self.mm(po[:], wo[:, kc, j * 128:(j + 1) * 128], mix[:, kc, sl], kc == 0, kc == 7,
                                    [wod, d_mix[kc][tb]], [do])
                        self.tt(self.hT[:, dc, sl], po[:], self.hT[:, dc, sl], ALU.add, [do, self.d_h[dc][tb]],
                                [self.d_h[dc][tb]])
            self.end_scope()

    def ple(self, blk):
        dr = self.dr
        nT, dn = self.nT, self.d_n
        with contextlib.ExitStack() as st:
            sbs = lambda shape, dt, name: self.sb(shape, dt, name, st)
            pst = [sbs([128, 256], F32, "pst") for _ in range(2)]; d_pst = mkdeps(2)
            pT = sbs([128, 2, TB], BF16, "pT"); d_pT = mkdeps(2)
            sg = [sbs([128, 512], F32, "sgp") for _ in range(2)]; d_sg = mkdeps(2)
            for t_ in range(8):
                r = t_ % 2
                src = dr['p'][blk * TB + t_ * 128:blk * TB + (t_ + 1) * 128, :]
                self.P.dma(SP, lambda e, r=r, src=src: e.dma_start(out=pst[r][:], in_=src), (), [d_pst[r]], semkey='pst%d' % r)
                bank, bd = self.ps()
                for kc in range(2):
                    self.tr(bank[:, kc * 128:(kc + 1) * 128], pst[r][:, kc * 128:(kc + 1) * 128], self.ident_f,
                            [d_pst[r], self.d_c], [bd])
                self.cp(pT[:, :, t_ * 128:(t_ + 1) * 128], bank[:, 0:256].rearrange("p (a b) -> p a b", a=2), [bd],
                        [d_pT[t_ // 4]])
            it = 0
            for fcp in range(4):
                (wg, wp), wd = self.wload([(dr['w_ple_gate'][:, fcp * 256:(fcp + 1) * 256], 8, 256),
                                           (dr['w_ple_proj'][:, fcp * 256:(fcp + 1) * 256], 2, 256)])
                for j in range(2):
                    fc = fcp * 2 + j
                    for tb in range(2):
                        sl = slice(tb * 512, (tb + 1) * 512)
                        r = it % 2
                        it += 1
                        pg, dg = self.ps()
                        for kc in range(8):
                            self.mm(pg[:], wg[:, kc, j * 128:(j + 1) * 128], nT[:, kc, sl], kc == 0, kc == 7,
                                    [wd, dn[kc][tb]], [dg])
                        pp, dp = self.ps()
                        for kc in range(2):
                            self.mm(pp[:], wp[:, kc, j * 128:(j + 1) * 128], pT[:, kc, sl], kc == 0, kc == 1,
                                    [wd, d_pT[tb]], [dp])
                        self.act(sg[r][:], pg[:], AF.Sigmoid, [dg], [d_sg[r]])
                        self.tt(sg[r][:], pp[:], sg[r][:], ALU.mult, [dp, d_sg[r]], [d_sg[r]])
                        self.tt(self.hT[:, fc, sl], self.hT[:, fc, sl], sg[r][:], ALU.add, [self.d_h[fc][tb], d_sg[r]],
                                [self.d_h[fc][tb]])
            self.end_scope()

    def store_out(self, blk, raw=False):
        dr = self.dr
        with contextlib.ExitStack() as st:
            os_ = [self.sb([128, 8, 128], F32, "ostage", st) for _ in range(2)]
            dos = mkdeps(2)
            for kc in range(8):
                s = os_[kc % 2]
                ds = dos[kc % 2]
                for half in range(2):
                    bank, bd = self.ps()
                    for q in range(4):
                        tt_ = half * 4 + q
                        self.tr(bank[:, q * 128:(q + 1) * 128], self.hT[:, kc, tt_ * 128:(tt_ + 1) * 128], self.ident_f,
                                [self.d_h[kc][half], self.d_c], [bd], sig=(q == 3))
                    self.cp(s[:, half * 4:(half + 1) * 4, :], bank[:].rearrange("p (a b) -> p a b", a=4), [bd], [ds])
                dst = dr['out'][blk * TB:(blk + 1) * TB, kc * 128:(kc + 1) * 128].rearrange("(t p) c -> p t c", p=128)
                self.P.dma(SP, lambda e, s=s, dst=dst: e.dma_start(out=dst, in_=s[:]), [ds], (),
                           semkey='os%d' % (kc % 2))
            self.end_scope()

    def run(self):
        dr = self.dr
        self.init()
        stages = ['load', 'norm', 'ffn1', 'hgrn', 'ssd', 'outproj', 'ffn2', 'ple', 'final']
        upto = len(stages) if self.stop_after is None else stages.index(self.stop_after) + 1
        act = stages[:upto]
        for blk in range(NBLK):
            self.load_x(blk)
            if 'norm' in act:
                self.norm(0)
            if 'ffn1' in act:
                self.ffn(dr['ffn1_w13'], dr['ffn1_w2'])
            if 'hgrn' in act:
                self.norm(1)
                self.hgrn(blk)
            if 'ssd' in act:
                self.ssd(blk)
            if 'outproj' in act:
                self.outproj(blk)
            if 'dbg' in dr:
                allb = [d for row in self.d_big for d in row]
                dst = dr['dbg'][blk * 128:(blk + 1) * 128, :]
                self.P.dma(SP, lambda e, dst=dst: e.dma_start(out=dst, in_=self.big[:].rearrange("p a b -> p (a b)")),
                           allb, (), semkey='dbg')
            if 'ffn2' in act:
                self.norm(2)
                self.ffn(dr['ffn2_w13'], dr['ffn2_w2'])
            if 'ple' in act:
                self.norm(3)
                self.ple(blk)
            if 'final' in act:
                self.norm(4, final=True)
            self.store_out(blk)


def _consts():
    c = np.zeros((128, 7, 128), np.float32)
    i = np.arange(128)
    same = (i[:, None] // 64) == (i[None, :] // 64)
    c[:, 0, :] = np.eye(128)
    c[:, 1, :] = ((i[:, None] <= i[None, :]) & same)
    c[:, 2, :] = ((i[:, None] > i[None, :]) & same)
    c[:, 3, :] = (i[:, None] <= i[None, :])
    c[:, 4, :] = 1.0
    c[0, 6, :] = 1.0
    c[:, 5, :] = np.where(i[:, None] > i[None, :], -1e30, 0.0)
    return c


_IN_SPECS = [
    ('x', [T, D]), ('p', [T, 256]), ('consts', [128, 7, 128]), ('gains', [128, 40]), ('hgn', [128, 8]),
    ('cw', [128, 96]), ('cbp', [128, 24]), ('cbrow', [1, 3072]), ('dt_bias', [32]), ('a_log', [32]),
    ('d_skip', [32]), ('ssm_norm', [2048]), ('hg_lb', [2, 1024]),
    ('ffn1_w13', [D, 2 * DFF]), ('ffn1_w2', [8, 128, DFF]), ('w_in', [D, DIN]),
    ('opA', [8, 128, 3072]), ('opB', [8, 128, 2048]), ('w_outr', [4, 128, 2048]),
    ('ffn2_w13', [D, 2 * DFF]), ('ffn2_w2', [8, 128, DFF]), ('w_ple_gate', [D, D]), ('w_ple_proj', [256, D]),
]


def build(stop_after=None):
    nc = bass.Bass("TRN2", target_bir_lowering=False)
    dram = {}
    for name, shape in _IN_SPECS:
        dram[name] = nc.dram_tensor(name, shape, F32, kind="ExternalInput").ap()
    dram['out'] = nc.dram_tensor("out", [T, D], F32, kind="ExternalOutput").ap()
    if stop_after in ('hgrn', 'ssd'):
        dram['dbg'] = nc.dram_tensor("dbg", [NBLK * 128, 24 * TB], BF16, kind="ExternalOutput").ap()
    with contextlib.ExitStack() as es:
        block = es.enter_context(nc.Block())
        kb = KB(nc, block, es, dram, stop_after=stop_after)
        kb.run()
        print("ops", len(kb.P.ops), "waits", kb.P.nwaits)
    return nc


def make_in_maps(inp):
    f = lambda a: np.ascontiguousarray(np.asarray(a, dtype=np.float32))
    gains = np.stack([f(inp['ffn1_norm'])[0], f(inp['mix_norm'])[0], f(inp['ffn2_norm'])[0],
                      f(inp['ple_norm'])[0], f(inp['final_norm'])], 0)
    gains = gains.reshape(5, 8, 128).transpose(2, 0, 1).reshape(128, 40)
    hgn = f(inp['hg_norm'])[0].reshape(8, 128).T
    cw = f(inp['conv_w'])[0].T.reshape(24, 128, 4).transpose(1, 0, 2).reshape(128, 96)
    cbp = f(inp['conv_b'])[0].reshape(24, 128).T
    def w2r(w):
        return f(w.reshape(NFC, 128, 8, 128).transpose(2, 1, 0, 3).reshape(8, 128, DFF))

    def colblk(w, c0, ncol, nk):
        return w[:, c0:c0 + ncol].reshape(nk, 128, ncol).transpose(1, 0, 2).reshape(128, nk * ncol)

    w_in_ = f(inp['w_in'])[0]
    whg = f(inp['w_hg_out'])[0]
    wssm = f(inp['w_ssm_out'])[0]
    wout = f(inp['w_out'])[0]
    opA = np.stack([np.concatenate([colblk(whg, fc * 128, 128, 8),
                                    colblk(w_in_, OFF_BRG + fc * 128, 128, 8),
                                    colblk(w_in_, OFF_BRG + 1024 + fc * 128, 128, 8)], axis=1) for fc in range(8)], 0)
    opB = np.stack([colblk(wssm, fc * 128, 128, 16) for fc in range(8)], 0)
    w_outr = np.stack([colblk(wout, dcp * 256, 256, 8) for dcp in range(4)], 0)
    shared = dict(
        consts=_consts(), gains=f(gains), hgn=f(hgn), cw=f(cw), cbp=f(cbp), cbrow=f(inp['conv_b']),
        dt_bias=f(inp['dt_bias'])[0], a_log=f(inp['a_log'])[0], d_skip=f(inp['d_skip'])[0],
        ssm_norm=f(inp['ssm_norm'])[0], hg_lb=f(inp['hg_lb']),
        ffn1_w13=f(inp['ffn1_w13'])[0], ffn1_w2=w2r(f(inp['ffn1_w2'])[0]), w_in=w_in_,
        opA=f(opA), opB=f(opB), w_outr=f(w_outr),
        ffn2_w13=f(inp['ffn2_w13'])[0], ffn2_w2=w2r(f(inp['ffn2_w2'])[0]), w_ple_gate=f(inp['w_ple_gate'])[0],
        w_ple_proj=f(inp['w_ple_proj'])[0],
    )
    x = f(inp['x'])
    p = f(inp['p'])[0]
    maps = []
    for b in range(8):
        m = dict(shared)
        m['x'] = x[b]
        m['p'] = p[b]
        maps.append(m)
    return maps


_NC_CACHE = {}


def kernel(**inputs):
    if 'nc' not in _NC_CACHE:
        _NC_CACHE['nc'] = build()
    nc = _NC_CACHE['nc']
    maps = make_in_maps(inputs)
    res = run_bass_kernel_spmd(nc, maps, core_ids=list(range(8)))
    out = np.stack([np.asarray(r['out'], dtype=np.float32) for r in res.results], 0)
    return out
```

```python
import contextlib
import numpy as np
import concourse.bass as bass
import concourse.mybir as mybir
from concourse.bass_utils import run_bass_kernel_spmd

F32 = mybir.dt.float32
BF16 = mybir.dt.bfloat16
AF = mybir.ActivationFunctionType
ALU = mybir.AluOpType

PE, ACT, DVE, POOL, SP = 'tensor', 'scalar', 'vector', 'gpsimd', 'sync'
ENGS = (PE, ACT, DVE, POOL, SP)
CENGS = (PE, ACT, DVE)

D = 1024
T = 2048
TB = 1024
NBLK = T // TB
DFF = 2816
NFC = DFF // 128
DIN = 11296
EPS = 1e-6
OFF_Q, OFF_F, OFF_V, OFF_G = 0, 1024, 2048, 3072
OFF_MZ = 4096
OFF_XBC = 6144
OFF_DT = 9216
OFF_BRG = 9248


class Dep:
    __slots__ = ('name', 'w', 'r')

    def __init__(self, name=''):
        self.name = name
        self.w = {}
        self.r = {}


def mkdeps(n):
    return [Dep() for _ in range(n)]


class Op:
    __slots__ = ('eng', 'fn', 'deps', 'idx', 'semkey', 'count', 'sig', 'is_dma')


class Prog:
    def __init__(self, nc, block):
        self.nc = nc
        self.block = block
        self.ops = []
        self.flushed = 0
        self.cnt = {e: 0 for e in ENGS}
        self.pending = {e: [] for e in ENGS}
        self.dma_counts = {}
        self.dsem = {}
        self.esem = {e: nc.alloc_semaphore('c_' + e) for e in (PE, ACT, DVE, POOL)}
        self.seen = {e: {} for e in ENGS}
        self.last = {e: None for e in ENGS}
        self.scope_dmas = []
        self.nwaits = 0

    def add(self, eng, fn, reads=(), writes=(), sig=True, is_dma=False, semkey=None, accum=False,
            extra_deps=()):
        op = Op()
        op.eng = eng
        op.fn = fn
        op.is_dma = is_dma
        op.idx = len(self.ops)
        op.semkey = semkey
        op.sig = sig
        op.count = None
        ds = set(extra_deps)
        for d in reads:
            ds.update(d.w.values())
        for d in writes:
            ds.update(d.r.values())
            if not accum:
                ds.update(d.w.values())
        op.deps = ds
        if is_dma:
            if semkey not in self.dsem:
                self.dsem[semkey] = self.nc.alloc_semaphore('d_' + str(semkey))
            c = self.dma_counts.get(semkey, 0) + 16
            self.dma_counts[semkey] = c
            op.count = c
            k = ('d', op.idx)
        else:
            k = eng
            if fn is not None:
                if sig:
                    self.cnt[eng] += 1
                    op.count = self.cnt[eng]
                    for p in self.pending[eng]:
                        p.count = op.count
                    self.pending[eng] = []
                else:
                    self.pending[eng].append(op)
                self.last[eng] = op
        for d in reads:
            d.r[k] = op
        for d in writes:
            if not accum:
                d.w = {}
            d.w[k] = op
            d.r = {}
        self.ops.append(op)
        return op

    def dma(self, eng, fn, reads=(), writes=(), semkey=None, accum=False):
        op = self.add(eng, fn, reads, writes, is_dma=True, semkey=semkey, accum=accum)
        if eng == SP:
            self.scope_dmas.append(op)
        return op

    def barrier(self):
        lasts = [self.last[e] for e in CENGS + (POOL,) if self.last[e] is not None]
        dm = list(self.scope_dmas)
        self.scope_dmas = []
        for e in CENGS + (SP,):
            self.add(e, None, extra_deps=[o for o in lasts if o.eng != e] + dm)

    def flush(self):
        ops = self.ops[self.flushed:]
        self.flushed = len(self.ops)
        for e in ENGS:
            assert not self.pending[e], "pending non-signalling ops on %s" % e
            ops_e = [op for op in ops if op.eng == e]
            if not ops_e:
                continue
            getattr(self.block, e)(lambda engh, e=e, ops_e=ops_e: self._run(e, engh, ops_e))

    def _run(self, e, engh, ops_e):
        seen = self.seen[e]
        for op in ops_e:
            need = {}
            for dj in op.deps:
                if dj.is_dma:
                    s = self.dsem[dj.semkey]
                else:
                    if dj.fn is None:
                        continue
                    if dj.eng == PE and e == PE and not op.is_dma and op.fn is not None:
                        continue
                    s = self.esem[dj.eng]
                v = dj.count
                assert v is not None
                key = id(s)
                if v > need.get(key, (None, 0))[1]:
                    need[key] = (s, v)
            for key, (s, v) in need.items():
                if v > seen.get(key, 0):
                    engh.wait_ge(s, v)
                    seen[key] = v
                    self.nwaits += 1
            if op.fn is None:
                continue
            inst = op.fn(engh)
            if op.is_dma:
                inst.then_inc(self.dsem[op.semkey], 16)
            elif op.sig:
                inst.then_inc(self.esem[e], 1)


class KB:
    def __init__(self, nc, block, es, dram, stop_after=None):
        self.nc = nc
        self.P = Prog(nc, block)
        self.es = es
        self.dr = dram
        self.stop_after = stop_after
        self.uid = 0
        self.banks = []
        self.bank_deps = []
        for i in range(8):
            self.banks.append(es.enter_context(nc.psum_tensor("bank%d" % i, [128, 512], F32)))
            self.bank_deps.append(Dep())
        self.bank_next = 0
        self.long_next = 0
        self.copy_rr = 0

    def sb(self, shape, dt, name=None, stack=None):
        self.uid += 1
        name = "%s_%d" % (name or "t", self.uid)
        return (stack or self.es).enter_context(self.nc.sbuf_tensor(name, shape, dt))

    def ps(self, long=False):
        if long:
            i = 6 + self.long_next
            self.long_next = (self.long_next + 1) % 2
        else:
            i = self.bank_next
            self.bank_next = (i + 1) % 6
        return self.banks[i], self.bank_deps[i]

    def mm(self, out, lhsT, rhs, start, stop, reads, writes):
        self.P.add(PE, lambda e: e.matmul(out, lhsT=lhsT, rhs=rhs, start=start, stop=stop),
                   reads, writes, sig=True)

    def tr(self, out, in_, ident, reads, writes, sig=True):
        self.P.add(PE, lambda e: e.transpose(out=out, in_=in_, identity=ident), reads, writes, sig=True)

    def act(self, out, in_, func, reads, writes, bias=None, scale=None, accum=None):
        kw = {}
        if bias is not None:
            kw['bias'] = bias
        if scale is not None:
            kw['scale'] = scale
        if accum is not None:
            kw['accum_out'] = accum
        self.P.add(ACT, lambda e: e.activation(out=out, in_=in_, func=func, **kw), reads, writes)

    def tt(self, out, in0, in1, op, reads, writes, eng=DVE):
        self.P.add(eng, lambda e: e.tensor_tensor(out=out, in0=in0, in1=in1, op=op), reads, writes)

    def stt(self, out, in0, scalar, in1, op0, op1, reads, writes):
        self.P.add(DVE, lambda e: e.scalar_tensor_tensor(out=out, in0=in0, scalar=scalar, in1=in1,
                                                         op0=op0, op1=op1), reads, writes)

    def ts(self, out, in0, s1, op0, reads, writes, s2=None, op1=None, eng=DVE):
        if op1 is None:
            self.P.add(eng, lambda e: e.tensor_scalar(out=out, in0=in0, scalar1=s1, scalar2=None, op0=op0),
                       reads, writes)
        else:
            self.P.add(eng, lambda e: e.tensor_scalar(out=out, in0=in0, scalar1=s1, scalar2=s2, op0=op0,
                                                      op1=op1), reads, writes)

    def cp(self, out, in_, reads, writes, eng=None):
        if eng is None:
            eng = (ACT, DVE)[self.copy_rr % 2]
            self.copy_rr += 1
        if eng == ACT:
            self.act(out, in_, AF.Copy, reads, writes)
        else:
            self.P.add(eng, lambda e: e.tensor_copy(out=out, in_=in_), reads, writes)

    def recip(self, out, in_, reads, writes):
        self.P.add(DVE, lambda e: e.reciprocal(out=out, in_=in_), reads, writes)

    def memset(self, ap, val, writes, eng=DVE):
        self.P.add(eng, lambda e: e.memset(ap, val), (), writes)

    def end_scope(self):
        self.P.barrier()
        self.P.flush()

    def init_wpool(self, nslots=3, nel=4096):
        self.wslots = [self.sb([128, nel], BF16, "wslot") for _ in range(nslots)]
        self.wdeps = mkdeps(nslots)
        self.wnext = 0
        self.wnel = nel

    def wload(self, parts):
        i = self.wnext
        self.wnext = (i + 1) % len(self.wslots)
        slot = self.wslots[i]
        dep = self.wdeps[i]
        views = []
        off = 0
        first = True
        for part in parts:
            if len(part) == 4:
                (src, nk, ncols, _) = part
                n = nk * ncols
                assert off + n <= self.wnel
                v = slot[:, off:off + n].rearrange("p (k c) -> p k c", k=nk)
                nch = 1
                while n // nch > 2048 or n % nch:
                    nch += 1
                vo = slot[:, off:off + n].rearrange("p (a b) -> p a b", a=nch)
                s = src.rearrange("p (a b) -> p a b", a=nch)
                self.P.dma(POOL, lambda e, vo=vo, s=s: e.dma_start(out=vo, in_=s), (), [dep],
                           semkey='w%d' % i, accum=not first)
            else:
                (src, nk, ncols) = part
                n = nk * ncols
                assert off + n <= self.wnel
                v = slot[:, off:off + n].rearrange("p (k c) -> p k c", k=nk)
                s = src.rearrange("(k p) c -> p k c", p=128)
                self.P.dma(POOL, lambda e, v=v, s=s: e.dma_start(out=v, in_=s), (), [dep],
                           semkey='w%d' % i, accum=not first)
            first = False
            views.append(v)
            off += n
        return views, dep

    def init(self):
        dr = self.dr
        P = self.P
        self.cf = self.sb([128, 7, 128], F32, "cf")
        self.cb16 = self.sb([128, 4, 128], BF16, "cb16")
        self.d_c = Dep()
        cdep = self.d_c
        P.dma(SP, lambda e: e.dma_start(out=self.cf[:], in_=dr['consts']), (), [cdep], semkey='c', accum=True)
        self.ident_f = self.cf[:, 0, :]
        self.triL64 = self.cf[:, 1, :]
        self.triU64 = self.cf[:, 2, :]
        self.triL128 = self.cf[:, 3, :]
        self.ones_f = self.cf[:, 4, :]
        self.gn = self.sb([128, 40], F32, "gn")
        P.dma(SP, lambda e: e.dma_start(out=self.gn[:], in_=dr['gains']), (), [cdep], semkey='c', accum=True)
        self.hgn = self.sb([128, 8], F32, "hgn")
        P.dma(SP, lambda e: e.dma_start(out=self.hgn[:], in_=dr['hgn']), (), [cdep], semkey='c', accum=True)
        self.cw = self.sb([128, 24, 4], F32, "cw")
        P.dma(SP, lambda e: e.dma_start(out=self.cw[:], in_=dr['cw'].rearrange("p (a b) -> p a b", a=24)),
              (), [cdep], semkey='c', accum=True)
        self.cbp = self.sb([128, 24], F32, "cbp")
        P.dma(SP, lambda e: e.dma_start(out=self.cbp[:], in_=dr['cbp']), (), [cdep], semkey='c', accum=True)
        self.dtb = self.sb([128, 32], F32, "dtb")
        self.aneg = self.sb([128, 32], F32, "aneg")
        self.dsk = self.sb([128, 32], F32, "dsk")
        P.dma(SP, lambda e: e.dma_start(out=self.dtb[:], in_=dr['dt_bias'].partition_broadcast(128)),
              (), [cdep], semkey='c', accum=True)
        P.dma(SP, lambda e: e.dma_start(out=self.aneg[:], in_=dr['a_log'].partition_broadcast(128)),
              (), [cdep], semkey='c', accum=True)
        P.dma(SP, lambda e: e.dma_start(out=self.dsk[:], in_=dr['d_skip'].partition_broadcast(128)),
              (), [cdep], semkey='c', accum=True)
        self.oml = self.sb([128, 1024], F32, "oml")
        self.S_hg = self.sb([128, 8, 128], F32, "S_hg")
        self.d_S_hg = mkdeps(8)
        self.S_ssd = self.sb([128, 2048], F32, "S_ssd")
        self.S_ssd_bf = self.sb([128, 2048], BF16, "S_ssd_bf")
        self.d_S_ssd = mkdeps(4)
        self.d_S_ssd_bf = mkdeps(4)
        self.xtail = self.sb([128, 24, 3], BF16, "xtail")
        self.d_xtail = mkdeps(24)
        self.hT = self.sb([128, 8, TB], F32, "hT")
        self.d_h = [[Dep() for _ in range(2)] for _ in range(8)]
        self.nT = self.sb([128, 8, TB], BF16, "nT")
        self.d_n = [[Dep() for _ in range(2)] for _ in range(8)]
        self.big = self.sb([128, 24, TB], BF16, "big")
        self.d_big = [[Dep() for _ in range(2)] for _ in range(24)]
        self.init_wpool()
        self.d_const = Dep()
        with contextlib.ExitStack() as st:
            lbt = self.sb([128, 2, 1024], F32, "lbt", st)
            P.dma(SP, lambda e: e.dma_start(out=lbt[:, 0, :], in_=dr['hg_lb'][0].partition_broadcast(128)),
                  (), [cdep], semkey='c', accum=True)
            P.dma(SP, lambda e: e.dma_start(out=lbt[:, 1, :], in_=dr['hg_lb'][1].partition_broadcast(128)),
                  (), [cdep], semkey='c', accum=True)
            dc = self.d_const
            self.tt(lbt[:, 0, :], lbt[:, 1, :], lbt[:, 0, :], ALU.subtract, [cdep], [dc])
            self.act(self.oml[:], lbt[:, 0, :], AF.Sigmoid, [dc], [dc])
            self.cp(self.cb16[:, 0, :], self.ident_f, [cdep], [dc], eng=DVE)
            self.cp(self.cb16[:, 1, :], self.ones_f, [cdep], [dc], eng=DVE)
            self.cp(self.cb16[:, 2, :], self.cf[:, 5, :], [cdep], [dc], eng=DVE)
            self.negm_b = self.cb16[:, 2, :]
            self.cp(self.cb16[:, 3, :], self.cf[:, 6, :], [cdep], [dc], eng=DVE)
            self.e0_b = self.cb16[:, 3, :]
            self.act(self.aneg[:], self.aneg[:], AF.Exp, [cdep], [dc])
            self.ts(self.aneg[:], self.aneg[:], -1.0, ALU.mult, [dc], [dc])
            self.memset(self.S_hg[:], 0.0, self.d_S_hg)
            self.memset(self.S_ssd[:], 0.0, self.d_S_ssd)
            self.memset(self.S_ssd_bf[:], 0.0, self.d_S_ssd_bf)
            self.memset(self.xtail[:], 0.0, self.d_xtail)
            if 'dbg' in dr:
                self.memset(self.big[:], 0.0, [d for row in self.d_big for d in row])
            self.ident_b = self.cb16[:, 0, :]
            self.ones_b = self.cb16[:, 1, :]
            self.end_scope()

    def load_x(self, blk):
        dr = self.dr
        with contextlib.ExitStack() as st:
            xs = [self.sb([128, 8, 128], F32, "xstage", st) for _ in range(2)]
            dxs = mkdeps(2)
            for kc in range(8):
                s = xs[kc % 2]
                ds = dxs[kc % 2]
                src = dr['x'][blk * TB:(blk + 1) * TB, kc * 128:(kc + 1) * 128].rearrange("(t p) c -> p t c", p=128)
                self.P.dma(SP, lambda e, s=s, src=src: e.dma_start(out=s[:], in_=src), (), [ds],
                           semkey='xs%d' % (kc % 2))
                for half in range(2):
                    bank, bd = self.ps()
                    for q in range(4):
                        tt_ = half * 4 + q
                        self.tr(bank[:, q * 128:(q + 1) * 128], s[:, tt_, :], self.ident_f, [ds, self.d_c], [bd],
                                sig=(q == 3))
                    self.cp(self.hT[:, kc, half * 512:(half + 1) * 512], bank[:], [bd], [self.d_h[kc][half]])
            self.end_scope()

    def norm(self, gi, final=False):
        with contextlib.ExitStack() as st:
            sq = [self.sb([128, TB], BF16, "sq", st) for _ in range(2)]
            dsq = mkdeps(2)
            rstd = self.sb([128, TB], F32, "rstd", st)
            drs = mkdeps(2)
            b0, bd0 = self.ps()
            b1, bd1 = self.ps()
            bks = ((b0, bd0), (b1, bd1))
            for kc in range(8):
                s = sq[kc % 2]
                ds = dsq[kc % 2]
                if kc % 3 == 2:
                    self.tt(s[:], self.hT[:, kc, :], self.hT[:, kc, :], ALU.mult, self.d_h[kc], [ds])
                else:
                    self.act(s[:], self.hT[:, kc, :], AF.Square, self.d_h[kc], [ds])
                for tb in range(2):
                    self.mm(bks[tb][0][:], self.ones_b, s[:, tb * 512:(tb + 1) * 512], kc == 0, kc == 7,
                            [ds, self.d_const], [bks[tb][1]])
            for tb in range(2):
                sl = slice(tb * 512, (tb + 1) * 512)
                self.act(rstd[:, sl], bks[tb][0][:], AF.Ln, [bks[tb][1]], [drs[tb]], bias=EPS, scale=1.0 / D)
                self.act(rstd[:, sl], rstd[:, sl], AF.Exp, [drs[tb]], [drs[tb]], scale=-0.5)
            for kc in range(8):
                for tb in range(2):
                    sl = slice(tb * 512, (tb + 1) * 512)
                    if final:
                        self.stt(self.hT[:, kc, sl], self.hT[:, kc, sl], self.gn[:, gi * 8 + kc:gi * 8 + kc + 1],
                                 rstd[:, sl], ALU.mult, ALU.mult, [self.d_h[kc][tb], drs[tb], self.d_c],
                                 [self.d_h[kc][tb]])
                    else:
                        self.stt(self.nT[:, kc, sl], self.hT[:, kc, sl], self.gn[:, gi * 8 + kc:gi * 8 + kc + 1],
                                 rstd[:, sl], ALU.mult, ALU.mult, [self.d_h[kc][tb], drs[tb], self.d_c],
                                 [self.d_n[kc][tb]])
            self.end_scope()

    def ffn(self, w13, w2):
        with contextlib.ExitStack() as st:
            sg = [self.sb([128, 512], F32, "sg", st) for _ in range(2)]
            dsg = mkdeps(2)
            it = 0
            for g2 in range(NFC // 2):
                (wg, wu), wd = self.wload([(w13[:, g2 * 256:(g2 + 1) * 256], 8, 256),
                                           (w13[:, DFF + g2 * 256:DFF + (g2 + 1) * 256], 8, 256)])
                for j in range(2):
                    fc = g2 * 2 + j
                    for tb in range(2):
                        sl = slice(tb * 512, (tb + 1) * 512)
                        pg, dg = self.ps()
                        pu, du = self.ps()
                        for kc in range(8):
                            self.mm(pg[:], wg[:, kc, j * 128:(j + 1) * 128], self.nT[:, kc, sl], kc == 0, kc == 7,
                                    [wd, self.d_n[kc][tb]], [dg])
                        for kc in range(8):
                            self.mm(pu[:], wu[:, kc, j * 128:(j + 1) * 128], self.nT[:, kc, sl], kc == 0, kc == 7,
                                    [wd, self.d_n[kc][tb]], [du])
                        s = sg[it % 2]
                        ds = dsg[it % 2]
                        it += 1
                        self.act(s[:], pg[:], AF.Silu, [dg], [ds])
                        self.tt(self.big[:, fc, sl], s[:], pu[:], ALU.mult, [ds, du], [self.d_big[fc][tb]])
            for dc in range(8):
                (wa,), wda = self.wload([(w2[dc], NFC, 128, 'r')])
                for tb in range(2):
                    sl = slice(tb * 512, (tb + 1) * 512)
                    po, do = self.ps()
                    for fc in range(NFC):
                        self.mm(po[:], wa[:, fc, :], self.big[:, fc, sl], fc == 0, fc == NFC - 1,
                                [wda, self.d_big[fc][tb]], [do])
                    self.stt(self.hT[:, dc, sl], po[:], 0.5, self.hT[:, dc, sl], ALU.mult, ALU.add,
                             [do, self.d_h[dc][tb]], [self.d_h[dc][tb]])
            self.end_scope()

    def hgrn(self, blk):
        w_in = self.dr['w_in']
        nT, dn = self.nT, self.d_n
        with contextlib.ExitStack() as st:
            sbs = lambda shape, dt, name: self.sb(shape, dt, name, st)
            dd2 = lambda: [mkdeps(2) for _ in range(2)]
            k_tm = sbs([128, 8, 128], F32, "k_tm"); d_k = mkdeps(2)
            logf = sbs([128, 8, 128], F32, "logf"); d_lf = mkdeps(2)
            qs = sbs([128, TB], BF16, "qs"); d_qs = mkdeps(2)
            eGi = sbs([128, TB], BF16, "eGi"); d_eGi = mkdeps(2)
            eGe = sbs([128, 8, 128], BF16, "eGe"); d_eGe = mkdeps(2)
            k_inv = sbs([128, TB], BF16, "k_inv"); d_ki = mkdeps(2)
            v_bf = [sbs([128, 8, 128], BF16, "v_bf") for _ in range(2)]; d_v = dd2()
            sgate = [sbs([128, TB], BF16, "sgate") for _ in range(2)]; d_sg = dd2()
            eG = [sbs([128, TB], F32, "eG") for _ in range(2)]; d_eG = dd2()
            q_dec = [sbs([128, TB], BF16, "q_dec") for _ in range(2)]; d_qd = dd2()
            k_end = [sbs([128, 8, 128], BF16, "k_end") for _ in range(2)]; d_ke = dd2()
            att = [sbs([128, 8, 128], BF16, "att") for _ in range(2)]; d_att = dd2()
            osq = [sbs([128, 512], BF16, "osq") for _ in range(2)]; d_osq = mkdeps(2)
            sd = [sbs([128, 512], F32, "sd") for _ in range(2)]; d_sd = mkdeps(2)
            tmp = [sbs([128, 512], F32, "tmp") for _ in range(2)]; d_tmp = mkdeps(2)
            Sbf = [sbs([128, 128], BF16, "Sbf") for _ in range(4)]; d_Sbf = mkdeps(4)
            self._sbi = 0
            v4 = lambda bank: bank[:].rearrange("p (a b) -> p a b", a=4)

            hw = {}

            def hload(j):
                hw[j] = self.wload([(w_in[:, OFF_Q + j * 128:OFF_Q + (j + 1) * 128], 8, 128),
                                    (w_in[:, OFF_F + j * 128:OFF_F + (j + 1) * 128], 8, 128),
                                    (w_in[:, OFF_V + j * 128:OFF_V + (j + 1) * 128], 8, 128),
                                    (w_in[:, OFF_G + j * 128:OFF_G + (j + 1) * 128], 8, 128)])

            hload(0)

            def front(j):
                p = j % 2
                if j + 1 < 8:
                    hload(j + 1)
                (wq, wf, wv, wg), wd = hw[j]
                for tb in range(2):
                    sl = slice(tb * 512, (tb + 1) * 512)
                    pq, dq = self.ps()
                    for kc in range(8):
                        self.mm(pq[:], wq[:, kc, :], nT[:, kc, sl], kc == 0, kc == 7, [wd, dn[kc][tb]], [dq])
                    self.act(qs[:, sl], pq[:], AF.Copy, [dq], [d_qs[tb]], scale=float(128 ** -0.5))
                    yield
                    pg, dg = self.ps()
                    for kc in range(8):
                        self.mm(pg[:], wg[:, kc, :], nT[:, kc, sl], kc == 0, kc == 7, [wd, dn[kc][tb]], [dg])
                    self.act(sgate[p][:, sl], pg[:], AF.Silu, [dg], [d_sg[p][tb]])
                    yield
                for hf in range(2):
                    sl = slice(hf * 512, (hf + 1) * 512)
                    h4 = slice(hf * 4, hf * 4 + 4)
                    pf, df = self.ps()
                    for q in range(4):
                        t_ = hf * 4 + q
                        for kc in range(8):
                            self.mm(pf[:, q * 128:(q + 1) * 128], nT[:, kc, t_ * 128:(t_ + 1) * 128], wf[:, kc, :],
                                    kc == 0, kc == 7, [wd, dn[kc][hf]], [df])
                    self.act(k_tm[:, h4, :], v4(pf), AF.Exp, [df], [d_k[hf]])
                    self.act(k_tm[:, h4, :], k_tm[:, h4, :], AF.Ln, [d_k[hf]], [d_k[hf]], bias=1.0)
                    self.act(k_tm[:, h4, :], k_tm[:, h4, :], AF.Exp, [d_k[hf]], [d_k[hf]], scale=-1.0)
                    yield
                    pv, dv = self.ps()
                    for q in range(4):
                        t_ = hf * 4 + q
                        for kc in range(8):
                            self.mm(pv[:, q * 128:(q + 1) * 128], nT[:, kc, t_ * 128:(t_ + 1) * 128], wv[:, kc, :],
                                    kc == 0, kc == 7, [wd, dn[kc][hf]], [dv])
                    self.cp(v_bf[p][:, h4, :], v4(pv), [dv], [d_v[p][hf]])
                    yield
                    self.tt(k_tm[:, h4, :], k_tm[:, h4, :],
                            self.oml[:, j * 128:(j + 1) * 128].unsqueeze(1).to_broadcast([128, 4, 128]), ALU.mult,
                            [d_k[hf], self.d_const], [d_k[hf]])
                    self.act(logf[:, h4, :], k_tm[:, h4, :], AF.Ln, [d_k[hf]], [d_lf[hf]], scale=-1.0, bias=1.0)
                    pG, dG = self.ps()
                    pE, dE = self.ps()
                    pK, dK = self.ps()
                    for q in range(4):
                        t_ = hf * 4 + q
                        cs = slice(q * 128, (q + 1) * 128)
                        self.mm(pG[:, cs], logf[:, t_, :], self.triL64, True, True, [d_lf[hf], self.d_c], [dG])
                        self.mm(pE[:, cs], self.triU64, logf[:, t_, :], True, True, [d_lf[hf], self.d_c], [dE])
                        self.tr(pK[:, cs], k_tm[:, t_, :], self.ident_f, [d_k[hf], self.d_c], [dK])
                    self.act(eG[p][:, sl], pG[:], AF.Exp, [dG], [d_eG[p][hf]])
                    self.act(eGi[:, sl], pG[:], AF.Exp, [dG], [d_eGi[hf]], scale=-1.0)
                    self.act(eGe[:, h4, :], v4(pE), AF.Exp, [dE], [d_eGe[hf]])
                    self.tt(q_dec[p][:, sl], qs[:, sl], eG[p][:, sl], ALU.mult, [d_qs[hf], d_eG[p][hf]], [d_qd[p][hf]])
                    self.tt(k_inv[:, sl], pK[:], eGi[:, sl], ALU.mult, [dK, d_eGi[hf]], [d_ki[hf]])
                    self.tt(k_end[p][:, h4, :], k_tm[:, h4, :], eGe[:, h4, :], ALU.mult, [d_k[hf], d_eGe[hf]],
                            [d_ke[p][hf]])
                    yield
                    pA, dA = self.ps()
                    for q in range(4):
                        t_ = hf * 4 + q
                        cs = slice(q * 128, (q + 1) * 128)
                        ts_ = slice(t_ * 128, (t_ + 1) * 128)
                        self.mm(pA[:, cs], k_inv[:, ts_], q_dec[p][:, ts_], True, True, [d_ki[hf], d_qd[p][hf]], [dA])
                    self.tt(att[p][:, h4, :], v4(pA), self.triL64.unsqueeze(1).to_broadcast([128, 4, 128]), ALU.mult,
                            [dA, self.d_c], [d_att[p][hf]])
                    yield

            def back(j):
                p = j % 2
                dS = self.d_S_hg[j]
                S = self.S_hg[:, j, :]
                cur = self._sbi % 4
                self._sbi += 1
                self.cp(Sbf[cur][:], S, [dS], [d_Sbf[cur]], eng=ACT)
                for hf in range(2):
                    sl = slice(hf * 512, (hf + 1) * 512)
                    pO, dO = self.ps(long=True)
                    for q in range(4):
                        t_ = hf * 4 + q
                        cs = slice(q * 128, (q + 1) * 128)
                        pS, dSp = self.ps()
                        pS1, dSp1 = self.ps()
                        self.mm(pS[:, 0:128], k_end[p][0:64, t_, :], v_bf[p][0:64, t_, :], True, True,
                                [d_ke[p][hf], d_v[p][hf]], [dSp])
                        self.mm(pS1[:, 0:128], k_end[p][64:128, t_, :], v_bf[p][64:128, t_, :], True, True,
                                [d_ke[p][hf], d_v[p][hf]], [dSp1])
                        self.mm(pO[:, cs], v_bf[p][:, t_, :], att[p][:, t_, :], True, False, [d_v[p][hf], d_att[p][hf]], [dO])
                        self.mm(pO[:, q * 128:q * 128 + 64], Sbf[cur][:], q_dec[p][:, t_ * 128:t_ * 128 + 64], False, False,
                                [d_Sbf[cur], d_qd[p][hf]], [dO])
                        c0 = t_ * 128 + 63
                        n1 = self._sbi % 4
                        self._sbi += 1
                        self.stt(Sbf[n1][:], S, eG[p][:, c0:c0 + 1], pS[:, 0:128], ALU.mult, ALU.add,
                                 [dS, d_eG[p][hf], dSp], [d_Sbf[n1]])
                        self.stt(S, S, eG[p][:, c0:c0 + 1], pS[:, 0:128], ALU.mult, ALU.add, [dS, d_eG[p][hf], dSp], [dS])
                        yield
                        self.mm(pO[:, q * 128 + 64:q * 128 + 128], Sbf[n1][:], q_dec[p][:, t_ * 128 + 64:t_ * 128 + 128],
                                False, True, [d_Sbf[n1], d_qd[p][hf]], [dO])
                        c1 = t_ * 128 + 127
                        cur = self._sbi % 4
                        self._sbi += 1
                        self.stt(Sbf[cur][:], S, eG[p][:, c1:c1 + 1], pS1[:, 0:128], ALU.mult, ALU.add,
                                 [dS, d_eG[p][hf], dSp1], [d_Sbf[cur]])
                        self.stt(S, S, eG[p][:, c1:c1 + 1], pS1[:, 0:128], ALU.mult, ALU.add, [dS, d_eG[p][hf], dSp1], [dS])
                        yield
                    r = hf
                    self.act(osq[r][:], pO[:], AF.Square, [dO], [d_osq[r]])
                    pSS, dSS = self.ps()
                    self.mm(pSS[:], self.ones_b, osq[r][:], True, True, [d_osq[r], self.d_const], [dSS])
                    self.act(sd[r][:], pSS[:], AF.Ln, [dSS], [d_sd[r]], scale=1.0 / 128, bias=EPS)
                    self.act(sd[r][:], sd[r][:], AF.Exp, [d_sd[r]], [d_sd[r]], scale=-0.5)
                    self.tt(tmp[r][:], pO[:], sd[r][:], ALU.mult, [dO, d_sd[r]], [d_tmp[r]])
                    self.stt(self.big[:, j, sl], tmp[r][:], self.hgn[:, j:j + 1], sgate[p][:, sl], ALU.mult, ALU.mult,
                             [d_tmp[r], self.d_c, d_sg[p][hf]], [self.d_big[j][hf]])
                    yield

            for _ in front(0):
                pass
            for j in range(8):
                b = back(j)
                f = front(j + 1) if j < 7 else None
                alive_b, alive_f = True, f is not None
                while alive_b or alive_f:
                    if alive_b:
                        try:
                            next(b)
                        except StopIteration:
                            alive_b = False
                    if alive_f:
                        try:
                            next(f)
                        except StopIteration:
                            alive_f = False
            self.end_scope()

    def ssd(self, blk):
        dr = self.dr
        w_in = dr['w_in']
        nT, dn = self.nT, self.d_n
        with contextlib.ExitStack() as st:
            sbs = lambda shape, dt, name: self.sb(shape, dt, name, st)
            sm = lambda name, dt=F32: sbs([128, 8, 32], dt, name)
            dtv, lndt, a_, acs, wst, wd_, dl, b2 = [sm(n) for n in ("dtv", "lndt", "a_", "acs", "wst", "wd_", "dl", "b2")]
            acs_hi = sm("acs_hi", BF16)
            acs_lo = sm("acs_lo", BF16)
            d_dt = Dep()
            xpT = sbs([128, 6, 3 + TB], BF16, "xpT"); d_xp = [[Dep() for _ in range(3)] for _ in range(6)]
            BT = sbs([128, TB], BF16, "BT"); d_BT = mkdeps(2)
            CT = sbs([128, TB], BF16, "CT"); d_CT = mkdeps(2)
            diag = sbs([128, 6, 4, 128], BF16, "diag"); d_dg = mkdeps(6)
            cbr = sbs([128, 640], BF16, "cbr"); d_cbr = Dep()
            self.memset(cbr[:], 0.0, [d_cbr])
            gss = sbs([128, 512], F32, "gss"); d_gss = Dep()
            R2 = 2
            xs_bf = [sbs([128, 512], BF16, "xs_bf") for _ in range(R2)]; d_xs = mkdeps(R2)
            xsD = [sbs([128, 512], BF16, "xsD") for _ in range(R2)]; d_xsD = mkdeps(R2)
            xsw = [sbs([128, 512], BF16, "xsw") for _ in range(R2)]; d_xsw = mkdeps(R2)
            B_tm = [sbs([128, 128], BF16, "B_tm") for _ in range(R2)]; d_Bt = mkdeps(R2)
            smz = [sbs([128, 512], BF16, "smz") for _ in range(R2)]; d_smz = mkdeps(R2)
            CBm = [sbs([128, 128], F32, "CBm") for _ in range(R2)]; d_CBm = mkdeps(R2)
            E = [sbs([128, 128], F32, "E") for _ in range(3)]; d_E = mkdeps(3)
            Mp = [sbs([128, 128], BF16, "Mp") for _ in range(16)]; d_Mp = mkdeps(16)
            t1_ = sbs([128, 512], F32, "t1"); t1 = [t1_] * R2; d_t1_ = Dep(); d_t1 = [d_t1_] * R2
            yv = [sbs([128, 512], F32, "yv") for _ in range(R2)]; d_yv = mkdeps(R2)
            ssq = [sbs([128, 1], F32, "ssq") for _ in range(R2)]; d_ssq = mkdeps(R2)
            yn = [sbs([128, 512], BF16, "yn") for _ in range(R2)]; d_yn = mkdeps(R2)

            (wdt,), wdd = self.wload([(w_in[:, OFF_DT:OFF_DT + 32], 8, 32)])
            pD, dD = self.ps()
            for t_ in range(8):
                for kc in range(8):
                    self.mm(pD[:, t_ * 32:(t_ + 1) * 32], nT[:, kc, t_ * 128:(t_ + 1) * 128], wdt[:, kc, :], kc == 0, kc == 7,
                            [wdd, dn[kc][t_ // 4]], [dD])
            v8 = lambda bank: bank[:, 0:256].rearrange("p (a b) -> p a b", a=8)
            bc8 = lambda ap: ap.unsqueeze(1).to_broadcast([128, 8, 32])
            dd = [d_dt]
            self.tt(dtv[:], v8(pD), bc8(self.dtb[:]), ALU.add, [dD, self.d_c], dd)
            self.act(dtv[:], dtv[:], AF.Exp, dd, dd)
            self.act(dtv[:], dtv[:], AF.Ln, dd, dd, bias=1.0)
            self.act(lndt[:], dtv[:], AF.Ln, dd, dd)
            self.tt(a_[:], dtv[:], bc8(self.aneg[:]), ALU.mult, dd + [self.d_const], dd)
            pAc, dAc = self.ps()
            pTo, dTo = self.ps()
            for t_ in range(8):
                self.mm(pAc[:, t_ * 32:(t_ + 1) * 32], self.triL128, a_[:, t_, :], True, True, dd + [self.d_c], [dAc])
                self.mm(pTo[:, t_ * 32:(t_ + 1) * 32], self.ones_f, a_[:, t_, :], True, True, dd + [self.d_c], [dTo])
            self.cp(acs[:], v8(pAc), [dAc], dd, eng=ACT)
            self.act(wst[:], v8(pAc), AF.Exp, [dAc], dd)
            self.act(dl[:], v8(pTo), AF.Exp, [dTo], dd)
            self.tt(wd_[:], v8(pTo), acs[:], ALU.subtract, [dTo] + dd, dd)
            self.act(wd_[:], wd_[:], AF.Exp, dd, dd)
            self.tt(wd_[:], wd_[:], dtv[:], ALU.mult, dd, dd)
            self.tt(b2[:], lndt[:], acs[:], ALU.subtract, dd, dd)
            self.cp(acs_hi[:], acs[:], dd, dd, eng=DVE)
            self.tt(acs_lo[:], acs[:], acs_hi[:], ALU.subtract, dd, dd)

            self._ei = 0
            gw = {}

            def gload_xb(g):
                gw[g] = (self.wload([(w_in[:, OFF_XBC + g * 512:OFF_XBC + (g + 1) * 512], 8, 512)]),
                         self.wload([(w_in[:, OFF_XBC + 2048 + g * 128:OFF_XBC + 2048 + (g + 1) * 128], 8, 128),
                                     (w_in[:, OFF_XBC + 2560 + g * 128:OFF_XBC + 2560 + (g + 1) * 128], 8, 128)]))

            gload_xb(0)
            for g in range(4):
                ((wx,), wxd), ((wB, wC), wbd) = gw[g]
                (wz,), wzd = self.wload([(w_in[:, OFF_MZ + g * 512:OFF_MZ + (g + 1) * 512], 8, 512)])
                chs = [4 * g, 4 * g + 1, 4 * g + 2, 4 * g + 3, 16 + g, 20 + g]
                self.P.dma(POOL, lambda e, g=g: e.dma_start(out=cbr[0:1, 0:512], in_=dr['cbrow'][0:1, g * 512:(g + 1) * 512]),
                           (), [d_cbr], semkey='cbr')
                self.P.dma(POOL, lambda e, g=g: e.dma_start(out=cbr[0:1, 512:640],
                                                           in_=dr['cbrow'][0:1, 2048 + g * 128:2048 + (g + 1) * 128]),
                           (), [d_cbr], semkey='cbr', accum=True)
                self.P.dma(SP, lambda e, g=g: e.dma_start(out=gss[:], in_=dr['ssm_norm'][g * 512:(g + 1) * 512].partition_broadcast(128)),
                           (), [d_gss], semkey='gss')
                for c6 in range(6):
                    ch = chs[c6]
                    self.cp(xpT[:, c6, 0:3], self.xtail[:, ch, :], [self.d_xtail[ch]], [d_xp[c6][0]], eng=DVE)
                    wsrc = wx[:, :, c6 * 128:(c6 + 1) * 128] if c6 < 4 else (wB if c6 == 4 else wC)
                    wdp = wxd if c6 < 4 else wbd
                    for tb in range(2):
                        sl = slice(tb * 512, (tb + 1) * 512)
                        px, dpx = self.ps()
                        for kc in range(8):
                            self.mm(px[:], wsrc[:, kc, :], nT[:, kc, sl], kc == 0, kc == 7, [wdp, dn[kc][tb]], [dpx])
                        self.cp(xpT[:, c6, 3 + tb * 512:3 + (tb + 1) * 512], px[:], [dpx], [d_xp[c6][1 + tb]])
                    self.cp(self.xtail[:, ch, :], xpT[:, c6, TB:TB + 3], [d_xp[c6][2]], [self.d_xtail[ch]], eng=DVE)
                    for tap in range(4):
                        self.ts(diag[:, c6, tap, :], self.ident_f, self.cw[:, ch, tap:tap + 1], ALU.mult,
                                [self.d_c], [d_dg[c6]])
                for (c6, dst, dd_, col) in ((4, BT, d_BT, 16 + g), (5, CT, d_CT, 20 + g)):
                    for tb in range(2):
                        sl = slice(tb * 512, (tb + 1) * 512)
                        pb, dpb = self.ps()
                        for tap in range(4):
                            self.mm(pb[:], diag[:, c6, tap, :], xpT[:, c6, tb * 512 + tap:tb * 512 + tap + 512], tap == 0,
                                    tap == 3, [d_dg[c6]] + d_xp[c6], [dpb])
                        self.act(dst[:, sl], pb[:], AF.Silu, [dpb, self.d_c], [dd_[tb]], bias=self.cbp[:, col:col + 1])
                v864 = lambda ap: ap.rearrange("p (a b) -> p a b", a=8)
                bch = lambda ap: ap.unsqueeze(2).to_broadcast([128, 8, 64])
                hs = slice(g * 8, (g + 1) * 8)
                Sg = self.S_ssd[:, g * 512:(g + 1) * 512]

                def front(t_, g=g, wz=wz, wzd=wzd, hs=hs):
                    hf = t_ // 4
                    tsl = slice(t_ * 128, (t_ + 1) * 128)
                    r = t_ % 2
                    pxs, dxs_ = self.ps()
                    for cc in range(4):
                        cs = slice(cc * 128, (cc + 1) * 128)
                        for tap in range(4):
                            self.mm(pxs[:, cs], xpT[:, cc, t_ * 128 + tap:t_ * 128 + tap + 128], diag[:, cc, tap, :], tap == 0,
                                    False, d_xp[cc] + [d_dg[cc]], [dxs_])
                        self.mm(pxs[:, cs], self.e0_b, cbr[:, cs], False, True, [self.d_const, d_cbr], [dxs_])
                    self.act(xs_bf[r][:], pxs[:], AF.Silu, [dxs_], [d_xs[r]])
                    pbt, dbt = self.ps()
                    for tap in range(4):
                        self.mm(pbt[:, 0:128], xpT[:, 4, t_ * 128 + tap:t_ * 128 + tap + 128], diag[:, 4, tap, :], tap == 0,
                                False, d_xp[4] + [d_dg[4]], [dbt])
                    self.mm(pbt[:, 0:128], self.e0_b, cbr[:, 512:640], False, True, [self.d_const, d_cbr], [dbt])
                    self.act(B_tm[r][:], pbt[:, 0:128], AF.Silu, [dbt], [d_Bt[r]])
                    pz, dz = self.ps()
                    for kc in range(8):
                        self.mm(pz[:], nT[:, kc, tsl], wz[:, kc, :], kc == 0, kc == 7, [wzd, dn[kc][hf]], [dz])
                    self.act(smz[r][:], pz[:], AF.Silu, [dz], [d_smz[r]])
                    self.tt(v864(xsD[r][:]), v864(xs_bf[r][:]), bch(self.dsk[:, hs]), ALU.mult, [d_xs[r], self.d_c],
                            [d_xsD[r]])
                    self.tt(v864(xsw[r][:]), v864(xs_bf[r][:]), bch(wd_[:, t_, hs]), ALU.mult, [d_xs[r], d_dt], [d_xsw[r]])
                    pcb, dcb = self.ps()
                    self.mm(pcb[:, 0:128], BT[:, tsl], CT[:, tsl], True, True, [d_BT[hf], d_CT[hf]], [dcb])
                    self.cp(CBm[r][:], pcb[:, 0:128], [dcb], [d_CBm[r]], eng=ACT)
                    for hq in range(2):
                        pab, dab = self.ps()
                        for q in range(4):
                            h = g * 8 + hq * 4 + q
                            cs = slice(q * 128, (q + 1) * 128)
                            self.mm(pab[:, cs], acs_hi[:, t_, h:h + 1].to_broadcast([128, 128]), self.ident_b, True, False,
                                    [d_dt, self.d_const], [dab])
                            self.mm(pab[:, cs], acs_lo[:, t_, h:h + 1].to_broadcast([128, 128]), self.ident_b, False, False,
                                    [d_dt, self.d_const], [dab])
                            self.mm(pab[:, cs], self.ident_b, self.negm_b, False, True, [self.d_const], [dab])
                        for q in range(4):
                            hh = hq * 4 + q
                            h = g * 8 + hh
                            cs = slice(q * 128, (q + 1) * 128)
                            e_ = self._ei % 3
                            self._ei += 1
                            m_ = r * 8 + hh
                            self.act(E[e_][:], pab[:, cs], AF.Exp, [dab, d_dt], [d_E[e_]], bias=b2[:, t_, h:h + 1])
                            self.tt(Mp[m_][:], E[e_][:], CBm[r][:], ALU.mult, [d_E[e_], d_CBm[r]], [d_Mp[m_]])

                def mid(t_, g=g, hs=hs, Sg=Sg):
                    hf = t_ // 4
                    tsl = slice(t_ * 128, (t_ + 1) * 128)
                    r = t_ % 2
                    py, dy = self.ps(long=True)
                    self.mm(py[:], self.ident_b, xsD[r][:], True, False, [self.d_const, d_xsD[r]], [dy])
                    for hh in range(8):
                        m_ = r * 8 + hh
                        self.mm(py[:, hh * 64:(hh + 1) * 64], Mp[m_][:], xs_bf[r][:, hh * 64:(hh + 1) * 64], False,
                                hh == 7, [d_Mp[m_], d_xs[r]], [dy])
                    pyb, dyb = self.ps()
                    self.mm(pyb[:], CT[:, tsl], self.S_ssd_bf[:, g * 512:(g + 1) * 512], True, True,
                            [d_CT[hf], self.d_S_ssd_bf[g]], [dyb])
                    pds, dds = self.ps()
                    self.mm(pds[:], B_tm[r][:], xsw[r][:], True, True, [d_Bt[r], d_xsw[r]], [dds])
                    self.tt(v864(t1[r][:]), v864(pyb[:]), bch(wst[:, t_, hs]), ALU.mult, [dyb, d_dt], [d_t1[r]])
                    self.tt(v864(Sg), v864(Sg), bch(dl[:, t_, hs]), ALU.mult, [self.d_S_ssd[g], d_dt], [self.d_S_ssd[g]])
                    self.tt(self.S_ssd_bf[:, g * 512:(g + 1) * 512], Sg, pds[:], ALU.add, [self.d_S_ssd[g], dds],
                            [self.d_S_ssd_bf[g]])
                    self.tt(Sg, Sg, pds[:], ALU.add, [self.d_S_ssd[g], dds], [self.d_S_ssd[g]])
                    self.tt(yv[r][:], py[:], t1[r][:], ALU.add, [dy, d_t1[r]], [d_yv[r]])
                    self.tt(yv[r][:], yv[r][:], smz[r][:], ALU.mult, [d_yv[r], d_smz[r]], [d_yv[r]])
                    self.act(t1[r][:], yv[r][:], AF.Square, [d_yv[r]], [d_t1[r], d_ssq[r]], accum=ssq[r][:])
                    self.act(ssq[r][:], ssq[r][:], AF.Ln, [d_ssq[r]], [d_ssq[r]], scale=1.0 / 512, bias=EPS)
                    self.act(ssq[r][:], ssq[r][:], AF.Exp, [d_ssq[r]], [d_ssq[r]], scale=-0.5)

                def tail_a(t_, g=g):
                    r = t_ % 2
                    self.stt(yn[r][:], yv[r][:], ssq[r][:, 0:1], gss[:], ALU.mult, ALU.mult, [d_yv[r], d_ssq[r], d_gss],
                             [d_yn[r]])

                def tail(t_, g=g):
                    hf = t_ // 4
                    tsl = slice(t_ * 128, (t_ + 1) * 128)
                    r = t_ % 2
                    pyt, dyt = self.ps()
                    pytb = pyt[:].bitcast(BF16)
                    for cc in range(4):
                        self.tr(pytb[:, cc * 128:(cc + 1) * 128], yn[r][:, cc * 128:(cc + 1) * 128], self.ident_b,
                                [d_yn[r], self.d_const], [dyt])
                    self.cp(self.big[:, 8 + 4 * g:12 + 4 * g, tsl], pytb[:, 0:512].rearrange("p (a b) -> p a b", a=4), [dyt],
                            [self.d_big[8 + 4 * g + cc][hf] for cc in range(4)], eng=ACT)

                if g + 1 < 4:
                    gload_xb(g + 1)
                front(0)
                for t_ in range(8):
                    if t_ > 0:
                        tail_a(t_ - 1)
                    if t_ < 7:
                        front(t_ + 1)
                    mid(t_)
                    if t_ > 0:
                        tail(t_ - 1)
                tail_a(7)
                tail(7)
            self.end_scope()

    def outproj(self, blk):
        dr = self.dr
        w_in = dr['w_in']
        nT, dn = self.nT, self.d_n
        with contextlib.ExitStack() as st:
            sbs = lambda shape, dt, name: self.sb(shape, dt, name, st)
            mix = sbs([128, 8, TB], BF16, "mix"); d_mix = [[Dep() for _ in range(2)] for _ in range(8)]
            sga = [sbs([128, 512], F32, "sga") for _ in range(2)]; d_sga = mkdeps(2)
            sgb = [sbs([128, 512], F32, "sgb") for _ in range(2)]; d_sgb = mkdeps(2)
            m1 = [sbs([128, 512], F32, "m1") for _ in range(2)]; d_m1 = mkdeps(2)
            it = 0
            for fc in range(8):
                cs = slice(fc * 128, (fc + 1) * 128)
                (wA3,), wad = self.wload([(dr['opA'][fc], 24, 128, 'r')])
                wA, wga, wgb = wA3[:, 0:8, :], wA3[:, 8:16, :], wA3[:, 16:24, :]
                (wBm,), wbd = self.wload([(dr['opB'][fc], 16, 128, 'r')])
                for tb in range(2):
                    sl = slice(tb * 512, (tb + 1) * 512)
                    r = it % 2
                    it += 1
                    pA, dA = self.ps()
                    for kc in range(8):
                        self.mm(pA[:], wA[:, kc, :], self.big[:, kc, sl], kc == 0, kc == 7, [wad, self.d_big[kc][tb]], [dA])
                    pB, dB = self.ps()
                    for kc in range(16):
                        self.mm(pB[:], wBm[:, kc, :], self.big[:, 8 + kc, sl], kc == 0, kc == 15,
                                [wbd, self.d_big[8 + kc][tb]], [dB])
                    pga, dga = self.ps()
                    for kc in range(8):
                        self.mm(pga[:], wga[:, kc, :], nT[:, kc, sl], kc == 0, kc == 7, [wad, dn[kc][tb]], [dga])
                    pgb, dgb = self.ps()
                    for kc in range(8):
                        self.mm(pgb[:], wgb[:, kc, :], nT[:, kc, sl], kc == 0, kc == 7, [wad, dn[kc][tb]], [dgb])
                    self.act(sga[r][:], pga[:], AF.Sigmoid, [dga], [d_sga[r]])
                    self.act(sgb[r][:], pgb[:], AF.Sigmoid, [dgb], [d_sgb[r]])
                    self.tt(m1[r][:], pA[:], sga[r][:], ALU.mult, [dA, d_sga[r]], [d_m1[r]])
                    self.tt(sgb[r][:], pB[:], sgb[r][:], ALU.mult, [dB, d_sgb[r]], [d_sgb[r]])
                    self.tt(mix[:, fc, sl], m1[r][:], sgb[r][:], ALU.add, [d_m1[r], d_sgb[r]], [d_mix[fc][tb]])
            for dcp in range(4):
                (wo,), wod = self.wload([(dr['w_outr'][dcp], 8, 256, 'r')])
                for j in range(2):
                    dc = dcp * 2 + j
                    for tb in range(2):
                        sl = slice(tb * 512, (tb + 1) * 512)
                        po, do = self.ps()
                        for kc in range(8):
                            self.mm(po[:], wo[:, kc, j * 128:(j + 1) * 128], mix[:, kc, sl], kc == 0, kc == 7,
                                    [wod, d_mix[kc][tb]], [do])
                        self.tt(self.hT[:, dc, sl], po[:], self.hT[:, dc, sl], ALU.add, [do, self.d_h[dc][tb]],
                                [self.d_h[dc][tb]])
            self.end_scope()

    def ple(self, blk):
        dr = self.dr
        nT, dn = self.nT, self.d_n
        with contextlib.ExitStack() as st:
            sbs = lambda shape, dt, name: self.sb(shape, dt, name, st)
            pst = [sbs([128, 256], F32, "pst") for _ in range(2)]; d_pst = mkdeps(2)
            pT = sbs([128, 2, TB], BF16, "pT"); d_pT = mkdeps(2)
            sg = [sbs([128, 512], F32, "sgp") for _ in range(2)]; d_sg = mkdeps(2)
            for t_ in range(8):
                r = t_ % 2
                src = dr['p'][blk * TB + t_ * 128:blk * TB + (t_ + 1) * 128, :]
                self.P.dma(SP, lambda e, r=r, src=src: e.dma_start(out=pst[r][:], in_=src), (), [d_pst[r]], semkey='pst%d' % r)
                bank, bd = self.ps()
                for kc in range(2):
                    self.tr(bank[:, kc * 128:(kc + 1) * 128], pst[r][:, kc * 128:(kc + 1) * 128], self.ident_f,
                            [d_pst[r], self.d_c], [bd])
                self.cp(pT[:, :, t_ * 128:(t_ + 1) * 128], bank[:, 0:256].rearrange("p (a b) -> p a b", a=2), [bd],
                        [d_pT[t_ // 4]])
            it = 0
            for fcp in range(4):
                (wg, wp), wd = self.wload([(dr['w_ple_gate'][:, fcp * 256:(fcp + 1) * 256], 8, 256),
                                           (dr['w_ple_proj'][:, fcp * 256:(fcp + 1) * 256], 2, 256)])
                for j in range(2):
                    fc = fcp * 2 + j
                    for tb in range(2):
                        sl = slice(tb * 512, (tb + 1) * 512)
                        r = it % 2
                        it += 1
                        pg, dg = self.ps()
                        for kc in range(8):
                            self.mm(pg[:], wg[:, kc, j * 128:(j + 1) * 128], nT[:, kc, sl], kc == 0, kc == 7,
                                    [wd, dn[kc][tb]], [dg])
                        pp, dp = self.ps()
                        for kc in range(2):
                            self.mm(pp[:], wp[:, kc, j * 128:(j + 1) * 128], pT[:, kc, sl], kc == 0, kc == 1,
                                    [wd, d_pT[tb]], [dp])
                        self.act(sg[r][:], pg[:], AF.Sigmoid, [dg], [d_sg[r]])
                        self.tt(sg[r][:], pp[:], sg[r][:], ALU.mult, [dp, d_sg[r]], [d_sg[r]])
                        self.tt(self.hT[:, fc, sl], self.hT[:, fc, sl], sg[r][:], ALU.add, [self.d_h[fc][tb], d_sg[r]],
                                [self.d_h[fc][tb]])
            self.end_scope()

    def store_out(self, blk, raw=False):
        dr = self.dr
        with contextlib.ExitStack() as st:
            os_ = [self.sb([128, 8, 128], F32, "ostage", st) for _ in range(2)]
            dos = mkdeps(2)
            for kc in range(8):
                s = os_[kc % 2]
                ds = dos[kc % 2]
                for half in range(2):
                    bank, bd = self.ps()
                    for q in range(4):
                        tt_ = half * 4 + q
                        self.tr(bank[:, q * 128:(q + 1) * 128], self.hT[:, kc, tt_ * 128:(tt_ + 1) * 128], self.ident_f,
                                [self.d_h[kc][half], self.d_c], [bd], sig=(q == 3))
                    self.cp(s[:, half * 4:(half + 1) * 4, :], bank[:].rearrange("p (a b) -> p a b", a=4), [bd], [ds])
                dst = dr['out'][blk * TB:(blk + 1) * TB, kc * 128:(kc + 1) * 128].rearrange("(t p) c -> p t c", p=128)
                self.P.dma(SP, lambda e, s=s, dst=dst: e.dma_start(out=dst, in_=s[:]), [ds], (),
                           semkey='os%d' % (kc % 2))
            self.end_scope()

    def run(self):
        dr = self.dr
        self.init()
        stages = ['load', 'norm', 'ffn1', 'hgrn', 'ssd', 'outproj', 'ffn2', 'ple', 'final']
        upto = len(stages) if self.stop_after is None else stages.index(self.stop_after) + 1
        act = stages[:upto]
        for blk in range(NBLK):
            self.load_x(blk)
            if 'norm' in act:
                self.norm(0)
            if 'ffn1' in act:
                self.ffn(dr['ffn1_w13'], dr['ffn1_w2'])
            if 'hgrn' in act:
                self.norm(1)
                self.hgrn(blk)
            if 'ssd' in act:
                self.ssd(blk)
            if 'outproj' in act:
                self.outproj(blk)
            if 'dbg' in dr:
                allb = [d for row in self.d_big for d in row]
                dst = dr['dbg'][blk * 128:(blk + 1) * 128, :]
                self.P.dma(SP, lambda e, dst=dst: e.dma_start(out=dst, in_=self.big[:].rearrange("p a b -> p (a b)")),
                           allb, (), semkey='dbg')
            if 'ffn2' in act:
                self.norm(2)
                self.ffn(dr['ffn2_w13'], dr['ffn2_w2'])
            if 'ple' in act:
                self.norm(3)
                self.ple(blk)
            if 'final' in act:
                self.norm(4, final=True)
            self.store_out(blk)


def _consts():
    c = np.zeros((128, 7, 128), np.float32)
    i = np.arange(128)
    same = (i[:, None] // 64) == (i[None, :] // 64)
    c[:, 0, :] = np.eye(128)
    c[:, 1, :] = ((i[:, None] <= i[None, :]) & same)
    c[:, 2, :] = ((i[:, None] > i[None, :]) & same)
    c[:, 3, :] = (i[:, None] <= i[None, :])
    c[:, 4, :] = 1.0
    c[0, 6, :] = 1.0
    c[:, 5, :] = np.where(i[:, None] > i[None, :], -1e30, 0.0)
    return c


_IN_SPECS = [
    ('x', [T, D]), ('p', [T, 256]), ('consts', [128, 7, 128]), ('gains', [128, 40]), ('hgn', [128, 8]),
    ('cw', [128, 96]), ('cbp', [128, 24]), ('cbrow', [1, 3072]), ('dt_bias', [32]), ('a_log', [32]),
    ('d_skip', [32]), ('ssm_norm', [2048]), ('hg_lb', [2, 1024]),
    ('ffn1_w13', [D, 2 * DFF]), ('ffn1_w2', [8, 128, DFF]), ('w_in', [D, DIN]),
    ('opA', [8, 128, 3072]), ('opB', [8, 128, 2048]), ('w_outr', [4, 128, 2048]),
    ('ffn2_w13', [D, 2 * DFF]), ('ffn2_w2', [8, 128, DFF]), ('w_ple_gate', [D, D]), ('w_ple_proj', [256, D]),
]


def build(stop_after=None):
    nc = bass.Bass("TRN2", target_bir_lowering=False)
    dram = {}
    for name, shape in _IN_SPECS:
        dram[name] = nc.dram_tensor(name, shape, F32, kind="ExternalInput").ap()
    dram['out'] = nc.dram_tensor("out", [T, D], F32, kind="ExternalOutput").ap()
    if stop_after in ('hgrn', 'ssd'):
        dram['dbg'] = nc.dram_tensor("dbg", [NBLK * 128, 24 * TB], BF16, kind="ExternalOutput").ap()
    with contextlib.ExitStack() as es:
        block = es.enter_context(nc.Block())
        kb = KB(nc, block, es, dram, stop_after=stop_after)
        kb.run()
        print("ops", len(kb.P.ops), "waits", kb.P.nwaits)
    return nc


def make_in_maps(inp):
    f = lambda a: np.ascontiguousarray(np.asarray(a, dtype=np.float32))
    gains = np.stack([f(inp['ffn1_norm'])[0], f(inp['mix_norm'])[0], f(inp['ffn2_norm'])[0],
                      f(inp['ple_norm'])[0], f(inp['final_norm'])], 0)
    gains = gains.reshape(5, 8, 128).transpose(2, 0, 1).reshape(128, 40)
    hgn = f(inp['hg_norm'])[0].reshape(8, 128).T
    cw = f(inp['conv_w'])[0].T.reshape(24, 128, 4).transpose(1, 0, 2).reshape(128, 96)
    cbp = f(inp['conv_b'])[0].reshape(24, 128).T
    def w2r(w):
        return f(w.reshape(NFC, 128, 8, 128).transpose(2, 1, 0, 3).reshape(8, 128, DFF))

    def colblk(w, c0, ncol, nk):
        return w[:, c0:c0 + ncol].reshape(nk, 128, ncol).transpose(1, 0, 2).reshape(128, nk * ncol)

    w_in_ = f(inp['w_in'])[0]
    whg = f(inp['w_hg_out'])[0]
    wssm = f(inp['w_ssm_out'])[0]
    wout = f(inp['w_out'])[0]
    opA = np.stack([np.concatenate([colblk(whg, fc * 128, 128, 8),
                                    colblk(w_in_, OFF_BRG + fc * 128, 128, 8),
                                    colblk(w_in_, OFF_BRG + 1024 + fc * 128, 128, 8)], axis=1) for fc in range(8)], 0)
    opB = np.stack([colblk(wssm, fc * 128, 128, 16) for fc in range(8)], 0)
    w_outr = np.stack([colblk(wout, dcp * 256, 256, 8) for dcp in range(4)], 0)
    shared = dict(
        consts=_consts(), gains=f(gains), hgn=f(hgn), cw=f(cw), cbp=f(cbp), cbrow=f(inp['conv_b']),
        dt_bias=f(inp['dt_bias'])[0], a_log=f(inp['a_log'])[0], d_skip=f(inp['d_skip'])[0],
        ssm_norm=f(inp['ssm_norm'])[0], hg_lb=f(inp['hg_lb']),
        ffn1_w13=f(inp['ffn1_w13'])[0], ffn1_w2=w2r(f(inp['ffn1_w2'])[0]), w_in=w_in_,
        opA=f(opA), opB=f(opB), w_outr=f(w_outr),
        ffn2_w13=f(inp['ffn2_w13'])[0], ffn2_w2=w2r(f(inp['ffn2_w2'])[0]), w_ple_gate=f(inp['w_ple_gate'])[0],
        w_ple_proj=f(inp['w_ple_proj'])[0],
    )
    x = f(inp['x'])
    p = f(inp['p'])[0]
    maps = []
    for b in range(8):
        m = dict(shared)
        m['x'] = x[b]
        m['p'] = p[b]
        maps.append(m)
    return maps


_NC_CACHE = {}


def kernel(**inputs):
    if 'nc' not in _NC_CACHE:
        _NC_CACHE['nc'] = build()
    nc = _NC_CACHE['nc']
    maps = make_in_maps(inputs)
    res = run_bass_kernel_spmd(nc, maps, core_ids=list(range(8)))
    out = np.stack([np.asarray(r['out'], dtype=np.float32) for r in res.results], 0)
    return out
```

```python
import contextlib
import numpy as np
import concourse.bass as bass
import concourse.mybir as mybir
from concourse.bass_utils import run_bass_kernel_spmd

F32 = mybir.dt.float32
BF16 = mybir.dt.bfloat16
AF = mybir.ActivationFunctionType
ALU = mybir.AluOpType

PE, ACT, DVE, POOL, SP = 'tensor', 'scalar', 'vector', 'gpsimd', 'sync'
ENGS = (PE, ACT, DVE, POOL, SP)
CENGS = (PE, ACT, DVE)

D = 1024
T = 2048
TB = 1024
NBLK = T // TB
DFF = 2816
NFC = DFF // 128
DIN = 11296
EPS = 1e-6
OFF_Q, OFF_F, OFF_V, OFF_G = 0, 1024, 2048, 3072
OFF_MZ = 4096
OFF_XBC = 6144
OFF_DT = 9216
OFF_BRG = 9248


class Dep:
    __slots__ = ('name', 'w', 'r')

    def __init__(self, name=''):
        self.name = name
        self.w = {}
        self.r = {}


def mkdeps(n):
    return [Dep() for _ in range(n)]


class Op:
    __slots__ = ('eng', 'fn', 'deps', 'idx', 'semkey', 'count', 'sig', 'is_dma')


class Prog:
    def __init__(self, nc, block):
        self.nc = nc
        self.block = block
        self.ops = []
        self.flushed = 0
        self.cnt = {e: 0 for e in ENGS}
        self.pending = {e: [] for e in ENGS}
        self.dma_counts = {}
        self.dsem = {}
        self.esem = {e: nc.alloc_semaphore('c_' + e) for e in (PE, ACT, DVE, POOL)}
        self.seen = {e: {} for e in ENGS}
        self.last = {e: None for e in ENGS}
        self.scope_dmas = []
        self.nwaits = 0

    def add(self, eng, fn, reads=(), writes=(), sig=True, is_dma=False, semkey=None, accum=False,
            extra_deps=()):
        op = Op()
        op.eng = eng
        op.fn = fn
        op.is_dma = is_dma
        op.idx = len(self.ops)
        op.semkey = semkey
        op.sig = sig
        op.count = None
        ds = set(extra_deps)
        for d in reads:
            ds.update(d.w.values())
        for d in writes:
            ds.update(d.r.values())
            if not accum:
                ds.update(d.w.values())
        op.deps = ds
        if is_dma:
            if semkey not in self.dsem:
                self.dsem[semkey] = self.nc.alloc_semaphore('d_' + str(semkey))
            c = self.dma_counts.get(semkey, 0) + 16
            self.dma_counts[semkey] = c
            op.count = c
            k = ('d', op.idx)
        else:
            k = eng
            if fn is not None:
                if sig:
                    self.cnt[eng] += 1
                    op.count = self.cnt[eng]
                    for p in self.pending[eng]:
                        p.count = op.count
                    self.pending[eng] = []
                else:
                    self.pending[eng].append(op)
                self.last[eng] = op
        for d in reads:
            d.r[k] = op
        for d in writes:
            if not accum:
                d.w = {}
            d.w[k] = op
            d.r = {}
        self.ops.append(op)
        return op

    def dma(self, eng, fn, reads=(), writes=(), semkey=None, accum=False):
        op = self.add(eng, fn, reads, writes, is_dma=True, semkey=semkey, accum=accum)
        if eng == SP:
            self.scope_dmas.append(op)
        return op

    def barrier(self):
        lasts = [self.last[e] for e in CENGS + (POOL,) if self.last[e] is not None]
        dm = list(self.scope_dmas)
        self.scope_dmas = []
        for e in CENGS + (SP,):
            self.add(e, None, extra_deps=[o for o in lasts if o.eng != e] + dm)

    def flush(self):
        ops = self.ops[self.flushed:]
        self.flushed = len(self.ops)
        for e in ENGS:
            assert not self.pending[e], "pending non-signalling ops on %s" % e
            ops_e = [op for op in ops if op.eng == e]
            if not ops_e:
                continue
            getattr(self.block, e)(lambda engh, e=e, ops_e=ops_e: self._run(e, engh, ops_e))

    def _run(self, e, engh, ops_e):
        seen = self.seen[e]
        for op in ops_e:
            need = {}
            for dj in op.deps:
                if dj.is_dma:
                    s = self.dsem[dj.semkey]
                else:
                    if dj.fn is None:
                        continue
                    if dj.eng == PE and e == PE and not op.is_dma and op.fn is not None:
                        continue
                    s = self.esem[dj.eng]
                v = dj.count
                assert v is not None
                key = id(s)
                if v > need.get(key, (None, 0))[1]:
                    need[key] = (s, v)
            for key, (s, v) in need.items():
                if v > seen.get(key, 0):
                    engh.wait_ge(s, v)
                    seen[key] = v
                    self.nwaits += 1
            if op.fn is None:
                continue
            inst = op.fn(engh)
            if op.is_dma:
                inst.then_inc(self.dsem[op.semkey], 16)
            elif op.sig:
                inst.then_inc(self.esem[e], 1)


class KB:
    def __init__(self, nc, block, es, dram, stop_after=None):
        self.nc = nc
        self.P = Prog(nc, block)
        self.es = es
        self.dr = dram
        self.stop_after = stop_after
        self.uid = 0
        self.banks = []
        self.bank_deps = []
        for i in range(8):
            self.banks.append(es.enter_context(nc.psum_tensor("bank%d" % i, [128, 512], F32)))
            self.bank_deps.append(Dep())
        self.bank_next = 0
        self.long_next = 0
        self.copy_rr = 0

    def sb(self, shape, dt, name=None, stack=None):
        self.uid += 1
        name = "%s_%d" % (name or "t", self.uid)
        return (stack or self.es).enter_context(self.nc.sbuf_tensor(name, shape, dt))

    def ps(self, long=False):
        if long:
            i = 6 + self.long_next
            self.long_next = (self.long_next + 1) % 2
        else:
            i = self.bank_next
            self.bank_next = (i + 1) % 6
        return self.banks[i], self.bank_deps[i]

    def mm(self, out, lhsT, rhs, start, stop, reads, writes):
        self.P.add(PE, lambda e: e.matmul(out, lhsT=lhsT, rhs=rhs, start=start, stop=stop),
                   reads, writes, sig=True)

    def tr(self, out, in_, ident, reads, writes, sig=True):
        self.P.add(PE, lambda e: e.transpose(out=out, in_=in_, identity=ident), reads, writes, sig=True)

    def act(self, out, in_, func, reads, writes, bias=None, scale=None, accum=None):
        kw = {}
        if bias is not None:
            kw['bias'] = bias
        if scale is not None:
            kw['scale'] = scale
        if accum is not None:
            kw['accum_out'] = accum
        self.P.add(ACT, lambda e: e.activation(out=out, in_=in_, func=func, **kw), reads, writes)

    def tt(self, out, in0, in1, op, reads, writes, eng=DVE):
        self.P.add(eng, lambda e: e.tensor_tensor(out=out, in0=in0, in1=in1, op=op), reads, writes)

    def stt(self, out, in0, scalar, in1, op0, op1, reads, writes):
        self.P.add(DVE, lambda e: e.scalar_tensor_tensor(out=out, in0=in0, scalar=scalar, in1=in1,
                                                         op0=op0, op1=op1), reads, writes)

    def ts(self, out, in0, s1, op0, reads, writes, s2=None, op1=None, eng=DVE):
        if op1 is None:
            self.P.add(eng, lambda e: e.tensor_scalar(out=out, in0=in0, scalar1=s1, scalar2=None, op0=op0),
                       reads, writes)
        else:
            self.P.add(eng, lambda e: e.tensor_scalar(out=out, in0=in0, scalar1=s1, scalar2=s2, op0=op0,
                                                      op1=op1), reads, writes)

    def cp(self, out, in_, reads, writes, eng=None):
        if eng is None:
            eng = (ACT, DVE)[self.copy_rr % 2]
            self.copy_rr += 1
        if eng == ACT:
            self.act(out, in_, AF.Copy, reads, writes)
        else:
            self.P.add(eng, lambda e: e.tensor_copy(out=out, in_=in_), reads, writes)

    def recip(self, out, in_, reads, writes):
        self.P.add(DVE, lambda e: e.reciprocal(out=out, in_=in_), reads, writes)

    def memset(self, ap, val, writes, eng=DVE):
        self.P.add(eng, lambda e: e.memset(ap, val), (), writes)

    def end_scope(self):
        self.P.barrier()
        self.P.flush()

    def init_wpool(self, nslots=3, nel=4096):
        self.wslots = [self.sb([128, nel], BF16, "wslot") for _ in range(nslots)]
        self.wdeps = mkdeps(nslots)
        self.wnext = 0
        self.wnel = nel

    def wload(self, parts):
        i = self.wnext
        self.wnext = (i + 1) % len(self.wslots)
        slot = self.wslots[i]
        dep = self.wdeps[i]
        views = []
        off = 0
        first = True
        for part in parts:
            if len(part) == 4:
                (src, nk, ncols, _) = part
                n = nk * ncols
                assert off + n <= self.wnel
                v = slot[:, off:off + n].rearrange("p (k c) -> p k c", k=nk)
                nch = 1
                while n // nch > 2048 or n % nch:
                    nch += 1
                vo = slot[:, off:off + n].rearrange("p (a b) -> p a b", a=nch)
                s = src.rearrange("p (a b) -> p a b", a=nch)
                self.P.dma(POOL, lambda e, vo=vo, s=s: e.dma_start(out=vo, in_=s), (), [dep],
                           semkey='w%d' % i, accum=not first)
            else:
                (src, nk, ncols) = part
                n = nk * ncols
                assert off + n <= self.wnel
                v = slot[:, off:off + n].rearrange("p (k c) -> p k c", k=nk)
                s = src.rearrange("(k p) c -> p k c", p=128)
                self.P.dma(POOL, lambda e, v=v, s=s: e.dma_start(out=v, in_=s), (), [dep],
                           semkey='w%d' % i, accum=not first)
            first = False
            views.append(v)
            off += n
        return views, dep

    def init(self):
        dr = self.dr
        P = self.P
        self.cf = self.sb([128, 7, 128], F32, "cf")
        self.cb16 = self.sb([128, 4, 128], BF16, "cb16")
        self.d_c = Dep()
        cdep = self.d_c
        P.dma(SP, lambda e: e.dma_start(out=self.cf[:], in_=dr['consts']), (), [cdep], semkey='c', accum=True)
        self.ident_f = self.cf[:, 0, :]
        self.triL64 = self.cf[:, 1, :]
        self.triU64 = self.cf[:, 2, :]
        self.triL128 = self.cf[:, 3, :]
        self.ones_f = self.cf[:, 4, :]
        self.gn = self.sb([128, 40], F32, "gn")
        P.dma(SP, lambda e: e.dma_start(out=self.gn[:], in_=dr['gains']), (), [cdep], semkey='c', accum=True)
        self.hgn = self.sb([128, 8], F32, "hgn")
        P.dma(SP, lambda e: e.dma_start(out=self.hgn[:], in_=dr['hgn']), (), [cdep], semkey='c', accum=True)
        self.cw = self.sb([128, 24, 4], F32, "cw")
        P.dma(SP, lambda e: e.dma_start(out=self.cw[:], in_=dr['cw'].rearrange("p (a b) -> p a b", a=24)),
              (), [cdep], semkey='c', accum=True)
        self.cbp = self.sb([128, 24], F32, "cbp")
        P.dma(SP, lambda e: e.dma_start(out=self.cbp[:], in_=dr['cbp']), (), [cdep], semkey='c', accum=True)
        self.dtb = self.sb([128, 32], F32, "dtb")
        self.aneg = self.sb([128, 32], F32, "aneg")
        self.dsk = self.sb([128, 32], F32, "dsk")
        P.dma(SP, lambda e: e.dma_start(out=self.dtb[:], in_=dr['dt_bias'].partition_broadcast(128)),
              (), [cdep], semkey='c', accum=True)
        P.dma(SP, lambda e: e.dma_start(out=self.aneg[:], in_=dr['a_log'].partition_broadcast(128)),
              (), [cdep], semkey='c', accum=True)
        P.dma(SP, lambda e: e.dma_start(out=self.dsk[:], in_=dr['d_skip'].partition_broadcast(128)),
              (), [cdep], semkey='c', accum=True)
        self.oml = self.sb([128, 1024], F32, "oml")
        self.S_hg = self.sb([128, 8, 128], F32, "S_hg")
        self.d_S_hg = mkdeps(8)
        self.S_ssd = self.sb([128, 2048], F32, "S_ssd")
        self.S_ssd_bf = self.sb([128, 2048], BF16, "S_ssd_bf")
        self.d_S_ssd = mkdeps(4)
        self.d_S_ssd_bf = mkdeps(4)
        self.xtail = self.sb([128, 24, 3], BF16, "xtail")
        self.d_xtail = mkdeps(24)
        self.hT = self.sb([128, 8, TB], F32, "hT")
        self.d_h = [[Dep() for _ in range(2)] for _ in range(8)]
        self.nT = self.sb([128, 8, TB], BF16, "nT")
        self.d_n = [[Dep() for _ in range(2)] for _ in range(8)]
        self.big = self.sb([128, 24, TB], BF16, "big")
        self.d_big = [[Dep() for _ in range(2)] for _ in range(24)]
        self.init_wpool()
        self.d_const = Dep()
        with contextlib.ExitStack() as st:
            lbt = self.sb([128, 2, 1024], F32, "lbt", st)
            P.dma(SP, lambda e: e.dma_start(out=lbt[:, 0, :], in_=dr['hg_lb'][0].partition_broadcast(128)),
                  (), [cdep], semkey='c', accum=True)
            P.dma(SP, lambda e: e.dma_start(out=lbt[:, 1, :], in_=dr['hg_lb'][1].partition_broadcast(128)),
                  (), [cdep], semkey='c', accum=True)
            dc = self.d_const
            self.tt(lbt[:, 0, :], lbt[:, 1, :], lbt[:, 0, :], ALU.subtract, [cdep], [dc])
            self.act(self.oml[:], lbt[:, 0, :], AF.Sigmoid, [dc], [dc])
            self.cp(self.cb16[:, 0, :], self.ident_f, [cdep], [dc], eng=DVE)
            self.cp(self.cb16[:, 1, :], self.ones_f, [cdep], [dc], eng=DVE)
            self.cp(self.cb16[:, 2, :], self.cf[:, 5, :], [cdep], [dc], eng=DVE)
            self.negm_b = self.cb16[:, 2, :]
            self.cp(self.cb16[:, 3, :], self.cf[:, 6, :], [cdep], [dc], eng=DVE)
            self.e0_b = self.cb16[:, 3, :]
            self.act(self.aneg[:], self.aneg[:], AF.Exp, [cdep], [dc])
            self.ts(self.aneg[:], self.aneg[:], -1.0, ALU.mult, [dc], [dc])
            self.memset(self.S_hg[:], 0.0, self.d_S_hg)
            self.memset(self.S_ssd[:], 0.0, self.d_S_ssd)
            self.memset(self.S_ssd_bf[:], 0.0, self.d_S_ssd_bf)
            self.memset(self.xtail[:], 0.0, self.d_xtail)
            if 'dbg' in dr:
                self.memset(self.big[:], 0.0, [d for row in self.d_big for d in row])
            self.ident_b = self.cb16[:, 0, :]
            self.ones_b = self.cb16[:, 1, :]
            self.end_scope()

    def load_x(self, blk):
        dr = self.dr
        with contextlib.ExitStack() as st:
            xs = [self.sb([128, 8, 128], F32, "xstage", st) for _ in range(2)]
            dxs = mkdeps(2)
            for kc in range(8):
                s = xs[kc % 2]
                ds = dxs[kc % 2]
                src = dr['x'][blk * TB:(blk + 1) * TB, kc * 128:(kc + 1) * 128].rearrange("(t p) c -> p t c", p=128)
                self.P.dma(SP, lambda e, s=s, src=src: e.dma_start(out=s[:], in_=src), (), [ds],
                           semkey='xs%d' % (kc % 2))
                for half in range(2):
                    bank, bd = self.ps()
                    for q in range(4):
                        tt_ = half * 4 + q
                        self.tr(bank[:, q * 128:(q + 1) * 128], s[:, tt_, :], self.ident_f, [ds, self.d_c], [bd],
                                sig=(q == 3))
                    self.cp(self.hT[:, kc, half * 512:(half + 1) * 512], bank[:], [bd], [self.d_h[kc][half]])
            self.end_scope()

    def norm(self, gi, final=False, stack=None):
        with contextlib.ExitStack() as st_own:
            st = stack if stack is not None else st_own
            sq = [self.sb([128, TB], BF16, "sq", st) for _ in range(2)]
            dsq = mkdeps(2)
            rstd = self.sb([128, TB], F32, "rstd", st)
            drs = mkdeps(2)
            b0, bd0 = self.ps()
            b1, bd1 = self.ps()
            bks = ((b0, bd0), (b1, bd1))
            for kc in range(8):
                s = sq[kc % 2]
                ds = dsq[kc % 2]
                if kc % 3 == 2:
                    self.tt(s[:], self.hT[:, kc, :], self.hT[:, kc, :], ALU.mult, self.d_h[kc], [ds])
                else:
                    self.act(s[:], self.hT[:, kc, :], AF.Square, self.d_h[kc], [ds])
                for tb in range(2):
                    self.mm(bks[tb][0][:], self.ones_b, s[:, tb * 512:(tb + 1) * 512], kc == 0, kc == 7,
                            [ds, self.d_const], [bks[tb][1]])
            for tb in range(2):
                sl = slice(tb * 512, (tb + 1) * 512)
                self.act(rstd[:, sl], bks[tb][0][:], AF.Ln, [bks[tb][1]], [drs[tb]], bias=EPS, scale=1.0 / D)
                self.act(rstd[:, sl], rstd[:, sl], AF.Exp, [drs[tb]], [drs[tb]], scale=-0.5)
            for tb in range(2):
                for kc in range(8):
                    sl = slice(tb * 512, (tb + 1) * 512)
                    if final:
                        self.stt(self.hT[:, kc, sl], self.hT[:, kc, sl], self.gn[:, gi * 8 + kc:gi * 8 + kc + 1],
                                 rstd[:, sl], ALU.mult, ALU.mult, [self.d_h[kc][tb], drs[tb], self.d_c],
                                 [self.d_h[kc][tb]])
                    else:
                        self.stt(self.nT[:, kc, sl], self.hT[:, kc, sl], self.gn[:, gi * 8 + kc:gi * 8 + kc + 1],
                                 rstd[:, sl], ALU.mult, ALU.mult, [self.d_h[kc][tb], drs[tb], self.d_c],
                                 [self.d_n[kc][tb]])
            if stack is None:
                self.end_scope()

    def ffn(self, w13, w2, gi):
        with contextlib.ExitStack() as st:
            self.norm(gi, stack=st)
            sg = [self.sb([128, 512], F32, "sg", st) for _ in range(2)]
            dsg = mkdeps(2)
            it = 0
            for g2 in range(NFC // 2):
                (wg, wu), wd = self.wload([(w13[:, g2 * 256:(g2 + 1) * 256], 8, 256),
                                           (w13[:, DFF + g2 * 256:DFF + (g2 + 1) * 256], 8, 256)])
                for j in range(2):
                    fc = g2 * 2 + j
                    for tb in range(2):
                        sl = slice(tb * 512, (tb + 1) * 512)
                        pg, dg = self.ps()
                        pu, du = self.ps()
                        for kc in range(8):
                            self.mm(pg[:], wg[:, kc, j * 128:(j + 1) * 128], self.nT[:, kc, sl], kc == 0, kc == 7,
                                    [wd, self.d_n[kc][tb]], [dg])
                        for kc in range(8):
                            self.mm(pu[:], wu[:, kc, j * 128:(j + 1) * 128], self.nT[:, kc, sl], kc == 0, kc == 7,
                                    [wd, self.d_n[kc][tb]], [du])
                        s = sg[it % 2]
                        ds = dsg[it % 2]
                        it += 1
                        self.act(s[:], pg[:], AF.Silu, [dg], [ds])
                        self.tt(self.big[:, fc, sl], s[:], pu[:], ALU.mult, [ds, du], [self.d_big[fc][tb]])
            for dc in range(8):
                (wa,), wda = self.wload([(w2[dc], NFC, 128, 'r')])
                for tb in range(2):
                    sl = slice(tb * 512, (tb + 1) * 512)
                    po, do = self.ps()
                    for fc in range(NFC):
                        self.mm(po[:], wa[:, fc, :], self.big[:, fc, sl], fc == 0, fc == NFC - 1,
                                [wda, self.d_big[fc][tb]], [do])
                    self.stt(self.hT[:, dc, sl], po[:], 0.5, self.hT[:, dc, sl], ALU.mult, ALU.add,
                             [do, self.d_h[dc][tb]], [self.d_h[dc][tb]])
            self.end_scope()

    def hgrn(self, blk):
        w_in = self.dr['w_in']
        nT, dn = self.nT, self.d_n
        with contextlib.ExitStack() as st:
            sbs = lambda shape, dt, name: self.sb(shape, dt, name, st)
            dd2 = lambda: [mkdeps(2) for _ in range(2)]
            k_tm = sbs([128, 8, 128], F32, "k_tm"); d_k = mkdeps(2)
            logf = sbs([128, 8, 128], F32, "logf"); d_lf = mkdeps(2)
            qs = sbs([128, TB], BF16, "qs"); d_qs = mkdeps(2)
            eGi = sbs([128, TB], BF16, "eGi"); d_eGi = mkdeps(2)
            eGe = sbs([128, 8, 128], BF16, "eGe"); d_eGe = mkdeps(2)
            k_inv = sbs([128, TB], BF16, "k_inv"); d_ki = mkdeps(2)
            v_bf = [sbs([128, 8, 128], BF16, "v_bf") for _ in range(2)]; d_v = dd2()
            sgate = [sbs([128, TB], BF16, "sgate") for _ in range(2)]; d_sg = dd2()
            eG = [sbs([128, TB], F32, "eG") for _ in range(2)]; d_eG = dd2()
            q_dec = [sbs([128, TB], BF16, "q_dec") for _ in range(2)]; d_qd = dd2()
            k_end = [sbs([128, 8, 128], BF16, "k_end") for _ in range(2)]; d_ke = dd2()
            att = [sbs([128, 8, 128], BF16, "att") for _ in range(2)]; d_att = dd2()
            osq = [sbs([128, 512], BF16, "osq") for _ in range(2)]; d_osq = mkdeps(2)
            sd = [sbs([128, 512], F32, "sd") for _ in range(2)]; d_sd = mkdeps(2)
            tmp = [sbs([128, 512], F32, "tmp") for _ in range(2)]; d_tmp = mkdeps(2)
            Sbf = [sbs([128, 128], BF16, "Sbf") for _ in range(4)]; d_Sbf = mkdeps(4)
            self._sbi = 0
            v4 = lambda bank: bank[:].rearrange("p (a b) -> p a b", a=4)

            hw = {}

            def hload(j):
                hw[j] = self.wload([(w_in[:, OFF_Q + j * 128:OFF_Q + (j + 1) * 128], 8, 128),
                                    (w_in[:, OFF_F + j * 128:OFF_F + (j + 1) * 128], 8, 128),
                                    (w_in[:, OFF_V + j * 128:OFF_V + (j + 1) * 128], 8, 128),
                                    (w_in[:, OFF_G + j * 128:OFF_G + (j + 1) * 128], 8, 128)])

            hload(0)

            def front(j):
                p = j % 2
                if j + 1 < 8:
                    hload(j + 1)
                (wq, wf, wv, wg), wd = hw[j]
                for tb in range(2):
                    sl = slice(tb * 512, (tb + 1) * 512)
                    pq, dq = self.ps()
                    for kc in range(8):
                        self.mm(pq[:], wq[:, kc, :], nT[:, kc, sl], kc == 0, kc == 7, [wd, dn[kc][tb]], [dq])
                    self.act(qs[:, sl], pq[:], AF.Copy, [dq], [d_qs[tb]], scale=float(128 ** -0.5))
                    yield
                    pg, dg = self.ps()
                    for kc in range(8):
                        self.mm(pg[:], wg[:, kc, :], nT[:, kc, sl], kc == 0, kc == 7, [wd, dn[kc][tb]], [dg])
                    self.act(sgate[p][:, sl], pg[:], AF.Silu, [dg], [d_sg[p][tb]])
                    yield
                for hf in range(2):
                    sl = slice(hf * 512, (hf + 1) * 512)
                    h4 = slice(hf * 4, hf * 4 + 4)
                    pf, df = self.ps()
                    for q in range(4):
                        t_ = hf * 4 + q
                        for kc in range(8):
                            self.mm(pf[:, q * 128:(q + 1) * 128], nT[:, kc, t_ * 128:(t_ + 1) * 128], wf[:, kc, :],
                                    kc == 0, kc == 7, [wd, dn[kc][hf]], [df])
                    self.act(k_tm[:, h4, :], v4(pf), AF.Exp, [df], [d_k[hf]])
                    self.act(k_tm[:, h4, :], k_tm[:, h4, :], AF.Ln, [d_k[hf]], [d_k[hf]], bias=1.0)
                    self.act(k_tm[:, h4, :], k_tm[:, h4, :], AF.Exp, [d_k[hf]], [d_k[hf]], scale=-1.0)
                    yield
                    pv, dv = self.ps()
                    for q in range(4):
                        t_ = hf * 4 + q
                        for kc in range(8):
                            self.mm(pv[:, q * 128:(q + 1) * 128], nT[:, kc, t_ * 128:(t_ + 1) * 128], wv[:, kc, :],
                                    kc == 0, kc == 7, [wd, dn[kc][hf]], [dv])
                    self.cp(v_bf[p][:, h4, :], v4(pv), [dv], [d_v[p][hf]])
                    yield
                    self.tt(k_tm[:, h4, :], k_tm[:, h4, :],
                            self.oml[:, j * 128:(j + 1) * 128].unsqueeze(1).to_broadcast([128, 4, 128]), ALU.mult,
                            [d_k[hf], self.d_const], [d_k[hf]])
                    self.act(logf[:, h4, :], k_tm[:, h4, :], AF.Ln, [d_k[hf]], [d_lf[hf]], scale=-1.0, bias=1.0)
                    pG, dG = self.ps()
                    pE, dE = self.ps()
                    pK, dK = self.ps()
                    for q in range(4):
                        t_ = hf * 4 + q
                        cs = slice(q * 128, (q + 1) * 128)
                        self.mm(pG[:, cs], logf[:, t_, :], self.triL64, True, True, [d_lf[hf], self.d_c], [dG])
                        self.mm(pE[:, cs], self.triU64, logf[:, t_, :], True, True, [d_lf[hf], self.d_c], [dE])
                        self.tr(pK[:, cs], k_tm[:, t_, :], self.ident_f, [d_k[hf], self.d_c], [dK])
                    self.act(eG[p][:, sl], pG[:], AF.Exp, [dG], [d_eG[p][hf]])
                    self.act(eGi[:, sl], pG[:], AF.Exp, [dG], [d_eGi[hf]], scale=-1.0)
                    self.act(eGe[:, h4, :], v4(pE), AF.Exp, [dE], [d_eGe[hf]])
                    self.tt(q_dec[p][:, sl], qs[:, sl], eG[p][:, sl], ALU.mult, [d_qs[hf], d_eG[p][hf]], [d_qd[p][hf]])
                    self.tt(k_inv[:, sl], pK[:], eGi[:, sl], ALU.mult, [dK, d_eGi[hf]], [d_ki[hf]])
                    self.tt(k_end[p][:, h4, :], k_tm[:, h4, :], eGe[:, h4, :], ALU.mult, [d_k[hf], d_eGe[hf]],
                            [d_ke[p][hf]])
                    yield
                    pA, dA = self.ps()
                    for q in range(4):
                        t_ = hf * 4 + q
                        cs = slice(q * 128, (q + 1) * 128)
                        ts_ = slice(t_ * 128, (t_ + 1) * 128)
                        self.mm(pA[:, cs], k_inv[:, ts_], q_dec[p][:, ts_], True, True, [d_ki[hf], d_qd[p][hf]], [dA])
                    self.tt(att[p][:, h4, :], v4(pA), self.triL64.unsqueeze(1).to_broadcast([128, 4, 128]), ALU.mult,
                            [dA, self.d_c], [d_att[p][hf]])
                    yield

            def back(j):
                p = j % 2
                dS = self.d_S_hg[j]
                S = self.S_hg[:, j, :]
                cur = self._sbi % 4
                self._sbi += 1
                self.cp(Sbf[cur][:], S, [dS], [d_Sbf[cur]], eng=ACT)
                for hf in range(2):
                    sl = slice(hf * 512, (hf + 1) * 512)
                    pO, dO = self.ps(long=True)
                    for q in range(4):
                        t_ = hf * 4 + q
                        cs = slice(q * 128, (q + 1) * 128)
                        pS, dSp = self.ps()
                        pS1, dSp1 = self.ps()
                        self.mm(pS[:, 0:128], k_end[p][0:64, t_, :], v_bf[p][0:64, t_, :], True, True,
                                [d_ke[p][hf], d_v[p][hf]], [dSp])
                        self.mm(pS1[:, 0:128], k_end[p][64:128, t_, :], v_bf[p][64:128, t_, :], True, True,
                                [d_ke[p][hf], d_v[p][hf]], [dSp1])
                        self.mm(pO[:, cs], v_bf[p][:, t_, :], att[p][:, t_, :], True, False, [d_v[p][hf], d_att[p][hf]], [dO])
                        self.mm(pO[:, q * 128:q * 128 + 64], Sbf[cur][:], q_dec[p][:, t_ * 128:t_ * 128 + 64], False, False,
                                [d_Sbf[cur], d_qd[p][hf]], [dO])
                        c0 = t_ * 128 + 63
                        n1 = self._sbi % 4
                        self._sbi += 1
                        self.stt(Sbf[n1][:], S, eG[p][:, c0:c0 + 1], pS[:, 0:128], ALU.mult, ALU.add,
                                 [dS, d_eG[p][hf], dSp], [d_Sbf[n1]])
                        self.stt(S, S, eG[p][:, c0:c0 + 1], pS[:, 0:128], ALU.mult, ALU.add, [dS, d_eG[p][hf], dSp], [dS])
                        yield
                        self.mm(pO[:, q * 128 + 64:q * 128 + 128], Sbf[n1][:], q_dec[p][:, t_ * 128 + 64:t_ * 128 + 128],
                                False, True, [d_Sbf[n1], d_qd[p][hf]], [dO])
                        c1 = t_ * 128 + 127
                        cur = self._sbi % 4
                        self._sbi += 1
                        self.stt(Sbf[cur][:], S, eG[p][:, c1:c1 + 1], pS1[:, 0:128], ALU.mult, ALU.add,
                                 [dS, d_eG[p][hf], dSp1], [d_Sbf[cur]])
                        self.stt(S, S, eG[p][:, c1:c1 + 1], pS1[:, 0:128], ALU.mult, ALU.add, [dS, d_eG[p][hf], dSp1], [dS])
                        yield
                    r = hf
                    self.act(osq[r][:], pO[:], AF.Square, [dO], [d_osq[r]])
                    pSS, dSS = self.ps()
                    self.mm(pSS[:], self.ones_b, osq[r][:], True, True, [d_osq[r], self.d_const], [dSS])
                    self.act(sd[r][:], pSS[:], AF.Ln, [dSS], [d_sd[r]], scale=1.0 / 128, bias=EPS)
                    self.act(sd[r][:], sd[r][:], AF.Exp, [d_sd[r]], [d_sd[r]], scale=-0.5)
                    self.tt(tmp[r][:], pO[:], sd[r][:], ALU.mult, [dO, d_sd[r]], [d_tmp[r]])
                    self.stt(self.big[:, j, sl], tmp[r][:], self.hgn[:, j:j + 1], sgate[p][:, sl], ALU.mult, ALU.mult,
                             [d_tmp[r], self.d_c, d_sg[p][hf]], [self.d_big[j][hf]])
                    yield

            for _ in front(0):
                pass
            for j in range(8):
                b = back(j)
                f = front(j + 1) if j < 7 else None
                alive_b, alive_f = True, f is not None
                while alive_b or alive_f:
                    if alive_b:
                        try:
                            next(b)
                        except StopIteration:
                            alive_b = False
                    if alive_f:
                        try:
                            next(f)
                        except StopIteration:
                            alive_f = False
            self.end_scope()

    def ssd(self, blk):
        dr = self.dr
        w_in = dr['w_in']
        nT, dn = self.nT, self.d_n
        with contextlib.ExitStack() as st:
            sbs = lambda shape, dt, name: self.sb(shape, dt, name, st)
            sm = lambda name, dt=F32: sbs([128, 8, 32], dt, name)
            dtv, lndt, a_, acs, wst, wd_, dl, b2 = [sm(n) for n in ("dtv", "lndt", "a_", "acs", "wst", "wd_", "dl", "b2")]
            acs_hi = sm("acs_hi", BF16)
            acs_lo = sm("acs_lo", BF16)
            d_dt = Dep()
            xpT = sbs([128, 6, 3 + TB], BF16, "xpT"); d_xp = [[Dep() for _ in range(3)] for _ in range(6)]
            BT = sbs([128, TB], BF16, "BT"); d_BT = mkdeps(2)
            CT = sbs([128, TB], BF16, "CT"); d_CT = mkdeps(2)
            diag = sbs([128, 6, 4, 128], BF16, "diag"); d_dg = mkdeps(6)
            cbr = sbs([128, 640], BF16, "cbr"); d_cbr = Dep()
            self.memset(cbr[:], 0.0, [d_cbr])
            gss = sbs([128, 512], F32, "gss"); d_gss = Dep()
            R2 = 2
            xs_bf = [sbs([128, 512], BF16, "xs_bf") for _ in range(R2)]; d_xs = mkdeps(R2)
            xsD = [sbs([128, 512], BF16, "xsD") for _ in range(R2)]; d_xsD = mkdeps(R2)
            xsw = [sbs([128, 512], BF16, "xsw") for _ in range(R2)]; d_xsw = mkdeps(R2)
            B_tm = [sbs([128, 128], BF16, "B_tm") for _ in range(R2)]; d_Bt = mkdeps(R2)
            smz = [sbs([128, 512], BF16, "smz") for _ in range(R2)]; d_smz = mkdeps(R2)
            CBm = [sbs([128, 128], F32, "CBm") for _ in range(R2)]; d_CBm = mkdeps(R2)
            E = [sbs([128, 128], F32, "E") for _ in range(3)]; d_E = mkdeps(3)
            Mp = [sbs([128, 128], BF16, "Mp") for _ in range(16)]; d_Mp = mkdeps(16)
            t1_ = sbs([128, 512], F32, "t1"); t1 = [t1_] * R2; d_t1_ = Dep(); d_t1 = [d_t1_] * R2
            yv = [sbs([128, 512], F32, "yv") for _ in range(R2)]; d_yv = mkdeps(R2)
            ssq = [sbs([128, 1], F32, "ssq") for _ in range(R2)]; d_ssq = mkdeps(R2)
            yn = [sbs([128, 512], BF16, "yn") for _ in range(R2)]; d_yn = mkdeps(R2)

            (wdt,), wdd = self.wload([(w_in[:, OFF_DT:OFF_DT + 32], 8, 32)])
            pD, dD = self.ps()
            for t_ in range(8):
                for kc in range(8):
                    self.mm(pD[:, t_ * 32:(t_ + 1) * 32], nT[:, kc, t_ * 128:(t_ + 1) * 128], wdt[:, kc, :], kc == 0, kc == 7,
                            [wdd, dn[kc][t_ // 4]], [dD])
            v8 = lambda bank: bank[:, 0:256].rearrange("p (a b) -> p a b", a=8)
            bc8 = lambda ap: ap.unsqueeze(1).to_broadcast([128, 8, 32])
            dd = [d_dt]
            self.tt(dtv[:], v8(pD), bc8(self.dtb[:]), ALU.add, [dD, self.d_c], dd)
            self.act(dtv[:], dtv[:], AF.Exp, dd, dd)
            self.act(dtv[:], dtv[:], AF.Ln, dd, dd, bias=1.0)
            self.act(lndt[:], dtv[:], AF.Ln, dd, dd)
            self.tt(a_[:], dtv[:], bc8(self.aneg[:]), ALU.mult, dd + [self.d_const], dd)
            pAc, dAc = self.ps()
            pTo, dTo = self.ps()
            for t_ in range(8):
                self.mm(pAc[:, t_ * 32:(t_ + 1) * 32], self.triL128, a_[:, t_, :], True, True, dd + [self.d_c], [dAc])
                self.mm(pTo[:, t_ * 32:(t_ + 1) * 32], self.ones_f, a_[:, t_, :], True, True, dd + [self.d_c], [dTo])
            self.cp(acs[:], v8(pAc), [dAc], dd, eng=ACT)
            self.act(wst[:], v8(pAc), AF.Exp, [dAc], dd)
            self.act(dl[:], v8(pTo), AF.Exp, [dTo], dd)
            self.tt(wd_[:], v8(pTo), acs[:], ALU.subtract, [dTo] + dd, dd)
            self.act(wd_[:], wd_[:], AF.Exp, dd, dd)
            self.tt(wd_[:], wd_[:], dtv[:], ALU.mult, dd, dd)
            self.tt(b2[:], lndt[:], acs[:], ALU.subtract, dd, dd)
            self.cp(acs_hi[:], acs[:], dd, dd, eng=DVE)
            self.tt(acs_lo[:], acs[:], acs_hi[:], ALU.subtract, dd, dd)

            self._ei = 0
            gw = {}

            def gload_xb(g):
                gw[g] = (self.wload([(w_in[:, OFF_XBC + g * 512:OFF_XBC + (g + 1) * 512], 8, 512)]),
                         self.wload([(w_in[:, OFF_XBC + 2048 + g * 128:OFF_XBC + 2048 + (g + 1) * 128], 8, 128),
                                     (w_in[:, OFF_XBC + 2560 + g * 128:OFF_XBC + 2560 + (g + 1) * 128], 8, 128)]))

            gload_xb(0)
            for g in range(4):
                ((wx,), wxd), ((wB, wC), wbd) = gw[g]
                (wz,), wzd = self.wload([(w_in[:, OFF_MZ + g * 512:OFF_MZ + (g + 1) * 512], 8, 512)])
                chs = [4 * g, 4 * g + 1, 4 * g + 2, 4 * g + 3, 16 + g, 20 + g]
                self.P.dma(POOL, lambda e, g=g: e.dma_start(out=cbr[0:1, 0:512], in_=dr['cbrow'][0:1, g * 512:(g + 1) * 512]),
                           (), [d_cbr], semkey='cbr')
                self.P.dma(POOL, lambda e, g=g: e.dma_start(out=cbr[0:1, 512:640],
                                                           in_=dr['cbrow'][0:1, 2048 + g * 128:2048 + (g + 1) * 128]),
                           (), [d_cbr], semkey='cbr', accum=True)
                self.P.dma(SP, lambda e, g=g: e.dma_start(out=gss[:], in_=dr['ssm_norm'][g * 512:(g + 1) * 512].partition_broadcast(128)),
                           (), [d_gss], semkey='gss')
                for c6 in range(6):
                    ch = chs[c6]
                    self.cp(xpT[:, c6, 0:3], self.xtail[:, ch, :], [self.d_xtail[ch]], [d_xp[c6][0]], eng=DVE)
                    wsrc = wx[:, :, c6 * 128:(c6 + 1) * 128] if c6 < 4 else (wB if c6 == 4 else wC)
                    wdp = wxd if c6 < 4 else wbd
                    for tb in range(2):
                        sl = slice(tb * 512, (tb + 1) * 512)
                        px, dpx = self.ps()
                        for kc in range(8):
                            self.mm(px[:], wsrc[:, kc, :], nT[:, kc, sl], kc == 0, kc == 7, [wdp, dn[kc][tb]], [dpx])
                        self.cp(xpT[:, c6, 3 + tb * 512:3 + (tb + 1) * 512], px[:], [dpx], [d_xp[c6][1 + tb]])
                    self.cp(self.xtail[:, ch, :], xpT[:, c6, TB:TB + 3], [d_xp[c6][2]], [self.d_xtail[ch]], eng=DVE)
                    for tap in range(4):
                        self.ts(diag[:, c6, tap, :], self.ident_f, self.cw[:, ch, tap:tap + 1], ALU.mult,
                                [self.d_c], [d_dg[c6]])
                for (c6, dst, dd_, col) in ((4, BT, d_BT, 16 + g), (5, CT, d_CT, 20 + g)):
                    for tb in range(2):
                        sl = slice(tb * 512, (tb + 1) * 512)
                        pb, dpb = self.ps()
                        for tap in range(4):
                            self.mm(pb[:], diag[:, c6, tap, :], xpT[:, c6, tb * 512 + tap:tb * 512 + tap + 512], tap == 0,
                                    tap == 3, [d_dg[c6]] + d_xp[c6], [dpb])
                        self.act(dst[:, sl], pb[:], AF.Silu, [dpb, self.d_c], [dd_[tb]], bias=self.cbp[:, col:col + 1])
                v864 = lambda ap: ap.rearrange("p (a b) -> p a b", a=8)
                bch = lambda ap: ap.unsqueeze(2).to_broadcast([128, 8, 64])
                hs = slice(g * 8, (g + 1) * 8)
                Sg = self.S_ssd[:, g * 512:(g + 1) * 512]

                def front(t_, g=g, wz=wz, wzd=wzd, hs=hs):
                    hf = t_ // 4
                    tsl = slice(t_ * 128, (t_ + 1) * 128)
                    r = t_ % 2
                    pxs, dxs_ = self.ps()
                    for cc in range(4):
                        cs = slice(cc * 128, (cc + 1) * 128)
                        for tap in range(4):
                            self.mm(pxs[:, cs], xpT[:, cc, t_ * 128 + tap:t_ * 128 + tap + 128], diag[:, cc, tap, :], tap == 0,
                                    False, d_xp[cc] + [d_dg[cc]], [dxs_])
                        self.mm(pxs[:, cs], self.e0_b, cbr[:, cs], False, True, [self.d_const, d_cbr], [dxs_])
                    self.act(xs_bf[r][:], pxs[:], AF.Silu, [dxs_], [d_xs[r]])
                    pbt, dbt = self.ps()
                    for tap in range(4):
                        self.mm(pbt[:, 0:128], xpT[:, 4, t_ * 128 + tap:t_ * 128 + tap + 128], diag[:, 4, tap, :], tap == 0,
                                False, d_xp[4] + [d_dg[4]], [dbt])
                    self.mm(pbt[:, 0:128], self.e0_b, cbr[:, 512:640], False, True, [self.d_const, d_cbr], [dbt])
                    self.act(B_tm[r][:], pbt[:, 0:128], AF.Silu, [dbt], [d_Bt[r]])
                    pz, dz = self.ps()
                    for kc in range(8):
                        self.mm(pz[:], nT[:, kc, tsl], wz[:, kc, :], kc == 0, kc == 7, [wzd, dn[kc][hf]], [dz])
                    self.act(smz[r][:], pz[:], AF.Silu, [dz], [d_smz[r]])
                    self.tt(v864(xsD[r][:]), v864(xs_bf[r][:]), bch(self.dsk[:, hs]), ALU.mult, [d_xs[r], self.d_c],
                            [d_xsD[r]])
                    self.tt(v864(xsw[r][:]), v864(xs_bf[r][:]), bch(wd_[:, t_, hs]), ALU.mult, [d_xs[r], d_dt], [d_xsw[r]])
                    pcb, dcb = self.ps()
                    self.mm(pcb[:, 0:128], BT[:, tsl], CT[:, tsl], True, True, [d_BT[hf], d_CT[hf]], [dcb])
                    self.cp(CBm[r][:], pcb[:, 0:128], [dcb], [d_CBm[r]], eng=ACT)
                    for hq in range(2):
                        pab, dab = self.ps()
                        for q in range(4):
                            h = g * 8 + hq * 4 + q
                            cs = slice(q * 128, (q + 1) * 128)
                            self.mm(pab[:, cs], acs_hi[:, t_, h:h + 1].to_broadcast([128, 128]), self.ident_b, True, False,
                                    [d_dt, self.d_const], [dab])
                            self.mm(pab[:, cs], acs_lo[:, t_, h:h + 1].to_broadcast([128, 128]), self.ident_b, False, False,
                                    [d_dt, self.d_const], [dab])
                            self.mm(pab[:, cs], self.ident_b, self.negm_b, False, True, [self.d_const], [dab])
                        for q in range(4):
                            hh = hq * 4 + q
                            h = g * 8 + hh
                            cs = slice(q * 128, (q + 1) * 128)
                            e_ = self._ei % 3
                            self._ei += 1
                            m_ = r * 8 + hh
                            self.act(E[e_][:], pab[:, cs], AF.Exp, [dab, d_dt], [d_E[e_]], bias=b2[:, t_, h:h + 1])
                            self.tt(Mp[m_][:], E[e_][:], CBm[r][:], ALU.mult, [d_E[e_], d_CBm[r]], [d_Mp[m_]])

                def mid(t_, g=g, hs=hs, Sg=Sg):
                    hf = t_ // 4
                    tsl = slice(t_ * 128, (t_ + 1) * 128)
                    r = t_ % 2
                    py, dy = self.ps(long=True)
                    self.mm(py[:], self.ident_b, xsD[r][:], True, False, [self.d_const, d_xsD[r]], [dy])
                    for hh in range(8):
                        m_ = r * 8 + hh
                        self.mm(py[:, hh * 64:(hh + 1) * 64], Mp[m_][:], xs_bf[r][:, hh * 64:(hh + 1) * 64], False,
                                hh == 7, [d_Mp[m_], d_xs[r]], [dy])
                    pyb, dyb = self.ps()
                    self.mm(pyb[:], CT[:, tsl], self.S_ssd_bf[:, g * 512:(g + 1) * 512], True, True,
                            [d_CT[hf], self.d_S_ssd_bf[g]], [dyb])
                    pds, dds = self.ps()
                    self.mm(pds[:], B_tm[r][:], xsw[r][:], True, True, [d_Bt[r], d_xsw[r]], [dds])
                    self.tt(v864(t1[r][:]), v864(pyb[:]), bch(wst[:, t_, hs]), ALU.mult, [dyb, d_dt], [d_t1[r]])
                    self.tt(v864(Sg), v864(Sg), bch(dl[:, t_, hs]), ALU.mult, [self.d_S_ssd[g], d_dt], [self.d_S_ssd[g]])
                    self.tt(self.S_ssd_bf[:, g * 512:(g + 1) * 512], Sg, pds[:], ALU.add, [self.d_S_ssd[g], dds],
                            [self.d_S_ssd_bf[g]])
                    self.tt(Sg, Sg, pds[:], ALU.add, [self.d_S_ssd[g], dds], [self.d_S_ssd[g]])
                    self.tt(yv[r][:], py[:], t1[r][:], ALU.add, [dy, d_t1[r]], [d_yv[r]])
                    self.tt(yv[r][:], yv[r][:], smz[r][:], ALU.mult, [d_yv[r], d_smz[r]], [d_yv[r]])
                    self.act(t1[r][:], yv[r][:], AF.Square, [d_yv[r]], [d_t1[r], d_ssq[r]], accum=ssq[r][:])
                    self.act(ssq[r][:], ssq[r][:], AF.Ln, [d_ssq[r]], [d_ssq[r]], scale=1.0 / 512, bias=EPS)
                    self.act(ssq[r][:], ssq[r][:], AF.Exp, [d_ssq[r]], [d_ssq[r]], scale=-0.5)

                def tail_a(t_, g=g):
                    r = t_ % 2
                    self.stt(yn[r][:], yv[r][:], ssq[r][:, 0:1], gss[:], ALU.mult, ALU.mult, [d_yv[r], d_ssq[r], d_gss],
                             [d_yn[r]])

                def tail(t_, g=g):
                    hf = t_ // 4
                    tsl = slice(t_ * 128, (t_ + 1) * 128)
                    r = t_ % 2
                    pyt, dyt = self.ps()
                    pytb = pyt[:].bitcast(BF16)
                    for cc in range(4):
                        self.tr(pytb[:, cc * 128:(cc + 1) * 128], yn[r][:, cc * 128:(cc + 1) * 128], self.ident_b,
                                [d_yn[r], self.d_const], [dyt])
                    self.cp(self.big[:, 8 + 4 * g:12 + 4 * g, tsl], pytb[:, 0:512].rearrange("p (a b) -> p a b", a=4), [dyt],
                            [self.d_big[8 + 4 * g + cc][hf] for cc in range(4)], eng=ACT)

                if g + 1 < 4:
                    gload_xb(g + 1)
                front(0)
                for t_ in range(8):
                    if t_ > 0:
                        tail_a(t_ - 1)
                    if t_ < 7:
                        front(t_ + 1)
                    mid(t_)
                    if t_ > 0:
                        tail(t_ - 1)
                tail_a(7)
                tail(7)
            self.end_scope()

    def outproj(self, blk):
        dr = self.dr
        w_in = dr['w_in']
        nT, dn = self.nT, self.d_n
        with contextlib.ExitStack() as st:
            sbs = lambda shape, dt, name: self.sb(shape, dt, name, st)
            mix = sbs([128, 8, TB], BF16, "mix"); d_mix = [[Dep() for _ in range(2)] for _ in range(8)]
            sga = [sbs([128, 512], F32, "sga") for _ in range(2)]; d_sga = mkdeps(2)
            sgb = [sbs([128, 512], F32, "sgb") for _ in range(2)]; d_sgb = mkdeps(2)
            m1 = [sbs([128, 512], F32, "m1") for _ in range(2)]; d_m1 = mkdeps(2)
            it = 0
            for fc in range(8):
                cs = slice(fc * 128, (fc + 1) * 128)
                (wA3,), wad = self.wload([(dr['opA'][fc], 24, 128, 'r')])
                wA, wga, wgb = wA3[:, 0:8, :], wA3[:, 8:16, :], wA3[:, 16:24, :]
                (wBm,), wbd = self.wload([(dr['opB'][fc], 16, 128, 'r')])
                for tb in range(2):
                    sl = slice(tb * 512, (tb + 1) * 512)
                    r = it % 2
                    it += 1
                    pA, dA = self.ps()
                    for kc in range(8):
                        self.mm(pA[:], wA[:, kc, :], self.big[:, kc, sl], kc == 0, kc == 7, [wad, self.d_big[kc][tb]], [dA])
                    pB, dB = self.ps()
                    for kc in range(16):
                        self.mm(pB[:], wBm[:, kc, :], self.big[:, 8 + kc, sl], kc == 0, kc == 15,
                                [wbd, self.d_big[8 + kc][tb]], [dB])
                    pga, dga = self.ps()
                    for kc in range(8):
                        self.mm(pga[:], wga[:, kc, :], nT[:, kc, sl], kc == 0, kc == 7, [wad, dn[kc][tb]], [dga])
                    pgb, dgb = self.ps()
                    for kc in range(8):
                        self.mm(pgb[:], wgb[:, kc, :], nT[:, kc, sl], kc == 0, kc == 7, [wad, dn[kc][tb]], [dgb])
                    self.act(sga[r][:], pga[:], AF.Sigmoid, [dga], [d_sga[r]])
                    self.act(sgb[r][:], pgb[:], AF.Sigmoid, [dgb], [d_sgb[r]])
                    self.tt(m1[r][:], pA[:], sga[r][:], ALU.mult, [dA, d_sga[r]], [d_m1[r]])
                    self.tt(sgb[r][:], pB[:], sgb[r][:], ALU.mult, [dB, d_sgb[r]], [d_sgb[r]])
                    self.tt(mix[:, fc, sl], m1[r][:], sgb[r][:], ALU.add, [d_m1[r], d_sgb[r]], [d_mix[fc][tb]])
            for dcp in range(4):
                (wo,), wod = self.wload([(dr['w_outr'][dcp], 8, 256, 'r')])
                for j in range(2):
                    dc = dcp * 2 + j
                    for tb in range(2):
                        sl = slice(tb * 512, (tb + 1) * 512)
                        po, do = self.ps()
                        for kc in range(8):
                            self.mm(po[:], wo[:, kc, j * 128:(j + 1) * 128], mix[:, kc, sl], kc == 0, kc == 7,
                                    [wod, d_mix[kc][tb]], [do])
                        self.tt(self.hT[:, dc, sl], po[:], self.hT[:, dc, sl], ALU.add, [do, self.d_h[dc][tb]],
                                [self.d_h[dc][tb]])
            self.end_scope()

    def ple(self, blk):
        dr = self.dr
        nT, dn = self.nT, self.d_n
        with contextlib.ExitStack() as st:
            sbs = lambda shape, dt, name: self.sb(shape, dt, name, st)
            pst = [sbs([128, 256], F32, "pst") for _ in range(2)]; d_pst = mkdeps(2)
            pT = sbs([128, 2, TB], BF16, "pT"); d_pT = mkdeps(2)
            sg = [sbs([128, 512], F32, "sgp") for _ in range(2)]; d_sg = mkdeps(2)
            for t_ in range(8):
                r = t_ % 2
                src = dr['p'][blk * TB + t_ * 128:blk * TB + (t_ + 1) * 128, :]
                self.P.dma(SP, lambda e, r=r, src=src: e.dma_start(out=pst[r][:], in_=src), (), [d_pst[r]], semkey='pst%d' % r)
                bank, bd = self.ps()
                for kc in range(2):
                    self.tr(bank[:, kc * 128:(kc + 1) * 128], pst[r][:, kc * 128:(kc + 1) * 128], self.ident_f,
                            [d_pst[r], self.d_c], [bd])
                self.cp(pT[:, :, t_ * 128:(t_ + 1) * 128], bank[:, 0:256].rearrange("p (a b) -> p a b", a=2), [bd],
                        [d_pT[t_ // 4]])
            it = 0
            for fcp in range(4):
                (wg, wp), wd = self.wload([(dr['w_ple_gate'][:, fcp * 256:(fcp + 1) * 256], 8, 256),
                                           (dr['w_ple_proj'][:, fcp * 256:(fcp + 1) * 256], 2, 256)])
                for j in range(2):
                    fc = fcp * 2 + j
                    for tb in range(2):
                        sl = slice(tb * 512, (tb + 1) * 512)
                        r = it % 2
                        it += 1
                        pg, dg = self.ps()
                        for kc in range(8):
                            self.mm(pg[:], wg[:, kc, j * 128:(j + 1) * 128], nT[:, kc, sl], kc == 0, kc == 7,
                                    [wd, dn[kc][tb]], [dg])
                        pp, dp = self.ps()
                        for kc in range(2):
                            self.mm(pp[:], wp[:, kc, j * 128:(j + 1) * 128], pT[:, kc, sl], kc == 0, kc == 1,
                                    [wd, d_pT[tb]], [dp])
                        self.act(sg[r][:], pg[:], AF.Sigmoid, [dg], [d_sg[r]])
                        self.tt(sg[r][:], pp[:], sg[r][:], ALU.mult, [dp, d_sg[r]], [d_sg[r]])
                        self.tt(self.hT[:, fc, sl], self.hT[:, fc, sl], sg[r][:], ALU.add, [self.d_h[fc][tb], d_sg[r]],
                                [self.d_h[fc][tb]])
            self.end_scope()

    def store_out(self, blk, raw=False):
        dr = self.dr
        with contextlib.ExitStack() as st:
            os_ = [self.sb([128, 8, 128], F32, "ostage", st) for _ in range(2)]
            dos = mkdeps(2)
            for kc in range(8):
                s = os_[kc % 2]
                ds = dos[kc % 2]
                for half in range(2):
                    bank, bd = self.ps()
                    for q in range(4):
                        tt_ = half * 4 + q
                        self.tr(bank[:, q * 128:(q + 1) * 128], self.hT[:, kc, tt_ * 128:(tt_ + 1) * 128], self.ident_f,
                                [self.d_h[kc][half], self.d_c], [bd], sig=(q == 3))
                    self.cp(s[:, half * 4:(half + 1) * 4, :], bank[:].rearrange("p (a b) -> p a b", a=4), [bd], [ds])
                dst = dr['out'][blk * TB:(blk + 1) * TB, kc * 128:(kc + 1) * 128].rearrange("(t p) c -> p t c", p=128)
                self.P.dma(SP, lambda e, s=s, dst=dst: e.dma_start(out=dst, in_=s[:]), [ds], (),
                           semkey='os%d' % (kc % 2))
            self.end_scope()

    def run(self):
        dr = self.dr
        self.init()
        stages = ['load', 'norm', 'ffn1', 'hgrn', 'ssd', 'outproj', 'ffn2', 'ple', 'final']
        upto = len(stages) if self.stop_after is None else stages.index(self.stop_after) + 1
        act = stages[:upto]
        for blk in range(NBLK):
            self.load_x(blk)
            if 'ffn1' in act:
                self.ffn(dr['ffn1_w13'], dr['ffn1_w2'], 0)
            if 'hgrn' in act:
                self.norm(1)
                self.hgrn(blk)
            if 'ssd' in act:
                self.ssd(blk)
            if 'outproj' in act:
                self.outproj(blk)
            if 'dbg' in dr:
                allb = [d for row in self.d_big for d in row]
                dst = dr['dbg'][blk * 128:(blk + 1) * 128, :]
                self.P.dma(SP, lambda e, dst=dst: e.dma_start(out=dst, in_=self.big[:].rearrange("p a b -> p (a b)")),
                           allb, (), semkey='dbg')
            if 'ffn2' in act:
                self.ffn(dr['ffn2_w13'], dr['ffn2_w2'], 2)
            if 'ple' in act:
                self.norm(3)
                self.ple(blk)
            if 'final' in act:
                self.norm(4, final=True)
            self.store_out(blk)


def _consts():
    c = np.zeros((128, 7, 128), np.float32)
    i = np.arange(128)
    same = (i[:, None] // 64) == (i[None, :] // 64)
    c[:, 0, :] = np.eye(128)
    c[:, 1, :] = ((i[:, None] <= i[None, :]) & same)
    c[:, 2, :] = ((i[:, None] > i[None, :]) & same)
    c[:, 3, :] = (i[:, None] <= i[None, :])
    c[:, 4, :] = 1.0
    c[0, 6, :] = 1.0
    c[:, 5, :] = np.where(i[:, None] > i[None, :], -1e30, 0.0)
    return c


_IN_SPECS = [
    ('x', [T, D]), ('p', [T, 256]), ('consts', [128, 7, 128]), ('gains', [128, 40]), ('hgn', [128, 8]),
    ('cw', [128, 96]), ('cbp', [128, 24]), ('cbrow', [1, 3072]), ('dt_bias', [32]), ('a_log', [32]),
    ('d_skip', [32]), ('ssm_norm', [2048]), ('hg_lb', [2, 1024]),
    ('ffn1_w13', [D, 2 * DFF]), ('ffn1_w2', [8, 128, DFF]), ('w_in', [D, DIN]),
    ('opA', [8, 128, 3072]), ('opB', [8, 128, 2048]), ('w_outr', [4, 128, 2048]),
    ('ffn2_w13', [D, 2 * DFF]), ('ffn2_w2', [8, 128, DFF]), ('w_ple_gate', [D, D]), ('w_ple_proj', [256, D]),
]


def build(stop_after=None):
    nc = bass.Bass("TRN2", target_bir_lowering=False)
    dram = {}
    for name, shape in _IN_SPECS:
        dram[name] = nc.dram_tensor(name, shape, F32, kind="ExternalInput").ap()
    dram['out'] = nc.dram_tensor("out", [T, D], F32, kind="ExternalOutput").ap()
    if stop_after in ('hgrn', 'ssd'):
        dram['dbg'] = nc.dram_tensor("dbg", [NBLK * 128, 24 * TB], BF16, kind="ExternalOutput").ap()
    with contextlib.ExitStack() as es:
        block = es.enter_context(nc.Block())
        kb = KB(nc, block, es, dram, stop_after=stop_after)
        kb.run()
        print("ops", len(kb.P.ops), "waits", kb.P.nwaits)
    return nc


def make_in_maps(inp):
    f = lambda a: np.ascontiguousarray(np.asarray(a, dtype=np.float32))
    gains = np.stack([f(inp['ffn1_norm'])[0], f(inp['mix_norm'])[0], f(inp['ffn2_norm'])[0],
                      f(inp['ple_norm'])[0], f(inp['final_norm'])], 0)
    gains = gains.reshape(5, 8, 128).transpose(2, 0, 1).reshape(128, 40)
    hgn = f(inp['hg_norm'])[0].reshape(8, 128).T
    cw = f(inp['conv_w'])[0].T.reshape(24, 128, 4).transpose(1, 0, 2).reshape(128, 96)
    cbp = f(inp['conv_b'])[0].reshape(24, 128).T
    def w2r(w):
        return f(w.reshape(NFC, 128, 8, 128).transpose(2, 1, 0, 3).reshape(8, 128, DFF))

    def colblk(w, c0, ncol, nk):
        return w[:, c0:c0 + ncol].reshape(nk, 128, ncol).transpose(1, 0, 2).reshape(128, nk * ncol)

    w_in_ = f(inp['w_in'])[0]
    whg = f(inp['w_hg_out'])[0]
    wssm = f(inp['w_ssm_out'])[0]
    wout = f(inp['w_out'])[0]
    opA = np.stack([np.concatenate([colblk(whg, fc * 128, 128, 8),
                                    colblk(w_in_, OFF_BRG + fc * 128, 128, 8),
                                    colblk(w_in_, OFF_BRG + 1024 + fc * 128, 128, 8)], axis=1) for fc in range(8)], 0)
    opB = np.stack([colblk(wssm, fc * 128, 128, 16) for fc in range(8)], 0)
    w_outr = np.stack([colblk(wout, dcp * 256, 256, 8) for dcp in range(4)], 0)
    shared = dict(
        consts=_consts(), gains=f(gains), hgn=f(hgn), cw=f(cw), cbp=f(cbp), cbrow=f(inp['conv_b']),
        dt_bias=f(inp['dt_bias'])[0], a_log=f(inp['a_log'])[0], d_skip=f(inp['d_skip'])[0],
        ssm_norm=f(inp['ssm_norm'])[0], hg_lb=f(inp['hg_lb']),
        ffn1_w13=f(inp['ffn1_w13'])[0], ffn1_w2=w2r(f(inp['ffn1_w2'])[0]), w_in=w_in_,
        opA=f(opA), opB=f(opB), w_outr=f(w_outr),
        ffn2_w13=f(inp['ffn2_w13'])[0], ffn2_w2=w2r(f(inp['ffn2_w2'])[0]), w_ple_gate=f(inp['w_ple_gate'])[0],
        w_ple_proj=f(inp['w_ple_proj'])[0],
    )
    x = f(inp['x'])
    p = f(inp['p'])[0]
    maps = []
    for b in range(8):
        m = dict(shared)
        m['x'] = x[b]
        m['p'] = p[b]
        maps.append(m)
    return maps


_NC_CACHE = {}


def kernel(**inputs):
    if 'nc' not in _NC_CACHE:
        _NC_CACHE['nc'] = build()
    nc = _NC_CACHE['nc']
    maps = make_in_maps(inputs)
    res = run_bass_kernel_spmd(nc, maps, core_ids=list(range(8)))
    out = np.stack([np.asarray(r['out'], dtype=np.float32) for r in res.results], 0)
    return out
```

```python
import contextlib
import numpy as np
import concourse.bass as bass
import concourse.mybir as mybir
from concourse.bass_utils import run_bass_kernel_spmd

F32 = mybir.dt.float32
BF16 = mybir.dt.bfloat16
AF = mybir.ActivationFunctionType
ALU = mybir.AluOpType

PE, ACT, DVE, POOL, SP = 'tensor', 'scalar', 'vector', 'gpsimd', 'sync'
ENGS = (PE, ACT, DVE, POOL, SP)
CENGS = (PE, ACT, DVE)

D = 1024
T = 2048
TB = 1024
NBLK = T // TB
DFF = 2816
NFC = DFF // 128
DIN = 11296
EPS = 1e-6
OFF_Q, OFF_F, OFF_V, OFF_G = 0, 1024, 2048, 3072
OFF_MZ = 4096
OFF_XBC = 6144
OFF_DT = 9216
OFF_BRG = 9248


class Dep:
    __slots__ = ('name', 'w', 'r')

    def __init__(self, name=''):
        self.name = name
        self.w = {}
        self.r = {}


def mkdeps(n):
    return [Dep() for _ in range(n)]


class Op:
    __slots__ = ('eng', 'fn', 'deps', 'idx', 'semkey', 'count', 'sig', 'is_dma')


class Prog:
    def __init__(self, nc, block):
        self.nc = nc
        self.block = block
        self.ops = []
        self.flushed = 0
        self.cnt = {e: 0 for e in ENGS}
        self.pending = {e: [] for e in ENGS}
        self.dma_counts = {}
        self.dsem = {}
        self.esem = {e: nc.alloc_semaphore('c_' + e) for e in (PE, ACT, DVE, POOL)}
        self.seen = {e: {} for e in ENGS}
        self.last = {e: None for e in ENGS}
        self.scope_dmas = []
        self.nwaits = 0

    def add(self, eng, fn, reads=(), writes=(), sig=True, is_dma=False, semkey=None, accum=False,
            extra_deps=()):
        op = Op()
        op.eng = eng
        op.fn = fn
        op.is_dma = is_dma
        op.idx = len(self.ops)
        op.semkey = semkey
        op.sig = sig
        op.count = None
        ds = set(extra_deps)
        for d in reads:
            ds.update(d.w.values())
        for d in writes:
            ds.update(d.r.values())
            if not accum:
                ds.update(d.w.values())
        op.deps = ds
        if is_dma:
            if semkey not in self.dsem:
                self.dsem[semkey] = self.nc.alloc_semaphore('d_' + str(semkey))
            c = self.dma_counts.get(semkey, 0) + 16
            self.dma_counts[semkey] = c
            op.count = c
            k = ('d', op.idx)
        else:
            k = eng
            if fn is not None:
                if sig:
                    self.cnt[eng] += 1
                    op.count = self.cnt[eng]
                    for p in self.pending[eng]:
                        p.count = op.count
                    self.pending[eng] = []
                else:
                    self.pending[eng].append(op)
                self.last[eng] = op
        for d in reads:
            d.r[k] = op
        for d in writes:
            if not accum:
                d.w = {}
            d.w[k] = op
            d.r = {}
        self.ops.append(op)
        return op

    def dma(self, eng, fn, reads=(), writes=(), semkey=None, accum=False):
        op = self.add(eng, fn, reads, writes, is_dma=True, semkey=semkey, accum=accum)
        if eng == SP:
            self.scope_dmas.append(op)
        return op

    def barrier(self):
        lasts = [self.last[e] for e in CENGS + (POOL,) if self.last[e] is not None]
        dm = list(self.scope_dmas)
        self.scope_dmas = []
        for e in CENGS + (SP,):
            self.add(e, None, extra_deps=[o for o in lasts if (o.eng != e or e != PE)] + dm)

    def flush(self):
        ops = self.ops[self.flushed:]
        self.flushed = len(self.ops)
        for e in ENGS:
            assert not self.pending[e], "pending non-signalling ops on %s" % e
            ops_e = [op for op in ops if op.eng == e]
            if not ops_e:
                continue
            getattr(self.block, e)(lambda engh, e=e, ops_e=ops_e: self._run(e, engh, ops_e))

    def _run(self, e, engh, ops_e):
        seen = self.seen[e]
        for op in ops_e:
            need = {}
            for dj in op.deps:
                if dj.is_dma:
                    s = self.dsem[dj.semkey]
                else:
                    if dj.fn is None:
                        continue
                    if dj.eng == PE and e == PE and not op.is_dma and op.fn is not None:
                        continue
                    s = self.esem[dj.eng]
                v = dj.count
                assert v is not None
                key = id(s)
                if v > need.get(key, (None, 0))[1]:
                    need[key] = (s, v)
            for key, (s, v) in need.items():
                if v > seen.get(key, 0):
                    engh.wait_ge(s, v)
                    seen[key] = v
                    self.nwaits += 1
            if op.fn is None:
                continue
            inst = op.fn(engh)
            if op.is_dma:
                inst.then_inc(self.dsem[op.semkey], 16)
            elif op.sig:
                inst.then_inc(self.esem[e], 1)


class KB:
    def __init__(self, nc, block, es, dram, stop_after=None):
        self.nc = nc
        self.P = Prog(nc, block)
        self.es = es
        self.dr = dram
        self.stop_after = stop_after
        self.uid = 0
        self.banks = []
        self.bank_deps = []
        for i in range(8):
            self.banks.append(es.enter_context(nc.psum_tensor("bank%d" % i, [128, 512], F32)))
            self.bank_deps.append(Dep())
        self.bank_next = 0
        self.long_next = 0
        self.copy_rr = 0

    def sb(self, shape, dt, name=None, stack=None):
        self.uid += 1
        name = "%s_%d" % (name or "t", self.uid)
        return (stack or self.es).enter_context(self.nc.sbuf_tensor(name, shape, dt))

    def ps(self, long=False):
        if long:
            i = 6 + self.long_next
            self.long_next = (self.long_next + 1) % 2
        else:
            i = self.bank_next
            self.bank_next = (i + 1) % 6
        return self.banks[i], self.bank_deps[i]

    def mm(self, out, lhsT, rhs, start, stop, reads, writes):
        self.P.add(PE, lambda e: e.matmul(out, lhsT=lhsT, rhs=rhs, start=start, stop=stop),
                   reads, writes, sig=True)

    def tr(self, out, in_, ident, reads, writes, sig=True):
        self.P.add(PE, lambda e: e.transpose(out=out, in_=in_, identity=ident), reads, writes, sig=True)

    def act(self, out, in_, func, reads, writes, bias=None, scale=None, accum=None):
        kw = {}
        if bias is not None:
            kw['bias'] = bias
        if scale is not None:
            kw['scale'] = scale
        if accum is not None:
            kw['accum_out'] = accum
        self.P.add(ACT, lambda e: e.activation(out=out, in_=in_, func=func, **kw), reads, writes)

    def tt(self, out, in0, in1, op, reads, writes, eng=DVE):
        self.P.add(eng, lambda e: e.tensor_tensor(out=out, in0=in0, in1=in1, op=op), reads, writes)

    def stt(self, out, in0, scalar, in1, op0, op1, reads, writes):
        self.P.add(DVE, lambda e: e.scalar_tensor_tensor(out=out, in0=in0, scalar=scalar, in1=in1,
                                                         op0=op0, op1=op1), reads, writes)

    def ts(self, out, in0, s1, op0, reads, writes, s2=None, op1=None, eng=DVE):
        if op1 is None:
            self.P.add(eng, lambda e: e.tensor_scalar(out=out, in0=in0, scalar1=s1, scalar2=None, op0=op0),
                       reads, writes)
        else:
            self.P.add(eng, lambda e: e.tensor_scalar(out=out, in0=in0, scalar1=s1, scalar2=s2, op0=op0,
                                                      op1=op1), reads, writes)

    def cp(self, out, in_, reads, writes, eng=None):
        if eng is None:
            eng = (ACT, DVE)[self.copy_rr % 2]
            self.copy_rr += 1
        if eng == ACT:
            self.act(out, in_, AF.Copy, reads, writes)
        else:
            self.P.add(eng, lambda e: e.tensor_copy(out=out, in_=in_), reads, writes)

    def recip(self, out, in_, reads, writes):
        self.P.add(DVE, lambda e: e.reciprocal(out=out, in_=in_), reads, writes)

    def memset(self, ap, val, writes, eng=DVE):
        self.P.add(eng, lambda e: e.memset(ap, val), (), writes)

    def end_scope(self):
        self.P.barrier()
        self.P.flush()

    def init_wpool(self, nslots=3, nel=4096):
        self.wslots = [self.sb([128, nel], BF16, "wslot") for _ in range(nslots)]
        self.wdeps = mkdeps(nslots)
        self.wnext = 0
        self.wnel = nel

    def wload(self, parts):
        i = self.wnext
        self.wnext = (i + 1) % len(self.wslots)
        slot = self.wslots[i]
        dep = self.wdeps[i]
        views = []
        off = 0
        first = True
        for part in parts:
            if len(part) == 4:
                (src, nk, ncols, _) = part
                n = nk * ncols
                assert off + n <= self.wnel
                v = slot[:, off:off + n].rearrange("p (k c) -> p k c", k=nk)
                nch = 1
                while n // nch > 2048 or n % nch:
                    nch += 1
                vo = slot[:, off:off + n].rearrange("p (a b) -> p a b", a=nch)
                s = src.rearrange("p (a b) -> p a b", a=nch)
                self.P.dma(POOL, lambda e, vo=vo, s=s: e.dma_start(out=vo, in_=s), (), [dep],
                           semkey='w%d' % i, accum=not first)
            else:
                (src, nk, ncols) = part
                n = nk * ncols
                assert off + n <= self.wnel
                v = slot[:, off:off + n].rearrange("p (k c) -> p k c", k=nk)
                s = src.rearrange("(k p) c -> p k c", p=128)
                self.P.dma(POOL, lambda e, v=v, s=s: e.dma_start(out=v, in_=s), (), [dep],
                           semkey='w%d' % i, accum=not first)
            first = False
            views.append(v)
            off += n
        return views, dep

    def init(self):
        dr = self.dr
        P = self.P
        self.cf = self.sb([128, 7, 128], F32, "cf")
        self.cb16 = self.sb([128, 4, 128], BF16, "cb16")
        self.d_c = Dep()
        cdep = self.d_c
        P.dma(SP, lambda e: e.dma_start(out=self.cf[:], in_=dr['consts']), (), [cdep], semkey='c', accum=True)
        self.ident_f = self.cf[:, 0, :]
        self.triL64 = self.cf[:, 1, :]
        self.triU64 = self.cf[:, 2, :]
        self.triL128 = self.cf[:, 3, :]
        self.ones_f = self.cf[:, 4, :]
        self.gn = self.sb([128, 40], F32, "gn")
        P.dma(SP, lambda e: e.dma_start(out=self.gn[:], in_=dr['gains']), (), [cdep], semkey='c', accum=True)
        self.hgn = self.sb([128, 8], F32, "hgn")
        P.dma(SP, lambda e: e.dma_start(out=self.hgn[:], in_=dr['hgn']), (), [cdep], semkey='c', accum=True)
        self.cw = self.sb([128, 24, 4], F32, "cw")
        P.dma(SP, lambda e: e.dma_start(out=self.cw[:], in_=dr['cw'].rearrange("p (a b) -> p a b", a=24)),
              (), [cdep], semkey='c', accum=True)
        self.cbp = self.sb([128, 24], F32, "cbp")
        P.dma(SP, lambda e: e.dma_start(out=self.cbp[:], in_=dr['cbp']), (), [cdep], semkey='c', accum=True)
        self.dtb = self.sb([128, 32], F32, "dtb")
        self.aneg = self.sb([128, 32], F32, "aneg")
        self.dsk = self.sb([128, 32], F32, "dsk")
        P.dma(SP, lambda e: e.dma_start(out=self.dtb[:], in_=dr['dt_bias'].partition_broadcast(128)),
              (), [cdep], semkey='c', accum=True)
        P.dma(SP, lambda e: e.dma_start(out=self.aneg[:], in_=dr['a_log'].partition_broadcast(128)),
              (), [cdep], semkey='c', accum=True)
        P.dma(SP, lambda e: e.dma_start(out=self.dsk[:], in_=dr['d_skip'].partition_broadcast(128)),
              (), [cdep], semkey='c', accum=True)
        self.oml = self.sb([128, 1024], F32, "oml")
        self.S_hg = self.sb([128, 8, 128], F32, "S_hg")
        self.d_S_hg = mkdeps(8)
        self.S_ssd = self.sb([128, 2048], F32, "S_ssd")
        self.S_ssd_bf = self.sb([128, 2048], BF16, "S_ssd_bf")
        self.d_S_ssd = mkdeps(4)
        self.d_S_ssd_bf = mkdeps(4)
        self.xtail = self.sb([128, 24, 3], BF16, "xtail")
        self.d_xtail = mkdeps(24)
        self.hT = self.sb([128, 8, TB], F32, "hT")
        self.d_h = [[Dep() for _ in range(2)] for _ in range(8)]
        self.nT = self.sb([128, 8, TB], BF16, "nT")
        self.d_n = [[Dep() for _ in range(2)] for _ in range(8)]
        self.big = self.sb([128, 24, TB], BF16, "big")
        self.d_big = [[Dep() for _ in range(2)] for _ in range(24)]
        self.init_wpool()
        self.d_const = Dep()
        with contextlib.ExitStack() as st:
            lbt = self.sb([128, 2, 1024], F32, "lbt", st)
            P.dma(SP, lambda e: e.dma_start(out=lbt[:, 0, :], in_=dr['hg_lb'][0].partition_broadcast(128)),
                  (), [cdep], semkey='c', accum=True)
            P.dma(SP, lambda e: e.dma_start(out=lbt[:, 1, :], in_=dr['hg_lb'][1].partition_broadcast(128)),
                  (), [cdep], semkey='c', accum=True)
            dc = self.d_const
            self.tt(lbt[:, 0, :], lbt[:, 1, :], lbt[:, 0, :], ALU.subtract, [cdep], [dc])
            self.act(self.oml[:], lbt[:, 0, :], AF.Sigmoid, [dc], [dc])
            self.cp(self.cb16[:, 0, :], self.ident_f, [cdep], [dc], eng=DVE)
            self.cp(self.cb16[:, 1, :], self.ones_f, [cdep], [dc], eng=DVE)
            self.cp(self.cb16[:, 2, :], self.cf[:, 5, :], [cdep], [dc], eng=DVE)
            self.negm_b = self.cb16[:, 2, :]
            self.cp(self.cb16[:, 3, :], self.cf[:, 6, :], [cdep], [dc], eng=DVE)
            self.e0_b = self.cb16[:, 3, :]
            self.act(self.aneg[:], self.aneg[:], AF.Exp, [cdep], [dc])
            self.ts(self.aneg[:], self.aneg[:], -1.0, ALU.mult, [dc], [dc])
            self.memset(self.S_hg[:], 0.0, self.d_S_hg)
            self.memset(self.S_ssd[:], 0.0, self.d_S_ssd)
            self.memset(self.S_ssd_bf[:], 0.0, self.d_S_ssd_bf)
            self.memset(self.xtail[:], 0.0, self.d_xtail)
            if 'dbg' in dr:
                self.memset(self.big[:], 0.0, [d for row in self.d_big for d in row])
            self.ident_b = self.cb16[:, 0, :]
            self.ones_b = self.cb16[:, 1, :]
            self.end_scope()

    def load_x(self, blk):
        dr = self.dr
        with contextlib.ExitStack() as st:
            xs = [self.sb([128, 8, 128], F32, "xstage", st) for _ in range(2)]
            dxs = mkdeps(2)
            for kc in range(8):
                s = xs[kc % 2]
                ds = dxs[kc % 2]
                src = dr['x'][blk * TB:(blk + 1) * TB, kc * 128:(kc + 1) * 128].rearrange("(t p) c -> p t c", p=128)
                self.P.dma(SP, lambda e, s=s, src=src: e.dma_start(out=s[:], in_=src), (), [ds],
                           semkey='xs%d' % (kc % 2))
                for half in range(2):
                    bank, bd = self.ps()
                    for q in range(4):
                        tt_ = half * 4 + q
                        self.tr(bank[:, q * 128:(q + 1) * 128], s[:, tt_, :], self.ident_f, [ds, self.d_c], [bd],
                                sig=(q == 3))
                    self.cp(self.hT[:, kc, half * 512:(half + 1) * 512], bank[:], [bd], [self.d_h[kc][half]])
            self.end_scope()

    def norm(self, gi, final=False, stack=None):
        with contextlib.ExitStack() as st_own:
            st = stack if stack is not None else st_own
            sq = [self.sb([128, TB], BF16, "sq", st) for _ in range(2)]
            dsq = mkdeps(2)
            rstd = self.sb([128, TB], F32, "rstd", st)
            drs = mkdeps(2)
            b0, bd0 = self.ps()
            b1, bd1 = self.ps()
            bks = ((b0, bd0), (b1, bd1))
            for kc in range(8):
                s = sq[kc % 2]
                ds = dsq[kc % 2]
                if kc % 3 == 2:
                    self.tt(s[:], self.hT[:, kc, :], self.hT[:, kc, :], ALU.mult, self.d_h[kc], [ds])
                else:
                    self.act(s[:], self.hT[:, kc, :], AF.Square, self.d_h[kc], [ds])
                for tb in range(2):
                    self.mm(bks[tb][0][:], self.ones_b, s[:, tb * 512:(tb + 1) * 512], kc == 0, kc == 7,
                            [ds, self.d_const], [bks[tb][1]])
            for tb in range(2):
                sl = slice(tb * 512, (tb + 1) * 512)
                self.act(rstd[:, sl], bks[tb][0][:], AF.Ln, [bks[tb][1]], [drs[tb]], bias=EPS, scale=1.0 / D)
                self.act(rstd[:, sl], rstd[:, sl], AF.Exp, [drs[tb]], [drs[tb]], scale=-0.5)
            for tb in range(2):
                for kc in range(8):
                    sl = slice(tb * 512, (tb + 1) * 512)
                    if final:
                        self.stt(self.hT[:, kc, sl], self.hT[:, kc, sl], self.gn[:, gi * 8 + kc:gi * 8 + kc + 1],
                                 rstd[:, sl], ALU.mult, ALU.mult, [self.d_h[kc][tb], drs[tb], self.d_c],
                                 [self.d_h[kc][tb]])
                    else:
                        self.stt(self.nT[:, kc, sl], self.hT[:, kc, sl], self.gn[:, gi * 8 + kc:gi * 8 + kc + 1],
                                 rstd[:, sl], ALU.mult, ALU.mult, [self.d_h[kc][tb], drs[tb], self.d_c],
                                 [self.d_n[kc][tb]])
            if stack is None:
                self.end_scope()

    def ffn(self, w13, w2, gi):
        with contextlib.ExitStack() as st:
            self.norm(gi, stack=st)
            sg = [self.sb([128, 512], F32, "sg", st) for _ in range(2)]
            dsg = mkdeps(2)
            it = 0
            for g2 in range(NFC // 2):
                (wg, wu), wd = self.wload([(w13[:, g2 * 256:(g2 + 1) * 256], 8, 256),
                                           (w13[:, DFF + g2 * 256:DFF + (g2 + 1) * 256], 8, 256)])
                for j in range(2):
                    fc = g2 * 2 + j
                    for tb in range(2):
                        sl = slice(tb * 512, (tb + 1) * 512)
                        pg, dg = self.ps()
                        pu, du = self.ps()
                        for kc in range(8):
                            self.mm(pg[:], wg[:, kc, j * 128:(j + 1) * 128], self.nT[:, kc, sl], kc == 0, kc == 7,
                                    [wd, self.d_n[kc][tb]], [dg])
                        for kc in range(8):
                            self.mm(pu[:], wu[:, kc, j * 128:(j + 1) * 128], self.nT[:, kc, sl], kc == 0, kc == 7,
                                    [wd, self.d_n[kc][tb]], [du])
                        s = sg[it % 2]
                        ds = dsg[it % 2]
                        it += 1
                        self.act(s[:], pg[:], AF.Silu, [dg], [ds])
                        self.tt(self.big[:, fc, sl], s[:], pu[:], ALU.mult, [ds, du], [self.d_big[fc][tb]])
            for dc in range(8):
                (wa,), wda = self.wload([(w2[dc], NFC, 128, 'r')])
                for tb in range(2):
                    sl = slice(tb * 512, (tb + 1) * 512)
                    po, do = self.ps()
                    for fc in range(NFC):
                        self.mm(po[:], wa[:, fc, :], self.big[:, fc, sl], fc == 0, fc == NFC - 1,
                                [wda, self.d_big[fc][tb]], [do])
                    self.stt(self.hT[:, dc, sl], po[:], 0.5, self.hT[:, dc, sl], ALU.mult, ALU.add,
                             [do, self.d_h[dc][tb]], [self.d_h[dc][tb]])
            self.end_scope()

    def hgrn(self, blk):
        w_in = self.dr['w_in']
        nT, dn = self.nT, self.d_n
        with contextlib.ExitStack() as st:
            self.norm(1, stack=st)
            sbs = lambda shape, dt, name: self.sb(shape, dt, name, st)
            dd2 = lambda: [mkdeps(2) for _ in range(2)]
            k_tm = sbs([128, 8, 128], F32, "k_tm"); d_k = mkdeps(2)
            logf = sbs([128, 8, 128], F32, "logf"); d_lf = mkdeps(2)
            qs = sbs([128, TB], BF16, "qs"); d_qs = mkdeps(2)
            eGi = sbs([128, TB], BF16, "eGi"); d_eGi = mkdeps(2)
            eGe = sbs([128, 8, 128], BF16, "eGe"); d_eGe = mkdeps(2)
            k_inv = sbs([128, TB], BF16, "k_inv"); d_ki = mkdeps(2)
            v_bf = [sbs([128, 8, 128], BF16, "v_bf") for _ in range(2)]; d_v = dd2()
            sgate = [sbs([128, TB], BF16, "sgate") for _ in range(2)]; d_sg = dd2()
            eG = [sbs([128, TB], F32, "eG") for _ in range(2)]; d_eG = dd2()
            q_dec = [sbs([128, TB], BF16, "q_dec") for _ in range(2)]; d_qd = dd2()
            k_end = [sbs([128, 8, 128], BF16, "k_end") for _ in range(2)]; d_ke = dd2()
            att = [sbs([128, 8, 128], BF16, "att") for _ in range(2)]; d_att = dd2()
            osq = [sbs([128, 512], BF16, "osq") for _ in range(2)]; d_osq = mkdeps(2)
            sd = [sbs([128, 512], F32, "sd") for _ in range(2)]; d_sd = mkdeps(2)
            tmp_ = sbs([128, 512], F32, "tmp"); tmp = [tmp_, tmp_]; d_tmp_ = Dep(); d_tmp = [d_tmp_, d_tmp_]
            Sbf = [sbs([128, 128], BF16, "Sbf") for _ in range(4)]; d_Sbf = mkdeps(4)
            self._sbi = 0
            v4 = lambda bank: bank[:].rearrange("p (a b) -> p a b", a=4)

            hw = {}

            def hload(j):
                hw[j] = self.wload([(w_in[:, OFF_Q + j * 128:OFF_Q + (j + 1) * 128], 8, 128),
                                    (w_in[:, OFF_F + j * 128:OFF_F + (j + 1) * 128], 8, 128),
                                    (w_in[:, OFF_V + j * 128:OFF_V + (j + 1) * 128], 8, 128),
                                    (w_in[:, OFF_G + j * 128:OFF_G + (j + 1) * 128], 8, 128)])

            hload(0)

            def front(j):
                p = j % 2
                if j + 1 < 8:
                    hload(j + 1)
                (wq, wf, wv, wg), wd = hw[j]
                for tb in range(2):
                    sl = slice(tb * 512, (tb + 1) * 512)
                    pq, dq = self.ps()
                    for kc in range(8):
                        self.mm(pq[:], wq[:, kc, :], nT[:, kc, sl], kc == 0, kc == 7, [wd, dn[kc][tb]], [dq])
                    self.act(qs[:, sl], pq[:], AF.Copy, [dq], [d_qs[tb]], scale=float(128 ** -0.5))
                    yield
                    pg, dg = self.ps()
                    for kc in range(8):
                        self.mm(pg[:], wg[:, kc, :], nT[:, kc, sl], kc == 0, kc == 7, [wd, dn[kc][tb]], [dg])
                    self.act(sgate[p][:, sl], pg[:], AF.Silu, [dg], [d_sg[p][tb]])
                    yield
                for hf in range(2):
                    sl = slice(hf * 512, (hf + 1) * 512)
                    h4 = slice(hf * 4, hf * 4 + 4)
                    pf, df = self.ps()
                    for q in range(4):
                        t_ = hf * 4 + q
                        for kc in range(8):
                            self.mm(pf[:, q * 128:(q + 1) * 128], nT[:, kc, t_ * 128:(t_ + 1) * 128], wf[:, kc, :],
                                    kc == 0, kc == 7, [wd, dn[kc][hf]], [df])
                    self.act(k_tm[:, h4, :], v4(pf), AF.Exp, [df], [d_k[hf]])
                    self.act(k_tm[:, h4, :], k_tm[:, h4, :], AF.Ln, [d_k[hf]], [d_k[hf]], bias=1.0)
                    self.act(k_tm[:, h4, :], k_tm[:, h4, :], AF.Exp, [d_k[hf]], [d_k[hf]], scale=-1.0)
                    yield
                    pv, dv = self.ps()
                    for q in range(4):
                        t_ = hf * 4 + q
                        for kc in range(8):
                            self.mm(pv[:, q * 128:(q + 1) * 128], nT[:, kc, t_ * 128:(t_ + 1) * 128], wv[:, kc, :],
                                    kc == 0, kc == 7, [wd, dn[kc][hf]], [dv])
                    self.cp(v_bf[p][:, h4, :], v4(pv), [dv], [d_v[p][hf]])
                    yield
                    self.tt(k_tm[:, h4, :], k_tm[:, h4, :],
                            self.oml[:, j * 128:(j + 1) * 128].unsqueeze(1).to_broadcast([128, 4, 128]), ALU.mult,
                            [d_k[hf], self.d_const], [d_k[hf]])
                    self.act(logf[:, h4, :], k_tm[:, h4, :], AF.Ln, [d_k[hf]], [d_lf[hf]], scale=-1.0, bias=1.0)
                    pG, dG = self.ps()
                    pE, dE = self.ps()
                    pK, dK = self.ps()
                    for q in range(4):
                        t_ = hf * 4 + q
                        cs = slice(q * 128, (q + 1) * 128)
                        self.mm(pG[:, cs], logf[:, t_, :], self.triL64, True, True, [d_lf[hf], self.d_c], [dG])
                        self.mm(pE[:, cs], self.triU64, logf[:, t_, :], True, True, [d_lf[hf], self.d_c], [dE])
                        self.tr(pK[:, cs], k_tm[:, t_, :], self.ident_f, [d_k[hf], self.d_c], [dK])
                    self.act(eG[p][:, sl], pG[:], AF.Exp, [dG], [d_eG[p][hf]])
                    self.act(eGi[:, sl], pG[:], AF.Exp, [dG], [d_eGi[hf]], scale=-1.0)
                    self.act(eGe[:, h4, :], v4(pE), AF.Exp, [dE], [d_eGe[hf]])
                    self.tt(q_dec[p][:, sl], qs[:, sl], eG[p][:, sl], ALU.mult, [d_qs[hf], d_eG[p][hf]], [d_qd[p][hf]])
                    self.tt(k_inv[:, sl], pK[:], eGi[:, sl], ALU.mult, [dK, d_eGi[hf]], [d_ki[hf]])
                    self.tt(k_end[p][:, h4, :], k_tm[:, h4, :], eGe[:, h4, :], ALU.mult, [d_k[hf], d_eGe[hf]],
                            [d_ke[p][hf]])
                    yield
                    pA, dA = self.ps()
                    for q in range(4):
                        t_ = hf * 4 + q
                        cs = slice(q * 128, (q + 1) * 128)
                        ts_ = slice(t_ * 128, (t_ + 1) * 128)
                        self.mm(pA[:, cs], k_inv[:, ts_], q_dec[p][:, ts_], True, True, [d_ki[hf], d_qd[p][hf]], [dA])
                    self.tt(att[p][:, h4, :], v4(pA), self.triL64.unsqueeze(1).to_broadcast([128, 4, 128]), ALU.mult,
                            [dA, self.d_c], [d_att[p][hf]])
                    yield

            def back(j):
                p = j % 2
                dS = self.d_S_hg[j]
                S = self.S_hg[:, j, :]
                cur = self._sbi % 4
                self._sbi += 1
                self.cp(Sbf[cur][:], S, [dS], [d_Sbf[cur]], eng=ACT)
                for hf in range(2):
                    sl = slice(hf * 512, (hf + 1) * 512)
                    pO, dO = self.ps(long=True)
                    for q in range(4):
                        t_ = hf * 4 + q
                        cs = slice(q * 128, (q + 1) * 128)
                        pS, dSp = self.ps()
                        pS1, dSp1 = self.ps()
                        self.mm(pS[:, 0:128], k_end[p][0:64, t_, :], v_bf[p][0:64, t_, :], True, True,
                                [d_ke[p][hf], d_v[p][hf]], [dSp])
                        self.mm(pS1[:, 0:128], k_end[p][64:128, t_, :], v_bf[p][64:128, t_, :], True, True,
                                [d_ke[p][hf], d_v[p][hf]], [dSp1])
                        self.mm(pO[:, cs], v_bf[p][:, t_, :], att[p][:, t_, :], True, False, [d_v[p][hf], d_att[p][hf]], [dO])
                        self.mm(pO[:, q * 128:q * 128 + 64], Sbf[cur][:], q_dec[p][:, t_ * 128:t_ * 128 + 64], False, False,
                                [d_Sbf[cur], d_qd[p][hf]], [dO])
                        c0 = t_ * 128 + 63
                        n1 = self._sbi % 4
                        self._sbi += 1
                        self.stt(Sbf[n1][:], S, eG[p][:, c0:c0 + 1], pS[:, 0:128], ALU.mult, ALU.add,
                                 [dS, d_eG[p][hf], dSp], [d_Sbf[n1]])
                        self.stt(S, S, eG[p][:, c0:c0 + 1], pS[:, 0:128], ALU.mult, ALU.add, [dS, d_eG[p][hf], dSp], [dS])
                        yield
                        self.mm(pO[:, q * 128 + 64:q * 128 + 128], Sbf[n1][:], q_dec[p][:, t_ * 128 + 64:t_ * 128 + 128],
                                False, True, [d_Sbf[n1], d_qd[p][hf]], [dO])
                        c1 = t_ * 128 + 127
                        cur = self._sbi % 4
                        self._sbi += 1
                        self.stt(Sbf[cur][:], S, eG[p][:, c1:c1 + 1], pS1[:, 0:128], ALU.mult, ALU.add,
                                 [dS, d_eG[p][hf], dSp1], [d_Sbf[cur]])
                        self.stt(S, S, eG[p][:, c1:c1 + 1], pS1[:, 0:128], ALU.mult, ALU.add, [dS, d_eG[p][hf], dSp1], [dS])
                        yield
                    r = hf
                    self.act(osq[r][:], pO[:], AF.Square, [dO], [d_osq[r]])
                    pSS, dSS = self.ps()
                    self.mm(pSS[:], self.ones_b, osq[r][:], True, True, [d_osq[r], self.d_const], [dSS])
                    self.act(sd[r][:], pSS[:], AF.Ln, [dSS], [d_sd[r]], scale=1.0 / 128, bias=EPS)
                    self.act(sd[r][:], sd[r][:], AF.Exp, [d_sd[r]], [d_sd[r]], scale=-0.5)
                    self.tt(tmp[r][:], pO[:], sd[r][:], ALU.mult, [dO, d_sd[r]], [d_tmp[r]])
                    self.stt(self.big[:, j, sl], tmp[r][:], self.hgn[:, j:j + 1], sgate[p][:, sl], ALU.mult, ALU.mult,
                             [d_tmp[r], self.d_c, d_sg[p][hf]], [self.d_big[j][hf]])
                    yield

            for _ in front(0):
                pass
            for j in range(8):
                b = back(j)
                f = front(j + 1) if j < 7 else None
                alive_b, alive_f = True, f is not None
                while alive_b or alive_f:
                    if alive_b:
                        try:
                            next(b)
                        except StopIteration:
                            alive_b = False
                    if alive_f:
                        try:
                            next(f)
                        except StopIteration:
                            alive_f = False
            self.end_scope()

    def ssd(self, blk):
        dr = self.dr
        w_in = dr['w_in']
        nT, dn = self.nT, self.d_n
        with contextlib.ExitStack() as st:
            sbs = lambda shape, dt, name: self.sb(shape, dt, name, st)
            sm = lambda name, dt=F32: sbs([128, 8, 32], dt, name)
            dtv, lndt, a_, acs, wst, wd_, dl, b2 = [sm(n) for n in ("dtv", "lndt", "a_", "acs", "wst", "wd_", "dl", "b2")]
            acs_hi = sm("acs_hi", BF16)
            acs_lo = sm("acs_lo", BF16)
            d_dt = Dep()
            xpT = sbs([128, 6, 3 + TB], BF16, "xpT"); d_xp = [[Dep() for _ in range(3)] for _ in range(6)]
            BT = sbs([128, TB], BF16, "BT"); d_BT = mkdeps(2)
            CT = sbs([128, TB], BF16, "CT"); d_CT = mkdeps(2)
            diag = sbs([128, 6, 4, 128], BF16, "diag"); d_dg = mkdeps(6)
            cbr = sbs([128, 640], BF16, "cbr"); d_cbr = Dep()
            self.memset(cbr[:], 0.0, [d_cbr])
            gss = sbs([128, 512], F32, "gss"); d_gss = Dep()
            R2 = 2
            xs_bf = [sbs([128, 512], BF16, "xs_bf") for _ in range(R2)]; d_xs = mkdeps(R2)
            xsD = [sbs([128, 512], BF16, "xsD") for _ in range(R2)]; d_xsD = mkdeps(R2)
            xsw = [sbs([128, 512], BF16, "xsw") for _ in range(R2)]; d_xsw = mkdeps(R2)
            B_tm = [sbs([128, 128], BF16, "B_tm") for _ in range(R2)]; d_Bt = mkdeps(R2)
            smz = [sbs([128, 512], BF16, "smz") for _ in range(R2)]; d_smz = mkdeps(R2)
            CBm = [sbs([128, 128], F32, "CBm") for _ in range(R2)]; d_CBm = mkdeps(R2)
            E = [sbs([128, 128], F32, "E") for _ in range(3)]; d_E = mkdeps(3)
            Mp = [sbs([128, 128], BF16, "Mp") for _ in range(16)]; d_Mp = mkdeps(16)
            t1_ = sbs([128, 512], F32, "t1"); t1 = [t1_] * R2; d_t1_ = Dep(); d_t1 = [d_t1_] * R2
            yv = [sbs([128, 512], F32, "yv") for _ in range(R2)]; d_yv = mkdeps(R2)
            ssq = [sbs([128, 1], F32, "ssq") for _ in range(R2)]; d_ssq = mkdeps(R2)
            yn = [sbs([128, 512], BF16, "yn") for _ in range(R2)]; d_yn = mkdeps(R2)

            (wdt,), wdd = self.wload([(w_in[:, OFF_DT:OFF_DT + 32], 8, 32)])
            pD, dD = self.ps()
            for t_ in range(8):
                for kc in range(8):
                    self.mm(pD[:, t_ * 32:(t_ + 1) * 32], nT[:, kc, t_ * 128:(t_ + 1) * 128], wdt[:, kc, :], kc == 0, kc == 7,
                            [wdd, dn[kc][t_ // 4]], [dD])
            v8 = lambda bank: bank[:, 0:256].rearrange("p (a b) -> p a b", a=8)
            bc8 = lambda ap: ap.unsqueeze(1).to_broadcast([128, 8, 32])
            dd = [d_dt]
            self.tt(dtv[:], v8(pD), bc8(self.dtb[:]), ALU.add, [dD, self.d_c], dd)
            self.act(dtv[:], dtv[:], AF.Exp, dd, dd)
            self.act(dtv[:], dtv[:], AF.Ln, dd, dd, bias=1.0)
            self.act(lndt[:], dtv[:], AF.Ln, dd, dd)
            self.tt(a_[:], dtv[:], bc8(self.aneg[:]), ALU.mult, dd + [self.d_const], dd)
            pAc, dAc = self.ps()
            pTo, dTo = self.ps()
            for t_ in range(8):
                self.mm(pAc[:, t_ * 32:(t_ + 1) * 32], self.triL128, a_[:, t_, :], True, True, dd + [self.d_c], [dAc])
                self.mm(pTo[:, t_ * 32:(t_ + 1) * 32], self.ones_f, a_[:, t_, :], True, True, dd + [self.d_c], [dTo])
            self.cp(acs[:], v8(pAc), [dAc], dd, eng=ACT)
            self.act(wst[:], v8(pAc), AF.Exp, [dAc], dd)
            self.act(dl[:], v8(pTo), AF.Exp, [dTo], dd)
            self.tt(wd_[:], v8(pTo), acs[:], ALU.subtract, [dTo] + dd, dd)
            self.act(wd_[:], wd_[:], AF.Exp, dd, dd)
            self.tt(wd_[:], wd_[:], dtv[:], ALU.mult, dd, dd)
            self.tt(b2[:], lndt[:], acs[:], ALU.subtract, dd, dd)
            self.cp(acs_hi[:], acs[:], dd, dd, eng=DVE)
            self.tt(acs_lo[:], acs[:], acs_hi[:], ALU.subtract, dd, dd)

            self._ei = 0
            gw = {}

            def gload_xb(g):
                gw[g] = (self.wload([(w_in[:, OFF_XBC + g * 512:OFF_XBC + (g + 1) * 512], 8, 512)]),
                         self.wload([(w_in[:, OFF_XBC + 2048 + g * 128:OFF_XBC + 2048 + (g + 1) * 128], 8, 128),
                                     (w_in[:, OFF_XBC + 2560 + g * 128:OFF_XBC + 2560 + (g + 1) * 128], 8, 128)]))

            gload_xb(0)
            for g in range(4):
                ((wx,), wxd), ((wB, wC), wbd) = gw[g]
                (wz,), wzd = self.wload([(w_in[:, OFF_MZ + g * 512:OFF_MZ + (g + 1) * 512], 8, 512)])
                chs = [4 * g, 4 * g + 1, 4 * g + 2, 4 * g + 3, 16 + g, 20 + g]
                self.P.dma(POOL, lambda e, g=g: e.dma_start(out=cbr[0:1, 0:512], in_=dr['cbrow'][0:1, g * 512:(g + 1) * 512]),
                           (), [d_cbr], semkey='cbr')
                self.P.dma(POOL, lambda e, g=g: e.dma_start(out=cbr[0:1, 512:640],
                                                           in_=dr['cbrow'][0:1, 2048 + g * 128:2048 + (g + 1) * 128]),
                           (), [d_cbr], semkey='cbr', accum=True)
                self.P.dma(SP, lambda e, g=g: e.dma_start(out=gss[:], in_=dr['ssm_norm'][g * 512:(g + 1) * 512].partition_broadcast(128)),
                           (), [d_gss], semkey='gss')
                for c6 in range(6):
                    ch = chs[c6]
                    self.cp(xpT[:, c6, 0:3], self.xtail[:, ch, :], [self.d_xtail[ch]], [d_xp[c6][0]], eng=DVE)
                    wsrc = wx[:, :, c6 * 128:(c6 + 1) * 128] if c6 < 4 else (wB if c6 == 4 else wC)
                    wdp = wxd if c6 < 4 else wbd
                    for tb in range(2):
                        sl = slice(tb * 512, (tb + 1) * 512)
                        px, dpx = self.ps()
                        for kc in range(8):
                            self.mm(px[:], wsrc[:, kc, :], nT[:, kc, sl], kc == 0, kc == 7, [wdp, dn[kc][tb]], [dpx])
                        self.cp(xpT[:, c6, 3 + tb * 512:3 + (tb + 1) * 512], px[:], [dpx], [d_xp[c6][1 + tb]])
                    self.cp(self.xtail[:, ch, :], xpT[:, c6, TB:TB + 3], [d_xp[c6][2]], [self.d_xtail[ch]], eng=DVE)
                    for tap in range(4):
                        self.ts(diag[:, c6, tap, :], self.ident_f, self.cw[:, ch, tap:tap + 1], ALU.mult,
                                [self.d_c], [d_dg[c6]])
                for (c6, dst, dd_, col) in ((4, BT, d_BT, 16 + g), (5, CT, d_CT, 20 + g)):
                    for tb in range(2):
                        sl = slice(tb * 512, (tb + 1) * 512)
                        pb, dpb = self.ps()
                        for tap in range(4):
                            self.mm(pb[:], diag[:, c6, tap, :], xpT[:, c6, tb * 512 + tap:tb * 512 + tap + 512], tap == 0,
                                    tap == 3, [d_dg[c6]] + d_xp[c6], [dpb])
                        self.act(dst[:, sl], pb[:], AF.Silu, [dpb, self.d_c], [dd_[tb]], bias=self.cbp[:, col:col + 1])
                v864 = lambda ap: ap.rearrange("p (a b) -> p a b", a=8)
                bch = lambda ap: ap.unsqueeze(2).to_broadcast([128, 8, 64])
                hs = slice(g * 8, (g + 1) * 8)
                Sg = self.S_ssd[:, g * 512:(g + 1) * 512]

                def front(t_, g=g, wz=wz, wzd=wzd, hs=hs):
                    hf = t_ // 4
                    tsl = slice(t_ * 128, (t_ + 1) * 128)
                    r = t_ % 2
                    pxs, dxs_ = self.ps()
                    for cc in range(4):
                        cs = slice(cc * 128, (cc + 1) * 128)
                        for tap in range(4):
                            self.mm(pxs[:, cs], xpT[:, cc, t_ * 128 + tap:t_ * 128 + tap + 128], diag[:, cc, tap, :], tap == 0,
                                    False, d_xp[cc] + [d_dg[cc]], [dxs_])
                        self.mm(pxs[:, cs], self.e0_b, cbr[:, cs], False, True, [self.d_const, d_cbr], [dxs_])
                    self.act(xs_bf[r][:], pxs[:], AF.Silu, [dxs_], [d_xs[r]])
                    pbt, dbt = self.ps()
                    for tap in range(4):
                        self.mm(pbt[:, 0:128], xpT[:, 4, t_ * 128 + tap:t_ * 128 + tap + 128], diag[:, 4, tap, :], tap == 0,
                                False, d_xp[4] + [d_dg[4]], [dbt])
                    self.mm(pbt[:, 0:128], self.e0_b, cbr[:, 512:640], False, True, [self.d_const, d_cbr], [dbt])
                    self.act(B_tm[r][:], pbt[:, 0:128], AF.Silu, [dbt], [d_Bt[r]])
                    pz, dz = self.ps()
                    for kc in range(8):
                        self.mm(pz[:], nT[:, kc, tsl], wz[:, kc, :], kc == 0, kc == 7, [wzd, dn[kc][hf]], [dz])
                    self.act(smz[r][:], pz[:], AF.Silu, [dz], [d_smz[r]])
                    self.tt(v864(xsD[r][:]), v864(xs_bf[r][:]), bch(self.dsk[:, hs]), ALU.mult, [d_xs[r], self.d_c],
                            [d_xsD[r]])
                    self.tt(v864(xsw[r][:]), v864(xs_bf[r][:]), bch(wd_[:, t_, hs]), ALU.mult, [d_xs[r], d_dt], [d_xsw[r]])
                    pcb, dcb = self.ps()
                    self.mm(pcb[:, 0:128], BT[:, tsl], CT[:, tsl], True, True, [d_BT[hf], d_CT[hf]], [dcb])
                    self.cp(CBm[r][:], pcb[:, 0:128], [dcb], [d_CBm[r]], eng=ACT)
                    for hq in range(2):
                        pab, dab = self.ps()
                        for q in range(4):
                            h = g * 8 + hq * 4 + q
                            cs = slice(q * 128, (q + 1) * 128)
                            self.mm(pab[:, cs], acs_hi[:, t_, h:h + 1].to_broadcast([128, 128]), self.ident_b, True, False,
                                    [d_dt, self.d_const], [dab])
                            self.mm(pab[:, cs], acs_lo[:, t_, h:h + 1].to_broadcast([128, 128]), self.ident_b, False, False,
                                    [d_dt, self.d_const], [dab])
                            self.mm(pab[:, cs], self.ident_b, self.negm_b, False, True, [self.d_const], [dab])
                        for q in range(4):
                            hh = hq * 4 + q
                            h = g * 8 + hh
                            cs = slice(q * 128, (q + 1) * 128)
                            e_ = self._ei % 3
                            self._ei += 1
                            m_ = r * 8 + hh
                            self.act(E[e_][:], pab[:, cs], AF.Exp, [dab, d_dt], [d_E[e_]], bias=b2[:, t_, h:h + 1])
                            self.tt(Mp[m_][:], E[e_][:], CBm[r][:], ALU.mult, [d_E[e_], d_CBm[r]], [d_Mp[m_]])

                def mid(t_, g=g, hs=hs, Sg=Sg):
                    hf = t_ // 4
                    tsl = slice(t_ * 128, (t_ + 1) * 128)
                    r = t_ % 2
                    py, dy = self.ps(long=True)
                    self.mm(py[:], self.ident_b, xsD[r][:], True, False, [self.d_const, d_xsD[r]], [dy])
                    for hh in range(8):
                        m_ = r * 8 + hh
                        self.mm(py[:, hh * 64:(hh + 1) * 64], Mp[m_][:], xs_bf[r][:, hh * 64:(hh + 1) * 64], False,
                                hh == 7, [d_Mp[m_], d_xs[r]], [dy])
                    pyb, dyb = self.ps()
                    self.mm(pyb[:], CT[:, tsl], self.S_ssd_bf[:, g * 512:(g + 1) * 512], True, True,
                            [d_CT[hf], self.d_S_ssd_bf[g]], [dyb])
                    pds, dds = self.ps()
                    self.mm(pds[:], B_tm[r][:], xsw[r][:], True, True, [d_Bt[r], d_xsw[r]], [dds])
                    self.tt(v864(t1[r][:]), v864(pyb[:]), bch(wst[:, t_, hs]), ALU.mult, [dyb, d_dt], [d_t1[r]])
                    self.tt(v864(Sg), v864(Sg), bch(dl[:, t_, hs]), ALU.mult, [self.d_S_ssd[g], d_dt], [self.d_S_ssd[g]])
                    self.tt(self.S_ssd_bf[:, g * 512:(g + 1) * 512], Sg, pds[:], ALU.add, [self.d_S_ssd[g], dds],
                            [self.d_S_ssd_bf[g]])
                    self.tt(Sg, Sg, pds[:], ALU.add, [self.d_S_ssd[g], dds], [self.d_S_ssd[g]])
                    self.tt(yv[r][:], py[:], t1[r][:], ALU.add, [dy, d_t1[r]], [d_yv[r]])
                    self.tt(yv[r][:], yv[r][:], smz[r][:], ALU.mult, [d_yv[r], d_smz[r]], [d_yv[r]])
                    self.act(t1[r][:], yv[r][:], AF.Square, [d_yv[r]], [d_t1[r], d_ssq[r]], accum=ssq[r][:])
                    self.act(ssq[r][:], ssq[r][:], AF.Ln, [d_ssq[r]], [d_ssq[r]], scale=1.0 / 512, bias=EPS)
                    self.act(ssq[r][:], ssq[r][:], AF.Exp, [d_ssq[r]], [d_ssq[r]], scale=-0.5)

                def tail_a(t_, g=g):
                    r = t_ % 2
                    self.stt(yn[r][:], yv[r][:], ssq[r][:, 0:1], gss[:], ALU.mult, ALU.mult, [d_yv[r], d_ssq[r], d_gss],
                             [d_yn[r]])

                def tail(t_, g=g):
                    hf = t_ // 4
                    tsl = slice(t_ * 128, (t_ + 1) * 128)
                    r = t_ % 2
                    pyt, dyt = self.ps()
                    pytb = pyt[:].bitcast(BF16)
                    for cc in range(4):
                        self.tr(pytb[:, cc * 128:(cc + 1) * 128], yn[r][:, cc * 128:(cc + 1) * 128], self.ident_b,
                                [d_yn[r], self.d_const], [dyt])
                    self.cp(self.big[:, 8 + 4 * g:12 + 4 * g, tsl], pytb[:, 0:512].rearrange("p (a b) -> p a b", a=4), [dyt],
                            [self.d_big[8 + 4 * g + cc][hf] for cc in range(4)], eng=ACT)

                if g + 1 < 4:
                    gload_xb(g + 1)
                front(0)
                for t_ in range(8):
                    if t_ > 0:
                        tail_a(t_ - 1)
                    if t_ < 7:
                        front(t_ + 1)
                    mid(t_)
                    if t_ > 0:
                        tail(t_ - 1)
                tail_a(7)
                tail(7)
            self.end_scope()

    def outproj(self, blk):
        dr = self.dr
        w_in = dr['w_in']
        nT, dn = self.nT, self.d_n
        with contextlib.ExitStack() as st:
            sbs = lambda shape, dt, name: self.sb(shape, dt, name, st)
            mix = sbs([128, 8, TB], BF16, "mix"); d_mix = [[Dep() for _ in range(2)] for _ in range(8)]
            sga = [sbs([128, 512], F32, "sga") for _ in range(2)]; d_sga = mkdeps(2)
            sgb = [sbs([128, 512], F32, "sgb") for _ in range(2)]; d_sgb = mkdeps(2)
            m1 = [sbs([128, 512], F32, "m1") for _ in range(2)]; d_m1 = mkdeps(2)
            it = 0
            for fc in range(8):
                cs = slice(fc * 128, (fc + 1) * 128)
                (wA3,), wad = self.wload([(dr['opA'][fc], 24, 128, 'r')])
                wA, wga, wgb = wA3[:, 0:8, :], wA3[:, 8:16, :], wA3[:, 16:24, :]
                (wBm,), wbd = self.wload([(dr['opB'][fc], 16, 128, 'r')])
                for tb in range(2):
                    sl = slice(tb * 512, (tb + 1) * 512)
                    r = it % 2
                    it += 1
                    pA, dA = self.ps()
                    for kc in range(8):
                        self.mm(pA[:], wA[:, kc, :], self.big[:, kc, sl], kc == 0, kc == 7, [wad, self.d_big[kc][tb]], [dA])
                    pB, dB = self.ps()
                    for kc in range(16):
                        self.mm(pB[:], wBm[:, kc, :], self.big[:, 8 + kc, sl], kc == 0, kc == 15,
                                [wbd, self.d_big[8 + kc][tb]], [dB])
                    pga, dga = self.ps()
                    for kc in range(8):
                        self.mm(pga[:], wga[:, kc, :], nT[:, kc, sl], kc == 0, kc == 7, [wad, dn[kc][tb]], [dga])
                    pgb, dgb = self.ps()
                    for kc in range(8):
                        self.mm(pgb[:], wgb[:, kc, :], nT[:, kc, sl], kc == 0, kc == 7, [wad, dn[kc][tb]], [dgb])
                    self.act(sga[r][:], pga[:], AF.Sigmoid, [dga], [d_sga[r]])
                    self.act(sgb[r][:], pgb[:], AF.Sigmoid, [dgb], [d_sgb[r]])
                    self.tt(m1[r][:], pA[:], sga[r][:], ALU.mult, [dA, d_sga[r]], [d_m1[r]])
                    self.tt(sgb[r][:], pB[:], sgb[r][:], ALU.mult, [dB, d_sgb[r]], [d_sgb[r]])
                    self.tt(mix[:, fc, sl], m1[r][:], sgb[r][:], ALU.add, [d_m1[r], d_sgb[r]], [d_mix[fc][tb]])
            for dcp in range(4):
                (wo,), wod = self.wload([(dr['w_outr'][dcp], 8, 256, 'r')])
                for j in range(2):
                    dc = dcp * 2 + j
                    for tb in range(2):
                        sl = slice(tb * 512, (tb + 1) * 512)
                        po, do = self.ps()
                        for kc in range(8):
                            self.mm(po[:], wo[:, kc, j * 128:(j + 1) * 128], mix[:, kc, sl], kc == 0, kc == 7,
                                    [wod, d_mix[kc][tb]], [do])
                        self.tt(self.hT[:, dc, sl], po[:], self.hT[:, dc, sl], ALU.add, [do, self.d_h[dc][tb]],
                                [self.d_h[dc][tb]])
            self.end_scope()

    def ple(self, blk):
        dr = self.dr
        nT, dn = self.nT, self.d_n
        with contextlib.ExitStack() as st:
            self.norm(3, stack=st)
            sbs = lambda shape, dt, name: self.sb(shape, dt, name, st)
            pst = [sbs([128, 256], F32, "pst") for _ in range(2)]; d_pst = mkdeps(2)
            pT = sbs([128, 2, TB], BF16, "pT"); d_pT = mkdeps(2)
            sg = [sbs([128, 512], F32, "sgp") for _ in range(2)]; d_sg = mkdeps(2)
            for t_ in range(8):
                r = t_ % 2
                src = dr['p'][blk * TB + t_ * 128:blk * TB + (t_ + 1) * 128, :]
                self.P.dma(SP, lambda e, r=r, src=src: e.dma_start(out=pst[r][:], in_=src), (), [d_pst[r]], semkey='pst%d' % r)
                bank, bd = self.ps()
                for kc in range(2):
                    self.tr(bank[:, kc * 128:(kc + 1) * 128], pst[r][:, kc * 128:(kc + 1) * 128], self.ident_f,
                            [d_pst[r], self.d_c], [bd])
                self.cp(pT[:, :, t_ * 128:(t_ + 1) * 128], bank[:, 0:256].rearrange("p (a b) -> p a b", a=2), [bd],
                        [d_pT[t_ // 4]])
            it = 0
            for fcp in range(4):
                (wg, wp), wd = self.wload([(dr['w_ple_gate'][:, fcp * 256:(fcp + 1) * 256], 8, 256),
                                           (dr['w_ple_proj'][:, fcp * 256:(fcp + 1) * 256], 2, 256)])
                for j in range(2):
                    fc = fcp * 2 + j
                    for tb in range(2):
                        sl = slice(tb * 512, (tb + 1) * 512)
                        r = it % 2
                        it += 1
                        pg, dg = self.ps()
                        for kc in range(8):
                            self.mm(pg[:], wg[:, kc, j * 128:(j + 1) * 128], nT[:, kc, sl], kc == 0, kc == 7,
                                    [wd, dn[kc][tb]], [dg])
                        pp, dp = self.ps()
                        for kc in range(2):
                            self.mm(pp[:], wp[:, kc, j * 128:(j + 1) * 128], pT[:, kc, sl], kc == 0, kc == 1,
                                    [wd, d_pT[tb]], [dp])
                        self.act(sg[r][:], pg[:], AF.Sigmoid, [dg], [d_sg[r]])
                        self.tt(sg[r][:], pp[:], sg[r][:], ALU.mult, [dp, d_sg[r]], [d_sg[r]])
                        self.tt(self.hT[:, fc, sl], self.hT[:, fc, sl], sg[r][:], ALU.add, [self.d_h[fc][tb], d_sg[r]],
                                [self.d_h[fc][tb]])
            self.end_scope()

    def store_out(self, blk, final=False):
        dr = self.dr
        with contextlib.ExitStack() as st:
            if final:
                self.norm(4, final=True, stack=st)
            os_ = [self.sb([128, 8, 128], F32, "ostage", st) for _ in range(2)]
            dos = mkdeps(2)
            for kc in range(8):
                s = os_[kc % 2]
                ds = dos[kc % 2]
                for half in range(2):
                    bank, bd = self.ps()
                    for q in range(4):
                        tt_ = half * 4 + q
                        self.tr(bank[:, q * 128:(q + 1) * 128], self.hT[:, kc, tt_ * 128:(tt_ + 1) * 128], self.ident_f,
                                [self.d_h[kc][half], self.d_c], [bd], sig=(q == 3))
                    self.cp(s[:, half * 4:(half + 1) * 4, :], bank[:].rearrange("p (a b) -> p a b", a=4), [bd], [ds])
                dst = dr['out'][blk * TB:(blk + 1) * TB, kc * 128:(kc + 1) * 128].rearrange("(t p) c -> p t c", p=128)
                self.P.dma(SP, lambda e, s=s, dst=dst: e.dma_start(out=dst, in_=s[:]), [ds], (),
                           semkey='os%d' % (kc % 2))
            self.end_scope()

    def run(self):
        dr = self.dr
        self.init()
        stages = ['load', 'norm', 'ffn1', 'hgrn', 'ssd', 'outproj', 'ffn2', 'ple', 'final']
        upto = len(stages) if self.stop_after is None else stages.index(self.stop_after) + 1
        act = stages[:upto]
        for blk in range(NBLK):
            self.load_x(blk)
            if 'ffn1' in act:
                self.ffn(dr['ffn1_w13'], dr['ffn1_w2'], 0)
            if 'hgrn' in act:
                self.hgrn(blk)
            if 'ssd' in act:
                self.ssd(blk)
            if 'outproj' in act:
                self.outproj(blk)
            if 'dbg' in dr:
                allb = [d for row in self.d_big for d in row]
                dst = dr['dbg'][blk * 128:(blk + 1) * 128, :]
                self.P.dma(SP, lambda e, dst=dst: e.dma_start(out=dst, in_=self.big[:].rearrange("p a b -> p (a b)")),
                           allb, (), semkey='dbg')
            if 'ffn2' in act:
                self.ffn(dr['ffn2_w13'], dr['ffn2_w2'], 2)
            if 'ple' in act:
                self.ple(blk)
            self.store_out(blk, final=('final' in act))


def _consts():
    c = np.zeros((128, 7, 128), np.float32)
    i = np.arange(128)
    same = (i[:, None] // 64) == (i[None, :] // 64)
    c[:, 0, :] = np.eye(128)
    c[:, 1, :] = ((i[:, None] <= i[None, :]) & same)
    c[:, 2, :] = ((i[:, None] > i[None, :]) & same)
    c[:, 3, :] = (i[:, None] <= i[None, :])
    c[:, 4, :] = 1.0
    c[0, 6, :] = 1.0
    c[:, 5, :] = np.where(i[:, None] > i[None, :], -1e30, 0.0)
    return c


_IN_SPECS = [
    ('x', [T, D]), ('p', [T, 256]), ('consts', [128, 7, 128]), ('gains', [128, 40]), ('hgn', [128, 8]),
    ('cw', [128, 96]), ('cbp', [128, 24]), ('cbrow', [1, 3072]), ('dt_bias', [32]), ('a_log', [32]),
    ('d_skip', [32]), ('ssm_norm', [2048]), ('hg_lb', [2, 1024]),
    ('ffn1_w13', [D, 2 * DFF]), ('ffn1_w2', [8, 128, DFF]), ('w_in', [D, DIN]),
    ('opA', [8, 128, 3072]), ('opB', [8, 128, 2048]), ('w_outr', [4, 128, 2048]),
    ('ffn2_w13', [D, 2 * DFF]), ('ffn2_w2', [8, 128, DFF]), ('w_ple_gate', [D, D]), ('w_ple_proj', [256, D]),
]


def build(stop_after=None):
    nc = bass.Bass("TRN2", target_bir_lowering=False)
    dram = {}
    for name, shape in _IN_SPECS:
        dram[name] = nc.dram_tensor(name, shape, F32, kind="ExternalInput").ap()
    dram['out'] = nc.dram_tensor("out", [T, D], F32, kind="ExternalOutput").ap()
    if stop_after in ('hgrn', 'ssd'):
        dram['dbg'] = nc.dram_tensor("dbg", [NBLK * 128, 24 * TB], BF16, kind="ExternalOutput").ap()
    with contextlib.ExitStack() as es:
        block = es.enter_context(nc.Block())
        kb = KB(nc, block, es, dram, stop_after=stop_after)
        kb.run()
        print("ops", len(kb.P.ops), "waits", kb.P.nwaits)
    return nc


def make_in_maps(inp):
    f = lambda a: np.ascontiguousarray(np.asarray(a, dtype=np.float32))
    gains = np.stack([f(inp['ffn1_norm'])[0], f(inp['mix_norm'])[0], f(inp['ffn2_norm'])[0],
                      f(inp['ple_norm'])[0], f(inp['final_norm'])], 0)
    gains = gains.reshape(5, 8, 128).transpose(2, 0, 1).reshape(128, 40)
    hgn = f(inp['hg_norm'])[0].reshape(8, 128).T
    cw = f(inp['conv_w'])[0].T.reshape(24, 128, 4).transpose(1, 0, 2).reshape(128, 96)
    cbp = f(inp['conv_b'])[0].reshape(24, 128).T
    def w2r(w):
        return f(w.reshape(NFC, 128, 8, 128).transpose(2, 1, 0, 3).reshape(8, 128, DFF))

    def colblk(w, c0, ncol, nk):
        return w[:, c0:c0 + ncol].reshape(nk, 128, ncol).transpose(1, 0, 2).reshape(128, nk * ncol)

    w_in_ = f(inp['w_in'])[0]
    whg = f(inp['w_hg_out'])[0]
    wssm = f(inp['w_ssm_out'])[0]
    wout = f(inp['w_out'])[0]
    opA = np.stack([np.concatenate([colblk(whg, fc * 128, 128, 8),
                                    colblk(w_in_, OFF_BRG + fc * 128, 128, 8),
                                    colblk(w_in_, OFF_BRG + 1024 + fc * 128, 128, 8)], axis=1) for fc in range(8)], 0)
    opB = np.stack([colblk(wssm, fc * 128, 128, 16) for fc in range(8)], 0)
    w_outr = np.stack([colblk(wout, dcp * 256, 256, 8) for dcp in range(4)], 0)
    shared = dict(
        consts=_consts(), gains=f(gains), hgn=f(hgn), cw=f(cw), cbp=f(cbp), cbrow=f(inp['conv_b']),
        dt_bias=f(inp['dt_bias'])[0], a_log=f(inp['a_log'])[0], d_skip=f(inp['d_skip'])[0],
        ssm_norm=f(inp['ssm_norm'])[0], hg_lb=f(inp['hg_lb']),
        ffn1_w13=f(inp['ffn1_w13'])[0], ffn1_w2=w2r(f(inp['ffn1_w2'])[0]), w_in=w_in_,
        opA=f(opA), opB=f(opB), w_outr=f(w_outr),
        ffn2_w13=f(inp['ffn2_w13'])[0], ffn2_w2=w2r(f(inp['ffn2_w2'])[0]), w_ple_gate=f(inp['w_ple_gate'])[0],
        w_ple_proj=f(inp['w_ple_proj'])[0],
    )
    x = f(inp['x'])
    p = f(inp['p'])[0]
    maps = []
    for b in range(8):
        m = dict(shared)
        m['x'] = x[b]
        m['p'] = p[b]
        maps.append(m)
    return maps


_NC_CACHE = {}


def kernel(**inputs):
    if 'nc' not in _NC_CACHE:
        _NC_CACHE['nc'] = build()
    nc = _NC_CACHE['nc']
    maps = make_in_maps(inputs)
    res = run_bass_kernel_spmd(nc, maps, core_ids=list(range(8)))
    out = np.stack([np.asarray(r['out'], dtype=np.float32) for r in res.results], 0)
    return out
```

```python
import contextlib
import numpy as np
import concourse.bass as bass
import concourse.mybir as mybir
from concourse.bass_utils import run_bass_kernel_spmd

F32 = mybir.dt.float32
BF16 = mybir.dt.bfloat16
AF = mybir.ActivationFunctionType
ALU = mybir.AluOpType

PE, ACT, DVE, POOL, SP = 'tensor', 'scalar', 'vector', 'gpsimd', 'sync'
ENGS = (PE, ACT, DVE, POOL, SP)
CENGS = (PE, ACT, DVE)

D = 1024
T = 2048
TB = 1024
NBLK = T // TB
DFF = 2816
NFC = DFF // 128
DIN = 11296
EPS = 1e-6
OFF_Q, OFF_F, OFF_V, OFF_G = 0, 1024, 2048, 3072
OFF_MZ = 4096
OFF_XBC = 6144
OFF_DT = 9216
OFF_BRG = 9248


class Dep:
    __slots__ = ('name', 'w', 'r')

    def __init__(self, name=''):
        self.name = name
        self.w = {}
        self.r = {}


def mkdeps(n):
    return [Dep() for _ in range(n)]


class Op:
    __slots__ = ('eng', 'fn', 'deps', 'idx', 'semkey', 'count', 'sig', 'is_dma')


class Prog:
    def __init__(self, nc, block):
        self.nc = nc
        self.block = block
        self.ops = []
        self.flushed = 0
        self.cnt = {e: 0 for e in ENGS}
        self.pending = {e: [] for e in ENGS}
        self.dma_counts = {}
        self.dsem = {}
        self.esem = {e: nc.alloc_semaphore('c_' + e) for e in (PE, ACT, DVE, POOL)}
        self.seen = {e: {} for e in ENGS}
        self.last = {e: None for e in ENGS}
        self.scope_dmas = []
        self.nwaits = 0

    def add(self, eng, fn, reads=(), writes=(), sig=True, is_dma=False, semkey=None, accum=False,
            extra_deps=()):
        op = Op()
        op.eng = eng
        op.fn = fn
        op.is_dma = is_dma
        op.idx = len(self.ops)
        op.semkey = semkey
        op.sig = sig
        op.count = None
        ds = set(extra_deps)
        for d in reads:
            ds.update(d.w.values())
        for d in writes:
            ds.update(d.r.values())
            if not accum:
                ds.update(d.w.values())
        op.deps = ds
        if is_dma:
            if semkey not in self.dsem:
                self.dsem[semkey] = self.nc.alloc_semaphore('d_' + str(semkey))
            c = self.dma_counts.get(semkey, 0) + 16
            self.dma_counts[semkey] = c
            op.count = c
            k = ('d', op.idx)
        else:
            k = eng
            if fn is not None:
                if sig:
                    self.cnt[eng] += 1
                    op.count = self.cnt[eng]
                    for p in self.pending[eng]:
                        p.count = op.count
                    self.pending[eng] = []
                else:
                    self.pending[eng].append(op)
                self.last[eng] = op
        for d in reads:
            d.r[k] = op
        for d in writes:
            if not accum:
                d.w = {}
            d.w[k] = op
            d.r = {}
        self.ops.append(op)
        return op

    def dma(self, eng, fn, reads=(), writes=(), semkey=None, accum=False):
        op = self.add(eng, fn, reads, writes, is_dma=True, semkey=semkey, accum=accum)
        if eng == SP:
            self.scope_dmas.append(op)
        return op

    def barrier(self):
        lasts = [self.last[e] for e in CENGS + (POOL,) if self.last[e] is not None]
        dm = list(self.scope_dmas)
        self.scope_dmas = []
        for e in CENGS + (SP,):
            self.add(e, None, extra_deps=[o for o in lasts if (o.eng != e or e != PE)] + dm)

    def flush(self):
        ops = self.ops[self.flushed:]
        self.flushed = len(self.ops)
        for e in ENGS:
            assert not self.pending[e], "pending non-signalling ops on %s" % e
            ops_e = [op for op in ops if op.eng == e]
            if not ops_e:
                continue
            getattr(self.block, e)(lambda engh, e=e, ops_e=ops_e: self._run(e, engh, ops_e))

    def _run(self, e, engh, ops_e):
        seen = self.seen[e]
        for op in ops_e:
            need = {}
            for dj in op.deps:
                if dj.is_dma:
                    s = self.dsem[dj.semkey]
                else:
                    if dj.fn is None:
                        continue
                    if dj.eng == PE and e == PE and not op.is_dma and op.fn is not None:
                        continue
                    s = self.esem[dj.eng]
                v = dj.count
                assert v is not None
                key = id(s)
                if v > need.get(key, (None, 0))[1]:
                    need[key] = (s, v)
            for key, (s, v) in need.items():
                if v > seen.get(key, 0):
                    engh.wait_ge(s, v)
                    seen[key] = v
                    self.nwaits += 1
            if op.fn is None:
                continue
            inst = op.fn(engh)
            if op.is_dma:
                inst.then_inc(self.dsem[op.semkey], 16)
            elif op.sig:
                inst.then_inc(self.esem[e], 1)


class KB:
    def __init__(self, nc, block, es, dram, stop_after=None):
        self.nc = nc
        self.P = Prog(nc, block)
        self.es = es
        self.dr = dram
        self.stop_after = stop_after
        self.uid = 0
        self.banks = []
        self.bank_deps = []
        for i in range(8):
            self.banks.append(es.enter_context(nc.psum_tensor("bank%d" % i, [128, 512], F32)))
            self.bank_deps.append(Dep())
        self.bank_next = 0
        self.long_next = 0
        self.copy_rr = 0

    def sb(self, shape, dt, name=None, stack=None):
        self.uid += 1
        name = "%s_%d" % (name or "t", self.uid)
        return (stack or self.es).enter_context(self.nc.sbuf_tensor(name, shape, dt))

    def ps(self, long=False):
        if long:
            i = 6 + self.long_next
            self.long_next = (self.long_next + 1) % 2
        else:
            i = self.bank_next
            self.bank_next = (i + 1) % 6
        return self.banks[i], self.bank_deps[i]

    def mm(self, out, lhsT, rhs, start, stop, reads, writes):
        self.P.add(PE, lambda e: e.matmul(out, lhsT=lhsT, rhs=rhs, start=start, stop=stop),
                   reads, writes, sig=True)

    def tr(self, out, in_, ident, reads, writes, sig=True):
        self.P.add(PE, lambda e: e.transpose(out=out, in_=in_, identity=ident), reads, writes, sig=True)

    def act(self, out, in_, func, reads, writes, bias=None, scale=None, accum=None):
        kw = {}
        if bias is not None:
            kw['bias'] = bias
        if scale is not None:
            kw['scale'] = scale
        if accum is not None:
            kw['accum_out'] = accum
        self.P.add(ACT, lambda e: e.activation(out=out, in_=in_, func=func, **kw), reads, writes)

    def tt(self, out, in0, in1, op, reads, writes, eng=DVE):
        self.P.add(eng, lambda e: e.tensor_tensor(out=out, in0=in0, in1=in1, op=op), reads, writes)

    def stt(self, out, in0, scalar, in1, op0, op1, reads, writes):
        self.P.add(DVE, lambda e: e.scalar_tensor_tensor(out=out, in0=in0, scalar=scalar, in1=in1,
                                                         op0=op0, op1=op1), reads, writes)

    def ts(self, out, in0, s1, op0, reads, writes, s2=None, op1=None, eng=DVE):
        if op1 is None:
            self.P.add(eng, lambda e: e.tensor_scalar(out=out, in0=in0, scalar1=s1, scalar2=None, op0=op0),
                       reads, writes)
        else:
            self.P.add(eng, lambda e: e.tensor_scalar(out=out, in0=in0, scalar1=s1, scalar2=s2, op0=op0,
                                                      op1=op1), reads, writes)

    def cp(self, out, in_, reads, writes, eng=None):
        if eng is None:
            eng = (ACT, DVE)[self.copy_rr % 2]
            self.copy_rr += 1
        if eng == ACT:
            self.act(out, in_, AF.Copy, reads, writes)
        else:
            self.P.add(eng, lambda e: e.tensor_copy(out=out, in_=in_), reads, writes)

    def recip(self, out, in_, reads, writes):
        self.P.add(DVE, lambda e: e.reciprocal(out=out, in_=in_), reads, writes)

    def memset(self, ap, val, writes, eng=DVE):
        self.P.add(eng, lambda e: e.memset(ap, val), (), writes)

    def end_scope(self):
        self.P.barrier()
        self.P.flush()

    def init_wpool(self, nslots=3, nel=4096):
        self.wslots = [self.sb([128, nel], BF16, "wslot") for _ in range(nslots)]
        self.wdeps = mkdeps(nslots)
        self.wnext = 0
        self.wnel = nel

    def wload(self, parts):
        i = self.wnext
        self.wnext = (i + 1) % len(self.wslots)
        slot = self.wslots[i]
        dep = self.wdeps[i]
        views = []
        off = 0
        first = True
        for part in parts:
            if len(part) == 4:
                (src, nk, ncols, _) = part
                n = nk * ncols
                assert off + n <= self.wnel
                v = slot[:, off:off + n].rearrange("p (k c) -> p k c", k=nk)
                nch = 1
                while n // nch > 2048 or n % nch:
                    nch += 1
                vo = slot[:, off:off + n].rearrange("p (a b) -> p a b", a=nch)
                s = src.rearrange("p (a b) -> p a b", a=nch)
                self.P.dma(POOL, lambda e, vo=vo, s=s: e.dma_start(out=vo, in_=s), (), [dep],
                           semkey='w%d' % i, accum=not first)
            else:
                (src, nk, ncols) = part
                n = nk * ncols
                assert off + n <= self.wnel
                v = slot[:, off:off + n].rearrange("p (k c) -> p k c", k=nk)
                s = src.rearrange("(k p) c -> p k c", p=128)
                self.P.dma(POOL, lambda e, v=v, s=s: e.dma_start(out=v, in_=s), (), [dep],
                           semkey='w%d' % i, accum=not first)
            first = False
            views.append(v)
            off += n
        return views, dep

    def init(self):
        dr = self.dr
        P = self.P
        self.cf = self.sb([128, 7, 128], F32, "cf")
        self.cb16 = self.sb([128, 4, 128], BF16, "cb16")
        self.d_c = Dep()
        cdep = self.d_c
        P.dma(SP, lambda e: e.dma_start(out=self.cf[:], in_=dr['consts']), (), [cdep], semkey='c', accum=True)
        self.ident_f = self.cf[:, 0, :]
        self.triL64 = self.cf[:, 1, :]
        self.triU64 = self.cf[:, 2, :]
        self.triL128 = self.cf[:, 3, :]
        self.ones_f = self.cf[:, 4, :]
        self.gn = self.sb([128, 40], F32, "gn")
        P.dma(SP, lambda e: e.dma_start(out=self.gn[:], in_=dr['gains']), (), [cdep], semkey='c', accum=True)
        self.hgn = self.sb([128, 8], F32, "hgn")
        P.dma(SP, lambda e: e.dma_start(out=self.hgn[:], in_=dr['hgn']), (), [cdep], semkey='c', accum=True)
        self.cw = self.sb([128, 24, 4], F32, "cw")
        P.dma(SP, lambda e: e.dma_start(out=self.cw[:], in_=dr['cw'].rearrange("p (a b) -> p a b", a=24)),
              (), [cdep], semkey='c', accum=True)
        self.cbp = self.sb([128, 24], F32, "cbp")
        P.dma(SP, lambda e: e.dma_start(out=self.cbp[:], in_=dr['cbp']), (), [cdep], semkey='c', accum=True)
        self.dtb = self.sb([128, 32], F32, "dtb")
        self.aneg = self.sb([128, 32], F32, "aneg")
        self.dsk = self.sb([128, 32], F32, "dsk")
        P.dma(SP, lambda e: e.dma_start(out=self.dtb[:], in_=dr['dt_bias'].partition_broadcast(128)),
              (), [cdep], semkey='c', accum=True)
        P.dma(SP, lambda e: e.dma_start(out=self.aneg[:], in_=dr['a_log'].partition_broadcast(128)),
              (), [cdep], semkey='c', accum=True)
        P.dma(SP, lambda e: e.dma_start(out=self.dsk[:], in_=dr['d_skip'].partition_broadcast(128)),
              (), [cdep], semkey='c', accum=True)
        self.oml = self.sb([128, 1024], F32, "oml")
        self.S_hg = self.sb([128, 8, 128], F32, "S_hg")
        self.d_S_hg = mkdeps(8)
        self.S_ssd = self.sb([128, 2048], F32, "S_ssd")
        self.S_ssd_bf = self.sb([128, 2048], BF16, "S_ssd_bf")
        self.d_S_ssd = mkdeps(4)
        self.d_S_ssd_bf = mkdeps(4)
        self.xtail = self.sb([128, 24, 3], BF16, "xtail")
        self.d_xtail = mkdeps(24)
        self.hT = self.sb([128, 8, TB], F32, "hT")
        self.d_h = [[Dep() for _ in range(2)] for _ in range(8)]
        self.nT = self.sb([128, 8, TB], BF16, "nT")
        self.d_n = [[Dep() for _ in range(2)] for _ in range(8)]
        self.big = self.sb([128, 24, TB], BF16, "big")
        self.d_big = [[Dep() for _ in range(2)] for _ in range(24)]
        self.init_wpool()
        self.d_const = Dep()
        with contextlib.ExitStack() as st:
            lbt = self.sb([128, 2, 1024], F32, "lbt", st)
            P.dma(SP, lambda e: e.dma_start(out=lbt[:, 0, :], in_=dr['hg_lb'][0].partition_broadcast(128)),
                  (), [cdep], semkey='c', accum=True)
            P.dma(SP, lambda e: e.dma_start(out=lbt[:, 1, :], in_=dr['hg_lb'][1].partition_broadcast(128)),
                  (), [cdep], semkey='c', accum=True)
            dc = self.d_const
            self.tt(lbt[:, 0, :], lbt[:, 1, :], lbt[:, 0, :], ALU.subtract, [cdep], [dc])
            self.act(self.oml[:], lbt[:, 0, :], AF.Sigmoid, [dc], [dc])
            self.cp(self.cb16[:, 0, :], self.ident_f, [cdep], [dc], eng=DVE)
            self.cp(self.cb16[:, 1, :], self.ones_f, [cdep], [dc], eng=DVE)
            self.cp(self.cb16[:, 2, :], self.cf[:, 5, :], [cdep], [dc], eng=DVE)
            self.negm_b = self.cb16[:, 2, :]
            self.cp(self.cb16[:, 3, :], self.cf[:, 6, :], [cdep], [dc], eng=DVE)
            self.e0_b = self.cb16[:, 3, :]
            self.act(self.aneg[:], self.aneg[:], AF.Exp, [cdep], [dc])
            self.ts(self.aneg[:], self.aneg[:], -1.0, ALU.mult, [dc], [dc])
            self.memset(self.S_hg[:], 0.0, self.d_S_hg)
            self.memset(self.S_ssd[:], 0.0, self.d_S_ssd)
            self.memset(self.S_ssd_bf[:], 0.0, self.d_S_ssd_bf)
            self.memset(self.xtail[:], 0.0, self.d_xtail)
            if 'dbg' in dr:
                self.memset(self.big[:], 0.0, [d for row in self.d_big for d in row])
            self.ident_b = self.cb16[:, 0, :]
            self.ones_b = self.cb16[:, 1, :]
            self.load_x(0, stack=st)
            self.end_scope()

    def load_x(self, blk, stack=None):
        dr = self.dr
        with contextlib.ExitStack() as st_own:
            st = stack if stack is not None else st_own
            xs = [self.sb([128, 8, 128], F32, "xstage", st) for _ in range(2)]
            dxs = mkdeps(2)
            for kc in range(8):
                s = xs[kc % 2]
                ds = dxs[kc % 2]
                src = dr['x'][blk * TB:(blk + 1) * TB, kc * 128:(kc + 1) * 128].rearrange("(t p) c -> p t c", p=128)
                self.P.dma(SP, lambda e, s=s, src=src: e.dma_start(out=s[:], in_=src), (), [ds],
                           semkey='xs%d' % (kc % 2))
                for half in range(2):
                    bank, bd = self.ps()
                    for q in range(4):
                        tt_ = half * 4 + q
                        self.tr(bank[:, q * 128:(q + 1) * 128], s[:, tt_, :], self.ident_f, [ds, self.d_c], [bd],
                                sig=(q == 3))
                    self.cp(self.hT[:, kc, half * 512:(half + 1) * 512], bank[:], [bd], [self.d_h[kc][half]])
            if stack is None:
                self.end_scope()

    def norm(self, gi, final=False, stack=None):
        with contextlib.ExitStack() as st_own:
            st = stack if stack is not None else st_own
            sq = [self.sb([128, TB], BF16, "sq", st) for _ in range(2)]
            dsq = mkdeps(2)
            rstd = self.sb([128, TB], F32, "rstd", st)
            drs = mkdeps(2)
            b0, bd0 = self.ps()
            b1, bd1 = self.ps()
            bks = ((b0, bd0), (b1, bd1))
            for kc in range(8):
                s = sq[kc % 2]
                ds = dsq[kc % 2]
                if kc % 3 == 2:
                    self.tt(s[:], self.hT[:, kc, :], self.hT[:, kc, :], ALU.mult, self.d_h[kc], [ds])
                else:
                    self.act(s[:], self.hT[:, kc, :], AF.Square, self.d_h[kc], [ds])
                for tb in range(2):
                    self.mm(bks[tb][0][:], self.ones_b, s[:, tb * 512:(tb + 1) * 512], kc == 0, kc == 7,
                            [ds, self.d_const], [bks[tb][1]])
            for tb in range(2):
                sl = slice(tb * 512, (tb + 1) * 512)
                self.act(rstd[:, sl], bks[tb][0][:], AF.Ln, [bks[tb][1]], [drs[tb]], bias=EPS, scale=1.0 / D)
                self.act(rstd[:, sl], rstd[:, sl], AF.Exp, [drs[tb]], [drs[tb]], scale=-0.5)
            for tb in range(2):
                for kc in range(8):
                    sl = slice(tb * 512, (tb + 1) * 512)
                    if final:
                        self.stt(self.hT[:, kc, sl], self.hT[:, kc, sl], self.gn[:, gi * 8 + kc:gi * 8 + kc + 1],
                                 rstd[:, sl], ALU.mult, ALU.mult, [self.d_h[kc][tb], drs[tb], self.d_c],
                                 [self.d_h[kc][tb]])
                    else:
                        self.stt(self.nT[:, kc, sl], self.hT[:, kc, sl], self.gn[:, gi * 8 + kc:gi * 8 + kc + 1],
                                 rstd[:, sl], ALU.mult, ALU.mult, [self.d_h[kc][tb], drs[tb], self.d_c],
                                 [self.d_n[kc][tb]])
            if stack is None:
                self.end_scope()

    def ffn(self, w13, w2, gi):
        with contextlib.ExitStack() as st:
            self.norm(gi, stack=st)
            sg = [self.sb([128, 512], F32, "sg", st) for _ in range(2)]
            dsg = mkdeps(2)
            it = 0
            for g2 in range(NFC // 2):
                (wg, wu), wd = self.wload([(w13[:, g2 * 256:(g2 + 1) * 256], 8, 256),
                                           (w13[:, DFF + g2 * 256:DFF + (g2 + 1) * 256], 8, 256)])
                for j in range(2):
                    fc = g2 * 2 + j
                    for tb in range(2):
                        sl = slice(tb * 512, (tb + 1) * 512)
                        pg, dg = self.ps()
                        pu, du = self.ps()
                        for kc in range(8):
                            self.mm(pg[:], wg[:, kc, j * 128:(j + 1) * 128], self.nT[:, kc, sl], kc == 0, kc == 7,
                                    [wd, self.d_n[kc][tb]], [dg])
                        for kc in range(8):
                            self.mm(pu[:], wu[:, kc, j * 128:(j + 1) * 128], self.nT[:, kc, sl], kc == 0, kc == 7,
                                    [wd, self.d_n[kc][tb]], [du])
                        s = sg[it % 2]
                        ds = dsg[it % 2]
                        it += 1
                        self.act(s[:], pg[:], AF.Silu, [dg], [ds])
                        self.tt(self.big[:, fc, sl], s[:], pu[:], ALU.mult, [ds, du], [self.d_big[fc][tb]])
            for dc in range(8):
                (wa,), wda = self.wload([(w2[dc], NFC, 128, 'r')])
                for tb in range(2):
                    sl = slice(tb * 512, (tb + 1) * 512)
                    po, do = self.ps()
                    for fc in range(NFC):
                        self.mm(po[:], wa[:, fc, :], self.big[:, fc, sl], fc == 0, fc == NFC - 1,
                                [wda, self.d_big[fc][tb]], [do])
                    self.stt(self.hT[:, dc, sl], po[:], 0.5, self.hT[:, dc, sl], ALU.mult, ALU.add,
                             [do, self.d_h[dc][tb]], [self.d_h[dc][tb]])
            self.end_scope()

    def hgrn(self, blk):
        w_in = self.dr['w_in']
        nT, dn = self.nT, self.d_n
        with contextlib.ExitStack() as st:
            self.norm(1, stack=st)
            sbs = lambda shape, dt, name: self.sb(shape, dt, name, st)
            dd2 = lambda: [mkdeps(2) for _ in range(2)]
            k_tm = sbs([128, 8, 128], F32, "k_tm"); d_k = mkdeps(2)
            logf = sbs([128, 8, 128], F32, "logf"); d_lf = mkdeps(2)
            qs = sbs([128, TB], BF16, "qs"); d_qs = mkdeps(2)
            eGi = sbs([128, TB], BF16, "eGi"); d_eGi = mkdeps(2)
            eGe = sbs([128, 8, 128], BF16, "eGe"); d_eGe = mkdeps(2)
            k_inv = sbs([128, TB], BF16, "k_inv"); d_ki = mkdeps(2)
            v_bf = [sbs([128, 8, 128], BF16, "v_bf") for _ in range(2)]; d_v = dd2()
            sgate = [sbs([128, TB], BF16, "sgate") for _ in range(2)]; d_sg = dd2()
            eG = [sbs([128, TB], F32, "eG") for _ in range(2)]; d_eG = dd2()
            q_dec = [sbs([128, TB], BF16, "q_dec") for _ in range(2)]; d_qd = dd2()
            k_end = [sbs([128, 8, 128], BF16, "k_end") for _ in range(2)]; d_ke = dd2()
            att = [sbs([128, 8, 128], BF16, "att") for _ in range(2)]; d_att = dd2()
            osq = [sbs([128, 512], BF16, "osq") for _ in range(2)]; d_osq = mkdeps(2)
            sd = [sbs([128, 512], F32, "sd") for _ in range(2)]; d_sd = mkdeps(2)
            tmp_ = sbs([128, 512], F32, "tmp"); tmp = [tmp_, tmp_]; d_tmp_ = Dep(); d_tmp = [d_tmp_, d_tmp_]
            Sbf = [sbs([128, 128], BF16, "Sbf") for _ in range(4)]; d_Sbf = mkdeps(4)
            self._sbi = 0
            v4 = lambda bank: bank[:].rearrange("p (a b) -> p a b", a=4)

            hw = {}

            def hload(j):
                hw[j] = self.wload([(w_in[:, OFF_Q + j * 128:OFF_Q + (j + 1) * 128], 8, 128),
                                    (w_in[:, OFF_F + j * 128:OFF_F + (j + 1) * 128], 8, 128),
                                    (w_in[:, OFF_V + j * 128:OFF_V + (j + 1) * 128], 8, 128),
                                    (w_in[:, OFF_G + j * 128:OFF_G + (j + 1) * 128], 8, 128)])

            hload(0)

            def front(j):
                p = j % 2
                if j + 1 < 8:
                    hload(j + 1)
                (wq, wf, wv, wg), wd = hw[j]
                for tb in range(2):
                    sl = slice(tb * 512, (tb + 1) * 512)
                    pq, dq = self.ps()
                    for kc in range(8):
                        self.mm(pq[:], wq[:, kc, :], nT[:, kc, sl], kc == 0, kc == 7, [wd, dn[kc][tb]], [dq])
                    self.act(qs[:, sl], pq[:], AF.Copy, [dq], [d_qs[tb]], scale=float(128 ** -0.5))
                    yield
                    pg, dg = self.ps()
                    for kc in range(8):
                        self.mm(pg[:], wg[:, kc, :], nT[:, kc, sl], kc == 0, kc == 7, [wd, dn[kc][tb]], [dg])
                    self.act(sgate[p][:, sl], pg[:], AF.Silu, [dg], [d_sg[p][tb]])
                    yield
                for hf in range(2):
                    sl = slice(hf * 512, (hf + 1) * 512)
                    h4 = slice(hf * 4, hf * 4 + 4)
                    pf, df = self.ps()
                    for q in range(4):
                        t_ = hf * 4 + q
                        for kc in range(8):
                            self.mm(pf[:, q * 128:(q + 1) * 128], nT[:, kc, t_ * 128:(t_ + 1) * 128], wf[:, kc, :],
                                    kc == 0, kc == 7, [wd, dn[kc][hf]], [df])
                    self.act(k_tm[:, h4, :], v4(pf), AF.Exp, [df], [d_k[hf]])
                    self.act(k_tm[:, h4, :], k_tm[:, h4, :], AF.Ln, [d_k[hf]], [d_k[hf]], bias=1.0)
                    self.act(k_tm[:, h4, :], k_tm[:, h4, :], AF.Exp, [d_k[hf]], [d_k[hf]], scale=-1.0)
                    yield
                    pv, dv = self.ps()
                    for q in range(4):
                        t_ = hf * 4 + q
                        for kc in range(8):
                            self.mm(pv[:, q * 128:(q + 1) * 128], nT[:, kc, t_ * 128:(t_ + 1) * 128], wv[:, kc, :],
                                    kc == 0, kc == 7, [wd, dn[kc][hf]], [dv])
                    self.cp(v_bf[p][:, h4, :], v4(pv), [dv], [d_v[p][hf]])
                    yield
                    self.tt(k_tm[:, h4, :], k_tm[:, h4, :],
                            self.oml[:, j * 128:(j + 1) * 128].unsqueeze(1).to_broadcast([128, 4, 128]), ALU.mult,
                            [d_k[hf], self.d_const], [d_k[hf]])
                    self.act(logf[:, h4, :], k_tm[:, h4, :], AF.Ln, [d_k[hf]], [d_lf[hf]], scale=-1.0, bias=1.0)
                    pG, dG = self.ps()
                    pE, dE = self.ps()
                    pK, dK = self.ps()
                    for q in range(4):
                        t_ = hf * 4 + q
                        cs = slice(q * 128, (q + 1) * 128)
                        self.mm(pG[:, cs], logf[:, t_, :], self.triL64, True, True, [d_lf[hf], self.d_c], [dG])
                        self.mm(pE[:, cs], self.triU64, logf[:, t_, :], True, True, [d_lf[hf], self.d_c], [dE])
                        self.tr(pK[:, cs], k_tm[:, t_, :], self.ident_f, [d_k[hf], self.d_c], [dK])
                    self.act(eG[p][:, sl], pG[:], AF.Exp, [dG], [d_eG[p][hf]])
                    self.act(eGi[:, sl], pG[:], AF.Exp, [dG], [d_eGi[hf]], scale=-1.0)
                    self.act(eGe[:, h4, :], v4(pE), AF.Exp, [dE], [d_eGe[hf]])
                    self.tt(q_dec[p][:, sl], qs[:, sl], eG[p][:, sl], ALU.mult, [d_qs[hf], d_eG[p][hf]], [d_qd[p][hf]])
                    self.tt(k_inv[:, sl], pK[:], eGi[:, sl], ALU.mult, [dK, d_eGi[hf]], [d_ki[hf]])
                    self.tt(k_end[p][:, h4, :], k_tm[:, h4, :], eGe[:, h4, :], ALU.mult, [d_k[hf], d_eGe[hf]],
                            [d_ke[p][hf]])
                    yield
                    pA, dA = self.ps()
                    for q in range(4):
                        t_ = hf * 4 + q
                        cs = slice(q * 128, (q + 1) * 128)
                        ts_ = slice(t_ * 128, (t_ + 1) * 128)
                        self.mm(pA[:, cs], k_inv[:, ts_], q_dec[p][:, ts_], True, True, [d_ki[hf], d_qd[p][hf]], [dA])
                    self.tt(att[p][:, h4, :], v4(pA), self.triL64.unsqueeze(1).to_broadcast([128, 4, 128]), ALU.mult,
                            [dA, self.d_c], [d_att[p][hf]])
                    yield

            def back(j):
                p = j % 2
                dS = self.d_S_hg[j]
                S = self.S_hg[:, j, :]
                cur = self._sbi % 4
                self._sbi += 1
                self.cp(Sbf[cur][:], S, [dS], [d_Sbf[cur]], eng=ACT)
                for hf in range(2):
                    sl = slice(hf * 512, (hf + 1) * 512)
                    pO, dO = self.ps(long=True)
                    for q in range(4):
                        t_ = hf * 4 + q
                        cs = slice(q * 128, (q + 1) * 128)
                        pS, dSp = self.ps()
                        pS1, dSp1 = self.ps()
                        self.mm(pS[:, 0:128], k_end[p][0:64, t_, :], v_bf[p][0:64, t_, :], True, True,
                                [d_ke[p][hf], d_v[p][hf]], [dSp])
                        self.mm(pS1[:, 0:128], k_end[p][64:128, t_, :], v_bf[p][64:128, t_, :], True, True,
                                [d_ke[p][hf], d_v[p][hf]], [dSp1])
                        self.mm(pO[:, cs], v_bf[p][:, t_, :], att[p][:, t_, :], True, False, [d_v[p][hf], d_att[p][hf]], [dO])
                        self.mm(pO[:, q * 128:q * 128 + 64], Sbf[cur][:], q_dec[p][:, t_ * 128:t_ * 128 + 64], False, False,
                                [d_Sbf[cur], d_qd[p][hf]], [dO])
                        c0 = t_ * 128 + 63
                        n1 = self._sbi % 4
                        self._sbi += 1
                        self.stt(Sbf[n1][:], S, eG[p][:, c0:c0 + 1], pS[:, 0:128], ALU.mult, ALU.add,
                                 [dS, d_eG[p][hf], dSp], [d_Sbf[n1]])
                        self.stt(S, S, eG[p][:, c0:c0 + 1], pS[:, 0:128], ALU.mult, ALU.add, [dS, d_eG[p][hf], dSp], [dS])
                        yield
                        self.mm(pO[:, q * 128 + 64:q * 128 + 128], Sbf[n1][:], q_dec[p][:, t_ * 128 + 64:t_ * 128 + 128],
                                False, True, [d_Sbf[n1], d_qd[p][hf]], [dO])
                        c1 = t_ * 128 + 127
                        cur = self._sbi % 4
                        self._sbi += 1
                        self.stt(Sbf[cur][:], S, eG[p][:, c1:c1 + 1], pS1[:, 0:128], ALU.mult, ALU.add,
                                 [dS, d_eG[p][hf], dSp1], [d_Sbf[cur]])
                        self.stt(S, S, eG[p][:, c1:c1 + 1], pS1[:, 0:128], ALU.mult, ALU.add, [dS, d_eG[p][hf], dSp1], [dS])
                        yield
                    r = hf
                    self.act(osq[r][:], pO[:], AF.Square, [dO], [d_osq[r]])
                    pSS, dSS = self.ps()
                    self.mm(pSS[:], self.ones_b, osq[r][:], True, True, [d_osq[r], self.d_const], [dSS])
                    self.act(sd[r][:], pSS[:], AF.Ln, [dSS], [d_sd[r]], scale=1.0 / 128, bias=EPS)
                    self.act(sd[r][:], sd[r][:], AF.Exp, [d_sd[r]], [d_sd[r]], scale=-0.5)
                    self.tt(tmp[r][:], pO[:], sd[r][:], ALU.mult, [dO, d_sd[r]], [d_tmp[r]])
                    self.stt(self.big[:, j, sl], tmp[r][:], self.hgn[:, j:j + 1], sgate[p][:, sl], ALU.mult, ALU.mult,
                             [d_tmp[r], self.d_c, d_sg[p][hf]], [self.d_big[j][hf]])
                    yield

            for _ in front(0):
                pass
            for j in range(8):
                b = back(j)
                f = front(j + 1) if j < 7 else None
                alive_b, alive_f = True, f is not None
                while alive_b or alive_f:
                    if alive_b:
                        try:
                            next(b)
                        except StopIteration:
                            alive_b = False
                    if alive_f:
                        try:
                            next(f)
                        except StopIteration:
                            alive_f = False
            self.end_scope()

    def ssd(self, blk):
        dr = self.dr
        w_in = dr['w_in']
        nT, dn = self.nT, self.d_n
        with contextlib.ExitStack() as st:
            sbs = lambda shape, dt, name: self.sb(shape, dt, name, st)
            sm = lambda name, dt=F32: sbs([128, 8, 32], dt, name)
            dtv, lndt, a_, acs, wst, wd_, dl, b2 = [sm(n) for n in ("dtv", "lndt", "a_", "acs", "wst", "wd_", "dl", "b2")]
            acs_hi = sm("acs_hi", BF16)
            acs_lo = sm("acs_lo", BF16)
            d_dt = Dep()
            xpT = sbs([128, 6, 3 + TB], BF16, "xpT"); d_xp = [[Dep() for _ in range(3)] for _ in range(6)]
            BT = sbs([128, TB], BF16, "BT"); d_BT = mkdeps(2)
            CT = sbs([128, TB], BF16, "CT"); d_CT = mkdeps(2)
            diag = sbs([128, 6, 4, 128], BF16, "diag"); d_dg = mkdeps(6)
            cbr = sbs([128, 640], BF16, "cbr"); d_cbr = Dep()
            self.memset(cbr[:], 0.0, [d_cbr])
            gss = sbs([128, 512], F32, "gss"); d_gss = Dep()
            R2 = 2
            xs_bf = [sbs([128, 512], BF16, "xs_bf") for _ in range(R2)]; d_xs = mkdeps(R2)
            xsD = [sbs([128, 512], BF16, "xsD") for _ in range(R2)]; d_xsD = mkdeps(R2)
            xsw = [sbs([128, 512], BF16, "xsw") for _ in range(R2)]; d_xsw = mkdeps(R2)
            B_tm = [sbs([128, 128], BF16, "B_tm") for _ in range(R2)]; d_Bt = mkdeps(R2)
            smz = [sbs([128, 512], BF16, "smz") for _ in range(R2)]; d_smz = mkdeps(R2)
            CBm = [sbs([128, 128], F32, "CBm") for _ in range(R2)]; d_CBm = mkdeps(R2)
            E = [sbs([128, 128], F32, "E") for _ in range(3)]; d_E = mkdeps(3)
            Mp = [sbs([128, 128], BF16, "Mp") for _ in range(16)]; d_Mp = mkdeps(16)
            t1_ = sbs([128, 512], F32, "t1"); t1 = [t1_] * R2; d_t1_ = Dep(); d_t1 = [d_t1_] * R2
            yv = [sbs([128, 512], F32, "yv") for _ in range(R2)]; d_yv = mkdeps(R2)
            ssq = [sbs([128, 1], F32, "ssq") for _ in range(R2)]; d_ssq = mkdeps(R2)
            yn = [sbs([128, 512], BF16, "yn") for _ in range(R2)]; d_yn = mkdeps(R2)

            (wdt,), wdd = self.wload([(w_in[:, OFF_DT:OFF_DT + 32], 8, 32)])
            pD, dD = self.ps()
            for t_ in range(8):
                for kc in range(8):
                    self.mm(pD[:, t_ * 32:(t_ + 1) * 32], nT[:, kc, t_ * 128:(t_ + 1) * 128], wdt[:, kc, :], kc == 0, kc == 7,
                            [wdd, dn[kc][t_ // 4]], [dD])
            v8 = lambda bank: bank[:, 0:256].rearrange("p (a b) -> p a b", a=8)
            bc8 = lambda ap: ap.unsqueeze(1).to_broadcast([128, 8, 32])
            dd = [d_dt]
            self.tt(dtv[:], v8(pD), bc8(self.dtb[:]), ALU.add, [dD, self.d_c], dd)
            self.act(dtv[:], dtv[:], AF.Exp, dd, dd)
            self.act(dtv[:], dtv[:], AF.Ln, dd, dd, bias=1.0)
            self.act(lndt[:], dtv[:], AF.Ln, dd, dd)
            self.tt(a_[:], dtv[:], bc8(self.aneg[:]), ALU.mult, dd + [self.d_const], dd)
            pAc, dAc = self.ps()
            pTo, dTo = self.ps()
            for t_ in range(8):
                self.mm(pAc[:, t_ * 32:(t_ + 1) * 32], self.triL128, a_[:, t_, :], True, True, dd + [self.d_c], [dAc])
                self.mm(pTo[:, t_ * 32:(t_ + 1) * 32], self.ones_f, a_[:, t_, :], True, True, dd + [self.d_c], [dTo])
            self.cp(acs[:], v8(pAc), [dAc], dd, eng=ACT)
            self.act(wst[:], v8(pAc), AF.Exp, [dAc], dd)
            self.act(dl[:], v8(pTo), AF.Exp, [dTo], dd)
            self.tt(wd_[:], v8(pTo), acs[:], ALU.subtract, [dTo] + dd, dd)
            self.act(wd_[:], wd_[:], AF.Exp, dd, dd)
            self.tt(wd_[:], wd_[:], dtv[:], ALU.mult, dd, dd)
            self.tt(b2[:], lndt[:], acs[:], ALU.subtract, dd, dd)
            self.cp(acs_hi[:], acs[:], dd, dd, eng=DVE)
            self.tt(acs_lo[:], acs[:], acs_hi[:], ALU.subtract, dd, dd)

            self._ei = 0
            gw = {}

            def gload_xb(g):
                gw[g] = (self.wload([(w_in[:, OFF_XBC + g * 512:OFF_XBC + (g + 1) * 512], 8, 512)]),
                         self.wload([(w_in[:, OFF_XBC + 2048 + g * 128:OFF_XBC + 2048 + (g + 1) * 128], 8, 128),
                                     (w_in[:, OFF_XBC + 2560 + g * 128:OFF_XBC + 2560 + (g + 1) * 128], 8, 128)]))

            gload_xb(0)
            for g in range(4):
                ((wx,), wxd), ((wB, wC), wbd) = gw[g]
                (wz,), wzd = self.wload([(w_in[:, OFF_MZ + g * 512:OFF_MZ + (g + 1) * 512], 8, 512)])
                chs = [4 * g, 4 * g + 1, 4 * g + 2, 4 * g + 3, 16 + g, 20 + g]
                self.P.dma(POOL, lambda e, g=g: e.dma_start(out=cbr[0:1, 0:512], in_=dr['cbrow'][0:1, g * 512:(g + 1) * 512]),
                           (), [d_cbr], semkey='cbr')
                self.P.dma(POOL, lambda e, g=g: e.dma_start(out=cbr[0:1, 512:640],
                                                           in_=dr['cbrow'][0:1, 2048 + g * 128:2048 + (g + 1) * 128]),
                           (), [d_cbr], semkey='cbr', accum=True)
                self.P.dma(SP, lambda e, g=g: e.dma_start(out=gss[:], in_=dr['ssm_norm'][g * 512:(g + 1) * 512].partition_broadcast(128)),
                           (), [d_gss], semkey='gss')
                for c6 in range(6):
                    ch = chs[c6]
                    self.cp(xpT[:, c6, 0:3], self.xtail[:, ch, :], [self.d_xtail[ch]], [d_xp[c6][0]], eng=DVE)
                    wsrc = wx[:, :, c6 * 128:(c6 + 1) * 128] if c6 < 4 else (wB if c6 == 4 else wC)
                    wdp = wxd if c6 < 4 else wbd
                    for tb in range(2):
                        sl = slice(tb * 512, (tb + 1) * 512)
                        px, dpx = self.ps()
                        for kc in range(8):
                            self.mm(px[:], wsrc[:, kc, :], nT[:, kc, sl], kc == 0, kc == 7, [wdp, dn[kc][tb]], [dpx])
                        self.cp(xpT[:, c6, 3 + tb * 512:3 + (tb + 1) * 512], px[:], [dpx], [d_xp[c6][1 + tb]])
                    self.cp(self.xtail[:, ch, :], xpT[:, c6, TB:TB + 3], [d_xp[c6][2]], [self.d_xtail[ch]], eng=DVE)
                    for tap in range(4):
                        self.ts(diag[:, c6, tap, :], self.ident_f, self.cw[:, ch, tap:tap + 1], ALU.mult,
                                [self.d_c], [d_dg[c6]])
                for (c6, dst, dd_, col) in ((4, BT, d_BT, 16 + g), (5, CT, d_CT, 20 + g)):
                    for tb in range(2):
                        sl = slice(tb * 512, (tb + 1) * 512)
                        pb, dpb = self.ps()
                        for tap in range(4):
                            self.mm(pb[:], diag[:, c6, tap, :], xpT[:, c6, tb * 512 + tap:tb * 512 + tap + 512], tap == 0,
                                    tap == 3, [d_dg[c6]] + d_xp[c6], [dpb])
                        self.act(dst[:, sl], pb[:], AF.Silu, [dpb, self.d_c], [dd_[tb]], bias=self.cbp[:, col:col + 1])
                v864 = lambda ap: ap.rearrange("p (a b) -> p a b", a=8)
                bch = lambda ap: ap.unsqueeze(2).to_broadcast([128, 8, 64])
                hs = slice(g * 8, (g + 1) * 8)
                Sg = self.S_ssd[:, g * 512:(g + 1) * 512]

                def front(t_, g=g, wz=wz, wzd=wzd, hs=hs):
                    hf = t_ // 4
                    tsl = slice(t_ * 128, (t_ + 1) * 128)
                    r = t_ % 2
                    pxs, dxs_ = self.ps()
                    for cc in range(4):
                        cs = slice(cc * 128, (cc + 1) * 128)
                        for tap in range(4):
                            self.mm(pxs[:, cs], xpT[:, cc, t_ * 128 + tap:t_ * 128 + tap + 128], diag[:, cc, tap, :], tap == 0,
                                    False, d_xp[cc] + [d_dg[cc]], [dxs_])
                        self.mm(pxs[:, cs], self.e0_b, cbr[:, cs], False, True, [self.d_const, d_cbr], [dxs_])
                    self.act(xs_bf[r][:], pxs[:], AF.Silu, [dxs_], [d_xs[r]])
                    pbt, dbt = self.ps()
                    for tap in range(4):
                        self.mm(pbt[:, 0:128], xpT[:, 4, t_ * 128 + tap:t_ * 128 + tap + 128], diag[:, 4, tap, :], tap == 0,
                                False, d_xp[4] + [d_dg[4]], [dbt])
                    self.mm(pbt[:, 0:128], self.e0_b, cbr[:, 512:640], False, True, [self.d_const, d_cbr], [dbt])
                    self.act(B_tm[r][:], pbt[:, 0:128], AF.Silu, [dbt], [d_Bt[r]])
                    pz, dz = self.ps()
                    for kc in range(8):
                        self.mm(pz[:], nT[:, kc, tsl], wz[:, kc, :], kc == 0, kc == 7, [wzd, dn[kc][hf]], [dz])
                    self.act(smz[r][:], pz[:], AF.Silu, [dz], [d_smz[r]])
                    self.tt(v864(xsD[r][:]), v864(xs_bf[r][:]), bch(self.dsk[:, hs]), ALU.mult, [d_xs[r], self.d_c],
                            [d_xsD[r]])
                    self.tt(v864(xsw[r][:]), v864(xs_bf[r][:]), bch(wd_[:, t_, hs]), ALU.mult, [d_xs[r], d_dt], [d_xsw[r]])
                    pcb, dcb = self.ps()
                    self.mm(pcb[:, 0:128], BT[:, tsl], CT[:, tsl], True, True, [d_BT[hf], d_CT[hf]], [dcb])
                    self.cp(CBm[r][:], pcb[:, 0:128], [dcb], [d_CBm[r]], eng=ACT)
                    for hq in range(2):
                        pab, dab = self.ps()
                        for q in range(4):
                            h = g * 8 + hq * 4 + q
                            cs = slice(q * 128, (q + 1) * 128)
                            self.mm(pab[:, cs], acs_hi[:, t_, h:h + 1].to_broadcast([128, 128]), self.ident_b, True, False,
                                    [d_dt, self.d_const], [dab])
                            self.mm(pab[:, cs], acs_lo[:, t_, h:h + 1].to_broadcast([128, 128]), self.ident_b, False, False,
                                    [d_dt, self.d_const], [dab])
                            self.mm(pab[:, cs], self.ident_b, self.negm_b, False, True, [self.d_const], [dab])
                        for q in range(4):
                            hh = hq * 4 + q
                            h = g * 8 + hh
                            cs = slice(q * 128, (q + 1) * 128)
                            e_ = self._ei % 3
                            self._ei += 1
                            m_ = r * 8 + hh
                            self.act(E[e_][:], pab[:, cs], AF.Exp, [dab, d_dt], [d_E[e_]], bias=b2[:, t_, h:h + 1])
                            self.tt(Mp[m_][:], E[e_][:], CBm[r][:], ALU.mult, [d_E[e_], d_CBm[r]], [d_Mp[m_]])

                def mid(t_, g=g, hs=hs, Sg=Sg):
                    hf = t_ // 4
                    tsl = slice(t_ * 128, (t_ + 1) * 128)
                    r = t_ % 2
                    py, dy = self.ps(long=True)
                    self.mm(py[:], self.ident_b, xsD[r][:], True, False, [self.d_const, d_xsD[r]], [dy])
                    for hh in range(8):
                        m_ = r * 8 + hh
                        self.mm(py[:, hh * 64:(hh + 1) * 64], Mp[m_][:], xs_bf[r][:, hh * 64:(hh + 1) * 64], False,
                                hh == 7, [d_Mp[m_], d_xs[r]], [dy])
                    pyb, dyb = self.ps()
                    self.mm(pyb[:], CT[:, tsl], self.S_ssd_bf[:, g * 512:(g + 1) * 512], True, True,
                            [d_CT[hf], self.d_S_ssd_bf[g]], [dyb])
                    pds, dds = self.ps()
                    self.mm(pds[:], B_tm[r][:], xsw[r][:], True, True, [d_Bt[r], d_xsw[r]], [dds])
                    self.tt(v864(t1[r][:]), v864(pyb[:]), bch(wst[:, t_, hs]), ALU.mult, [dyb, d_dt], [d_t1[r]])
                    self.tt(v864(Sg), v864(Sg), bch(dl[:, t_, hs]), ALU.mult, [self.d_S_ssd[g], d_dt], [self.d_S_ssd[g]])
                    self.tt(self.S_ssd_bf[:, g * 512:(g + 1) * 512], Sg, pds[:], ALU.add, [self.d_S_ssd[g], dds],
                            [self.d_S_ssd_bf[g]])
                    self.tt(Sg, Sg, pds[:], ALU.add, [self.d_S_ssd[g], dds], [self.d_S_ssd[g]])
                    self.tt(yv[r][:], py[:], t1[r][:], ALU.add, [dy, d_t1[r]], [d_yv[r]])
                    self.tt(yv[r][:], yv[r][:], smz[r][:], ALU.mult, [d_yv[r], d_smz[r]], [d_yv[r]])
                    self.act(t1[r][:], yv[r][:], AF.Square, [d_yv[r]], [d_t1[r], d_ssq[r]], accum=ssq[r][:])
                    self.act(ssq[r][:], ssq[r][:], AF.Ln, [d_ssq[r]], [d_ssq[r]], scale=1.0 / 512, bias=EPS)
                    self.act(ssq[r][:], ssq[r][:], AF.Exp, [d_ssq[r]], [d_ssq[r]], scale=-0.5)

                def tail_a(t_, g=g):
                    r = t_ % 2
                    self.stt(yn[r][:], yv[r][:], ssq[r][:, 0:1], gss[:], ALU.mult, ALU.mult, [d_yv[r], d_ssq[r], d_gss],
                             [d_yn[r]])

                def tail(t_, g=g):
                    hf = t_ // 4
                    tsl = slice(t_ * 128, (t_ + 1) * 128)
                    r = t_ % 2
                    pyt, dyt = self.ps()
                    pytb = pyt[:].bitcast(BF16)
                    for cc in range(4):
                        self.tr(pytb[:, cc * 128:(cc + 1) * 128], yn[r][:, cc * 128:(cc + 1) * 128], self.ident_b,
                                [d_yn[r], self.d_const], [dyt])
                    self.cp(self.big[:, 8 + 4 * g:12 + 4 * g, tsl], pytb[:, 0:512].rearrange("p (a b) -> p a b", a=4), [dyt],
                            [self.d_big[8 + 4 * g + cc][hf] for cc in range(4)], eng=ACT)

                if g + 1 < 4:
                    gload_xb(g + 1)
                front(0)
                for t_ in range(8):
                    if t_ > 0:
                        tail_a(t_ - 1)
                    if t_ < 7:
                        front(t_ + 1)
                    mid(t_)
                    if t_ > 0:
                        tail(t_ - 1)
                tail_a(7)
                tail(7)
            self.end_scope()

    def outproj(self, blk):
        dr = self.dr
        w_in = dr['w_in']
        nT, dn = self.nT, self.d_n
        with contextlib.ExitStack() as st:
            sbs = lambda shape, dt, name: self.sb(shape, dt, name, st)
            mix = sbs([128, 8, TB], BF16, "mix"); d_mix = [[Dep() for _ in range(2)] for _ in range(8)]
            sga = [sbs([128, 512], F32, "sga") for _ in range(2)]; d_sga = mkdeps(2)
            sgb = [sbs([128, 512], F32, "sgb") for _ in range(2)]; d_sgb = mkdeps(2)
            m1 = [sbs([128, 512], F32, "m1") for _ in range(2)]; d_m1 = mkdeps(2)
            it = 0
            for fc in range(8):
                cs = slice(fc * 128, (fc + 1) * 128)
                (wA3,), wad = self.wload([(dr['opA'][fc], 24, 128, 'r')])
                wA, wga, wgb = wA3[:, 0:8, :], wA3[:, 8:16, :], wA3[:, 16:24, :]
                (wBm,), wbd = self.wload([(dr['opB'][fc], 16, 128, 'r')])
                for tb in range(2):
                    sl = slice(tb * 512, (tb + 1) * 512)
                    r = it % 2
                    it += 1
                    pA, dA = self.ps()
                    for kc in range(8):
                        self.mm(pA[:], wA[:, kc, :], self.big[:, kc, sl], kc == 0, kc == 7, [wad, self.d_big[kc][tb]], [dA])
                    pB, dB = self.ps()
                    for kc in range(16):
                        self.mm(pB[:], wBm[:, kc, :], self.big[:, 8 + kc, sl], kc == 0, kc == 15,
                                [wbd, self.d_big[8 + kc][tb]], [dB])
                    pga, dga = self.ps()
                    for kc in range(8):
                        self.mm(pga[:], wga[:, kc, :], nT[:, kc, sl], kc == 0, kc == 7, [wad, dn[kc][tb]], [dga])
                    pgb, dgb = self.ps()
                    for kc in range(8):
                        self.mm(pgb[:], wgb[:, kc, :], nT[:, kc, sl], kc == 0, kc == 7, [wad, dn[kc][tb]], [dgb])
                    self.act(sga[r][:], pga[:], AF.Sigmoid, [dga], [d_sga[r]])
                    self.act(sgb[r][:], pgb[:], AF.Sigmoid, [dgb], [d_sgb[r]])
                    self.tt(m1[r][:], pA[:], sga[r][:], ALU.mult, [dA, d_sga[r]], [d_m1[r]])
                    self.tt(sgb[r][:], pB[:], sgb[r][:], ALU.mult, [dB, d_sgb[r]], [d_sgb[r]])
                    self.tt(mix[:, fc, sl], m1[r][:], sgb[r][:], ALU.add, [d_m1[r], d_sgb[r]], [d_mix[fc][tb]])
            for dcp in range(4):
                (wo,), wod = self.wload([(dr['w_outr'][dcp], 8, 256, 'r')])
                for j in range(2):
                    dc = dcp * 2 + j
                    for tb in range(2):
                        sl = slice(tb * 512, (tb + 1) * 512)
                        po, do = self.ps()
                        for kc in range(8):
                            self.mm(po[:], wo[:, kc, j * 128:(j + 1) * 128], mix[:, kc, sl], kc == 0, kc == 7,
                                    [wod, d_mix[kc][tb]], [do])
                        self.tt(self.hT[:, dc, sl], po[:], self.hT[:, dc, sl], ALU.add, [do, self.d_h[dc][tb]],
                                [self.d_h[dc][tb]])
            self.end_scope()

    def ple(self, blk):
        dr = self.dr
        nT, dn = self.nT, self.d_n
        with contextlib.ExitStack() as st:
            self.norm(3, stack=st)
            sbs = lambda shape, dt, name: self.sb(shape, dt, name, st)
            pst = [sbs([128, 256], F32, "pst") for _ in range(2)]; d_pst = mkdeps(2)
            pT = sbs([128, 2, TB], BF16, "pT"); d_pT = mkdeps(2)
            sg = [sbs([128, 512], F32, "sgp") for _ in range(2)]; d_sg = mkdeps(2)
            for t_ in range(8):
                r = t_ % 2
                src = dr['p'][blk * TB + t_ * 128:blk * TB + (t_ + 1) * 128, :]
                self.P.dma(SP, lambda e, r=r, src=src: e.dma_start(out=pst[r][:], in_=src), (), [d_pst[r]], semkey='pst%d' % r)
                bank, bd = self.ps()
                for kc in range(2):
                    self.tr(bank[:, kc * 128:(kc + 1) * 128], pst[r][:, kc * 128:(kc + 1) * 128], self.ident_f,
                            [d_pst[r], self.d_c], [bd])
                self.cp(pT[:, :, t_ * 128:(t_ + 1) * 128], bank[:, 0:256].rearrange("p (a b) -> p a b", a=2), [bd],
                        [d_pT[t_ // 4]])
            it = 0
            for fcp in range(4):
                (wg, wp), wd = self.wload([(dr['w_ple_gate'][:, fcp * 256:(fcp + 1) * 256], 8, 256),
                                           (dr['w_ple_proj'][:, fcp * 256:(fcp + 1) * 256], 2, 256)])
                for j in range(2):
                    fc = fcp * 2 + j
                    for tb in range(2):
                        sl = slice(tb * 512, (tb + 1) * 512)
                        r = it % 2
                        it += 1
                        pg, dg = self.ps()
                        for kc in range(8):
                            self.mm(pg[:], wg[:, kc, j * 128:(j + 1) * 128], nT[:, kc, sl], kc == 0, kc == 7,
                                    [wd, dn[kc][tb]], [dg])
                        pp, dp = self.ps()
                        for kc in range(2):
                            self.mm(pp[:], wp[:, kc, j * 128:(j + 1) * 128], pT[:, kc, sl], kc == 0, kc == 1,
                                    [wd, d_pT[tb]], [dp])
                        self.act(sg[r][:], pg[:], AF.Sigmoid, [dg], [d_sg[r]])
                        self.tt(sg[r][:], pp[:], sg[r][:], ALU.mult, [dp, d_sg[r]], [d_sg[r]])
                        self.tt(self.hT[:, fc, sl], self.hT[:, fc, sl], sg[r][:], ALU.add, [self.d_h[fc][tb], d_sg[r]],
                                [self.d_h[fc][tb]])
            self.end_scope()

    def store_out(self, blk, final=False):
        dr = self.dr
        with contextlib.ExitStack() as st:
            if final:
                self.norm(4, final=True, stack=st)
            os_ = [self.sb([128, 8, 128], F32, "ostage", st) for _ in range(2)]
            dos = mkdeps(2)
            for kc in range(8):
                s = os_[kc % 2]
                ds = dos[kc % 2]
                for half in range(2):
                    bank, bd = self.ps()
                    for q in range(4):
                        tt_ = half * 4 + q
                        self.tr(bank[:, q * 128:(q + 1) * 128], self.hT[:, kc, tt_ * 128:(tt_ + 1) * 128], self.ident_f,
                                [self.d_h[kc][half], self.d_c], [bd], sig=(q == 3))
                    self.cp(s[:, half * 4:(half + 1) * 4, :], bank[:].rearrange("p (a b) -> p a b", a=4), [bd], [ds])
                dst = dr['out'][blk * TB:(blk + 1) * TB, kc * 128:(kc + 1) * 128].rearrange("(t p) c -> p t c", p=128)
                self.P.dma(SP, lambda e, s=s, dst=dst: e.dma_start(out=dst, in_=s[:]), [ds], (),
                           semkey='os%d' % (kc % 2))
            self.end_scope()

    def run(self):
        dr = self.dr
        self.init()
        stages = ['load', 'norm', 'ffn1', 'hgrn', 'ssd', 'outproj', 'ffn2', 'ple', 'final']
        upto = len(stages) if self.stop_after is None else stages.index(self.stop_after) + 1
        act = stages[:upto]
        for blk in range(NBLK):
            if blk > 0:
                self.load_x(blk)
            if 'ffn1' in act:
                self.ffn(dr['ffn1_w13'], dr['ffn1_w2'], 0)
            if 'hgrn' in act:
                self.hgrn(blk)
            if 'ssd' in act:
                self.ssd(blk)
            if 'outproj' in act:
                self.outproj(blk)
            if 'dbg' in dr:
                allb = [d for row in self.d_big for d in row]
                dst = dr['dbg'][blk * 128:(blk + 1) * 128, :]
                self.P.dma(SP, lambda e, dst=dst: e.dma_start(out=dst, in_=self.big[:].rearrange("p a b -> p (a b)")),
                           allb, (), semkey='dbg')
            if 'ffn2' in act:
                self.ffn(dr['ffn2_w13'], dr['ffn2_w2'], 2)
            if 'ple' in act:
                self.ple(blk)
            self.store_out(blk, final=('final' in act))


def _consts():
    c = np.zeros((128, 7, 128), np.float32)
    i = np.arange(128)
    same = (i[:, None] // 64) == (i[None, :] // 64)
    c[:, 0, :] = np.eye(128)
    c[:, 1, :] = ((i[:, None] <= i[None, :]) & same)
    c[:, 2, :] = ((i[:, None] > i[None, :]) & same)
    c[:, 3, :] = (i[:, None] <= i[None, :])
    c[:, 4, :] = 1.0
    c[0, 6, :] = 1.0
    c[:, 5, :] = np.where(i[:, None] > i[None, :], -1e30, 0.0)
    return c


_IN_SPECS = [
    ('x', [T, D]), ('p', [T, 256]), ('consts', [128, 7, 128]), ('gains', [128, 40]), ('hgn', [128, 8]),
    ('cw', [128, 96]), ('cbp', [128, 24]), ('cbrow', [1, 3072]), ('dt_bias', [32]), ('a_log', [32]),
    ('d_skip', [32]), ('ssm_norm', [2048]), ('hg_lb', [2, 1024]),
    ('ffn1_w13', [D, 2 * DFF]), ('ffn1_w2', [8, 128, DFF]), ('w_in', [D, DIN]),
    ('opA', [8, 128, 3072]), ('opB', [8, 128, 2048]), ('w_outr', [4, 128, 2048]),
    ('ffn2_w13', [D, 2 * DFF]), ('ffn2_w2', [8, 128, DFF]), ('w_ple_gate', [D, D]), ('w_ple_proj', [256, D]),
]


def build(stop_after=None):
    nc = bass.Bass("TRN2", target_bir_lowering=False)
    dram = {}
    for name, shape in _IN_SPECS:
        dram[name] = nc.dram_tensor(name, shape, F32, kind="ExternalInput").ap()
    dram['out'] = nc.dram_tensor("out", [T, D], F32, kind="ExternalOutput").ap()
    if stop_after in ('hgrn', 'ssd'):
        dram['dbg'] = nc.dram_tensor("dbg", [NBLK * 128, 24 * TB], BF16, kind="ExternalOutput").ap()
    with contextlib.ExitStack() as es:
        block = es.enter_context(nc.Block())
        kb = KB(nc, block, es, dram, stop_after=stop_after)
        kb.run()
        print("ops", len(kb.P.ops), "waits", kb.P.nwaits)
    return nc


def make_in_maps(inp):
    f = lambda a: np.ascontiguousarray(np.asarray(a, dtype=np.float32))
    gains = np.stack([f(inp['ffn1_norm'])[0], f(inp['mix_norm'])[0], f(inp['ffn2_norm'])[0],
                      f(inp['ple_norm'])[0], f(inp['final_norm'])], 0)
    gains = gains.reshape(5, 8, 128).transpose(2, 0, 1).reshape(128, 40)
    hgn = f(inp['hg_norm'])[0].reshape(8, 128).T
    cw = f(inp['conv_w'])[0].T.reshape(24, 128, 4).transpose(1, 0, 2).reshape(128, 96)
    cbp = f(inp['conv_b'])[0].reshape(24, 128).T
    def w2r(w):
        return f(w.reshape(NFC, 128, 8, 128).transpose(2, 1, 0, 3).reshape(8, 128, DFF))

    def colblk(w, c0, ncol, nk):
        return w[:, c0:c0 + ncol].reshape(nk, 128, ncol).transpose(1, 0, 2).reshape(128, nk * ncol)

    w_in_ = f(inp['w_in'])[0]
    whg = f(inp['w_hg_out'])[0]
    wssm = f(inp['w_ssm_out'])[0]
    wout = f(inp['w_out'])[0]
    opA = np.stack([np.concatenate([colblk(whg, fc * 128, 128, 8),
                                    colblk(w_in_, OFF_BRG + fc * 128, 128, 8),
                                    colblk(w_in_, OFF_BRG + 1024 + fc * 128, 128, 8)], axis=1) for fc in range(8)], 0)
    opB = np.stack([colblk(wssm, fc * 128, 128, 16) for fc in range(8)], 0)
    w_outr = np.stack([colblk(wout, dcp * 256, 256, 8) for dcp in range(4)], 0)
    shared = dict(
        consts=_consts(), gains=f(gains), hgn=f(hgn), cw=f(cw), cbp=f(cbp), cbrow=f(inp['conv_b']),
        dt_bias=f(inp['dt_bias'])[0], a_log=f(inp['a_log'])[0], d_skip=f(inp['d_skip'])[0],
        ssm_norm=f(inp['ssm_norm'])[0], hg_lb=f(inp['hg_lb']),
        ffn1_w13=f(inp['ffn1_w13'])[0], ffn1_w2=w2r(f(inp['ffn1_w2'])[0]), w_in=w_in_,
        opA=f(opA), opB=f(opB), w_outr=f(w_outr),
        ffn2_w13=f(inp['ffn2_w13'])[0], ffn2_w2=w2r(f(inp['ffn2_w2'])[0]), w_ple_gate=f(inp['w_ple_gate'])[0],
        w_ple_proj=f(inp['w_ple_proj'])[0],
    )
    x = f(inp['x'])
    p = f(inp['p'])[0]
    maps = []
    for b in range(8):
        m = dict(shared)
        m['x'] = x[b]
        m['p'] = p[b]
        maps.append(m)
    return maps


_NC_CACHE = {}


def kernel(**inputs):
    if 'nc' not in _NC_CACHE:
        _NC_CACHE['nc'] = build()
    nc = _NC_CACHE['nc']
    maps = make_in_maps(inputs)
    res = run_bass_kernel_spmd(nc, maps, core_ids=list(range(8)))
    out = np.stack([np.asarray(r['out'], dtype=np.float32) for r in res.results], 0)
    return out
```
